# Optimizing a Trainium2 kernel written in Bass

```python
import math
import jax
import jax.numpy as jnp
from jax import lax
import numpy as np

D_MODEL = 1024
BATCH = 8
SEQ = 4096
DEPTH = 2

CTX_LEN = 256
GRID_W = 64
N_EVEN = (DEPTH + 1) // 2
N_ODD = DEPTH // 2
RMS_EPS = 1e-6
NEG_INF = -1e30
MOD_INIT = 0.5

SSD_HEADS = 16
SSD_HEAD_DIM = 64
SSD_INNER = SSD_HEADS * SSD_HEAD_DIM
SSD_GROUPS = 2
SSD_STATE = 128
SSD_CONV_W = 3
SSD_CHUNK = 128
DT_MIN = 0.001
DT_MAX = 0.1
SSD_GN = SSD_GROUPS * SSD_STATE

NA_HEADS = 16
NA_HEAD_DIM = 64
NA_INNER = NA_HEADS * NA_HEAD_DIM
NA_WIN_R = 8
NA_WIN_C = 16
NA_QBLOCK = 16
NA_KCOLS = 2 * NA_WIN_C

SC_INNER = D_MODEL
SC_CONV_W = 3

D_FF = 4 * D_MODEL

OFF_X = 0
OFF_B = OFF_X + SSD_INNER
OFF_DT = OFF_B + SSD_GN
OFF_K = OFF_DT + 2 * SSD_HEADS
OFF_V = OFF_K + NA_INNER
OFF_C = OFF_V + NA_INNER
OFF_Z = OFF_C + SSD_GN
OFF_Q = OFF_Z + SSD_INNER
IN0_COLS = OFF_Q + NA_INNER
CONV0_CH = SSD_INNER + 2 * SSD_GN

kernel_name = "hybrid_ssd_natten_shortconv_dit"


def rmsnorm(x, w):
    xf = x.astype(jnp.float32)
    xf = xf * lax.rsqrt(jnp.mean(xf * xf, axis=-1, keepdims=True) + RMS_EPS)
    return (xf * w.astype(jnp.float32)).astype(x.dtype)


def modulate(h, shift, scale):
    return h * (1 + scale) + shift


def dwconv_centred(u, w, b=None):
    width, ch = w.shape
    y = lax.conv_general_dilated(
        u, w[:, None, :].astype(u.dtype), window_strides=(1,),
        padding=[(width // 2, width // 2)],
        dimension_numbers=('NWC', 'WIO', 'NWC'), feature_group_count=ch)
    return y if b is None else y + b


def ssd_chunk_inputs(x, dt, A, bm):
    bsz, length, heads, hd = x.shape
    nc, rep = length // SSD_CHUNK, heads // SSD_GROUPS
    xdt = (x * dt[..., None]).reshape(bsz, nc, SSD_CHUNK, SSD_GROUPS, rep, hd)
    a_cum = jnp.cumsum((dt * A).reshape(bsz, nc, SSD_CHUNK, SSD_GROUPS, rep), axis=2)
    bc = bm.reshape(bsz, nc, SSD_CHUNK, SSD_GROUPS, SSD_STATE)
    return xdt, a_cum, bc


def ssd_carry(xdt, a_cum, bc, h0):
    to_end = jnp.exp(a_cum[:, :, -1:] - a_cum)
    chunk_states = jnp.einsum('bcqgn,bcqgrp->bcgrpn', bc, to_end[..., None] * xdt)
    chunk_decay = jnp.exp(a_cum[:, :, -1])

    def step(h, inp):
        s, dcy = inp
        return dcy[..., None, None] * h + s, h

    h_final, h_start = lax.scan(step, h0, (jnp.moveaxis(chunk_states, 1, 0), jnp.moveaxis(chunk_decay, 1, 0)))
    return jnp.moveaxis(h_start, 0, 1), h_final


def ssd_readout(xdt, a_cum, bc, cc, h_start):
    q = a_cum.shape[2]
    lower = jnp.tril(jnp.ones((q, q), dtype=bool))
    a_t = jnp.moveaxis(a_cum, 2, -1)
    seg = a_t[..., :, None] - a_t[..., None, :]
    decay = jnp.exp(jnp.where(lower, seg, -jnp.inf))
    cb = jnp.einsum('bcign,bcjgn->bcgij', cc, bc)
    y_diag = jnp.einsum('bcgrij,bcjgrp->bcigrp', cb[:, :, :, None] * decay, xdt)
    y_off = jnp.einsum('bcign,bcgrpn->bcigrp', cc, h_start) * jnp.exp(a_cum)[..., None]
    return y_diag + y_off


def bidirectional_ssd(x_l, b_l, c_l, dt_l, x_c, b_c, dt_c, dt_bias, a_log, d_skip):
    out_dtype = x_l.dtype
    f32 = jnp.float32
    x_l, b_l, c_l, x_c, b_c = (t.astype(f32) for t in (x_l, b_l, c_l, x_c, b_c))
    bsz, seq, heads, hd = x_l.shape
    rep = heads // SSD_GROUPS
    y = d_skip.astype(f32)[:, None] * x_l
    for d in range(2):
        rev = (lambda t: jnp.flip(t, axis=1)) if d == 1 else (lambda t: t)
        A = -jnp.exp(a_log[d].astype(f32))
        dtc = jax.nn.softplus(dt_c[:, :, d].astype(f32) + dt_bias[d].astype(f32))
        dtl = jax.nn.softplus(dt_l[:, :, d].astype(f32) + dt_bias[d].astype(f32))
        h0 = jnp.zeros((bsz, SSD_GROUPS, rep, hd, SSD_STATE), f32)
        _, h_ctx = ssd_carry(*ssd_chunk_inputs(rev(x_c), rev(dtc), A, rev(b_c)), h0)
        xdt, a_cum, bc = ssd_chunk_inputs(rev(x_l), rev(dtl), A, rev(b_l))
        h_start, _ = ssd_carry(xdt, a_cum, bc, h_ctx)
        cc = rev(c_l).reshape(bc.shape)
        y = y + rev(ssd_readout(xdt, a_cum, bc, cc, h_start).reshape(bsz, seq, heads, hd))
    return y.reshape(bsz, seq, heads * hd).astype(out_dtype)


def neighbourhood_attention(q, k, v, k_ctx, v_ctx, rpb):
    bsz, seq, heads, hd = q.shape
    rows = seq // GRID_W
    wr = min(NA_WIN_R, rows)
    n_blk = GRID_W // NA_QBLOCK
    qg = (q * hd ** -0.5).reshape(bsz, rows, GRID_W, heads, hd)
    kg = k.reshape(bsz, rows, GRID_W, heads, hd)
    vg = v.reshape(bsz, rows, GRID_W, heads, hd)

    q_col = jnp.arange(GRID_W).reshape(n_blk, NA_QBLOCK)
    kc0 = jnp.clip(jnp.arange(n_blk) * NA_QBLOCK - NA_WIN_C // 2, 0, GRID_W - NA_KCOLS)
    k_col = kc0[:, None] + jnp.arange(NA_KCOLS)
    win_start = jnp.clip(q_col - NA_WIN_C // 2, 0, GRID_W - NA_WIN_C)
    kcb = k_col[:, None, :]
    col_ok = (kcb >= win_start[..., None]) & (kcb < win_start[..., None] + NA_WIN_C)
    dc_idx = jnp.clip(kcb - q_col[..., None], -(NA_WIN_C - 1), NA_WIN_C - 1) + NA_WIN_C - 1
    rpb_c = rpb[:, :, dc_idx]
    n_win = wr * NA_KCOLS

    def row_fn(r):
        rs = jnp.clip(r - NA_WIN_R // 2, 0, rows - wr)
        k_win = lax.dynamic_slice_in_dim(kg, rs, wr, axis=1)[:, :, k_col]
        v_win = lax.dynamic_slice_in_dim(vg, rs, wr, axis=1)[:, :, k_col]
        q_r = lax.dynamic_index_in_dim(qg, r, axis=1, keepdims=False).reshape(bsz, n_blk, NA_QBLOCK, heads, hd)
        s_win = jnp.einsum('bjqhd,bwjkhd->bhjqwk', q_r, k_win).astype(jnp.float32)
        dr_idx = rs + jnp.arange(wr) - r + NA_WIN_R - 1
        bias = jnp.take(rpb_c, dr_idx, axis=1).transpose(0, 2, 3, 1, 4).astype(jnp.float32)
        s_win = jnp.where(col_ok[:, :, None, :], s_win + bias, NEG_INF)
        s_ctx = jnp.einsum('bjqhd,bchd->bhjqc', q_r, k_ctx).astype(jnp.float32)
        s = jnp.concatenate([s_win.reshape(bsz, heads, n_blk, NA_QBLOCK, n_win), s_ctx], axis=-1)
        p = jax.nn.softmax(s, axis=-1).astype(v.dtype)
        p_win = p[..., :n_win].reshape(bsz, heads, n_blk, NA_QBLOCK, wr, NA_KCOLS)
        o = (jnp.einsum('bhjqwk,bwjkhd->bjqhd', p_win, v_win)
             + jnp.einsum('bhjqc,bchd->bjqhd', p[..., n_win:], v_ctx))
        return o.reshape(bsz, GRID_W, heads * hd)

    out = lax.map(row_fn, jnp.arange(rows))
    return jnp.moveaxis(out, 0, 1).reshape(bsz, seq, heads * hd)


def ssd_na_mixer(h_lat, h_ctx, in_w, conv_w, conv_b, dt_bias, a_log, d_skip, norm_w, rpb, out_w):
    bsz, seq, _ = h_lat.shape
    n_ctx = h_ctx.shape[1]
    pc = h_ctx @ in_w[:, :OFF_C]
    xb_c = jax.nn.silu(dwconv_centred(pc[..., :OFF_DT], conv_w[:, :OFF_DT], conv_b[:OFF_DT]))
    x_c = xb_c[..., :SSD_INNER].reshape(bsz, n_ctx, SSD_HEADS, SSD_HEAD_DIM)
    b_c = xb_c[..., SSD_INNER:].reshape(bsz, n_ctx, SSD_GROUPS, SSD_STATE)
    dt_c = pc[..., OFF_DT:OFF_K].reshape(bsz, n_ctx, 2, SSD_HEADS)
    k_c = pc[..., OFF_K:OFF_V].reshape(bsz, n_ctx, NA_HEADS, NA_HEAD_DIM)
    v_c = pc[..., OFF_V:OFF_C].reshape(bsz, n_ctx, NA_HEADS, NA_HEAD_DIM)
    pl = h_lat @ in_w
    xbc = jax.nn.silu(dwconv_centred(
        jnp.concatenate([pl[..., :OFF_DT], pl[..., OFF_C:OFF_Z]], axis=-1), conv_w, conv_b))
    x_l = xbc[..., :SSD_INNER].reshape(bsz, seq, SSD_HEADS, SSD_HEAD_DIM)
    b_l = xbc[..., SSD_INNER:OFF_DT].reshape(bsz, seq, SSD_GROUPS, SSD_STATE)
    c_l = xbc[..., OFF_DT:].reshape(bsz, seq, SSD_GROUPS, SSD_STATE)
    dt_l = pl[..., OFF_DT:OFF_K].reshape(bsz, seq, 2, SSD_HEADS)
    k_l = pl[..., OFF_K:OFF_V].reshape(bsz, seq, NA_HEADS, NA_HEAD_DIM)
    v_l = pl[..., OFF_V:OFF_C].reshape(bsz, seq, NA_HEADS, NA_HEAD_DIM)
    z = pl[..., OFF_Z:OFF_Q]
    q_l = pl[..., OFF_Q:].reshape(bsz, seq, NA_HEADS, NA_HEAD_DIM)

    y_ssd = bidirectional_ssd(x_l, b_l, c_l, dt_l, x_c, b_c, dt_c, dt_bias, a_log, d_skip)
    y_ssd = rmsnorm(y_ssd * jax.nn.silu(z), norm_w)
    y_na = neighbourhood_attention(q_l, k_l, v_l, k_c, v_c, rpb)
    return jnp.concatenate([y_ssd.astype(y_na.dtype), y_na], axis=-1) @ out_w


def short_conv_mixer(h, in_w, conv_w, out_w):
    gate_b, gate_c, val = jnp.split(h @ in_w, 3, axis=-1)
    return (gate_b * dwconv_centred(gate_c * val, conv_w)) @ out_w


def sq_relu_mlp(h, w1, w2):
    return jnp.square(jax.nn.relu(h @ w1)) @ w2


def setup_inputs(seed: int = 0) -> dict:
    key = jax.random.key(seed)
    ks = jax.random.split(key, 23)
    D = D_MODEL
    nrm = jax.random.normal
    f32 = jnp.float32
    x = nrm(ks[0], (BATCH, SEQ, D), f32)
    c = nrm(ks[1], (BATCH, D), f32)
    ctx = nrm(ks[2], (BATCH, CTX_LEN, D), f32)
    c_ctx = nrm(ks[3], (D,), f32)
    mod_w = nrm(ks[4], (DEPTH, D, 6 * D), f32) * (MOD_INIT * D ** -0.5)
    mod_b = 0.02 * nrm(ks[5], (DEPTH, 6 * D), f32)
    norm_mix_w = 1.0 + 0.02 * nrm(ks[6], (DEPTH, D), f32)
    norm_mlp_w = 1.0 + 0.02 * nrm(ks[7], (DEPTH, D), f32)
    mlp_w1 = nrm(ks[8], (DEPTH, D, D_FF), f32) * D ** -0.5
    mlp_w2 = nrm(ks[9], (DEPTH, D_FF, D), f32) * D_FF ** -0.5
    ssdna_in_w = nrm(ks[10], (N_EVEN, D, IN0_COLS), f32) * D ** -0.5
    ssdna_conv_w = nrm(ks[11], (N_EVEN, SSD_CONV_W, CONV0_CH), f32) * SSD_CONV_W ** -0.5
    ssdna_conv_b = 0.02 * nrm(ks[12], (N_EVEN, CONV0_CH), f32)
    dt0 = jnp.exp(jax.random.uniform(ks[13], (N_EVEN, 2, SSD_HEADS), f32,
                                     minval=math.log(DT_MIN), maxval=math.log(DT_MAX)))
    ssd_dt_bias = dt0 + jnp.log(-jnp.expm1(-dt0))
    ssd_a_log = jnp.log(jax.random.uniform(ks[14], (N_EVEN, 2, SSD_HEADS), f32, minval=1.0, maxval=16.0))
    ssd_d = 1.0 + 0.02 * nrm(ks[15], (N_EVEN, SSD_HEADS), f32)
    ssd_norm_w = 1.0 + 0.02 * nrm(ks[16], (N_EVEN, SSD_INNER), f32)
    na_rpb = 0.02 * nrm(ks[17], (N_EVEN, NA_HEADS, 2 * NA_WIN_R - 1, 2 * NA_WIN_C - 1), f32)
    ssdna_out_w = nrm(ks[18], (N_EVEN, SSD_INNER + NA_INNER, D), f32) * (SSD_INNER + NA_INNER) ** -0.5
    sc_in_w = nrm(ks[19], (N_ODD, D, 3 * SC_INNER), f32) * D ** -0.5
    sc_conv_w = nrm(ks[20], (N_ODD, SC_CONV_W, SC_INNER), f32) * SC_CONV_W ** -0.5
    sc_out_w = nrm(ks[21], (N_ODD, SC_INNER, D), f32) * SC_INNER ** -0.5
    final_norm_w = 1.0 + 0.02 * nrm(ks[22], (D,), f32)
    return {"x": x, "c": c, "ctx": ctx, "c_ctx": c_ctx,
            "mod_w": mod_w, "mod_b": mod_b, "norm_mix_w": norm_mix_w, "norm_mlp_w": norm_mlp_w,
            "mlp_w1": mlp_w1, "mlp_w2": mlp_w2,
            "ssdna_in_w": ssdna_in_w, "ssdna_conv_w": ssdna_conv_w, "ssdna_conv_b": ssdna_conv_b,
            "ssd_dt_bias": ssd_dt_bias, "ssd_a_log": ssd_a_log, "ssd_d": ssd_d, "ssd_norm_w": ssd_norm_w,
            "na_rpb": na_rpb, "ssdna_out_w": ssdna_out_w,
            "sc_in_w": sc_in_w, "sc_conv_w": sc_conv_w, "sc_out_w": sc_out_w,
            "final_norm_w": final_norm_w}


def reference(x, c, ctx, c_ctx, mod_w, mod_b, norm_mix_w, norm_mlp_w, mlp_w1, mlp_w2,
              ssdna_in_w, ssdna_conv_w, ssdna_conv_b, ssd_dt_bias, ssd_a_log, ssd_d, ssd_norm_w,
              na_rpb, ssdna_out_w, sc_in_w, sc_conv_w, sc_out_w, final_norm_w):
    D = D_MODEL
    for i in range(DEPTH):
        mod = jax.nn.silu(c) @ mod_w[i] + mod_b[i]
        shift_a, scale_a, gate_a, shift_f, scale_f, gate_f = jnp.split(mod[:, None, :], 6, axis=-1)
        h = modulate(rmsnorm(x, norm_mix_w[i]), shift_a, scale_a)
        if i % 2 == 0:
            e = i // 2
            mod_ctx = jax.nn.silu(c_ctx) @ mod_w[i][:, :2 * D] + mod_b[i][:2 * D]
            h_ctx = modulate(rmsnorm(ctx, norm_mix_w[i]), mod_ctx[:D], mod_ctx[D:])
            y = ssd_na_mixer(h, h_ctx, ssdna_in_w[e], ssdna_conv_w[e], ssdna_conv_b[e], ssd_dt_bias[e],
                             ssd_a_log[e], ssd_d[e], ssd_norm_w[e], na_rpb[e], ssdna_out_w[e])
        else:
            o = i // 2
            y = short_conv_mixer(h, sc_in_w[o], sc_conv_w[o], sc_out_w[o])
        x = x + gate_a * y
        h = modulate(rmsnorm(x, norm_mlp_w[i]), shift_f, scale_f)
        x = x + gate_f * sq_relu_mlp(h, mlp_w1[i], mlp_w2[i])
    return rmsnorm(x, final_norm_w)
```

```python
from contextlib import ExitStack
import numpy as np
import concourse.bass as bass
import concourse.mybir as mybir
from concourse.bass_utils import run_bass_kernel_spmd

F32 = mybir.dt.float32
BF16 = mybir.dt.bfloat16
AF = mybir.ActivationFunctionType
ALU = mybir.AluOpType

ENGS = ("pe", "act", "dve", "pool", "sp")

D = 1024
SEQ = 4096
CTX = 256
NTOK = SEQ + CTX
OFF_X, OFF_B, OFF_DT, OFF_K, OFF_V, OFF_C, OFF_Z, OFF_Q = 0, 1024, 1280, 1312, 2336, 3360, 3616, 4640
NEG = -30000.0
EPS = 1e-6


class Buf:
    __slots__ = ("name", "lw", "rd")

    def __init__(self, name=""):
        self.name = name
        self.lw = None
        self.rd = []


class Sched:
    def __init__(self, nc, n_lanes=6, same_eng_sync=True):
        self.nc = nc
        self.q = {e: [] for e in ENGS}
        self.cnt = {e: 0 for e in ENGS}
        self.seen = {e: {} for e in ENGS}
        self.same_eng_sync = same_eng_sync
        self.lanes = {}
        self.lane_rr = {}
        for qe in ("sp", "act", "pool"):
            self.lanes[qe] = [["dma_%s_%d" % (qe, i), 0] for i in range(n_lanes)]
            self.lane_rr[qe] = 0
        self.semkeys = list(ENGS) + [l[0] for qe in self.lanes for l in self.lanes[qe]]

    def _deps(self, reads, writes):
        deps = set()
        for b in reads:
            if b.lw is not None:
                deps.add((b.lw[0], b.lw[1], True))
        for b in writes:
            if b.lw is not None:
                deps.add((b.lw[0], b.lw[1], False))
            for r in b.rd:
                deps.add((r[0], r[1], False))
        return deps

    def _emit_waits(self, eng, deps):
        mx = {}
        for k, v, raw in deps:
            if k == eng:
                if not (raw and self.same_eng_sync and eng in ("act", "dve", "pool")):
                    continue
            if mx.get(k, 0) < v:
                mx[k] = v
        for k, v in mx.items():
            if self.seen[eng].get(k, 0) >= v:
                continue
            self.seen[eng][k] = v
            self.q[eng].append(("wait", k, v))

    def _attr(self, ev, reads, writes):
        for b in reads:
            if len(b.rd) > 64:
                mx = {}
                for k, v in b.rd:
                    if mx.get(k, 0) < v:
                        mx[k] = v
                b.rd = list(mx.items())
            b.rd.append(ev)
        for b in writes:
            b.lw = ev
            b.rd = []

    def op(self, eng, fn, reads=(), writes=(), signal=True):
        self._emit_waits(eng, self._deps(reads, writes))
        ev = (eng, self.cnt[eng] + 1)
        self._attr(ev, reads, writes)
        if signal:
            self.cnt[eng] += 1
            self.q[eng].append(("op", fn, eng))
        else:
            self.q[eng].append(("op", fn, None))
        return ev

    def dma(self, qe, out, in_, reads=(), writes=()):
        lanes = self.lanes[qe]
        i = self.lane_rr[qe]
        self.lane_rr[qe] = (i + 1) % len(lanes)
        lane = lanes[i]
        deps = self._deps(reads, writes)
        if lane[1] > 0:
            deps.add((lane[0], 16 * lane[1], True))
        self._emit_waits(qe, deps)
        lane[1] += 1
        ev = (lane[0], 16 * lane[1])
        self._attr(ev, reads, writes)
        self.q[qe].append(("dma", (out, in_), lane[0]))
        return ev

    def _all_events(self):
        evs = [(e, self.cnt[e]) for e in ENGS if self.cnt[e] > 0]
        for qe in self.lanes:
            for key, c in self.lanes[qe]:
                if c > 0:
                    evs.append((key, 16 * c))
        return evs

    def _force(self, eng, evs):
        for k, v in evs:
            if k == eng or self.seen[eng].get(k, 0) >= v:
                continue
            self.seen[eng][k] = v
            self.q[eng].append(("wait", k, v))

    def barrier(self):
        evs = self._all_events()
        for e in ENGS:
            self._force(e, evs)

    def final_wait(self, eng="sp"):
        self._force(eng, self._all_events())

    def replay(self, stack):
        nc = self.nc
        used = set()
        for e in ENGS:
            for it in self.q[e]:
                if it[0] == "wait":
                    used.add(it[1])
                elif it[2] is not None:
                    used.add(it[2])
        sems = {k: stack.enter_context(nc.semaphore("s_" + k)) for k in self.semkeys if k in used}
        block = stack.enter_context(nc.Block())

        def run(eng_name):
            def body(e):
                for it in self.q[eng_name]:
                    if it[0] == "wait":
                        e.wait_ge(sems[it[1]], it[2])
                    elif it[0] == "op":
                        ins = it[1](e)
                        if it[2] is not None:
                            ins.then_inc(sems[it[2]], 1)
                    else:
                        out, in_ = it[1]
                        e.dma_start(out=out, in_=in_).then_inc(sems[it[2]], 16)
            return body

        if self.q["sp"]:
            block.sync(run("sp"))
        if self.q["pe"]:
            block.tensor(run("pe"))
        if self.q["act"]:
            block.scalar(run("act"))
        if self.q["dve"]:
            block.vector(run("dve"))
        if self.q["pool"]:
            block.gpsimd(run("pool"))


class Arena:
    def __init__(self, ap, words):
        self.ap = ap
        self.words = words
        self.top = 0

    def mark(self):
        return self.top

    def release(self, m):
        self.top = m

    def alloc(self, shape, dtype, parts=128):
        n = 1
        for s in shape:
            n *= s
        nbytes = n * (4 if dtype == F32 else 2)
        w = (nbytes + 3) // 4
        w = (w + 7) // 8 * 8
        assert self.top + w <= self.words, "SBUF arena overflow: need %d have %d" % (self.top + w, self.words)
        v = self.ap[:, self.top:self.top + (nbytes + 3) // 4]
        self.top += w
        if dtype != F32:
            v = v.bitcast(dtype)
        if len(shape) == 2:
            v = v.rearrange("p (a b) -> p a b", a=shape[0])
        elif len(shape) == 3:
            v = v.rearrange("p (a b c) -> p a b c", a=shape[0], b=shape[1])
        return v


def _col(v):
    v = np.asarray(v, np.float32)
    return np.ascontiguousarray(v.reshape(-1, 128).T)


def _wtile(w, ncols):
    K, N = w.shape
    assert K % 128 == 0 and N % ncols == 0
    t = w.reshape(K // 128, 128, N // ncols, ncols).transpose(2, 1, 0, 3)
    return np.ascontiguousarray(t, dtype=np.float32)


IN_GROUPS = [
    ("x0", 0, 512), ("x1", 512, 512), ("B", OFF_B, 256), ("dt", OFF_DT, 32),
    ("k0", OFF_K, 512), ("k1", OFF_K + 512, 512), ("v0", OFF_V, 512), ("v1", OFF_V + 512, 512),
    ("C", OFF_C, 256), ("z0", OFF_Z, 512), ("z1", OFF_Z + 512, 512), ("q0", OFF_Q, 512), ("q1", OFF_Q + 512, 512),
]
IN_GIDX = {g[0]: i for i, g in enumerate(IN_GROUPS)}

COLS = {}
_n = 0
for _name, _w in [("c", 8), ("cctx", 8), ("modb0", 48), ("modb1", 48), ("nmix0", 8), ("nmix1", 8), ("nmlp0", 8),
                  ("nmlp1", 8), ("fnw", 8), ("cw0", 12), ("cw1", 12), ("cw2", 12), ("cb", 12),
                  ("scw0", 8), ("scw1", 8), ("scw2", 8)]:
    COLS[_name] = (_n, _w)
    _n += _w
NCOL = _n
ROWS = {}
_n = 0
for _name, _w in [("ssdnw", 1024), ("dtb", 32), ("alog", 32), ("dskip", 16)]:
    ROWS[_name] = (_n, _w)
    _n += _w
NROW = _n
NCONST = 5 * 128


def prep_inputs(inp):
    f = lambda a: np.ascontiguousarray(np.asarray(a, np.float32))
    shared = {}
    shared["modw"] = np.stack([_wtile(f(inp["mod_w"][l]), 1024) for l in range(2)])
    in_w = f(inp["ssdna_in_w"][0])
    inw = np.zeros((len(IN_GROUPS), 128, 8, 512), np.float32)
    for gi, (nm, c0, nc_) in enumerate(IN_GROUPS):
        inw[gi, :, :, :nc_] = in_w[:, c0:c0 + nc_].reshape(8, 128, nc_).transpose(1, 0, 2)
    shared["inw"] = inw
    shared["outw0"] = _wtile(f(inp["ssdna_out_w"][0]), 256)
    shared["w1"] = np.stack([_wtile(f(inp["mlp_w1"][l]), 512) for l in range(2)])
    shared["w2"] = np.stack([_wtile(f(inp["mlp_w2"][l]), 128) for l in range(2)])
    shared["scin"] = _wtile(f(inp["sc_in_w"][0]), 512)
    shared["scout"] = _wtile(f(inp["sc_out_w"][0]), 512)
    rows = np.zeros((1, NROW), np.float32)
    rows[0, ROWS["ssdnw"][0]:ROWS["ssdnw"][0] + 1024] = f(inp["ssd_norm_w"][0])
    rows[0, ROWS["dtb"][0]:ROWS["dtb"][0] + 32] = f(inp["ssd_dt_bias"][0]).reshape(32)
    rows[0, ROWS["alog"][0]:ROWS["alog"][0] + 32] = f(inp["ssd_a_log"][0]).reshape(32)
    rows[0, ROWS["dskip"][0]:ROWS["dskip"][0] + 16] = f(inp["ssd_d"][0])
    shared["rows"] = rows
    k = np.arange(128)
    consts = np.concatenate([np.eye(128, dtype=np.float32),
                             (k[:, None] <= k[None, :]).astype(np.float32),
                             (k[:, None] >= k[None, :]).astype(np.float32),
                             (k[:, None] < k[None, :]).astype(np.float32),
                             (k[:, None] > k[None, :]).astype(np.float32)], axis=1)
    shared["consts"] = np.ascontiguousarray(consts)
    rpb = f(inp["na_rpb"][0])
    c_ = np.arange(64)
    ws = np.clip(c_ - 8, 0, 48)
    kc_ = np.arange(64)
    ok = (kc_[:, None] >= ws[None, :]) & (kc_[:, None] < ws[None, :] + 16)
    dc = np.clip(kc_[:, None] - c_[None, :], -15, 15) + 15
    tblk = rpb[:, :, dc]
    tblk = np.where(ok[None, None], tblk, np.float32(NEG)).astype(np.float32)
    shared["tblk"] = np.ascontiguousarray(tblk)
    cols_common = np.zeros((128, NCOL), np.float32)

    def put(name, arr):
        o, w = COLS[name]
        cols_common[:, o:o + w] = arr
    put("cctx", _col(inp["c_ctx"]))
    for l in range(2):
        put("modb%d" % l, _col(inp["mod_b"][l]))
        put("nmix%d" % l, _col(inp["norm_mix_w"][l]))
        put("nmlp%d" % l, _col(inp["norm_mlp_w"][l]))
    put("fnw", _col(inp["final_norm_w"]))
    cw = f(inp["ssdna_conv_w"][0])
    for t in range(3):
        put("cw%d" % t, _col(cw[t]))
        put("scw%d" % t, _col(f(inp["sc_conv_w"][0])[t]))
    put("cb", _col(inp["ssdna_conv_b"][0]))
    per_core = []
    x = np.asarray(inp["x"], np.float32)
    ctx = np.asarray(inp["ctx"], np.float32)
    for b in range(8):
        cols = cols_common.copy()
        o, w = COLS["c"]
        cols[:, o:o + w] = _col(inp["c"][b])
        xT = np.ascontiguousarray(x[b].reshape(SEQ, 8, 128).transpose(2, 1, 0))
        cT = np.ascontiguousarray(ctx[b].reshape(CTX, 8, 128).transpose(2, 1, 0))
        per_core.append({"xT": xT, "ctxT": cT, "cols": cols})
    return shared, per_core


class Prog:
    def __init__(self, stop_after=None, debug=()):
        self.stop_after = stop_after
        self.debug = debug
        nc = self.nc = bass.Bass("TRN2", target_bir_lowering=False)
        self.dr = {}
        ein = lambda n, s: nc.dram_tensor(n, list(s), F32, kind="ExternalInput").ap()
        self.xT = ein("xT", (128, 8, SEQ))
        self.ctxT = ein("ctxT", (128, 8, CTX))
        self.cols_d = ein("cols", (128, NCOL))
        self.rows_d = ein("rows", (1, NROW))
        self.consts_d = ein("consts", (128, NCONST))
        self.modw = ein("modw", (2, 6, 128, 8, 1024))
        self.inw = ein("inw", (len(IN_GROUPS), 128, 8, 512))
        self.outw0 = ein("outw0", (4, 128, 16, 256))
        self.w1 = ein("w1", (2, 8, 128, 8, 512))
        self.w2 = ein("w2", (2, 8, 128, 32, 128))
        self.scin = ein("scin", (6, 128, 8, 512))
        self.scout = ein("scout", (2, 128, 8, 512))
        self.tblk = ein("tblk", (16, 15, 64, 64))
        self.outT = nc.dram_tensor("outT", [128, 8, SEQ], F32, kind="ExternalOutput").ap()
        scr = lambda n, s, dt: nc.dram_tensor(n, list(s), dt, kind="Internal").ap()
        self.s_xtok = scr("s_xtok", (NTOK, 1024), BF16)
        self.s_BT = scr("s_BT", (256, NTOK), BF16)
        self.s_CT = scr("s_CT", (256, SEQ), BF16)
        self.s_Btok = scr("s_Btok", (NTOK, 256), BF16)
        self.s_z = scr("s_z", (SEQ, 1024), BF16)
        self.s_qT2 = scr("s_qT", (128, 8, SEQ), BF16)
        self.s_kT2 = scr("s_kT", (128, 8, NTOK), BF16)
        self.s_qT = self.s_qT2.rearrange("(par d) c t -> d c par t", par=2)
        self.s_kT = self.s_kT2.rearrange("(par d) c t -> d c par t", par=2)
        self.s_vtok = scr("s_vtok", (NTOK, 16, 65), BF16)
        self.s_ycatT = scr("s_ycatT", (128, 16, SEQ), BF16)
        self.wsb = {"in": scr("wsb_in", (len(IN_GROUPS), 128, 4096), BF16), "out0": scr("wsb_out0", (4, 128, 4096), BF16),
                    "w1": scr("wsb_w1", (2, 8, 128, 4096), BF16), "w2": scr("wsb_w2", (2, 8, 128, 4096), BF16),
                    "scin": scr("wsb_scin", (6, 128, 4096), BF16), "scout": scr("wsb_scout", (2, 128, 4096), BF16)}
        self.b_wsb = {}
        self.dbg_out = {}

    def dbg_tensor(self, name, shape, dtype=F32):
        ap = self.nc.dram_tensor("dbg_" + name, list(shape), dtype, kind="ExternalOutput").ap()
        self.dbg_out[name] = ap
        return ap

    def mm(self, out, lhsT, rhs, start, stop, reads, writes, signal, skip=False):
        if skip:
            self.S.op("pe", lambda e, o=out, l=lhsT, r=rhs, s=start, t=stop: e.matmul(o, l, r, start=s, stop=t, skip_group_check=True),
                      reads=reads, writes=writes, signal=signal)
            return
        self.S.op("pe", lambda e, o=out, l=lhsT, r=rhs, s=start, t=stop: e.matmul(o, l, r, start=s, stop=t),
                  reads=reads, writes=writes, signal=signal)

    def tr(self, out, in_, ident, reads, writes, signal=True):
        self.S.op("pe", lambda e, o=out, i=in_, d=ident: e.transpose(o, i, d), reads=reads, writes=writes, signal=signal)

    def act(self, out, in_, func, reads, writes, bias=None, scale=None, accum_out=None, eng="act"):
        kw = {}
        if bias is not None:
            kw["bias"] = bias
        if scale is not None:
            kw["scale"] = scale
        if accum_out is not None:
            kw["accum_out"] = accum_out
        self.S.op(eng, lambda e, o=out, i=in_, f=func, kw=kw: e.activation(out=o, in_=i, func=f, **kw),
                  reads=reads, writes=writes)

    def tt(self, eng, out, in0, in1, op, reads, writes):
        self.S.op(eng, lambda e, o=out, a=in0, b=in1, p=op: e.tensor_tensor(out=o, in0=a, in1=b, op=p),
                  reads=reads, writes=writes)

    def ts(self, eng, out, in0, s1, op0, reads, writes, s2=None, op1=None):
        if op1 is None:
            self.S.op(eng, lambda e, o=out, a=in0, s=s1, p=op0: e.tensor_scalar(out=o, in0=a, scalar1=s, scalar2=None, op0=p),
                      reads=reads, writes=writes)
        else:
            self.S.op(eng, lambda e, o=out, a=in0, s=s1, t=s2, p=op0, q=op1:
                      e.tensor_scalar(out=o, in0=a, scalar1=s, scalar2=t, op0=p, op1=q), reads=reads, writes=writes)

    def stt(self, eng, out, in0, scalar, in1, op0, op1, reads, writes):
        self.S.op(eng, lambda e, o=out, a=in0, s=scalar, b=in1, p=op0, q=op1:
                  e.scalar_tensor_tensor(out=o, in0=a, scalar=s, in1=b, op0=p, op1=q), reads=reads, writes=writes)

    def cp(self, eng, out, in_, reads, writes):
        if eng == "act":
            self.S.op("act", lambda e, o=out, i=in_: e.activation(out=o, in_=i, func=AF.Copy), reads=reads, writes=writes)
        else:
            self.S.op(eng, lambda e, o=out, i=in_: e.tensor_copy(out=o, in_=i), reads=reads, writes=writes)

    def recip(self, out, in_, reads, writes):
        self.S.op("dve", lambda e, o=out, i=in_: e.reciprocal(out=o, in_=i), reads=reads, writes=writes)

    def memset(self, eng, ap, val, writes):
        self.S.op(eng, lambda e, a=ap, v=val: e.memset(a, v), reads=(), writes=writes)

    def dma(self, qe, out, in_, reads=(), writes=()):
        self.S.dma(qe, out, in_, reads=reads, writes=writes)

    def build(self):
        nc = self.nc
        with ExitStack() as st:
            words = 51 * 1024
            arena_t = st.enter_context(nc.sbuf_tensor("arena", [128, words], F32))
            self.A = Arena(arena_t, words)
            self.ps = [st.enter_context(nc.psum_tensor("ps%d" % i, [128, 512], F32)) for i in range(8)]
            self.psb = [Buf("ps%d" % i) for i in range(8)]
            self.S = Sched(nc)
            self.phase_setup()
            self.precast("in")
            done = self.run_phases()
            self.S.barrier()
            self.S.final_wait("sp")
            self.S.replay(st)
        return nc

    def run_phases(self):
        self.phase_mod_and_h()
        if self.stop_after == "h":
            return
        self.S.barrier()
        self.phase_inproj()
        if self.stop_after == "inproj":
            return
        self.S.barrier()
        if "skip_ssd" not in self.debug:
            self.phase_ssd()
        else:
            self.b_ycat = Buf("ycat")
        if self.stop_after == "ssd":
            return
        self.S.barrier()
        if "skip_na" not in self.debug:
            self.phase_na()
        if self.stop_after == "na":
            return
        self.S.barrier()
        self.phase_tail()

    def phase_setup(self):
        A = self.A
        self.cols = A.alloc([NCOL], F32)
        self.b_cols = Buf("cols")
        self.dma("sp", self.cols, self.cols_d, writes=[self.b_cols])
        self.consts = A.alloc([NCONST], F32)
        self.b_consts = Buf("consts")
        self.dma("sp", self.consts, self.consts_d, writes=[self.b_consts])
        self.ident_f = self.consts[:, 0:128]
        self.L_f = self.consts[:, 128:256]
        self.U_f = self.consts[:, 256:384]
        self.SL_f = self.consts[:, 384:512]
        self.SU_f = self.consts[:, 512:640]
        self.dt_raw = A.alloc([34, 32], F32)
        self.b_dtraw = Buf("dtraw")
        self.ident_b = A.alloc([128], BF16)
        self.ones_b = A.alloc([128], BF16)
        self.b_cb = Buf("constb")
        self.cp("dve", self.ident_b, self.ident_f, [self.b_consts], [self.b_cb])
        self.memset("dve", self.ones_b, 1.0, [self.b_cb])
        self.modv = [A.alloc([48], F32) for _ in range(2)]
        self.b_modv = [[Buf("modv%d_%d" % (l, v)) for v in range(6)] for l in range(2)]
        self.modc = A.alloc([16], F32)
        self.b_modc = Buf("modc")
        self.gcols = A.alloc([64], F32)
        self.b_g = {}

    def precast(self, which):
        def one(key, idx, src):
            b = Buf("wsb_%s_%s" % (key, idx))
            self.b_wsb[(key,) + idx] = b
            dst = self.wsb[key]
            for i in idx:
                dst = dst[i]
                src = src[i]
            self.dma("pool", dst, src.rearrange("p k n -> p (k n)"), writes=[b])
        if which == "in":
            for g in range(len(IN_GROUPS)):
                one("in", (g,), self.inw)
            return
        for g in range(4):
            one("out0", (g,), self.outw0)
        for l in range(2):
            if l == 1:
                for g in range(6):
                    one("scin", (g,), self.scin)
                for g in range(2):
                    one("scout", (g,), self.scout)
            for g in range(8):
                one("w1", (l, g), self.w1)
            for g in range(8):
                one("w2", (l, g), self.w2)

    def wsrc(self, key, idx, kc):
        src = self.wsb[key]
        for i in idx:
            src = src[i]
        return src.rearrange("p (k n) -> p k n", k=kc), self.b_wsb[(key,) + idx]

    def colv(self, name, j=None):
        o, w = COLS[name]
        if j is None:
            return self.cols[:, o:o + w]
        return self.cols[:, o + j:o + j + 1]

    def phase_mod_and_h(self):
        A, S = self.A, self.S
        m0 = A.mark()
        self.mark_base = m0
        self.hT = A.alloc([8, SEQ], BF16)
        self.hcT = A.alloc([8, CTX], BF16)
        self.b_hT = [Buf("hT%d" % t) for t in range(8)]
        self.b_hcT = Buf("hcT")
        m1 = A.mark()
        s2 = A.alloc([8, 2], F32)
        b_s2 = Buf("s2")
        self.act(s2[:, :, 0], self.colv("c"), AF.Silu, [self.b_cols], [b_s2])
        self.act(s2[:, :, 1], self.colv("cctx"), AF.Silu, [self.b_cols], [b_s2])
        wbuf = [A.alloc([8, 1024], F32) for _ in range(2)]
        b_w = [Buf("modw%d" % i) for i in range(2)]
        modcnt = [0]

        def mod_group(l, v):
            n = modcnt[0]
            modcnt[0] += 1
            wb, bw = wbuf[n % 2], b_w[n % 2]
            self.dma("sp", wb, self.modw[l, v], writes=[bw])
            pst = self.ps[n % 2][:, 0:16].rearrange("p (j t) -> p j t", t=2)
            bps = self.psb[n % 2]
            for j in range(8):
                for kc in range(8):
                    self.mm(pst[:, j, :], wb[:, kc, j * 128:(j + 1) * 128], s2[:, kc, :], kc == 0, kc == 7,
                            [bw, b_s2], [bps], signal=(j == 7 and kc == 7))
            o, _ = COLS["modb%d" % l]
            self.tt("dve", self.modv[l][:, v * 8:(v + 1) * 8], pst[:, :, 0], self.cols[:, o + v * 8:o + v * 8 + 8],
                    ALU.add, [bps, self.b_cols], [self.b_modv[l][v]])
            if l == 0 and v < 2:
                self.tt("dve", self.modc[:, v * 8:(v + 1) * 8], pst[:, :, 1], self.cols[:, o + v * 8:o + v * 8 + 8],
                        ALU.add, [bps, self.b_cols], [self.b_modc])

        mod_group(0, 0)
        mod_group(0, 1)
        rest = [(0, v) for v in range(2, 6)] + [(1, v) for v in range(6)]
        def gain(slot, normname, scale_ap, rd):
            g = self.gcols[:, slot * 8:(slot + 1) * 8]
            b = Buf("g%d" % slot)
            self.stt("dve", g, scale_ap, 1.0, self.colv(normname), ALU.add, ALU.mult, rd + [self.b_cols], [b])
            return g, b
        self.g_a0, self.b_ga0 = gain(0, "nmix0", self.modv[0][:, 8:16], [self.b_modv[0][1]])
        self.g_c, self.b_gc = gain(4, "nmix0", self.modc[:, 8:16], [self.b_modc])
        xbuf = [A.alloc([8, 512], F32) for _ in range(2)]
        b_x = [Buf("xb%d" % i) for i in range(2)]
        wk = self.norm_alloc(512)
        wk["use_pool"] = False
        for t in range(9):
            xb, bx = xbuf[t % 2], b_x[t % 2]
            if t < 8:
                ntok = 512
                self.dma("sp", xb, self.xT[:, :, t * 512:(t + 1) * 512], writes=[bx])
                self.norm_tile(xb, bx, ntok, self.g_a0, self.b_ga0, self.modv[0][:, 0:8], self.b_modv[0][0],
                               self.hT[:, :, t * 512:(t + 1) * 512], self.b_hT[t], wk, t)
            else:
                ntok = CTX
                self.dma("sp", xb[:, :, 0:CTX], self.ctxT, writes=[bx])
                self.norm_tile(xb[:, :, 0:CTX], bx, ntok, self.g_c, self.b_gc, self.modc[:, 0:8], self.b_modc,
                               self.hcT, self.b_hcT, wk, t)
            for _ in range(2 if t == 0 else 1):
                if rest:
                    mod_group(*rest.pop(0))
        while rest:
            mod_group(*rest.pop(0))
        self.g_f0, self.b_gf0 = gain(1, "nmlp0", self.modv[0][:, 32:40], [self.b_modv[0][4]])
        self.g_a1, self.b_ga1 = gain(2, "nmix1", self.modv[1][:, 8:16], [self.b_modv[1][1]])
        self.g_f1, self.b_gf1 = gain(3, "nmlp1", self.modv[1][:, 32:40], [self.b_modv[1][4]])
        self.precast("tail")
        if "h" in self.debug:
            d = self.dbg_tensor("hT", (128, 8, SEQ), BF16)
            self.dma("sp", d, self.hT, reads=self.b_hT)
            d = self.dbg_tensor("hcT", (128, 8, CTX), BF16)
            self.dma("sp", d, self.hcT, reads=[self.b_hcT])
            d = self.dbg_tensor("modv0", (128, 48))
            self.dma("sp", d, self.modv[0], reads=self.b_modv[0])
            d = self.dbg_tensor("modv1", (128, 48))
            self.dma("sp", d, self.modv[1], reads=self.b_modv[1])
        self.mark_after_h = m1

    def norm_alloc(self, ntok):
        A = self.A
        wk = {"sq": [A.alloc([8, ntok], BF16) for _ in range(2)], "b_sq": [Buf("sq0"), Buf("sq1")],
              "rstd": [A.alloc([ntok], F32) for _ in range(2)], "b_rstd": [Buf("rs0"), Buf("rs1")],
              "tmp": [A.alloc([ntok], F32) for _ in range(3)], "b_tmp": [Buf("nt%d" % i) for i in range(3)],
              "n": 0}
        return wk

    def norm_accum(self, x, bx, m, wk, it, psi):
        i2 = it % 2
        sq = wk["sq"][i2]
        bsq = wk["b_sqc"][m]
        ps, bps = self.ps[psi + i2], self.psb[psi + i2]
        self.act(sq[:, m, :], x[:, m, :], AF.Square, [bx], [bsq])
        return lambda: self.mm(ps[:, :], self.ones_b, sq[:, m, :], m == 0, m == 7, [bsq, self.b_cb], [bps], signal=(m == 7))

    def norm_tile(self, x, bx, ntok, g, bg, shift, bshift, out, bout, wk, it, psi=2, final=False, accum_done=False):
        i2 = it % 2
        sq, bsq = wk["sq"][i2], wk["b_sq"][i2]
        rstd, brs = wk["rstd"][i2], wk["b_rstd"][i2]
        ps, bps = self.ps[psi + i2], self.psb[psi + i2]
        if not accum_done:
            self.act(sq[:, :, 0:ntok], x, AF.Square, [bx], [bsq])
            for kc in range(8):
                self.mm(ps[:, 0:ntok], self.ones_b, sq[:, kc, 0:ntok], kc == 0, kc == 7, [bsq, self.b_cb], [bps], signal=(kc == 7))
        self.ts("dve", rstd[:, 0:ntok], ps[:, 0:ntok], 1.0 / D, ALU.mult, [bps], [brs], s2=EPS, op1=ALU.add)
        self.act(rstd[:, 0:ntok], rstd[:, 0:ntok], AF.Sqrt, [brs], [brs])
        self.recip(rstd[:, 0:ntok], rstd[:, 0:ntok], [brs], [brs])
        for kc in range(8):
            k3 = wk["n"] % 3
            wk["n"] += 1
            tmp, btmp = wk["tmp"][k3], wk["b_tmp"][k3]
            bxk = bx[kc] if isinstance(bx, list) else bx
            self.tt("dve" if (kc % 2 == 0 or not wk.get("use_pool", True)) else "pool", tmp[:, 0:ntok], x[:, kc, :], rstd[:, 0:ntok],
                    ALU.mult, [bxk, brs], [btmp])
            boutk = bout[kc] if isinstance(bout, list) else bout
            if final:
                self.act(out[:, kc, :], tmp[:, 0:ntok], AF.Identity, [btmp, bg], [boutk], scale=g[:, kc:kc + 1])
            else:
                self.act(out[:, kc, :], tmp[:, 0:ntok], AF.Identity, [btmp, bshift, bg], [boutk], bias=shift[:, kc:kc + 1],
                         scale=g[:, kc:kc + 1])

    def phase_inproj(self):
        A, S = self.A, self.S
        A.release(self.mark_after_h)
        wts = [A.alloc([8, 512], BF16) for _ in range(3)]
        b_wts = [Buf("inw%d" % i) for i in range(3)]
        pre = [A.alloc([SEQ + 2 + CTX + 2], F32) for _ in range(2)]
        b_pre = [Buf("pre0"), Buf("pre1")]
        for i in range(2):
            for c in (0, SEQ + 1, SEQ + 2, SEQ + 2 + CTX + 1):
                self.memset("pool", pre[i][:, c:c + 1], 0.0, [b_pre[i]])
        cacc = A.alloc([NTOK], F32)
        b_cacc = Buf("cacc")
        cv = [A.alloc([NTOK], BF16) for _ in range(2)]
        b_cv = [Buf("cv0"), Buf("cv1")]
        xst = [A.alloc([34, 128], BF16) for _ in range(2)]
        b_xst = [Buf("xst0"), Buf("xst1")]
        NK = 6
        kst = [A.alloc([512], BF16) for _ in range(NK)]
        b_kst = [Buf("kst%d" % i) for i in range(NK)]
        vst = [A.alloc([16, 65], BF16) for _ in range(2)]
        b_vst = [Buf("vst0"), Buf("vst1")]
        for i in range(2):
            self.memset("pool", vst[i][:, :, 64:65], 1.0, [b_vst[i]])
        zst = [A.alloc([1024], BF16) for _ in range(2)]
        b_zst = [Buf("zst0"), Buf("zst1")]
        st = {"w": 0, "ps": 0, "pre": 0, "cv": 0, "xst": 0, "kst": 0, "vst": 0, "zst": 0, "tp": 0}
        psT = [self.ps[4][:, :].bitcast(BF16), self.ps[5][:, :].bitcast(BF16)]
        b_scr = {k: Buf(k) for k in ("xtok", "BT", "CT", "Btok", "z", "qT", "kT", "vtok")}
        self.b_scr = b_scr

        def load_w(gname):
            i = st["w"] % 3
            st["w"] += 1
            src, bsrc = self.wsrc("in", (IN_GIDX[gname],), 8)
            self.dma("sp", wts[i], src, reads=[bsrc], writes=[b_wts[i]])
            return wts[i], b_wts[i]

        def fm_tile(wt, bw, j, tt):
            pi = st["ps"] % 4
            st["ps"] += 1
            ps, bps = self.ps[pi], self.psb[pi]
            if tt < 8:
                ntok = 512
                rhs = lambda kc: self.hT[:, kc, tt * 512:(tt + 1) * 512]
                rb = self.b_hT[tt]
            else:
                ntok = CTX
                rhs = lambda kc: self.hcT[:, kc, :]
                rb = self.b_hcT
            for kc in range(8):
                self.mm(ps[:, 0:ntok], wt[:, kc, j * 128:(j + 1) * 128], rhs(kc), kc == 0, kc == 7, [bw, rb], [bps],
                        signal=(kc == 7))
            return ps[:, 0:ntok], bps

        def conv_chunk(wt, bw, j, ch, has_ctx):
            i = st["pre"] % 2
            st["pre"] += 1
            p, bp = pre[i], b_pre[i]
            for tt in range(9 if has_ctx else 8):
                ps, bps = fm_tile(wt, bw, j, tt)
                if tt < 8:
                    dst = p[:, 1 + tt * 512:1 + (tt + 1) * 512]
                else:
                    dst = p[:, SEQ + 3:SEQ + 3 + CTX]
                self.cp("act", dst, ps, [bps], [bp])
            ci = st["cv"] % 2
            st["cv"] += 1
            c, bc = cv[ci], b_cv[ci]
            w0, w1, w2, cb = (self.colv("cw0", ch), self.colv("cw1", ch), self.colv("cw2", ch), self.colv("cb", ch))
            segs = [(0, 0, SEQ)] + ([(SEQ + 2, SEQ, CTX)] if has_ctx else [])
            for (po, co, n) in segs:
                self.act(cacc[:, co:co + n], p[:, po:po + n], AF.Identity, [bp, self.b_cols], [b_cacc], bias=cb, scale=w0)
                self.stt("dve", cacc[:, co:co + n], p[:, po + 1:po + 1 + n], w1, cacc[:, co:co + n], ALU.mult, ALU.add,
                         [bp, b_cacc, self.b_cols], [b_cacc])
                self.stt("dve", cacc[:, co:co + n], p[:, po + 2:po + 2 + n], w2, cacc[:, co:co + n], ALU.mult, ALU.add,
                         [bp, b_cacc, self.b_cols], [b_cacc])
                self.act(c[:, co:co + n], cacc[:, co:co + n], AF.Silu, [b_cacc], [bc])
            return c, bc

        def to_tokmajor(c, bc, nblk, dst_ap, bdst):
            xi = st["xst"] % 2
            st["xst"] += 1
            xs_, bxs = xst[xi], b_xst[xi]
            blk = 0
            while blk < nblk:
                nb = min(8, nblk - blk)
                ti = st["tp"] % 2
                st["tp"] += 1
                pt, bpt = psT[ti], self.psb[4 + ti]
                for q in range(nb):
                    self.tr(pt[:, q * 128:(q + 1) * 128], c[:, (blk + q) * 128:(blk + q + 1) * 128], self.ident_b,
                            [bc, self.b_cb], [bpt], signal=(q == nb - 1))
                self.cp("dve", xs_[:, blk:blk + nb, :], pt[:, 0:nb * 128].rearrange("p (a b) -> p a b", a=nb), [bpt], [bxs])
                blk += nb
            for b0 in range(0, nblk, 9):
                b1 = min(nblk, b0 + 9)
                self.dma("sp", dst_ap[:, b0:b1, :], xs_[:, b0:b1, :], reads=[bxs], writes=[bdst])

        xtok_v = self.s_xtok.rearrange("(b p) c -> p b c", p=128)
        btok_v = self.s_Btok.rearrange("(b p) c -> p b c", p=128)
        deferred = []

        def defer(fn):
            if deferred:
                deferred.pop(0)()
            if fn is not None:
                deferred.append(fn)

        for gi, gname in enumerate(("x0", "x1")):
            wt, bw = load_w(gname)
            for j in range(4):
                ch = gi * 4 + j
                c, bc = conv_chunk(wt, bw, j, ch, True)
                defer(lambda c=c, bc=bc, ch=ch: to_tokmajor(c, bc, 34, xtok_v[:, :, ch * 128:(ch + 1) * 128], b_scr["xtok"]))
        wt, bw = load_w("B")
        for j in range(2):
            c, bc = conv_chunk(wt, bw, j, 8 + j, True)
            self.dma("sp", self.s_BT[j * 128:(j + 1) * 128, :], c, reads=[bc], writes=[b_scr["BT"]])
            defer(lambda c=c, bc=bc, j=j: to_tokmajor(c, bc, 34, btok_v[:, :, j * 128:(j + 1) * 128], b_scr["Btok"]))
        wt, bw = load_w("C")
        for j in range(2):
            c, bc = conv_chunk(wt, bw, j, 10 + j, False)
            self.dma("sp", self.s_CT[j * 128:(j + 1) * 128, :], c[:, 0:SEQ], reads=[bc], writes=[b_scr["CT"]])
            defer(None)
        wt, bw = load_w("dt")
        for blk in range(34):
            ps, bps = self.ps[6], self.psb[6]
            o = (blk % 8) * 32
            for kc in range(8):
                lhs = self.hT[:, kc, blk * 128:(blk + 1) * 128] if blk < 32 else self.hcT[:, kc, (blk - 32) * 128:(blk - 31) * 128]
                rb = self.b_hT[blk // 4] if blk < 32 else self.b_hcT
                self.mm(ps[:, o:o + 32], lhs, wt[:, kc, 0:32], kc == 0, kc == 7, [bw, rb], [bps], signal=(kc == 7))
            self.cp("dve", self.dt_raw[:, blk, :], ps[:, o:o + 32], [bps], [self.b_dtraw])
        for gname, dst, bd, has_ctx, scale in (("k0", self.s_kT2, b_scr["kT"], True, None), ("k1", self.s_kT2, b_scr["kT"], True, None),
                                               ("q0", self.s_qT2, b_scr["qT"], False, 0.125), ("q1", self.s_qT2, b_scr["qT"], False, 0.125)):
            wt, bw = load_w(gname)
            for j in range(4):
                hp = (int(gname[1]) * 4 + j) * 2
                for tt in range(9 if has_ctx else 8):
                    ps, bps = fm_tile(wt, bw, j, tt)
                    ntok = 512 if tt < 8 else CTX
                    ki = st["kst"] % NK
                    st["kst"] += 1
                    ks, bks = kst[ki], b_kst[ki]
                    if scale is None:
                        self.cp("act", ks[:, 0:ntok], ps, [bps], [bks])
                    else:
                        self.act(ks[:, 0:ntok], ps, AF.Copy, [bps], [bks], scale=scale)
                    t0 = tt * 512
                    self.dma("sp", dst[:, hp // 2, t0:t0 + ntok], ks[:, 0:ntok], reads=[bks], writes=[bd])
        for which in ("v", "z"):
            w0, bw0 = load_w(which + "0")
            w1, bw1 = load_w(which + "1")
            nblk = 34 if which == "v" else 32
            for blk in range(nblk):
                lhs = (lambda kc, blk=blk: self.hT[:, kc, blk * 128:(blk + 1) * 128]) if blk < 32 else \
                    (lambda kc, blk=blk: self.hcT[:, kc, (blk - 32) * 128:(blk - 31) * 128])
                rb = self.b_hT[blk // 4] if blk < 32 else self.b_hcT
                if which == "v":
                    si = st["vst"] % 2
                    st["vst"] += 1
                    sb, bsb = vst[si], b_vst[si]
                else:
                    si = st["zst"] % 2
                    st["zst"] += 1
                    sb, bsb = zst[si], b_zst[si]
                for half, (wt, bw) in enumerate(((w0, bw0), (w1, bw1))):
                    pi = st["ps"] % 4
                    st["ps"] += 1
                    ps, bps = self.ps[pi], self.psb[pi]
                    for kc in range(8):
                        self.mm(ps[:, :], lhs(kc), wt[:, kc, :], kc == 0, kc == 7, [bw, rb], [bps], signal=(kc == 7))
                    if which == "v":
                        self.cp("act" if half == 0 else "dve", sb[:, half * 8:(half + 1) * 8, 0:64],
                                ps[:, :].rearrange("p (a b) -> p a b", a=8), [bps], [bsb])
                    else:
                        self.act(sb[:, half * 512:(half + 1) * 512], ps[:, :], AF.Silu, [bps], [bsb])
                if which == "v":
                    self.dma("sp", self.s_vtok[blk * 128:(blk + 1) * 128, :, :], sb, reads=[bsb], writes=[b_scr["vtok"]])
                else:
                    self.dma("sp", self.s_z[blk * 128:(blk + 1) * 128, :], sb, reads=[bsb], writes=[b_scr["z"]])
        if "inproj" in self.debug:
            for nm, ap in (("xtok", self.s_xtok), ("BT", self.s_BT), ("CT", self.s_CT), ("Btok", self.s_Btok), ("z", self.s_z),
                           ("qT", self.s_qT2), ("kT", self.s_kT2), ("vtok", self.s_vtok)):
                d = self.dbg_tensor(nm, ap.shape, BF16)
                self.dma("sp", d, ap, reads=[b_scr[nm]])
            d = self.dbg_tensor("dtraw", (128, 34, 32))
            self.dma("sp", d, self.dt_raw, reads=[self.b_dtraw])


    def phase_ssd(self):
        A, S = self.A, self.S
        A.release(self.mark_base)
        b_scr = self.b_scr
        self.b_ycat = Buf("ycat")
        rows = A.alloc([NROW], F32)
        b_rows = Buf("rows")
        self.dma("sp", rows, self.rows_d.partition_broadcast(128), writes=[b_rows])
        rv = lambda nm: rows[:, ROWS[nm][0]:ROWS[nm][0] + ROWS[nm][1]]
        normw_bc, dtb_bc, alog_bc, D_bc = rv("ssdnw"), rv("dtb"), rv("alog"), rv("dskip")
        dtv = A.alloc([34, 32], F32)
        av = A.alloc([34, 32], F32)
        aneg = A.alloc([32], F32)
        b_dt = Buf("dtv")
        self.tt("dve", dtv, self.dt_raw, dtb_bc[:, None, :].to_broadcast([128, 34, 32]), ALU.add, [self.b_dtraw, b_rows], [b_dt])
        self.act(dtv, dtv, AF.Exp, [b_dt], [b_dt])
        self.act(dtv, dtv, AF.Ln, [b_dt], [b_dt], bias=1.0, scale=1.0)
        self.act(aneg, alog_bc, AF.Exp, [b_rows], [b_dt])
        self.ts("dve", aneg, aneg, -1.0, ALU.mult, [b_dt], [b_dt])
        self.tt("dve", av, dtv, aneg[:, None, :].to_broadcast([128, 34, 32]), ALU.mult, [b_dt], [b_dt])
        negL = A.alloc([128], F32)
        negU = A.alloc([128], F32)
        ones_f = A.alloc([128], F32)
        b_c2 = Buf("c2")
        self.ts("dve", negL, self.L_f, -1.0, ALU.mult, [self.b_consts], [b_c2])
        self.ts("dve", negU, self.U_f, -1.0, ALU.mult, [self.b_consts], [b_c2])
        self.memset("dve", ones_f, 1.0, [b_c2])
        hstore = A.alloc([32, 1024], BF16)
        b_hst = [Buf("hst%d" % i) for i in range(32)]
        hf = [A.alloc([1024], F32) for _ in range(2)]
        b_hf = [Buf("hf0"), Buf("hf1")]
        hb = A.alloc([1024], BF16)
        b_hb = Buf("hb")
        for d in range(2):
            self.memset("pool", hf[d], 0.0, [b_hf[d]])
        NB = 2
        xt = [A.alloc([16, 64], BF16) for _ in range(NB)]
        Bt = [A.alloc([256], BF16) for _ in range(NB)]
        BTc = [A.alloc([2, 128], BF16) for _ in range(NB)]
        CTc = [A.alloc([2, 128], BF16) for _ in range(NB)]
        zc = [A.alloc([1024], BF16) for _ in range(NB)]
        b_ld = [{k: Buf("%s%d" % (k, i)) for k in ("xt", "Bt", "BTc", "CTc", "zc")} for i in range(NB)]
        ex = [A.alloc([48], F32) for _ in range(2)]
        ws = [A.alloc([16], F32) for _ in range(2)]
        b_ex = [Buf("ex0"), Buf("ex1")]
        xdt = [A.alloc([16, 64], BF16) for _ in range(2)]
        b_xdt = [Buf("xdt0"), Buf("xdt1")]
        xs = [A.alloc([16, 64], BF16) for _ in range(2)]
        b_xs = [Buf("xs0"), Buf("xs1")]
        xD = A.alloc([16, 64], BF16)
        b_xD = Buf("xD")
        G = [A.alloc([16, 128], F32) for _ in range(2)]
        b_G = [Buf("G0"), Buf("G1")]
        Ebc = [A.alloc([4, 128], F32) for _ in range(3)]
        b_Ebc = [Buf("Ebc%d" % i) for i in range(3)]
        E = [A.alloc([4, 128], F32) for _ in range(3)]
        b_E = [Buf("E%d" % i) for i in range(3)]
        MT = [A.alloc([4, 128], BF16) for _ in range(3)]
        b_MT = [Buf("MT%d" % i) for i in range(3)]
        Cp = [A.alloc([4, 128], BF16) for _ in range(3)]
        b_Cp = [Buf("Cp%d" % i) for i in range(3)]
        cbm = [A.alloc([2, 128], F32) for _ in range(2)]
        b_cbm = Buf("cbm")
        yg = A.alloc([1024], F32)
        b_yg = Buf("yg")
        ysq = A.alloc([1024], F32)
        ss = A.alloc([2], F32)
        b_ss = Buf("ss")
        yn = A.alloc([1024], BF16)
        b_yn = Buf("yn")
        ycst = [A.alloc([8, 512], BF16) for _ in range(2)]
        b_ycst = [Buf("ycst0"), Buf("ycst1")]
        stmp = A.alloc([1024], F32)
        b_stmp = Buf("stmp")
        cnt = {"ld": 0, "ex": 0, "eb": 0, "mt": 0, "seg": 0}
        tri = [self.L_f, self.U_f]
        stri = [self.SU_f, self.SL_f]
        negtri = [negL, negU]
        BTv = self.s_BT.rearrange("(g n) t -> n g t", g=2)
        CTv = self.s_CT.rearrange("(g n) t -> n g t", g=2)
        ps_small, b_small = self.ps[0], self.psb[0]
        ps_cb, b_cb_ps = self.ps[1], self.psb[1]
        ps_S, b_S = self.ps[6], self.psb[6]
        psT, b_psT = self.ps[7][:, :].bitcast(BF16), self.psb[7]

        def load(blk, full):
            i = cnt["ld"] % NB
            cnt["ld"] += 1
            L = b_ld[i]
            self.dma("sp", xt[i], self.s_xtok[blk * 128:(blk + 1) * 128, :].rearrange("p (h d) -> p h d", h=16),
                     reads=[b_scr["xtok"]], writes=[L["xt"]])
            self.dma("sp", Bt[i], self.s_Btok[blk * 128:(blk + 1) * 128, :], reads=[b_scr["Btok"]], writes=[L["Bt"]])
            if full:
                self.dma("sp", BTc[i], BTv[:, :, blk * 128:(blk + 1) * 128], reads=[b_scr["BT"]], writes=[L["BTc"]])
                self.dma("sp", CTc[i], CTv[:, :, blk * 128:(blk + 1) * 128], reads=[b_scr["CT"]], writes=[L["CTc"]])
                self.dma("sp", zc[i], self.s_z[blk * 128:(blk + 1) * 128, :], reads=[b_scr["z"]], writes=[L["zc"]])
            return i

        def small(blk, d):
            a = av[:, blk, d * 16:(d + 1) * 16]
            o = d * 64
            self.mm(ps_small[:, o:o + 16], tri[d], a, True, True, [self.b_consts, b_dt], [b_small], signal=False)
            self.mm(ps_small[:, o + 16:o + 32], stri[d], a, True, True, [self.b_consts, b_dt], [b_small], signal=False)
            self.mm(ps_small[:, o + 32:o + 48], ones_f, a, True, True, [b_c2, b_dt], [b_small], signal=True)
            self.act(ex[d], ps_small[:, o:o + 48], AF.Exp, [b_small], [b_ex[d]])
            self.tt("dve", ws[d], ex[d][:, 16:32], dtv[:, blk, d * 16:(d + 1) * 16], ALU.mult, [b_ex[d], b_dt], [b_ex[d]])

        def xs_prep(d, li):
            self.tt("pool", xs[d], xt[li], ws[d][:, :, None].to_broadcast([128, 16, 64]), ALU.mult,
                    [b_ld[li]["xt"], b_ex[d]], [b_xs[d]])

        def state_step(d, li, store_blk=None, prep=True):
            if prep:
                xs_prep(d, li)
            if store_blk is not None:
                self.cp("act", hstore[:, store_blk, :], hf[d], [b_hf[d]], [b_hst[store_blk]])
            for g in range(2):
                self.mm(ps_S[:, :], Bt[li][:, g * 128:(g + 1) * 128], xs[d][:, g * 8:(g + 1) * 8, :], True, True,
                        [b_ld[li]["Bt"], b_xs[d]], [b_S], signal=True)
                hv = hf[d][:, g * 512:(g + 1) * 512].rearrange("p (h q) -> p h q", h=8)
                tv = stmp[:, g * 512:(g + 1) * 512].rearrange("p (h q) -> p h q", h=8)
                self.tt("pool", tv, hv, ex[d][:, 32 + g * 8:32 + (g + 1) * 8][:, :, None].to_broadcast([128, 8, 64]), ALU.mult,
                        [b_hf[d], b_ex[d]], [b_stmp])
                self.tt("dve", hf[d][:, g * 512:(g + 1) * 512], stmp[:, g * 512:(g + 1) * 512], ps_S[:, :], ALU.add,
                        [b_stmp, b_S], [b_hf[d]])

        for blk in [33, 32] + list(range(31, -1, -1)):
            li = load(blk, False)
            small(blk, 1)
            state_step(1, li, store_blk=(blk if blk < 32 else None))
        for blk in (32, 33):
            li = load(blk, False)
            small(blk, 0)
            state_step(0, li)
        self.cp("act", hb, hf[0], [b_hf[0]], [b_hb])
        pending_fin = []
        for c in range(32):
            li = load(c, True)
            L = b_ld[li]
            for g in range(2):
                self.mm(ps_cb[:, g * 128:(g + 1) * 128], BTc[li][:, g, :], CTc[li][:, g, :], True, True, [L["BTc"], L["CTc"]],
                        [b_cb_ps], signal=(g == 1))
            pcb = ps_cb[:, 0:256].rearrange("p (g i) -> p g i", g=2)
            self.tt("dve", cbm[0], pcb, self.L_f[:, None, :].to_broadcast([128, 2, 128]), ALU.mult, [b_cb_ps, self.b_consts], [b_cbm])
            self.tt("dve", cbm[1], pcb, self.U_f[:, None, :].to_broadcast([128, 2, 128]), ALU.mult, [b_cb_ps, self.b_consts], [b_cbm])
            self.tt("pool", xD, xt[li], D_bc[:, :, None].to_broadcast([128, 16, 64]), ALU.mult, [L["xt"], b_rows], [b_xD])
            for g in range(2):
                self.mm(self.ps[4 + g][:, :], self.ident_b, xD[:, g * 8:(g + 1) * 8, :], True, False, [self.b_cb, b_xD],
                        [self.psb[4 + g]], signal=False)
            for d in range(2):
                small(c, d)
                if d == 0:
                    xs_prep(0, li)
                dtc = dtv[:, c, d * 16:(d + 1) * 16]
                a = av[:, c, d * 16:(d + 1) * 16]
                self.tt("pool", xdt[d], xt[li], dtc[:, :, None].to_broadcast([128, 16, 64]), ALU.mult, [L["xt"], b_dt], [b_xdt[d]])
                self.tt("pool" if "g_pool" in self.debug else "dve", G[d], tri[d][:, None, :].to_broadcast([128, 16, 128]),
                        a[:, :, None].to_broadcast([128, 16, 128]), ALU.mult, [self.b_consts, b_dt], [b_G[d]])

            def mm1(it):
                d, q = it // 4, it % 4
                k = it % 3
                pseg, b_seg = self.ps[2 + it % 2], self.psb[2 + it % 2]
                self.mm(pseg[:, :], ones_f, G[d][:, 4 * q:4 * q + 4, :], True, True, [b_c2, b_G[d]], [b_seg], signal=True)
                self.act(Ebc[k], pseg[:, :].rearrange("p (a b) -> p a b", a=4), AF.Exp, [b_seg], [b_Ebc[k]])

            def mm2(it):
                d, q = it // 4, it % 4
                k = it % 3
                g = q // 2
                a = av[:, c, d * 16:(d + 1) * 16]
                pseg, b_seg = self.ps[2 + it % 2], self.psb[2 + it % 2]
                self.mm(pseg[:, :].rearrange("p (a b) -> p a b", a=4), negtri[d],
                        a[:, 4 * q:4 * q + 4][:, :, None].to_broadcast([128, 4, 128]), False, True, [b_c2, b_dt, b_Ebc[k]], [b_seg],
                        signal=True, skip=True)
                self.act(E[k], pseg[:, :].rearrange("p (a b) -> p a b", a=4), AF.Exp, [b_seg], [b_E[k]])
                self.stt("dve", MT[k], E[k], 1.0, cbm[d][:, g, :][:, None, :].to_broadcast([128, 4, 128]), ALU.min, ALU.mult,
                         [b_E[k], b_cbm], [b_MT[k]])
                self.tt("pool", Cp[k], Ebc[k], CTc[li][:, g, :][:, None, :].to_broadcast([128, 4, 128]), ALU.mult,
                        [b_Ebc[k], L["CTc"]], [b_Cp[k]])

            def ymm(it):
                d, q = it // 4, it % 4
                k = it % 3
                g = q // 2
                for hh in range(4):
                    h = 4 * q + hh
                    hl = h % 8
                    yps = self.ps[4 + g][:, hl * 64:(hl + 1) * 64]
                    hsrc = hb[:, h * 64:(h + 1) * 64] if d == 0 else hstore[:, c, h * 64:(h + 1) * 64]
                    hbuf = b_hb if d == 0 else b_hst[c]
                    self.mm(yps, MT[k][:, hh, :], xdt[d][:, h, :], False, False, [b_MT[k], b_xdt[d]], [self.psb[4 + g]], signal=False)
                    self.mm(yps, Cp[k][:, hh, :], hsrc, False, (d == 1), [b_Cp[k], hbuf], [self.psb[4 + g]], signal=(hh == 3))

            if "ssd_seq" in self.debug:
                for it in range(8):
                    mm1(it)
                    mm2(it)
                    ymm(it)
            else:
                for s_ in range(10):
                    if s_ < 8:
                        mm1(s_)
                    if 1 <= s_ <= 8:
                        mm2(s_ - 1)
                    if s_ >= 2:
                        ymm(s_ - 2)
                    if s_ == 3 and pending_fin:
                        pending_fin.pop(0)()
            state_step(0, li, prep=False)
            self.cp("act", hb, hf[0], [b_hf[0]], [b_hb])
            for g in range(2):
                self.tt("dve", yg[:, g * 512:(g + 1) * 512], self.ps[4 + g][:, :], zc[li][:, g * 512:(g + 1) * 512], ALU.mult,
                        [self.psb[4 + g], L["zc"]], [b_yg])
            self.memset("dve", ss, 0.0, [b_ss])
            self.act(ysq, yg, AF.Square, [b_yg, b_ss], [b_ss], accum_out=ss[:, 0:1])
            self.ts("dve", ss[:, 1:2], ss[:, 0:1], 1.0 / 1024, ALU.mult, [b_ss], [b_ss], s2=EPS, op1=ALU.add)
            self.act(ss[:, 1:2], ss[:, 1:2], AF.Sqrt, [b_ss], [b_ss])
            self.recip(ss[:, 1:2], ss[:, 1:2], [b_ss], [b_ss])
            self.stt("dve", yn, yg, ss[:, 1:2], normw_bc, ALU.mult, ALU.mult, [b_yg, b_ss, b_rows], [b_yn])
            def fin(c=c):
                for j in range(8):
                    self.tr(psT[:, j * 128:(j + 1) * 128], yn[:, j * 128:(j + 1) * 128], self.ident_b, [b_yn, self.b_cb], [b_psT],
                            signal=(j == 7))
                yi = (c // 4) % 2
                self.cp("act", ycst[yi][:, :, (c % 4) * 128:(c % 4 + 1) * 128], psT[:, :].rearrange("p (a b) -> p a b", a=8),
                        [b_psT], [b_ycst[yi]])
                if c % 4 == 3:
                    t0 = (c // 4) * 512
                    self.dma("sp", self.s_ycatT[:, 0:8, t0:t0 + 512], ycst[yi], reads=[b_ycst[yi]], writes=[self.b_ycat])
            pending_fin.append(fin)
        while pending_fin:
            pending_fin.pop(0)()
        if "ssd" in self.debug:
            d_ = self.dbg_tensor("ycatT", (128, 16, SEQ), BF16)
            self.dma("sp", d_, self.s_ycatT, reads=[self.b_ycat])
            d_ = self.dbg_tensor("hstore", (128, 32, 1024), BF16)
            self.dma("sp", d_, hstore, reads=b_hst)
            d_ = self.dbg_tensor("dtv", (128, 34, 32))
            self.dma("sp", d_, dtv, reads=[b_dt])


    def phase_na(self):
        A, S = self.A, self.S
        A.release(self.mark_base)
        b_scr = self.b_scr
        Kc = A.alloc([16, CTX], BF16)
        Vc = A.alloc([2, 16, 65], BF16)
        b_ctx = Buf("nactx")
        self.dma("sp", Kc[0:64].rearrange("p (c par) t -> p c par t", par=2), self.s_kT[:, :, :, SEQ:NTOK], reads=[b_scr["kT"]], writes=[b_ctx])
        self.dma("sp", Vc, self.s_vtok[SEQ:NTOK].rearrange("(b p) h e -> p b h e", p=128), reads=[b_scr["vtok"]], writes=[b_ctx])
        bias = A.alloc([16, 5, 128], F32)
        b_bias = Buf("nabias")
        Kwin = [A.alloc([16, 576], BF16) for _ in range(2)]
        Qp = [A.alloc([16, 128], BF16) for _ in range(2)]
        Vwin = [A.alloc([5, 16, 65], BF16) for _ in range(2)]
        b_win = [{k: Buf(k + str(i)) for k in ("K", "Q", "V")} for i in range(2)]
        Sw = [A.alloc([5, 128], F32) for _ in range(2)]
        b_Sw = [Buf("Sw0"), Buf("Sw1")]
        for i in range(2):
            self.memset("pool", Sw[i][64:128, 4, :], NEG, [b_Sw[i]])
        PT = [A.alloc([7, 128], BF16) for _ in range(3)]
        b_PT = [Buf("PT%d" % i) for i in range(3)]
        Otok = [A.alloc([16, 64], BF16) for _ in range(2)]
        b_Otok = [Buf("Ot0"), Buf("Ot1")]
        rinv = A.alloc([16], F32)
        b_rinv = Buf("rinv")
        ycst = [A.alloc([8, 512], BF16) for _ in range(2)]
        b_ycst = [Buf("nyc0"), Buf("nyc1")]
        psT, b_psT = self.ps[7][:, :].bitcast(BF16), self.psb[7]
        variants = {0: ((0, 0), (0, -1)), 1: ((0, -2), (0, -3)), 30: ((1, -5), (1, -6)), 31: ((1, -7), (1, -8))}
        interior = ((0, -4), (1, -5))
        cur_var = [None]

        def build_bias(var):
            self.memset("pool", bias, NEG, [b_bias])
            for rq in range(2):
                lo, off = var[rq]
                for ck in range(5):
                    js = [jr for jr in (2 * ck, 2 * ck + 1) if lo <= jr < lo + 8 and jr <= 8]
                    if not js:
                        continue
                    p0 = (js[0] % 2) * 64
                    dr0 = js[0] + off + 7
                    src = self.tblk[:, dr0:dr0 + len(js), :, :].rearrange("h j k c -> (j k) h c")
                    self.dma("sp", bias[p0:p0 + 64 * len(js), :, ck, rq * 64:(rq + 1) * 64], src, writes=[b_bias])

        hgroups = [(4, 0, 7), (5, 7, 14), (6, 14, 16)]
        hg_of = {}
        for (ob_, h0_, h1_) in hgroups:
            for h_ in range(h0_, h1_):
                hg_of[h_] = (ob_, h0_, h1_)
        cnt = {"sw": 0, "pt": 0, "st": 0}
        state = {}

        def emit_loads(rp):
            r = 2 * rp
            kb = min(max(r - 4, 0), 55)
            w0 = kb * 64
            wi = rp % 2
            W = b_win[wi]
            self.dma("sp", Kwin[wi][0:64].rearrange("p (c par) t -> p c par t", par=2), self.s_kT[:, :, :, w0:w0 + 576],
                     reads=[b_scr["kT"]], writes=[W["K"]])
            self.dma("sp", Qp[wi][0:64].rearrange("p (c par) t -> p c par t", par=2), self.s_qT[:, :, :, r * 64:r * 64 + 128],
                     reads=[b_scr["qT"]], writes=[W["Q"]])
            self.dma("sp", Vwin[wi][:, 0:4], self.s_vtok[w0:w0 + 512].rearrange("(b p) h e -> p b h e", p=128),
                     reads=[b_scr["vtok"]], writes=[W["V"]])
            self.dma("sp", Vwin[wi][0:64, 4], self.s_vtok[w0 + 512:w0 + 576], reads=[b_scr["vtok"]], writes=[W["V"]])

        def emit_st(rp, h):
            var = variants.get(rp, interior)
            if var != cur_var[0]:
                build_bias(var)
                cur_var[0] = var
            wi = rp % 2
            W = b_win[wi]
            si = cnt["st"] % 2
            cnt["st"] += 1
            psA, bA = self.ps[2 * si], self.psb[2 * si]
            psB, bB = self.ps[2 * si + 1], self.psb[2 * si + 1]
            q = Qp[wi][0:64, h, :]
            for ck in range(4):
                self.mm(psA[:, ck * 128:(ck + 1) * 128], Kwin[wi][0:64, h, ck * 128:(ck + 1) * 128], q, True, True,
                        [W["K"], W["Q"]], [bA], signal=(ck == 3))
            self.mm(psB[0:64, 0:128], Kwin[wi][0:64, h, 512:576], q, True, True, [W["K"], W["Q"]], [bB], signal=False)
            self.mm(psB[:, 128:256], Kc[0:64, h, 0:128], q, True, True, [b_ctx, W["Q"]], [bB], signal=False)
            self.mm(psB[:, 256:384], Kc[0:64, h, 128:256], q, True, True, [b_ctx, W["Q"]], [bB], signal=True)
            wi2 = cnt["sw"] % 2
            cnt["sw"] += 1
            sw, bsw = Sw[wi2], b_Sw[wi2]
            self.tt("dve", sw[:, 0:4, :], psA[:, :].rearrange("p (a b) -> p a b", a=4), bias[:, h, 0:4, :], ALU.add,
                    [bA, b_bias], [bsw])
            self.tt("dve", sw[0:64, 4, :], psB[0:64, 0:128], bias[0:64, h, 4, :], ALU.add, [bB, b_bias], [bsw])
            pi = cnt["pt"] % 3
            cnt["pt"] += 1
            pt, bpt = PT[pi], b_PT[pi]
            self.act(pt[:, 0:5, :], sw, AF.Exp, [bsw], [bpt])
            self.act(pt[:, 5:7, :], psB[:, 128:384].rearrange("p (a b) -> p a b", a=2), AF.Exp, [bB], [bpt])
            state[(rp, h)] = (pt, bpt)

        def emit_pv(rp, h):
            wi = rp % 2
            W = b_win[wi]
            pt, bpt = state.pop((rp, h))
            obank, h0, h1 = hg_of[h]
            ops, bops = self.ps[obank], self.psb[obank]
            ot, bot = Otok[rp % 2], b_Otok[rp % 2]
            o_ap = ops[:, (h - h0) * 65:(h - h0 + 1) * 65]
            for ck in range(7):
                M = 64 if ck == 4 else 128
                v = Vwin[wi][0:M, ck, h, :] if ck < 5 else Vc[:, ck - 5, h, :]
                vb = W["V"] if ck < 5 else b_ctx
                self.mm(o_ap, pt[0:M, ck, :], v, ck == 0, ck == 6, [bpt, vb], [bops], signal=(ck == 6))
            if h == h1 - 1:
                nh = h1 - h0
                o3 = ops[:, 0:nh * 65].rearrange("p (h e) -> p h e", h=nh)
                self.recip(rinv[:, h0:h1], o3[:, :, 64], [bops], [b_rinv])
                self.tt("dve", ot[:, h0:h1, :], o3[:, :, 0:64], rinv[:, h0:h1][:, :, None].to_broadcast([128, nh, 64]), ALU.mult,
                        [bops, b_rinv], [bot])
            if h == 15:
                otf = ot.rearrange("p h d -> p (h d)")
                for j in range(8):
                    self.tr(psT[:, j * 128:(j + 1) * 128], otf[:, j * 128:(j + 1) * 128], self.ident_b, [bot, self.b_cb], [b_psT],
                            signal=(j == 7))
                yi = (rp // 4) % 2
                self.cp("act", ycst[yi][:, :, (rp % 4) * 128:(rp % 4 + 1) * 128], psT[:, :].rearrange("p (a b) -> p a b", a=8),
                        [b_psT], [b_ycst[yi]])
                if rp % 4 == 3:
                    t0 = (rp // 4) * 512
                    self.dma("sp", self.s_ycatT[:, 8:16, t0:t0 + 512], ycst[yi], reads=[b_ycst[yi]], writes=[self.b_ycat])

        seq = [(rp, h) for rp in range(32) for h in range(16)]
        emit_loads(0)
        emit_st(*seq[0])
        for i, (rp, h) in enumerate(seq):
            if h == 0 and rp + 1 < 32:
                emit_loads(rp + 1)
            if i + 1 < len(seq):
                emit_st(*seq[i + 1])
            emit_pv(rp, h)
        if "na" in self.debug:
            d_ = self.dbg_tensor("ycatT2", (128, 16, SEQ), BF16)
            self.dma("sp", d_, self.s_ycatT, reads=[self.b_ycat])


    def phase_tail(self):
        A, S = self.A, self.S
        A.release(self.mark_base)
        NW = 4
        wb = [A.alloc([4096], BF16) for _ in range(NW)]
        b_wb = [Buf("tw%d" % i) for i in range(NW)]
        R1 = A.alloc([32, 512], BF16)
        b_R1 = Buf("R1")
        hid = R1
        ycat = R1[:, 0:16, :]
        ob = R1.rearrange("p a b -> p (a b)")[:, 0:8192].bitcast(F32).rearrange("p (a b) -> p a b", a=8)
        h = A.alloc([8, 512], BF16)
        b_h = [Buf("h%d" % i) for i in range(8)]
        sets = []
        for i in range(2):
            sets.append({"x": A.alloc([8, 512], F32), "bx": Buf("sx%d" % i), "bxm": [Buf("sx%d_%d" % (i, m)) for m in range(8)], "u": A.alloc([8, 514], F32), "bu": Buf("su%d" % i),
                         "gb": A.alloc([8, 512], BF16), "bgb": Buf("sg%d" % i)})
        gct = [A.alloc([512], F32) for _ in range(2)]
        b_gct = [Buf("gct0"), Buf("gct1")]
        rl = [A.alloc([512], F32) for _ in range(2)]
        b_rl = [Buf("rl0"), Buf("rl1")]
        cva = [A.alloc([512], F32) for _ in range(2)]
        b_cva = [Buf("cva0"), Buf("cva1")]
        gg = A.alloc([8, 512], BF16)
        b_gg = Buf("gg")
        wk = {"sq": [A.alloc([8, 512], BF16)] * 2, "b_sq": [Buf("sq")] * 2,
              "rstd": [A.alloc([512], F32) for _ in range(2)], "b_rstd": [Buf("rs0"), Buf("rs1")],
              "tmp": [A.alloc([512], F32) for _ in range(3)], "b_tmp": [Buf("nt%d" % i) for i in range(3)], "n": 0,
              "b_sqc": [Buf("sqc%d" % m) for m in range(8)]}
        cnt = {"w": 0, "ps": 0, "gct": 0, "rl": 0, "cva": 0, "norm": 0}

        def wload(key, idx, kc):
            i = cnt["w"] % NW
            cnt["w"] += 1
            v = wb[i].rearrange("p (k n) -> p k n", k=kc)
            src, bsrc = self.wsrc(key, idx, kc)
            self.dma("sp", v, src, reads=[bsrc], writes=[b_wb[i]])
            return v, b_wb[i]

        def nextps():
            pi = cnt["ps"] % 4
            cnt["ps"] += 1
            return self.ps[pi], self.psb[pi]

        def resid_evac(ps, bps, xs, m, gate_col, bgate):
            bxm = xs["bxm"][m]
            flush()
            self.stt("dve", xs["x"][:, m, :], ps[:, :], gate_col, xs["x"][:, m, :], ALU.mult, ALU.add, [bps, bgate, bxm], [bxm])
            pend.append(self.norm_accum(xs["x"], bxm, m, wk, cnt["norm"], 4))

        pend = []

        def flush():
            while pend:
                pend.pop(0)()

        def norm(xs, g, bg, shift, bshift, out, bout, final=False):
            flush()
            self.norm_tile(xs["x"], xs["bxm"], 512, g, bg, shift, bshift, out, bout, wk, cnt["norm"], psi=4, final=final,
                           accum_done=True)
            cnt["norm"] += 1

        def mlp(l, xs):
            gate = self.modv[l][:, 40:48]
            bgate = self.b_modv[l][5]
            for grp in range(8):
                wt, bw = wload("w1", (l, grp), 8)
                if grp == 0:
                    pss = [nextps() for _ in range(4)]
                    for kc in range(8):
                        for jj in range(4):
                            self.mm(pss[jj][0][:, :], wt[:, kc, jj * 128:(jj + 1) * 128], h[:, kc, :], kc == 0, kc == 7, [bw, b_h[kc]],
                                    [pss[jj][1]], signal=(kc == 7))
                for jj in range(4):
                    mh = grp * 4 + jj
                    if grp == 0:
                        ps, bps = pss[jj]
                    else:
                        ps, bps = nextps()
                        for kc in range(8):
                            self.mm(ps[:, :], wt[:, kc, jj * 128:(jj + 1) * 128], h[:, kc, :], kc == 0, kc == 7, [bw, b_h[kc]], [bps], signal=(kc == 7))
                    ri = cnt["rl"] % 2
                    cnt["rl"] += 1
                    self.act(rl[ri], ps[:, :], AF.Relu, [bps], [b_rl[ri]])
                    self.tt("pool", hid[:, mh, :], rl[ri], rl[ri], ALU.mult, [b_rl[ri]], [b_R1])
            for m in range(8):
                wt, bw = wload("w2", (l, m), 32)
                ps, bps = nextps()
                for kc in range(32):
                    self.mm(ps[:, :], wt[:, kc, :], hid[:, kc, :], kc == 0, kc == 31, [bw, b_R1], [bps], signal=(kc == 31))
                resid_evac(ps, bps, xs, m, gate[:, m:m + 1], bgate)

        def conv_b(xs, m):
            ci = cnt["cva"] % 2
            cnt["cva"] += 1
            cv_, bcv = cva[ci], b_cva[ci]
            self.act(cv_, xs["u"][:, m, 0:512], AF.Identity, [xs["bu"], self.b_cols], [bcv], scale=self.colv("scw0", m))
            self.stt("dve", cv_, xs["u"][:, m, 1:513], self.colv("scw1", m), cv_, ALU.mult, ALU.add, [xs["bu"], self.b_cols, bcv], [bcv])
            self.stt("dve", cv_, xs["u"][:, m, 2:514], self.colv("scw2", m), cv_, ALU.mult, ALU.add, [xs["bu"], self.b_cols, bcv], [bcv])
            self.tt("pool", gg[:, m, :], cv_, xs["gb"][:, m, :], ALU.mult, [bcv, xs["bgb"]], [b_gg])

        def load_ycat(t):
            self.dma("sp", ycat, self.s_ycatT[:, :, t * 512:(t + 1) * 512], reads=[self.b_ycat], writes=[b_R1])

        def stage_a(t, xs, prev):
            t0 = t * 512
            if t <= 1:
                load_ycat(t)
            self.dma("sp", xs["x"], self.xT[:, :, t0:t0 + 512], writes=xs["bxm"])
            for grp in range(4):
                wt, bw = wload("out0", (grp,), 16)
                for jj in range(2):
                    m = grp * 2 + jj
                    ps, bps = nextps()
                    for kc in range(16):
                        self.mm(ps[:, :], wt[:, kc, jj * 128:(jj + 1) * 128], ycat[:, kc, :], kc == 0, kc == 15, [bw, b_R1], [bps],
                                signal=(kc == 15))
                    resid_evac(ps, bps, xs, m, self.modv[0][:, 16 + m:17 + m], self.b_modv[0][2])
            norm(xs, self.g_f0, self.b_gf0, self.modv[0][:, 24:32], self.b_modv[0][3], h, b_h)
            mlp(0, xs)
            norm(xs, self.g_a1, self.b_ga1, self.modv[1][:, 0:8], self.b_modv[1][0], h, b_h)
            for grp in range(2):
                wt, bw = wload("scin", (grp,), 8)
                for jj in range(4):
                    m = grp * 4 + jj
                    ps, bps = nextps()
                    for kc in range(8):
                        self.mm(ps[:, :], wt[:, kc, jj * 128:(jj + 1) * 128], h[:, kc, :], kc == 0, kc == 7, [bw, b_h[kc]], [bps], signal=(kc == 7))
                    self.cp("act", xs["gb"][:, m, :], ps[:, :], [bps], [xs["bgb"]])
            for half in range(2):
                wc, bwc = wload("scin", (2 + half,), 8)
                wv, bwv = wload("scin", (4 + half,), 8)
                for jj in range(4):
                    m = half * 4 + jj
                    ps, bps = nextps()
                    for kc in range(8):
                        self.mm(ps[:, :], wc[:, kc, jj * 128:(jj + 1) * 128], h[:, kc, :], kc == 0, kc == 7, [bwc, b_h[kc]], [bps], signal=(kc == 7))
                    gi = cnt["gct"] % 2
                    cnt["gct"] += 1
                    self.cp("act", gct[gi], ps[:, :], [bps], [b_gct[gi]])
                    ps2, bps2 = nextps()
                    for kc in range(8):
                        self.mm(ps2[:, :], wv[:, kc, jj * 128:(jj + 1) * 128], h[:, kc, :], kc == 0, kc == 7, [bwv, b_h[kc]], [bps2], signal=(kc == 7))
                    self.tt("dve", xs["u"][:, m, 1:513], ps2[:, :], gct[gi], ALU.mult, [bps2, b_gct[gi]], [xs["bu"]])
                    if t == 0:
                        self.memset("pool", xs["u"][:, m, 0:1], 0.0, [xs["bu"]])
                    else:
                        self.cp("pool", xs["u"][:, m, 0:1], prev["u"][:, m, 512:513], [prev["bu"]], [xs["bu"]])
                        self.cp("pool", prev["u"][:, m, 513:514], xs["u"][:, m, 1:2], [xs["bu"]], [prev["bu"]])
                        conv_b(prev, m)
                    if t == 7:
                        self.memset("pool", xs["u"][:, m, 513:514], 0.0, [xs["bu"]])

        def stage_b(t, xs, do_conv):
            t0 = t * 512
            if do_conv:
                for m in range(8):
                    conv_b(xs, m)
            for grp in range(2):
                wt, bw = wload("scout", (grp,), 8)
                for jj in range(4):
                    m = grp * 4 + jj
                    ps, bps = nextps()
                    for kc in range(8):
                        self.mm(ps[:, :], wt[:, kc, jj * 128:(jj + 1) * 128], gg[:, kc, :], kc == 0, kc == 7, [bw, b_gg], [bps], signal=(kc == 7))
                    resid_evac(ps, bps, xs, m, self.modv[1][:, 16 + m:17 + m], self.b_modv[1][2])
            norm(xs, self.g_f1, self.b_gf1, self.modv[1][:, 24:32], self.b_modv[1][3], h, b_h)
            mlp(1, xs)
            if t + 2 < 8:
                load_ycat(t + 2)
            norm(xs, self.colv("fnw"), self.b_cols, None, None, xs["x"], xs["bxm"], final=True)
            self.dma("sp", self.outT[:, :, t0:t0 + 512], xs["x"], reads=xs["bxm"])

        for t in range(9):
            cur = sets[t % 2]
            prev = sets[(t - 1) % 2]
            if t < 8:
                stage_a(t, cur, prev)
            if t >= 1:
                stage_b(t - 1, prev, do_conv=(t == 8))


def build_program(stop_after=None, debug=()):
    p = Prog(stop_after, debug)
    p.build()
    return p


_CACHE = {}


def kernel(**inputs):
    shared, per_core = prep_inputs(inputs)
    p = build_program()
    in_maps = []
    for b in range(8):
        m = dict(shared)
        m.update(per_core[b])
        in_maps.append(m)
    res = run_bass_kernel_spmd(p.nc, in_maps, core_ids=list(range(8)))
    out = np.empty((8, SEQ, D), np.float32)
    for b in range(8):
        oT = np.asarray(res.results[b]["outT"])
        out[b] = oT.transpose(2, 1, 0).reshape(SEQ, D)
    return out
```

```python
from contextlib import ExitStack
import numpy as np
import concourse.bass as bass
import concourse.mybir as mybir
from concourse.bass_utils import run_bass_kernel_spmd

F32 = mybir.dt.float32
BF16 = mybir.dt.bfloat16
AF = mybir.ActivationFunctionType
ALU = mybir.AluOpType

ENGS = ("pe", "act", "dve", "pool", "sp")

D = 1024
SEQ = 4096
CTX = 256
NTOK = SEQ + CTX
OFF_X, OFF_B, OFF_DT, OFF_K, OFF_V, OFF_C, OFF_Z, OFF_Q = 0, 1024, 1280, 1312, 2336, 3360, 3616, 4640
NEG = -30000.0
EPS = 1e-6


class Buf:
    __slots__ = ("name", "lw", "rd")

    def __init__(self, name=""):
        self.name = name
        self.lw = None
        self.rd = []


class Sched:
    def __init__(self, nc, n_lanes=6, same_eng_sync=True):
        self.nc = nc
        self.q = {e: [] for e in ENGS}
        self.cnt = {e: 0 for e in ENGS}
        self.seen = {e: {} for e in ENGS}
        self.same_eng_sync = same_eng_sync
        self.lanes = {}
        self.lane_rr = {}
        for qe in ("sp", "act", "pool"):
            self.lanes[qe] = [["dma_%s_%d" % (qe, i), 0] for i in range(n_lanes)]
            self.lane_rr[qe] = 0
        self.semkeys = list(ENGS) + [l[0] for qe in self.lanes for l in self.lanes[qe]]

    def _deps(self, reads, writes):
        deps = set()
        for b in reads:
            if b.lw is not None:
                deps.add((b.lw[0], b.lw[1], True))
        for b in writes:
            if b.lw is not None:
                deps.add((b.lw[0], b.lw[1], False))
            for r in b.rd:
                deps.add((r[0], r[1], False))
        return deps

    def _emit_waits(self, eng, deps):
        mx = {}
        for k, v, raw in deps:
            if k == eng:
                if not (raw and self.same_eng_sync and eng in ("act", "dve", "pool")):
                    continue
            if mx.get(k, 0) < v:
                mx[k] = v
        for k, v in mx.items():
            if self.seen[eng].get(k, 0) >= v:
                continue
            self.seen[eng][k] = v
            self.q[eng].append(("wait", k, v))

    def _attr(self, ev, reads, writes):
        for b in reads:
            if len(b.rd) > 64:
                mx = {}
                for k, v in b.rd:
                    if mx.get(k, 0) < v:
                        mx[k] = v
                b.rd = list(mx.items())
            b.rd.append(ev)
        for b in writes:
            b.lw = ev
            b.rd = []

    def op(self, eng, fn, reads=(), writes=(), signal=True):
        self._emit_waits(eng, self._deps(reads, writes))
        ev = (eng, self.cnt[eng] + 1)
        self._attr(ev, reads, writes)
        if signal:
            self.cnt[eng] += 1
            self.q[eng].append(("op", fn, eng))
        else:
            self.q[eng].append(("op", fn, None))
        return ev

    def dma(self, qe, out, in_, reads=(), writes=()):
        lanes = self.lanes[qe]
        i = self.lane_rr[qe]
        self.lane_rr[qe] = (i + 1) % len(lanes)
        lane = lanes[i]
        deps = self._deps(reads, writes)
        if lane[1] > 0:
            deps.add((lane[0], 16 * lane[1], True))
        self._emit_waits(qe, deps)
        lane[1] += 1
        ev = (lane[0], 16 * lane[1])
        self._attr(ev, reads, writes)
        self.q[qe].append(("dma", (out, in_), lane[0]))
        return ev

    def _all_events(self):
        evs = [(e, self.cnt[e]) for e in ENGS if self.cnt[e] > 0]
        for qe in self.lanes:
            for key, c in self.lanes[qe]:
                if c > 0:
                    evs.append((key, 16 * c))
        return evs

    def _force(self, eng, evs):
        for k, v in evs:
            if k == eng or self.seen[eng].get(k, 0) >= v:
                continue
            self.seen[eng][k] = v
            self.q[eng].append(("wait", k, v))

    def barrier(self):
        evs = self._all_events()
        for e in ENGS:
            self._force(e, evs)

    def final_wait(self, eng="sp"):
        self._force(eng, self._all_events())

    def replay(self, stack):
        nc = self.nc
        used = set()
        for e in ENGS:
            for it in self.q[e]:
                if it[0] == "wait":
                    used.add(it[1])
                elif it[2] is not None:
                    used.add(it[2])
        sems = {k: stack.enter_context(nc.semaphore("s_" + k)) for k in self.semkeys if k in used}
        block = stack.enter_context(nc.Block())

        def run(eng_name):
            def body(e):
                for it in self.q[eng_name]:
                    if it[0] == "wait":
                        e.wait_ge(sems[it[1]], it[2])
                    elif it[0] == "op":
                        ins = it[1](e)
                        if it[2] is not None:
                            ins.then_inc(sems[it[2]], 1)
                    else:
                        out, in_ = it[1]
                        e.dma_start(out=out, in_=in_).then_inc(sems[it[2]], 16)
            return body

        if self.q["sp"]:
            block.sync(run("sp"))
        if self.q["pe"]:
            block.tensor(run("pe"))
        if self.q["act"]:
            block.scalar(run("act"))
        if self.q["dve"]:
            block.vector(run("dve"))
        if self.q["pool"]:
            block.gpsimd(run("pool"))


class Arena:
    def __init__(self, ap, words):
        self.ap = ap
        self.words = words
        self.top = 0

    def mark(self):
        return self.top

    def release(self, m):
        self.top = m

    def alloc(self, shape, dtype, parts=128):
        n = 1
        for s in shape:
            n *= s
        nbytes = n * (4 if dtype == F32 else 2)
        w = (nbytes + 3) // 4
        w = (w + 7) // 8 * 8
        assert self.top + w <= self.words, "SBUF arena overflow: need %d have %d" % (self.top + w, self.words)
        v = self.ap[:, self.top:self.top + (nbytes + 3) // 4]
        self.top += w
        if dtype != F32:
            v = v.bitcast(dtype)
        if len(shape) == 2:
            v = v.rearrange("p (a b) -> p a b", a=shape[0])
        elif len(shape) == 3:
            v = v.rearrange("p (a b c) -> p a b c", a=shape[0], b=shape[1])
        return v


def _col(v):
    v = np.asarray(v, np.float32)
    return np.ascontiguousarray(v.reshape(-1, 128).T)


def _wtile(w, ncols):
    K, N = w.shape
    assert K % 128 == 0 and N % ncols == 0
    t = w.reshape(K // 128, 128, N // ncols, ncols).transpose(2, 1, 0, 3)
    return np.ascontiguousarray(t, dtype=np.float32)


IN_GROUPS = [
    ("x0", 0, 512), ("x1", 512, 512), ("B", OFF_B, 256), ("dt", OFF_DT, 32),
    ("k0", OFF_K, 512), ("k1", OFF_K + 512, 512), ("v0", OFF_V, 512), ("v1", OFF_V + 512, 512),
    ("C", OFF_C, 256), ("z0", OFF_Z, 512), ("z1", OFF_Z + 512, 512), ("q0", OFF_Q, 512), ("q1", OFF_Q + 512, 512),
]
IN_GIDX = {g[0]: i for i, g in enumerate(IN_GROUPS)}

COLS = {}
_n = 0
for _name, _w in [("c", 8), ("cctx", 8), ("modb0", 48), ("modb1", 48), ("nmix0", 8), ("nmix1", 8), ("nmlp0", 8),
                  ("nmlp1", 8), ("fnw", 8), ("cw0", 12), ("cw1", 12), ("cw2", 12), ("cb", 12),
                  ("scw0", 8), ("scw1", 8), ("scw2", 8)]:
    COLS[_name] = (_n, _w)
    _n += _w
NCOL = _n
ROWS = {}
_n = 0
for _name, _w in [("ssdnw", 1024), ("dtb", 32), ("alog", 32), ("dskip", 16)]:
    ROWS[_name] = (_n, _w)
    _n += _w
NROW = _n
NCONST = 5 * 128


def prep_inputs(inp):
    f = lambda a: np.ascontiguousarray(np.asarray(a, np.float32))
    shared = {}
    shared["modw"] = np.stack([_wtile(f(inp["mod_w"][l]), 1024) for l in range(2)])
    in_w = f(inp["ssdna_in_w"][0])
    inw = np.zeros((len(IN_GROUPS), 128, 8, 512), np.float32)
    for gi, (nm, c0, nc_) in enumerate(IN_GROUPS):
        inw[gi, :, :, :nc_] = in_w[:, c0:c0 + nc_].reshape(8, 128, nc_).transpose(1, 0, 2)
    shared["inw"] = inw
    shared["outw0"] = _wtile(f(inp["ssdna_out_w"][0]), 256)
    shared["w1"] = np.stack([_wtile(f(inp["mlp_w1"][l]), 512) for l in range(2)])
    shared["w2"] = np.stack([_wtile(f(inp["mlp_w2"][l]), 128) for l in range(2)])
    shared["scin"] = _wtile(f(inp["sc_in_w"][0]), 512)
    shared["scout"] = _wtile(f(inp["sc_out_w"][0]), 512)
    rows = np.zeros((1, NROW), np.float32)
    rows[0, ROWS["ssdnw"][0]:ROWS["ssdnw"][0] + 1024] = f(inp["ssd_norm_w"][0])
    rows[0, ROWS["dtb"][0]:ROWS["dtb"][0] + 32] = f(inp["ssd_dt_bias"][0]).reshape(32)
    rows[0, ROWS["alog"][0]:ROWS["alog"][0] + 32] = f(inp["ssd_a_log"][0]).reshape(32)
    rows[0, ROWS["dskip"][0]:ROWS["dskip"][0] + 16] = f(inp["ssd_d"][0])
    shared["rows"] = rows
    k = np.arange(128)
    consts = np.concatenate([np.eye(128, dtype=np.float32),
                             (k[:, None] <= k[None, :]).astype(np.float32),
                             (k[:, None] >= k[None, :]).astype(np.float32),
                             (k[:, None] < k[None, :]).astype(np.float32),
                             (k[:, None] > k[None, :]).astype(np.float32)], axis=1)
    shared["consts"] = np.ascontiguousarray(consts)
    rpb = f(inp["na_rpb"][0])
    c_ = np.arange(64)
    ws = np.clip(c_ - 8, 0, 48)
    kc_ = np.arange(64)
    ok = (kc_[:, None] >= ws[None, :]) & (kc_[:, None] < ws[None, :] + 16)
    dc = np.clip(kc_[:, None] - c_[None, :], -15, 15) + 15
    tblk = rpb[:, :, dc]
    tblk = np.where(ok[None, None], tblk, np.float32(NEG)).astype(np.float32)
    shared["tblk"] = np.ascontiguousarray(tblk)
    cols_common = np.zeros((128, NCOL), np.float32)

    def put(name, arr):
        o, w = COLS[name]
        cols_common[:, o:o + w] = arr
    put("cctx", _col(inp["c_ctx"]))
    for l in range(2):
        put("modb%d" % l, _col(inp["mod_b"][l]))
        put("nmix%d" % l, _col(inp["norm_mix_w"][l]))
        put("nmlp%d" % l, _col(inp["norm_mlp_w"][l]))
    put("fnw", _col(inp["final_norm_w"]))
    cw = f(inp["ssdna_conv_w"][0])
    for t in range(3):
        put("cw%d" % t, _col(cw[t]))
        put("scw%d" % t, _col(f(inp["sc_conv_w"][0])[t]))
    put("cb", _col(inp["ssdna_conv_b"][0]))
    per_core = []
    x = np.asarray(inp["x"], np.float32)
    ctx = np.asarray(inp["ctx"], np.float32)
    for b in range(8):
        cols = cols_common.copy()
        o, w = COLS["c"]
        cols[:, o:o + w] = _col(inp["c"][b])
        xT = np.ascontiguousarray(x[b].reshape(SEQ, 8, 128).transpose(2, 1, 0))
        cT = np.ascontiguousarray(ctx[b].reshape(CTX, 8, 128).transpose(2, 1, 0))
        per_core.append({"xT": xT, "ctxT": cT, "cols": cols})
    return shared, per_core


class Prog:
    def __init__(self, stop_after=None, debug=()):
        self.stop_after = stop_after
        self.debug = debug
        nc = self.nc = bass.Bass("TRN2", target_bir_lowering=False)
        self.dr = {}
        ein = lambda n, s: nc.dram_tensor(n, list(s), F32, kind="ExternalInput").ap()
        self.xT = ein("xT", (128, 8, SEQ))
        self.ctxT = ein("ctxT", (128, 8, CTX))
        self.cols_d = ein("cols", (128, NCOL))
        self.rows_d = ein("rows", (1, NROW))
        self.consts_d = ein("consts", (128, NCONST))
        self.modw = ein("modw", (2, 6, 128, 8, 1024))
        self.inw = ein("inw", (len(IN_GROUPS), 128, 8, 512))
        self.outw0 = ein("outw0", (4, 128, 16, 256))
        self.w1 = ein("w1", (2, 8, 128, 8, 512))
        self.w2 = ein("w2", (2, 8, 128, 32, 128))
        self.scin = ein("scin", (6, 128, 8, 512))
        self.scout = ein("scout", (2, 128, 8, 512))
        self.tblk = ein("tblk", (16, 15, 64, 64))
        self.outT = nc.dram_tensor("outT", [128, 8, SEQ], F32, kind="ExternalOutput").ap()
        scr = lambda n, s, dt: nc.dram_tensor(n, list(s), dt, kind="Internal").ap()
        self.s_xtok = scr("s_xtok", (NTOK, 1024), BF16)
        self.s_BT = scr("s_BT", (256, NTOK), BF16)
        self.s_CT = scr("s_CT", (256, SEQ), BF16)
        self.s_Btok = scr("s_Btok", (NTOK, 256), BF16)
        self.s_z = scr("s_z", (SEQ, 1024), BF16)
        self.s_qT2 = scr("s_qT", (128, 8, SEQ), BF16)
        self.s_kT2 = scr("s_kT", (128, 8, NTOK), BF16)
        self.s_qT = self.s_qT2.rearrange("(par d) c t -> d c par t", par=2)
        self.s_kT = self.s_kT2.rearrange("(par d) c t -> d c par t", par=2)
        self.s_vtok = scr("s_vtok", (NTOK, 16, 65), BF16)
        self.s_ycatT = scr("s_ycatT", (128, 16, SEQ), BF16)
        self.wsb = {"in": scr("wsb_in", (len(IN_GROUPS), 128, 4096), BF16), "out0": scr("wsb_out0", (4, 128, 4096), BF16),
                    "w1": scr("wsb_w1", (2, 8, 128, 4096), BF16), "w2": scr("wsb_w2", (2, 8, 128, 4096), BF16),
                    "scin": scr("wsb_scin", (6, 128, 4096), BF16), "scout": scr("wsb_scout", (2, 128, 4096), BF16)}
        self.b_wsb = {}
        self.tail_casts = []
        self.dbg_out = {}

    def dbg_tensor(self, name, shape, dtype=F32):
        ap = self.nc.dram_tensor("dbg_" + name, list(shape), dtype, kind="ExternalOutput").ap()
        self.dbg_out[name] = ap
        return ap

    def mm(self, out, lhsT, rhs, start, stop, reads, writes, signal, skip=False):
        if skip:
            self.S.op("pe", lambda e, o=out, l=lhsT, r=rhs, s=start, t=stop: e.matmul(o, l, r, start=s, stop=t, skip_group_check=True),
                      reads=reads, writes=writes, signal=signal)
            return
        self.S.op("pe", lambda e, o=out, l=lhsT, r=rhs, s=start, t=stop: e.matmul(o, l, r, start=s, stop=t),
                  reads=reads, writes=writes, signal=signal)

    def tr(self, out, in_, ident, reads, writes, signal=True):
        self.S.op("pe", lambda e, o=out, i=in_, d=ident: e.transpose(o, i, d), reads=reads, writes=writes, signal=signal)

    def act(self, out, in_, func, reads, writes, bias=None, scale=None, accum_out=None, eng="act"):
        kw = {}
        if bias is not None:
            kw["bias"] = bias
        if scale is not None:
            kw["scale"] = scale
        if accum_out is not None:
            kw["accum_out"] = accum_out
        self.S.op(eng, lambda e, o=out, i=in_, f=func, kw=kw: e.activation(out=o, in_=i, func=f, **kw),
                  reads=reads, writes=writes)

    def tt(self, eng, out, in0, in1, op, reads, writes):
        self.S.op(eng, lambda e, o=out, a=in0, b=in1, p=op: e.tensor_tensor(out=o, in0=a, in1=b, op=p),
                  reads=reads, writes=writes)

    def ts(self, eng, out, in0, s1, op0, reads, writes, s2=None, op1=None):
        if op1 is None:
            self.S.op(eng, lambda e, o=out, a=in0, s=s1, p=op0: e.tensor_scalar(out=o, in0=a, scalar1=s, scalar2=None, op0=p),
                      reads=reads, writes=writes)
        else:
            self.S.op(eng, lambda e, o=out, a=in0, s=s1, t=s2, p=op0, q=op1:
                      e.tensor_scalar(out=o, in0=a, scalar1=s, scalar2=t, op0=p, op1=q), reads=reads, writes=writes)

    def stt(self, eng, out, in0, scalar, in1, op0, op1, reads, writes):
        self.S.op(eng, lambda e, o=out, a=in0, s=scalar, b=in1, p=op0, q=op1:
                  e.scalar_tensor_tensor(out=o, in0=a, scalar=s, in1=b, op0=p, op1=q), reads=reads, writes=writes)

    def cp(self, eng, out, in_, reads, writes):
        if eng == "act":
            self.S.op("act", lambda e, o=out, i=in_: e.activation(out=o, in_=i, func=AF.Copy), reads=reads, writes=writes)
        else:
            self.S.op(eng, lambda e, o=out, i=in_: e.tensor_copy(out=o, in_=i), reads=reads, writes=writes)

    def recip(self, out, in_, reads, writes):
        self.S.op("dve", lambda e, o=out, i=in_: e.reciprocal(out=o, in_=i), reads=reads, writes=writes)

    def memset(self, eng, ap, val, writes):
        self.S.op(eng, lambda e, a=ap, v=val: e.memset(a, v), reads=(), writes=writes)

    def dma(self, qe, out, in_, reads=(), writes=()):
        self.S.dma(qe, out, in_, reads=reads, writes=writes)

    def build(self):
        nc = self.nc
        with ExitStack() as st:
            words = 51 * 1024
            arena_t = st.enter_context(nc.sbuf_tensor("arena", [128, words], F32))
            self.A = Arena(arena_t, words)
            self.ps = [st.enter_context(nc.psum_tensor("ps%d" % i, [128, 512], F32)) for i in range(8)]
            self.psb = [Buf("ps%d" % i) for i in range(8)]
            self.S = Sched(nc)
            self.phase_setup()
            self.precast("in")
            done = self.run_phases()
            self.S.barrier()
            self.S.final_wait("sp")
            self.S.replay(st)
        return nc

    def run_phases(self):
        self.phase_mod_and_h()
        if self.stop_after == "h":
            return
        self.S.barrier()
        self.phase_inproj()
        if self.stop_after == "inproj":
            return
        self.S.barrier()
        if "skip_ssd" not in self.debug:
            self.phase_ssd()
        else:
            self.b_ycat = Buf("ycat")
        if self.stop_after == "ssd":
            return
        self.S.barrier()
        if "skip_na" not in self.debug:
            self.phase_na()
        if self.stop_after == "na":
            return
        self.S.barrier()
        self.phase_tail()

    def phase_setup(self):
        A = self.A
        self.cols = A.alloc([NCOL], F32)
        self.b_cols = Buf("cols")
        self.dma("sp", self.cols, self.cols_d, writes=[self.b_cols])
        self.consts = A.alloc([NCONST], F32)
        self.b_consts = Buf("consts")
        self.dma("sp", self.consts, self.consts_d, writes=[self.b_consts])
        self.ident_f = self.consts[:, 0:128]
        self.L_f = self.consts[:, 128:256]
        self.U_f = self.consts[:, 256:384]
        self.SL_f = self.consts[:, 384:512]
        self.SU_f = self.consts[:, 512:640]
        self.dt_raw = A.alloc([34, 32], F32)
        self.b_dtraw = Buf("dtraw")
        self.ident_b = A.alloc([128], BF16)
        self.ones_b = A.alloc([128], BF16)
        self.b_cb = Buf("constb")
        self.cp("dve", self.ident_b, self.ident_f, [self.b_consts], [self.b_cb])
        self.memset("dve", self.ones_b, 1.0, [self.b_cb])
        self.modv = [A.alloc([48], F32) for _ in range(2)]
        self.b_modv = [[Buf("modv%d_%d" % (l, v)) for v in range(6)] for l in range(2)]
        self.modc = A.alloc([16], F32)
        self.b_modc = Buf("modc")
        self.gcols = A.alloc([64], F32)
        self.b_g = {}

    def precast(self, which):
        def one(key, idx, src):
            b = Buf("wsb_%s_%s" % (key, idx))
            self.b_wsb[(key,) + idx] = b
            dst = self.wsb[key]
            for i in idx:
                dst = dst[i]
                src = src[i]
            if which == "in":
                self.dma("pool", dst, src.rearrange("p k n -> p (k n)"), writes=[b])
            else:
                self.tail_casts.append(lambda gate, dst=dst, src=src, b=b: self.dma(
                    "pool", dst, src.rearrange("p k n -> p (k n)"), reads=[gate], writes=[b]))
        if which == "in":
            for g in range(len(IN_GROUPS)):
                one("in", (g,), self.inw)
            return
        for g in range(4):
            one("out0", (g,), self.outw0)
        for l in range(2):
            if l == 1:
                for g in range(6):
                    one("scin", (g,), self.scin)
                for g in range(2):
                    one("scout", (g,), self.scout)
            for g in range(8):
                one("w1", (l, g), self.w1)
            for g in range(8):
                one("w2", (l, g), self.w2)

    def cast_some(self, n, gate):
        for _ in range(n):
            if self.tail_casts:
                self.tail_casts.pop(0)(gate)

    def wsrc(self, key, idx, kc):
        src = self.wsb[key]
        for i in idx:
            src = src[i]
        return src.rearrange("p (k n) -> p k n", k=kc), self.b_wsb[(key,) + idx]

    def colv(self, name, j=None):
        o, w = COLS[name]
        if j is None:
            return self.cols[:, o:o + w]
        return self.cols[:, o + j:o + j + 1]

    def phase_mod_and_h(self):
        A, S = self.A, self.S
        m0 = A.mark()
        self.mark_base = m0
        self.hT = A.alloc([8, SEQ], BF16)
        self.hcT = A.alloc([8, CTX], BF16)
        self.b_hT = [Buf("hT%d" % t) for t in range(8)]
        self.b_hcT = Buf("hcT")
        m1 = A.mark()
        s2 = A.alloc([8, 2], F32)
        b_s2 = Buf("s2")
        self.act(s2[:, :, 0], self.colv("c"), AF.Silu, [self.b_cols], [b_s2])
        self.act(s2[:, :, 1], self.colv("cctx"), AF.Silu, [self.b_cols], [b_s2])
        wbuf = [A.alloc([8, 1024], F32) for _ in range(2)]
        b_w = [Buf("modw%d" % i) for i in range(2)]
        modcnt = [0]

        def mod_group(l, v):
            n = modcnt[0]
            modcnt[0] += 1
            wb, bw = wbuf[n % 2], b_w[n % 2]
            self.dma("sp", wb, self.modw[l, v], writes=[bw])
            pst = self.ps[n % 2][:, 0:16].rearrange("p (j t) -> p j t", t=2)
            bps = self.psb[n % 2]
            for j in range(8):
                for kc in range(8):
                    self.mm(pst[:, j, :], wb[:, kc, j * 128:(j + 1) * 128], s2[:, kc, :], kc == 0, kc == 7,
                            [bw, b_s2], [bps], signal=(j == 7 and kc == 7))
            o, _ = COLS["modb%d" % l]
            self.tt("dve", self.modv[l][:, v * 8:(v + 1) * 8], pst[:, :, 0], self.cols[:, o + v * 8:o + v * 8 + 8],
                    ALU.add, [bps, self.b_cols], [self.b_modv[l][v]])
            if l == 0 and v < 2:
                self.tt("dve", self.modc[:, v * 8:(v + 1) * 8], pst[:, :, 1], self.cols[:, o + v * 8:o + v * 8 + 8],
                        ALU.add, [bps, self.b_cols], [self.b_modc])

        mod_group(0, 0)
        mod_group(0, 1)
        rest = [(0, v) for v in range(2, 6)] + [(1, v) for v in range(6)]
        def gain(slot, normname, scale_ap, rd):
            g = self.gcols[:, slot * 8:(slot + 1) * 8]
            b = Buf("g%d" % slot)
            self.stt("dve", g, scale_ap, 1.0, self.colv(normname), ALU.add, ALU.mult, rd + [self.b_cols], [b])
            return g, b
        self.g_a0, self.b_ga0 = gain(0, "nmix0", self.modv[0][:, 8:16], [self.b_modv[0][1]])
        self.g_c, self.b_gc = gain(4, "nmix0", self.modc[:, 8:16], [self.b_modc])
        xbuf = [A.alloc([8, 512], F32) for _ in range(2)]
        b_x = [Buf("xb%d" % i) for i in range(2)]
        wk = self.norm_alloc(512)
        wk["use_pool"] = False
        for t in range(9):
            xb, bx = xbuf[t % 2], b_x[t % 2]
            if t < 8:
                ntok = 512
                self.dma("sp", xb, self.xT[:, :, t * 512:(t + 1) * 512], writes=[bx])
                self.norm_tile(xb, bx, ntok, self.g_a0, self.b_ga0, self.modv[0][:, 0:8], self.b_modv[0][0],
                               self.hT[:, :, t * 512:(t + 1) * 512], self.b_hT[t], wk, t)
            else:
                ntok = CTX
                self.dma("sp", xb[:, :, 0:CTX], self.ctxT, writes=[bx])
                self.norm_tile(xb[:, :, 0:CTX], bx, ntok, self.g_c, self.b_gc, self.modc[:, 0:8], self.b_modc,
                               self.hcT, self.b_hcT, wk, t)
            for _ in range(2 if t == 0 else 1):
                if rest:
                    mod_group(*rest.pop(0))
        while rest:
            mod_group(*rest.pop(0))
        self.g_f0, self.b_gf0 = gain(1, "nmlp0", self.modv[0][:, 32:40], [self.b_modv[0][4]])
        self.g_a1, self.b_ga1 = gain(2, "nmix1", self.modv[1][:, 8:16], [self.b_modv[1][1]])
        self.g_f1, self.b_gf1 = gain(3, "nmlp1", self.modv[1][:, 32:40], [self.b_modv[1][4]])
        self.precast("tail")
        if "h" in self.debug:
            d = self.dbg_tensor("hT", (128, 8, SEQ), BF16)
            self.dma("sp", d, self.hT, reads=self.b_hT)
            d = self.dbg_tensor("hcT", (128, 8, CTX), BF16)
            self.dma("sp", d, self.hcT, reads=[self.b_hcT])
            d = self.dbg_tensor("modv0", (128, 48))
            self.dma("sp", d, self.modv[0], reads=self.b_modv[0])
            d = self.dbg_tensor("modv1", (128, 48))
            self.dma("sp", d, self.modv[1], reads=self.b_modv[1])
        self.mark_after_h = m1

    def norm_alloc(self, ntok):
        A = self.A
        wk = {"sq": [A.alloc([8, ntok], BF16) for _ in range(2)], "b_sq": [Buf("sq0"), Buf("sq1")],
              "rstd": [A.alloc([ntok], F32) for _ in range(2)], "b_rstd": [Buf("rs0"), Buf("rs1")],
              "tmp": [A.alloc([ntok], F32) for _ in range(3)], "b_tmp": [Buf("nt%d" % i) for i in range(3)],
              "n": 0}
        return wk

    def norm_accum(self, x, bx, m, wk, it, psi):
        i2 = it % 2
        sq = wk["sq"][i2]
        bsq = wk["b_sqc"][m]
        ps, bps = self.ps[psi + i2], self.psb[psi + i2]
        self.act(sq[:, m, :], x[:, m, :], AF.Square, [bx], [bsq])
        return lambda: self.mm(ps[:, :], self.ones_b, sq[:, m, :], m == 0, m == 7, [bsq, self.b_cb], [bps], signal=(m == 7))

    def norm_tile(self, x, bx, ntok, g, bg, shift, bshift, out, bout, wk, it, psi=2, final=False, accum_done=False):
        i2 = it % 2
        sq, bsq = wk["sq"][i2], wk["b_sq"][i2]
        rstd, brs = wk["rstd"][i2], wk["b_rstd"][i2]
        ps, bps = self.ps[psi + i2], self.psb[psi + i2]
        if not accum_done:
            self.act(sq[:, :, 0:ntok], x, AF.Square, [bx], [bsq])
            for kc in range(8):
                self.mm(ps[:, 0:ntok], self.ones_b, sq[:, kc, 0:ntok], kc == 0, kc == 7, [bsq, self.b_cb], [bps], signal=(kc == 7))
        self.ts("dve", rstd[:, 0:ntok], ps[:, 0:ntok], 1.0 / D, ALU.mult, [bps], [brs], s2=EPS, op1=ALU.add)
        self.act(rstd[:, 0:ntok], rstd[:, 0:ntok], AF.Sqrt, [brs], [brs])
        self.recip(rstd[:, 0:ntok], rstd[:, 0:ntok], [brs], [brs])
        for kc in range(8):
            k3 = wk["n"] % 3
            wk["n"] += 1
            tmp, btmp = wk["tmp"][k3], wk["b_tmp"][k3]
            bxk = bx[kc] if isinstance(bx, list) else bx
            self.tt("dve" if (kc % 2 == 0 or not wk.get("use_pool", True)) else "pool", tmp[:, 0:ntok], x[:, kc, :], rstd[:, 0:ntok],
                    ALU.mult, [bxk, brs], [btmp])
            boutk = bout[kc] if isinstance(bout, list) else bout
            if final:
                self.act(out[:, kc, :], tmp[:, 0:ntok], AF.Identity, [btmp, bg], [boutk], scale=g[:, kc:kc + 1])
            else:
                self.act(out[:, kc, :], tmp[:, 0:ntok], AF.Identity, [btmp, bshift, bg], [boutk], bias=shift[:, kc:kc + 1],
                         scale=g[:, kc:kc + 1])

    def phase_inproj(self):
        A, S = self.A, self.S
        A.release(self.mark_after_h)
        wts = [A.alloc([8, 512], BF16) for _ in range(3)]
        b_wts = [Buf("inw%d" % i) for i in range(3)]
        pre = [A.alloc([SEQ + 2 + CTX + 2], F32) for _ in range(2)]
        b_pre = [Buf("pre0"), Buf("pre1")]
        for i in range(2):
            for c in (0, SEQ + 1, SEQ + 2, SEQ + 2 + CTX + 1):
                self.memset("pool", pre[i][:, c:c + 1], 0.0, [b_pre[i]])
        cacc = A.alloc([NTOK], F32)
        b_cacc = Buf("cacc")
        cv = [A.alloc([NTOK], BF16) for _ in range(2)]
        b_cv = [Buf("cv0"), Buf("cv1")]
        xst = [A.alloc([34, 128], BF16) for _ in range(2)]
        b_xst = [Buf("xst0"), Buf("xst1")]
        NK = 6
        kst = [A.alloc([512], BF16) for _ in range(NK)]
        b_kst = [Buf("kst%d" % i) for i in range(NK)]
        vst = [A.alloc([16, 65], BF16) for _ in range(2)]
        b_vst = [Buf("vst0"), Buf("vst1")]
        for i in range(2):
            self.memset("pool", vst[i][:, :, 64:65], 1.0, [b_vst[i]])
        zst = [A.alloc([1024], BF16) for _ in range(2)]
        b_zst = [Buf("zst0"), Buf("zst1")]
        st = {"w": 0, "ps": 0, "pre": 0, "cv": 0, "xst": 0, "kst": 0, "vst": 0, "zst": 0, "tp": 0}
        psT = [self.ps[4][:, :].bitcast(BF16), self.ps[5][:, :].bitcast(BF16)]
        b_scr = {k: Buf(k) for k in ("xtok", "BT", "CT", "Btok", "z", "qT", "kT", "vtok")}
        self.b_scr = b_scr

        def load_w(gname):
            i = st["w"] % 3
            st["w"] += 1
            src, bsrc = self.wsrc("in", (IN_GIDX[gname],), 8)
            self.dma("sp", wts[i], src, reads=[bsrc], writes=[b_wts[i]])
            return wts[i], b_wts[i]

        def fm_tile(wt, bw, j, tt):
            pi = st["ps"] % 4
            st["ps"] += 1
            ps, bps = self.ps[pi], self.psb[pi]
            if tt < 8:
                ntok = 512
                rhs = lambda kc: self.hT[:, kc, tt * 512:(tt + 1) * 512]
                rb = self.b_hT[tt]
            else:
                ntok = CTX
                rhs = lambda kc: self.hcT[:, kc, :]
                rb = self.b_hcT
            for kc in range(8):
                self.mm(ps[:, 0:ntok], wt[:, kc, j * 128:(j + 1) * 128], rhs(kc), kc == 0, kc == 7, [bw, rb], [bps],
                        signal=(kc == 7))
            return ps[:, 0:ntok], bps

        def conv_chunk(wt, bw, j, ch, has_ctx):
            i = st["pre"] % 2
            st["pre"] += 1
            p, bp = pre[i], b_pre[i]
            for tt in range(9 if has_ctx else 8):
                ps, bps = fm_tile(wt, bw, j, tt)
                if tt < 8:
                    dst = p[:, 1 + tt * 512:1 + (tt + 1) * 512]
                else:
                    dst = p[:, SEQ + 3:SEQ + 3 + CTX]
                self.cp("act", dst, ps, [bps], [bp])
            ci = st["cv"] % 2
            st["cv"] += 1
            c, bc = cv[ci], b_cv[ci]
            w0, w1, w2, cb = (self.colv("cw0", ch), self.colv("cw1", ch), self.colv("cw2", ch), self.colv("cb", ch))
            segs = [(0, 0, SEQ)] + ([(SEQ + 2, SEQ, CTX)] if has_ctx else [])
            for (po, co, n) in segs:
                self.act(cacc[:, co:co + n], p[:, po:po + n], AF.Identity, [bp, self.b_cols], [b_cacc], bias=cb, scale=w0)
                self.stt("dve", cacc[:, co:co + n], p[:, po + 1:po + 1 + n], w1, cacc[:, co:co + n], ALU.mult, ALU.add,
                         [bp, b_cacc, self.b_cols], [b_cacc])
                self.stt("dve", cacc[:, co:co + n], p[:, po + 2:po + 2 + n], w2, cacc[:, co:co + n], ALU.mult, ALU.add,
                         [bp, b_cacc, self.b_cols], [b_cacc])
                self.act(c[:, co:co + n], cacc[:, co:co + n], AF.Silu, [b_cacc], [bc])
            self.cast_some(2, bc)
            return c, bc

        def to_tokmajor(c, bc, nblk, dst_ap, bdst):
            xi = st["xst"] % 2
            st["xst"] += 1
            xs_, bxs = xst[xi], b_xst[xi]
            blk = 0
            while blk < nblk:
                nb = min(8, nblk - blk)
                ti = st["tp"] % 2
                st["tp"] += 1
                pt, bpt = psT[ti], self.psb[4 + ti]
                for q in range(nb):
                    self.tr(pt[:, q * 128:(q + 1) * 128], c[:, (blk + q) * 128:(blk + q + 1) * 128], self.ident_b,
                            [bc, self.b_cb], [bpt], signal=(q == nb - 1))
                self.cp("dve", xs_[:, blk:blk + nb, :], pt[:, 0:nb * 128].rearrange("p (a b) -> p a b", a=nb), [bpt], [bxs])
                blk += nb
            for b0 in range(0, nblk, 9):
                b1 = min(nblk, b0 + 9)
                self.dma("sp", dst_ap[:, b0:b1, :], xs_[:, b0:b1, :], reads=[bxs], writes=[bdst])

        xtok_v = self.s_xtok.rearrange("(b p) c -> p b c", p=128)
        btok_v = self.s_Btok.rearrange("(b p) c -> p b c", p=128)
        deferred = []

        def defer(fn):
            if deferred:
                deferred.pop(0)()
            if fn is not None:
                deferred.append(fn)

        for gi, gname in enumerate(("x0", "x1")):
            wt, bw = load_w(gname)
            for j in range(4):
                ch = gi * 4 + j
                c, bc = conv_chunk(wt, bw, j, ch, True)
                defer(lambda c=c, bc=bc, ch=ch: to_tokmajor(c, bc, 34, xtok_v[:, :, ch * 128:(ch + 1) * 128], b_scr["xtok"]))
        wt, bw = load_w("B")
        for j in range(2):
            c, bc = conv_chunk(wt, bw, j, 8 + j, True)
            self.dma("sp", self.s_BT[j * 128:(j + 1) * 128, :], c, reads=[bc], writes=[b_scr["BT"]])
            defer(lambda c=c, bc=bc, j=j: to_tokmajor(c, bc, 34, btok_v[:, :, j * 128:(j + 1) * 128], b_scr["Btok"]))
        wt, bw = load_w("C")
        for j in range(2):
            c, bc = conv_chunk(wt, bw, j, 10 + j, False)
            self.dma("sp", self.s_CT[j * 128:(j + 1) * 128, :], c[:, 0:SEQ], reads=[bc], writes=[b_scr["CT"]])
            defer(None)
        wt, bw = load_w("dt")
        for blk in range(34):
            ps, bps = self.ps[6], self.psb[6]
            o = (blk % 8) * 32
            for kc in range(8):
                lhs = self.hT[:, kc, blk * 128:(blk + 1) * 128] if blk < 32 else self.hcT[:, kc, (blk - 32) * 128:(blk - 31) * 128]
                rb = self.b_hT[blk // 4] if blk < 32 else self.b_hcT
                self.mm(ps[:, o:o + 32], lhs, wt[:, kc, 0:32], kc == 0, kc == 7, [bw, rb], [bps], signal=(kc == 7))
            self.cp("dve", self.dt_raw[:, blk, :], ps[:, o:o + 32], [bps], [self.b_dtraw])
        for gname, dst, bd, has_ctx, scale in (("k0", self.s_kT2, b_scr["kT"], True, None), ("k1", self.s_kT2, b_scr["kT"], True, None),
                                               ("q0", self.s_qT2, b_scr["qT"], False, 0.125), ("q1", self.s_qT2, b_scr["qT"], False, 0.125)):
            wt, bw = load_w(gname)
            for j in range(4):
                hp = (int(gname[1]) * 4 + j) * 2
                for tt in range(9 if has_ctx else 8):
                    ps, bps = fm_tile(wt, bw, j, tt)
                    ntok = 512 if tt < 8 else CTX
                    ki = st["kst"] % NK
                    st["kst"] += 1
                    ks, bks = kst[ki], b_kst[ki]
                    if scale is None:
                        self.cp("act", ks[:, 0:ntok], ps, [bps], [bks])
                    else:
                        self.act(ks[:, 0:ntok], ps, AF.Copy, [bps], [bks], scale=scale)
                    t0 = tt * 512
                    self.dma("sp", dst[:, hp // 2, t0:t0 + ntok], ks[:, 0:ntok], reads=[bks], writes=[bd])
                self.cast_some(2, bks)
        for which in ("v", "z"):
            w0, bw0 = load_w(which + "0")
            w1, bw1 = load_w(which + "1")
            nblk = 34 if which == "v" else 32
            for blk in range(nblk):
                lhs = (lambda kc, blk=blk: self.hT[:, kc, blk * 128:(blk + 1) * 128]) if blk < 32 else \
                    (lambda kc, blk=blk: self.hcT[:, kc, (blk - 32) * 128:(blk - 31) * 128])
                rb = self.b_hT[blk // 4] if blk < 32 else self.b_hcT
                if which == "v":
                    si = st["vst"] % 2
                    st["vst"] += 1
                    sb, bsb = vst[si], b_vst[si]
                else:
                    si = st["zst"] % 2
                    st["zst"] += 1
                    sb, bsb = zst[si], b_zst[si]
                for half, (wt, bw) in enumerate(((w0, bw0), (w1, bw1))):
                    pi = st["ps"] % 4
                    st["ps"] += 1
                    ps, bps = self.ps[pi], self.psb[pi]
                    for kc in range(8):
                        self.mm(ps[:, :], lhs(kc), wt[:, kc, :], kc == 0, kc == 7, [bw, rb], [bps], signal=(kc == 7))
                    if which == "v":
                        self.cp("act" if half == 0 else "dve", sb[:, half * 8:(half + 1) * 8, 0:64],
                                ps[:, :].rearrange("p (a b) -> p a b", a=8), [bps], [bsb])
                    else:
                        self.act(sb[:, half * 512:(half + 1) * 512], ps[:, :], AF.Silu, [bps], [bsb])
                if which == "v":
                    self.dma("sp", self.s_vtok[blk * 128:(blk + 1) * 128, :, :], sb, reads=[bsb], writes=[b_scr["vtok"]])
                else:
                    self.dma("sp", self.s_z[blk * 128:(blk + 1) * 128, :], sb, reads=[bsb], writes=[b_scr["z"]])
        self.cast_some(1000, self.b_dtraw)
        if "inproj" in self.debug:
            for nm, ap in (("xtok", self.s_xtok), ("BT", self.s_BT), ("CT", self.s_CT), ("Btok", self.s_Btok), ("z", self.s_z),
                           ("qT", self.s_qT2), ("kT", self.s_kT2), ("vtok", self.s_vtok)):
                d = self.dbg_tensor(nm, ap.shape, BF16)
                self.dma("sp", d, ap, reads=[b_scr[nm]])
            d = self.dbg_tensor("dtraw", (128, 34, 32))
            self.dma("sp", d, self.dt_raw, reads=[self.b_dtraw])


    def phase_ssd(self):
        A, S = self.A, self.S
        A.release(self.mark_base)
        b_scr = self.b_scr
        self.b_ycat = Buf("ycat")
        rows = A.alloc([NROW], F32)
        b_rows = Buf("rows")
        self.dma("sp", rows, self.rows_d.partition_broadcast(128), writes=[b_rows])
        rv = lambda nm: rows[:, ROWS[nm][0]:ROWS[nm][0] + ROWS[nm][1]]
        normw_bc, dtb_bc, alog_bc, D_bc = rv("ssdnw"), rv("dtb"), rv("alog"), rv("dskip")
        dtv = A.alloc([34, 32], F32)
        av = A.alloc([34, 32], F32)
        aneg = A.alloc([32], F32)
        b_dt = Buf("dtv")
        self.tt("dve", dtv, self.dt_raw, dtb_bc[:, None, :].to_broadcast([128, 34, 32]), ALU.add, [self.b_dtraw, b_rows], [b_dt])
        self.act(dtv, dtv, AF.Exp, [b_dt], [b_dt])
        self.act(dtv, dtv, AF.Ln, [b_dt], [b_dt], bias=1.0, scale=1.0)
        self.act(aneg, alog_bc, AF.Exp, [b_rows], [b_dt])
        self.ts("dve", aneg, aneg, -1.0, ALU.mult, [b_dt], [b_dt])
        self.tt("dve", av, dtv, aneg[:, None, :].to_broadcast([128, 34, 32]), ALU.mult, [b_dt], [b_dt])
        negL = A.alloc([128], F32)
        negU = A.alloc([128], F32)
        ones_f = A.alloc([128], F32)
        b_c2 = Buf("c2")
        self.ts("dve", negL, self.L_f, -1.0, ALU.mult, [self.b_consts], [b_c2])
        self.ts("dve", negU, self.U_f, -1.0, ALU.mult, [self.b_consts], [b_c2])
        self.memset("dve", ones_f, 1.0, [b_c2])
        hstore = A.alloc([32, 1024], BF16)
        b_hst = [Buf("hst%d" % i) for i in range(32)]
        hf = [A.alloc([1024], F32) for _ in range(2)]
        b_hf = [Buf("hf0"), Buf("hf1")]
        hb = A.alloc([1024], BF16)
        b_hb = Buf("hb")
        for d in range(2):
            self.memset("pool", hf[d], 0.0, [b_hf[d]])
        NB = 2
        xt = [A.alloc([16, 64], BF16) for _ in range(NB)]
        Bt = [A.alloc([256], BF16) for _ in range(NB)]
        BTc = [A.alloc([2, 128], BF16) for _ in range(NB)]
        CTc = [A.alloc([2, 128], BF16) for _ in range(NB)]
        zc = [A.alloc([1024], BF16) for _ in range(NB)]
        b_ld = [{k: Buf("%s%d" % (k, i)) for k in ("xt", "Bt", "BTc", "CTc", "zc")} for i in range(NB)]
        ex = [A.alloc([48], F32) for _ in range(2)]
        ws = [A.alloc([16], F32) for _ in range(2)]
        b_ex = [Buf("ex0"), Buf("ex1")]
        xdt = [A.alloc([16, 64], BF16) for _ in range(2)]
        b_xdt = [Buf("xdt0"), Buf("xdt1")]
        xs = [A.alloc([16, 64], BF16) for _ in range(2)]
        b_xs = [Buf("xs0"), Buf("xs1")]
        xD = A.alloc([16, 64], BF16)
        b_xD = Buf("xD")
        G = [A.alloc([16, 128], F32) for _ in range(2)]
        b_G = [Buf("G0"), Buf("G1")]
        Ebc = [A.alloc([4, 128], F32) for _ in range(3)]
        b_Ebc = [Buf("Ebc%d" % i) for i in range(3)]
        E = [A.alloc([4, 128], F32) for _ in range(3)]
        b_E = [Buf("E%d" % i) for i in range(3)]
        MT = [A.alloc([4, 128], BF16) for _ in range(3)]
        b_MT = [Buf("MT%d" % i) for i in range(3)]
        Cp = [A.alloc([4, 128], BF16) for _ in range(3)]
        b_Cp = [Buf("Cp%d" % i) for i in range(3)]
        cbm = [A.alloc([2, 128], F32) for _ in range(2)]
        b_cbm = Buf("cbm")
        yg = A.alloc([1024], F32)
        b_yg = Buf("yg")
        ysq = A.alloc([1024], F32)
        ss = A.alloc([2], F32)
        b_ss = Buf("ss")
        yn = A.alloc([1024], BF16)
        b_yn = Buf("yn")
        ycst = [A.alloc([8, 512], BF16) for _ in range(2)]
        b_ycst = [Buf("ycst0"), Buf("ycst1")]
        stmp = A.alloc([1024], F32)
        b_stmp = Buf("stmp")
        cnt = {"ld": 0, "ex": 0, "eb": 0, "mt": 0, "seg": 0}
        tri = [self.L_f, self.U_f]
        stri = [self.SU_f, self.SL_f]
        negtri = [negL, negU]
        BTv = self.s_BT.rearrange("(g n) t -> n g t", g=2)
        CTv = self.s_CT.rearrange("(g n) t -> n g t", g=2)
        ps_small, b_small = self.ps[0], self.psb[0]
        ps_cb, b_cb_ps = self.ps[1], self.psb[1]
        ps_S, b_S = self.ps[6], self.psb[6]
        psT, b_psT = self.ps[7][:, :].bitcast(BF16), self.psb[7]

        def load(blk, full):
            i = cnt["ld"] % NB
            cnt["ld"] += 1
            L = b_ld[i]
            self.dma("sp", xt[i], self.s_xtok[blk * 128:(blk + 1) * 128, :].rearrange("p (h d) -> p h d", h=16),
                     reads=[b_scr["xtok"]], writes=[L["xt"]])
            self.dma("sp", Bt[i], self.s_Btok[blk * 128:(blk + 1) * 128, :], reads=[b_scr["Btok"]], writes=[L["Bt"]])
            if full:
                self.dma("sp", BTc[i], BTv[:, :, blk * 128:(blk + 1) * 128], reads=[b_scr["BT"]], writes=[L["BTc"]])
                self.dma("sp", CTc[i], CTv[:, :, blk * 128:(blk + 1) * 128], reads=[b_scr["CT"]], writes=[L["CTc"]])
                self.dma("sp", zc[i], self.s_z[blk * 128:(blk + 1) * 128, :], reads=[b_scr["z"]], writes=[L["zc"]])
            return i

        def small(blk, d):
            a = av[:, blk, d * 16:(d + 1) * 16]
            o = d * 64
            self.mm(ps_small[:, o:o + 16], tri[d], a, True, True, [self.b_consts, b_dt], [b_small], signal=False)
            self.mm(ps_small[:, o + 16:o + 32], stri[d], a, True, True, [self.b_consts, b_dt], [b_small], signal=False)
            self.mm(ps_small[:, o + 32:o + 48], ones_f, a, True, True, [b_c2, b_dt], [b_small], signal=True)
            self.act(ex[d], ps_small[:, o:o + 48], AF.Exp, [b_small], [b_ex[d]])
            self.tt("dve", ws[d], ex[d][:, 16:32], dtv[:, blk, d * 16:(d + 1) * 16], ALU.mult, [b_ex[d], b_dt], [b_ex[d]])

        def xs_prep(d, li):
            self.tt("pool", xs[d], xt[li], ws[d][:, :, None].to_broadcast([128, 16, 64]), ALU.mult,
                    [b_ld[li]["xt"], b_ex[d]], [b_xs[d]])

        def state_step(d, li, store_blk=None, prep=True):
            if prep:
                xs_prep(d, li)
            if store_blk is not None:
                self.cp("act", hstore[:, store_blk, :], hf[d], [b_hf[d]], [b_hst[store_blk]])
            for g in range(2):
                self.mm(ps_S[:, :], Bt[li][:, g * 128:(g + 1) * 128], xs[d][:, g * 8:(g + 1) * 8, :], True, True,
                        [b_ld[li]["Bt"], b_xs[d]], [b_S], signal=True)
                hv = hf[d][:, g * 512:(g + 1) * 512].rearrange("p (h q) -> p h q", h=8)
                tv = stmp[:, g * 512:(g + 1) * 512].rearrange("p (h q) -> p h q", h=8)
                self.tt("pool", tv, hv, ex[d][:, 32 + g * 8:32 + (g + 1) * 8][:, :, None].to_broadcast([128, 8, 64]), ALU.mult,
                        [b_hf[d], b_ex[d]], [b_stmp])
                self.tt("dve", hf[d][:, g * 512:(g + 1) * 512], stmp[:, g * 512:(g + 1) * 512], ps_S[:, :], ALU.add,
                        [b_stmp, b_S], [b_hf[d]])

        for blk in [33, 32] + list(range(31, -1, -1)):
            li = load(blk, False)
            small(blk, 1)
            state_step(1, li, store_blk=(blk if blk < 32 else None))
        for blk in (32, 33):
            li = load(blk, False)
            small(blk, 0)
            state_step(0, li)
        self.cp("act", hb, hf[0], [b_hf[0]], [b_hb])
        pending_fin = []
        for c in range(32):
            li = load(c, True)
            L = b_ld[li]
            for g in range(2):
                self.mm(ps_cb[:, g * 128:(g + 1) * 128], BTc[li][:, g, :], CTc[li][:, g, :], True, True, [L["BTc"], L["CTc"]],
                        [b_cb_ps], signal=(g == 1))
            pcb = ps_cb[:, 0:256].rearrange("p (g i) -> p g i", g=2)
            self.tt("dve", cbm[0], pcb, self.L_f[:, None, :].to_broadcast([128, 2, 128]), ALU.mult, [b_cb_ps, self.b_consts], [b_cbm])
            self.tt("dve", cbm[1], pcb, self.U_f[:, None, :].to_broadcast([128, 2, 128]), ALU.mult, [b_cb_ps, self.b_consts], [b_cbm])
            self.tt("pool", xD, xt[li], D_bc[:, :, None].to_broadcast([128, 16, 64]), ALU.mult, [L["xt"], b_rows], [b_xD])
            for g in range(2):
                self.mm(self.ps[4 + g][:, :], self.ident_b, xD[:, g * 8:(g + 1) * 8, :], True, False, [self.b_cb, b_xD],
                        [self.psb[4 + g]], signal=False)
            for d in range(2):
                small(c, d)
                if d == 0:
                    xs_prep(0, li)
                dtc = dtv[:, c, d * 16:(d + 1) * 16]
                a = av[:, c, d * 16:(d + 1) * 16]
                self.tt("pool", xdt[d], xt[li], dtc[:, :, None].to_broadcast([128, 16, 64]), ALU.mult, [L["xt"], b_dt], [b_xdt[d]])
                self.tt("pool" if "g_pool" in self.debug else "dve", G[d], tri[d][:, None, :].to_broadcast([128, 16, 128]),
                        a[:, :, None].to_broadcast([128, 16, 128]), ALU.mult, [self.b_consts, b_dt], [b_G[d]])

            def mm1(it):
                d, q = it // 4, it % 4
                k = it % 3
                pseg, b_seg = self.ps[2 + it % 2], self.psb[2 + it % 2]
                self.mm(pseg[:, :], ones_f, G[d][:, 4 * q:4 * q + 4, :], True, True, [b_c2, b_G[d]], [b_seg], signal=True)
                self.act(Ebc[k], pseg[:, :].rearrange("p (a b) -> p a b", a=4), AF.Exp, [b_seg], [b_Ebc[k]])

            def mm2(it):
                d, q = it // 4, it % 4
                k = it % 3
                g = q // 2
                a = av[:, c, d * 16:(d + 1) * 16]
                pseg, b_seg = self.ps[2 + it % 2], self.psb[2 + it % 2]
                self.mm(pseg[:, :].rearrange("p (a b) -> p a b", a=4), negtri[d],
                        a[:, 4 * q:4 * q + 4][:, :, None].to_broadcast([128, 4, 128]), False, True, [b_c2, b_dt, b_Ebc[k]], [b_seg],
                        signal=True, skip=True)
                self.act(E[k], pseg[:, :].rearrange("p (a b) -> p a b", a=4), AF.Exp, [b_seg], [b_E[k]])
                self.stt("dve", MT[k], E[k], 1.0, cbm[d][:, g, :][:, None, :].to_broadcast([128, 4, 128]), ALU.min, ALU.mult,
                         [b_E[k], b_cbm], [b_MT[k]])
                self.tt("pool", Cp[k], Ebc[k], CTc[li][:, g, :][:, None, :].to_broadcast([128, 4, 128]), ALU.mult,
                        [b_Ebc[k], L["CTc"]], [b_Cp[k]])

            def ymm(it):
                d, q = it // 4, it % 4
                k = it % 3
                g = q // 2
                for hh in range(4):
                    h = 4 * q + hh
                    hl = h % 8
                    yps = self.ps[4 + g][:, hl * 64:(hl + 1) * 64]
                    hsrc = hb[:, h * 64:(h + 1) * 64] if d == 0 else hstore[:, c, h * 64:(h + 1) * 64]
                    hbuf = b_hb if d == 0 else b_hst[c]
                    self.mm(yps, MT[k][:, hh, :], xdt[d][:, h, :], False, False, [b_MT[k], b_xdt[d]], [self.psb[4 + g]], signal=False)
                    self.mm(yps, Cp[k][:, hh, :], hsrc, False, (d == 1), [b_Cp[k], hbuf], [self.psb[4 + g]], signal=(hh == 3))

            if "ssd_seq" in self.debug:
                for it in range(8):
                    mm1(it)
                    mm2(it)
                    ymm(it)
            else:
                for s_ in range(10):
                    if s_ < 8:
                        mm1(s_)
                    if 1 <= s_ <= 8:
                        mm2(s_ - 1)
                    if s_ >= 2:
                        ymm(s_ - 2)
                    if s_ == 3 and pending_fin:
                        pending_fin.pop(0)()
            state_step(0, li, prep=False)
            self.cp("act", hb, hf[0], [b_hf[0]], [b_hb])
            for g in range(2):
                self.tt("dve", yg[:, g * 512:(g + 1) * 512], self.ps[4 + g][:, :], zc[li][:, g * 512:(g + 1) * 512], ALU.mult,
                        [self.psb[4 + g], L["zc"]], [b_yg])
            self.memset("dve", ss, 0.0, [b_ss])
            self.act(ysq, yg, AF.Square, [b_yg, b_ss], [b_ss], accum_out=ss[:, 0:1])
            self.ts("dve", ss[:, 1:2], ss[:, 0:1], 1.0 / 1024, ALU.mult, [b_ss], [b_ss], s2=EPS, op1=ALU.add)
            self.act(ss[:, 1:2], ss[:, 1:2], AF.Sqrt, [b_ss], [b_ss])
            self.recip(ss[:, 1:2], ss[:, 1:2], [b_ss], [b_ss])
            self.stt("dve", yn, yg, ss[:, 1:2], normw_bc, ALU.mult, ALU.mult, [b_yg, b_ss, b_rows], [b_yn])
            def fin(c=c):
                for j in range(8):
                    self.tr(psT[:, j * 128:(j + 1) * 128], yn[:, j * 128:(j + 1) * 128], self.ident_b, [b_yn, self.b_cb], [b_psT],
                            signal=(j == 7))
                yi = (c // 4) % 2
                self.cp("act", ycst[yi][:, :, (c % 4) * 128:(c % 4 + 1) * 128], psT[:, :].rearrange("p (a b) -> p a b", a=8),
                        [b_psT], [b_ycst[yi]])
                if c % 4 == 3:
                    t0 = (c // 4) * 512
                    self.dma("sp", self.s_ycatT[:, 0:8, t0:t0 + 512], ycst[yi], reads=[b_ycst[yi]], writes=[self.b_ycat])
            pending_fin.append(fin)
        while pending_fin:
            pending_fin.pop(0)()
        if "ssd" in self.debug:
            d_ = self.dbg_tensor("ycatT", (128, 16, SEQ), BF16)
            self.dma("sp", d_, self.s_ycatT, reads=[self.b_ycat])
            d_ = self.dbg_tensor("hstore", (128, 32, 1024), BF16)
            self.dma("sp", d_, hstore, reads=b_hst)
            d_ = self.dbg_tensor("dtv", (128, 34, 32))
            self.dma("sp", d_, dtv, reads=[b_dt])


    def phase_na(self):
        A, S = self.A, self.S
        A.release(self.mark_base)
        b_scr = self.b_scr
        Kc = A.alloc([16, CTX], BF16)
        Vc = A.alloc([2, 16, 65], BF16)
        b_ctx = Buf("nactx")
        self.dma("sp", Kc[0:64].rearrange("p (c par) t -> p c par t", par=2), self.s_kT[:, :, :, SEQ:NTOK], reads=[b_scr["kT"]], writes=[b_ctx])
        self.dma("sp", Vc, self.s_vtok[SEQ:NTOK].rearrange("(b p) h e -> p b h e", p=128), reads=[b_scr["vtok"]], writes=[b_ctx])
        bias = A.alloc([16, 5, 128], F32)
        b_bias = Buf("nabias")
        Kwin = [A.alloc([16, 576], BF16) for _ in range(2)]
        Qp = [A.alloc([16, 128], BF16) for _ in range(2)]
        Vwin = [A.alloc([5, 16, 65], BF16) for _ in range(2)]
        b_win = [{k: Buf(k + str(i)) for k in ("K", "Q", "V")} for i in range(2)]
        Sw = [A.alloc([5, 128], F32) for _ in range(2)]
        b_Sw = [Buf("Sw0"), Buf("Sw1")]
        for i in range(2):
            self.memset("pool", Sw[i][64:128, 4, :], NEG, [b_Sw[i]])
        PT = [A.alloc([7, 128], BF16) for _ in range(3)]
        b_PT = [Buf("PT%d" % i) for i in range(3)]
        Otok = [A.alloc([16, 64], BF16) for _ in range(2)]
        b_Otok = [Buf("Ot0"), Buf("Ot1")]
        rinv = A.alloc([16], F32)
        b_rinv = Buf("rinv")
        ycst = [A.alloc([8, 512], BF16) for _ in range(2)]
        b_ycst = [Buf("nyc0"), Buf("nyc1")]
        psT, b_psT = self.ps[7][:, :].bitcast(BF16), self.psb[7]
        variants = {0: ((0, 0), (0, -1)), 1: ((0, -2), (0, -3)), 30: ((1, -5), (1, -6)), 31: ((1, -7), (1, -8))}
        interior = ((0, -4), (1, -5))
        cur_var = [None]

        def build_bias(var):
            self.memset("pool", bias, NEG, [b_bias])
            for rq in range(2):
                lo, off = var[rq]
                for ck in range(5):
                    js = [jr for jr in (2 * ck, 2 * ck + 1) if lo <= jr < lo + 8 and jr <= 8]
                    if not js:
                        continue
                    p0 = (js[0] % 2) * 64
                    dr0 = js[0] + off + 7
                    src = self.tblk[:, dr0:dr0 + len(js), :, :].rearrange("h j k c -> (j k) h c")
                    self.dma("sp", bias[p0:p0 + 64 * len(js), :, ck, rq * 64:(rq + 1) * 64], src, writes=[b_bias])

        hgroups = [(4, 0, 7), (5, 7, 14), (6, 14, 16)]
        hg_of = {}
        for (ob_, h0_, h1_) in hgroups:
            for h_ in range(h0_, h1_):
                hg_of[h_] = (ob_, h0_, h1_)
        cnt = {"sw": 0, "pt": 0, "st": 0}
        state = {}

        def emit_loads(rp):
            r = 2 * rp
            kb = min(max(r - 4, 0), 55)
            w0 = kb * 64
            wi = rp % 2
            W = b_win[wi]
            self.dma("sp", Kwin[wi][0:64].rearrange("p (c par) t -> p c par t", par=2), self.s_kT[:, :, :, w0:w0 + 576],
                     reads=[b_scr["kT"]], writes=[W["K"]])
            self.dma("sp", Qp[wi][0:64].rearrange("p (c par) t -> p c par t", par=2), self.s_qT[:, :, :, r * 64:r * 64 + 128],
                     reads=[b_scr["qT"]], writes=[W["Q"]])
            self.dma("sp", Vwin[wi][:, 0:4], self.s_vtok[w0:w0 + 512].rearrange("(b p) h e -> p b h e", p=128),
                     reads=[b_scr["vtok"]], writes=[W["V"]])
            self.dma("sp", Vwin[wi][0:64, 4], self.s_vtok[w0 + 512:w0 + 576], reads=[b_scr["vtok"]], writes=[W["V"]])

        def emit_st(rp, h):
            var = variants.get(rp, interior)
            if var != cur_var[0]:
                build_bias(var)
                cur_var[0] = var
            wi = rp % 2
            W = b_win[wi]
            si = cnt["st"] % 2
            cnt["st"] += 1
            psA, bA = self.ps[2 * si], self.psb[2 * si]
            psB, bB = self.ps[2 * si + 1], self.psb[2 * si + 1]
            q = Qp[wi][0:64, h, :]
            for ck in range(4):
                self.mm(psA[:, ck * 128:(ck + 1) * 128], Kwin[wi][0:64, h, ck * 128:(ck + 1) * 128], q, True, True,
                        [W["K"], W["Q"]], [bA], signal=(ck == 3))
            self.mm(psB[0:64, 0:128], Kwin[wi][0:64, h, 512:576], q, True, True, [W["K"], W["Q"]], [bB], signal=False)
            self.mm(psB[:, 128:256], Kc[0:64, h, 0:128], q, True, True, [b_ctx, W["Q"]], [bB], signal=False)
            self.mm(psB[:, 256:384], Kc[0:64, h, 128:256], q, True, True, [b_ctx, W["Q"]], [bB], signal=True)
            wi2 = cnt["sw"] % 2
            cnt["sw"] += 1
            sw, bsw = Sw[wi2], b_Sw[wi2]
            self.tt("dve", sw[:, 0:4, :], psA[:, :].rearrange("p (a b) -> p a b", a=4), bias[:, h, 0:4, :], ALU.add,
                    [bA, b_bias], [bsw])
            self.tt("dve", sw[0:64, 4, :], psB[0:64, 0:128], bias[0:64, h, 4, :], ALU.add, [bB, b_bias], [bsw])
            pi = cnt["pt"] % 3
            cnt["pt"] += 1
            pt, bpt = PT[pi], b_PT[pi]
            self.act(pt[:, 0:5, :], sw, AF.Exp, [bsw], [bpt])
            self.act(pt[:, 5:7, :], psB[:, 128:384].rearrange("p (a b) -> p a b", a=2), AF.Exp, [bB], [bpt])
            state[(rp, h)] = (pt, bpt)

        def emit_pv(rp, h):
            wi = rp % 2
            W = b_win[wi]
            pt, bpt = state.pop((rp, h))
            obank, h0, h1 = hg_of[h]
            ops, bops = self.ps[obank], self.psb[obank]
            ot, bot = Otok[rp % 2], b_Otok[rp % 2]
            o_ap = ops[:, (h - h0) * 65:(h - h0 + 1) * 65]
            for ck in range(7):
                M = 64 if ck == 4 else 128
                v = Vwin[wi][0:M, ck, h, :] if ck < 5 else Vc[:, ck - 5, h, :]
                vb = W["V"] if ck < 5 else b_ctx
                self.mm(o_ap, pt[0:M, ck, :], v, ck == 0, ck == 6, [bpt, vb], [bops], signal=(ck == 6))
            if h == h1 - 1:
                nh = h1 - h0
                o3 = ops[:, 0:nh * 65].rearrange("p (h e) -> p h e", h=nh)
                self.recip(rinv[:, h0:h1], o3[:, :, 64], [bops], [b_rinv])
                self.tt("dve", ot[:, h0:h1, :], o3[:, :, 0:64], rinv[:, h0:h1][:, :, None].to_broadcast([128, nh, 64]), ALU.mult,
                        [bops, b_rinv], [bot])
            if h == 15:
                otf = ot.rearrange("p h d -> p (h d)")
                for j in range(8):
                    self.tr(psT[:, j * 128:(j + 1) * 128], otf[:, j * 128:(j + 1) * 128], self.ident_b, [bot, self.b_cb], [b_psT],
                            signal=(j == 7))
                yi = (rp // 4) % 2
                self.cp("act", ycst[yi][:, :, (rp % 4) * 128:(rp % 4 + 1) * 128], psT[:, :].rearrange("p (a b) -> p a b", a=8),
                        [b_psT], [b_ycst[yi]])
                if rp % 4 == 3:
                    t0 = (rp // 4) * 512
                    self.dma("sp", self.s_ycatT[:, 8:16, t0:t0 + 512], ycst[yi], reads=[b_ycst[yi]], writes=[self.b_ycat])

        seq = [(rp, h) for rp in range(32) for h in range(16)]
        emit_loads(0)
        emit_st(*seq[0])
        for i, (rp, h) in enumerate(seq):
            if h == 0 and rp + 1 < 32:
                emit_loads(rp + 1)
            if i + 1 < len(seq):
                emit_st(*seq[i + 1])
            emit_pv(rp, h)
        if "na" in self.debug:
            d_ = self.dbg_tensor("ycatT2", (128, 16, SEQ), BF16)
            self.dma("sp", d_, self.s_ycatT, reads=[self.b_ycat])


    def phase_tail(self):
        A, S = self.A, self.S
        A.release(self.mark_base)
        NW = 4
        wb = [A.alloc([4096], BF16) for _ in range(NW)]
        b_wb = [Buf("tw%d" % i) for i in range(NW)]
        R1 = A.alloc([32, 512], BF16)
        b_R1 = Buf("R1")
        hid = R1
        ycat = R1[:, 0:16, :]
        ob = R1.rearrange("p a b -> p (a b)")[:, 0:8192].bitcast(F32).rearrange("p (a b) -> p a b", a=8)
        h = A.alloc([8, 512], BF16)
        b_h = [Buf("h%d" % i) for i in range(8)]
        sets = []
        for i in range(2):
            sets.append({"x": A.alloc([8, 512], F32), "bx": Buf("sx%d" % i), "bxm": [Buf("sx%d_%d" % (i, m)) for m in range(8)], "u": A.alloc([8, 514], F32), "bu": Buf("su%d" % i),
                         "gb": A.alloc([8, 512], BF16), "bgb": Buf("sg%d" % i)})
        gct = [A.alloc([512], F32) for _ in range(2)]
        b_gct = [Buf("gct0"), Buf("gct1")]
        rl = [A.alloc([512], F32) for _ in range(2)]
        b_rl = [Buf("rl0"), Buf("rl1")]
        cva = [A.alloc([512], F32) for _ in range(2)]
        b_cva = [Buf("cva0"), Buf("cva1")]
        gg = A.alloc([8, 512], BF16)
        b_gg = Buf("gg")
        wk = {"sq": [A.alloc([8, 512], BF16)] * 2, "b_sq": [Buf("sq")] * 2,
              "rstd": [A.alloc([512], F32) for _ in range(2)], "b_rstd": [Buf("rs0"), Buf("rs1")],
              "tmp": [A.alloc([512], F32) for _ in range(3)], "b_tmp": [Buf("nt%d" % i) for i in range(3)], "n": 0,
              "b_sqc": [Buf("sqc%d" % m) for m in range(8)]}
        cnt = {"w": 0, "ps": 0, "gct": 0, "rl": 0, "cva": 0, "norm": 0}

        def wload(key, idx, kc):
            i = cnt["w"] % NW
            cnt["w"] += 1
            v = wb[i].rearrange("p (k n) -> p k n", k=kc)
            src, bsrc = self.wsrc(key, idx, kc)
            self.dma("sp", v, src, reads=[bsrc], writes=[b_wb[i]])
            return v, b_wb[i]

        def nextps():
            pi = cnt["ps"] % 4
            cnt["ps"] += 1
            return self.ps[pi], self.psb[pi]

        def resid_evac(ps, bps, xs, m, gate_col, bgate):
            bxm = xs["bxm"][m]
            flush()
            self.stt("dve", xs["x"][:, m, :], ps[:, :], gate_col, xs["x"][:, m, :], ALU.mult, ALU.add, [bps, bgate, bxm], [bxm])
            pend.append(self.norm_accum(xs["x"], bxm, m, wk, cnt["norm"], 4))

        pend = []

        def flush():
            while pend:
                pend.pop(0)()

        def norm(xs, g, bg, shift, bshift, out, bout, final=False):
            flush()
            self.norm_tile(xs["x"], xs["bxm"], 512, g, bg, shift, bshift, out, bout, wk, cnt["norm"], psi=4, final=final,
                           accum_done=True)
            cnt["norm"] += 1

        def mlp(l, xs):
            gate = self.modv[l][:, 40:48]
            bgate = self.b_modv[l][5]
            for grp in range(8):
                wt, bw = wload("w1", (l, grp), 8)
                if grp == 0:
                    pss = [nextps() for _ in range(4)]
                    for kc in range(8):
                        for jj in range(4):
                            self.mm(pss[jj][0][:, :], wt[:, kc, jj * 128:(jj + 1) * 128], h[:, kc, :], kc == 0, kc == 7, [bw, b_h[kc]],
                                    [pss[jj][1]], signal=(kc == 7))
                for jj in range(4):
                    mh = grp * 4 + jj
                    if grp == 0:
                        ps, bps = pss[jj]
                    else:
                        ps, bps = nextps()
                        for kc in range(8):
                            self.mm(ps[:, :], wt[:, kc, jj * 128:(jj + 1) * 128], h[:, kc, :], kc == 0, kc == 7, [bw, b_h[kc]], [bps], signal=(kc == 7))
                    ri = cnt["rl"] % 2
                    cnt["rl"] += 1
                    self.act(rl[ri], ps[:, :], AF.Relu, [bps], [b_rl[ri]])
                    self.tt("pool", hid[:, mh, :], rl[ri], rl[ri], ALU.mult, [b_rl[ri]], [b_R1])
            for m in range(8):
                wt, bw = wload("w2", (l, m), 32)
                ps, bps = nextps()
                for kc in range(32):
                    self.mm(ps[:, :], wt[:, kc, :], hid[:, kc, :], kc == 0, kc == 31, [bw, b_R1], [bps], signal=(kc == 31))
                resid_evac(ps, bps, xs, m, gate[:, m:m + 1], bgate)

        def conv_b(xs, m):
            ci = cnt["cva"] % 2
            cnt["cva"] += 1
            cv_, bcv = cva[ci], b_cva[ci]
            self.act(cv_, xs["u"][:, m, 0:512], AF.Identity, [xs["bu"], self.b_cols], [bcv], scale=self.colv("scw0", m))
            self.stt("dve", cv_, xs["u"][:, m, 1:513], self.colv("scw1", m), cv_, ALU.mult, ALU.add, [xs["bu"], self.b_cols, bcv], [bcv])
            self.stt("dve", cv_, xs["u"][:, m, 2:514], self.colv("scw2", m), cv_, ALU.mult, ALU.add, [xs["bu"], self.b_cols, bcv], [bcv])
            self.tt("pool", gg[:, m, :], cv_, xs["gb"][:, m, :], ALU.mult, [bcv, xs["bgb"]], [b_gg])

        def load_ycat(t):
            self.dma("sp", ycat, self.s_ycatT[:, :, t * 512:(t + 1) * 512], reads=[self.b_ycat], writes=[b_R1])

        def stage_a(t, xs, prev):
            t0 = t * 512
            if t <= 1:
                load_ycat(t)
            self.dma("sp", xs["x"], self.xT[:, :, t0:t0 + 512], writes=xs["bxm"])
            for grp in range(4):
                wt, bw = wload("out0", (grp,), 16)
                for jj in range(2):
                    m = grp * 2 + jj
                    ps, bps = nextps()
                    for kc in range(16):
                        self.mm(ps[:, :], wt[:, kc, jj * 128:(jj + 1) * 128], ycat[:, kc, :], kc == 0, kc == 15, [bw, b_R1], [bps],
                                signal=(kc == 15))
                    resid_evac(ps, bps, xs, m, self.modv[0][:, 16 + m:17 + m], self.b_modv[0][2])
            norm(xs, self.g_f0, self.b_gf0, self.modv[0][:, 24:32], self.b_modv[0][3], h, b_h)
            mlp(0, xs)
            norm(xs, self.g_a1, self.b_ga1, self.modv[1][:, 0:8], self.b_modv[1][0], h, b_h)
            for grp in range(2):
                wt, bw = wload("scin", (grp,), 8)
                for jj in range(4):
                    m = grp * 4 + jj
                    ps, bps = nextps()
                    for kc in range(8):
                        self.mm(ps[:, :], wt[:, kc, jj * 128:(jj + 1) * 128], h[:, kc, :], kc == 0, kc == 7, [bw, b_h[kc]], [bps], signal=(kc == 7))
                    self.cp("act", xs["gb"][:, m, :], ps[:, :], [bps], [xs["bgb"]])
            for half in range(2):
                wc, bwc = wload("scin", (2 + half,), 8)
                wv, bwv = wload("scin", (4 + half,), 8)
                for jj in range(4):
                    m = half * 4 + jj
                    ps, bps = nextps()
                    for kc in range(8):
                        self.mm(ps[:, :], wc[:, kc, jj * 128:(jj + 1) * 128], h[:, kc, :], kc == 0, kc == 7, [bwc, b_h[kc]], [bps], signal=(kc == 7))
                    gi = cnt["gct"] % 2
                    cnt["gct"] += 1
                    self.cp("act", gct[gi], ps[:, :], [bps], [b_gct[gi]])
                    ps2, bps2 = nextps()
                    for kc in range(8):
                        self.mm(ps2[:, :], wv[:, kc, jj * 128:(jj + 1) * 128], h[:, kc, :], kc == 0, kc == 7, [bwv, b_h[kc]], [bps2], signal=(kc == 7))
                    self.tt("dve", xs["u"][:, m, 1:513], ps2[:, :], gct[gi], ALU.mult, [bps2, b_gct[gi]], [xs["bu"]])
                    if t == 0:
                        self.memset("pool", xs["u"][:, m, 0:1], 0.0, [xs["bu"]])
                    else:
                        self.cp("pool", xs["u"][:, m, 0:1], prev["u"][:, m, 512:513], [prev["bu"]], [xs["bu"]])
                        self.cp("pool", prev["u"][:, m, 513:514], xs["u"][:, m, 1:2], [xs["bu"]], [prev["bu"]])
                        conv_b(prev, m)
                    if t == 7:
                        self.memset("pool", xs["u"][:, m, 513:514], 0.0, [xs["bu"]])

        def stage_b(t, xs, do_conv):
            t0 = t * 512
            if do_conv:
                for m in range(8):
                    conv_b(xs, m)
            for grp in range(2):
                wt, bw = wload("scout", (grp,), 8)
                for jj in range(4):
                    m = grp * 4 + jj
                    ps, bps = nextps()
                    for kc in range(8):
                        self.mm(ps[:, :], wt[:, kc, jj * 128:(jj + 1) * 128], gg[:, kc, :], kc == 0, kc == 7, [bw, b_gg], [bps], signal=(kc == 7))
                    resid_evac(ps, bps, xs, m, self.modv[1][:, 16 + m:17 + m], self.b_modv[1][2])
            norm(xs, self.g_f1, self.b_gf1, self.modv[1][:, 24:32], self.b_modv[1][3], h, b_h)
            mlp(1, xs)
            if t + 2 < 8:
                load_ycat(t + 2)
            norm(xs, self.colv("fnw"), self.b_cols, None, None, xs["x"], xs["bxm"], final=True)
            self.dma("sp", self.outT[:, :, t0:t0 + 512], xs["x"], reads=xs["bxm"])

        for t in range(9):
            cur = sets[t % 2]
            prev = sets[(t - 1) % 2]
            if t < 8:
                stage_a(t, cur, prev)
            if t >= 1:
                stage_b(t - 1, prev, do_conv=(t == 8))


def build_program(stop_after=None, debug=()):
    p = Prog(stop_after, debug)
    p.build()
    return p


_CACHE = {}


def kernel(**inputs):
    shared, per_core = prep_inputs(inputs)
    p = build_program()
    in_maps = []
    for b in range(8):
        m = dict(shared)
        m.update(per_core[b])
        in_maps.append(m)
    res = run_bass_kernel_spmd(p.nc, in_maps, core_ids=list(range(8)))
    out = np.empty((8, SEQ, D), np.float32)
    for b in range(8):
        oT = np.asarray(res.results[b]["outT"])
        out[b] = oT.transpose(2, 1, 0).reshape(SEQ, D)
    return out
```

```python
from contextlib import ExitStack
import numpy as np
import concourse.bass as bass
import concourse.mybir as mybir
from concourse.bass_utils import run_bass_kernel_spmd

F32 = mybir.dt.float32
BF16 = mybir.dt.bfloat16
AF = mybir.ActivationFunctionType
ALU = mybir.AluOpType

ENGS = ("pe", "act", "dve", "pool", "sp")

D = 1024
SEQ = 4096
CTX = 256
NTOK = SEQ + CTX
OFF_X, OFF_B, OFF_DT, OFF_K, OFF_V, OFF_C, OFF_Z, OFF_Q = 0, 1024, 1280, 1312, 2336, 3360, 3616, 4640
NEG = -30000.0
EPS = 1e-6


class Buf:
    __slots__ = ("name", "lw", "rd")

    def __init__(self, name=""):
        self.name = name
        self.lw = None
        self.rd = []


class Sched:
    def __init__(self, nc, n_lanes=6, same_eng_sync=True):
        self.nc = nc
        self.q = {e: [] for e in ENGS}
        self.cnt = {e: 0 for e in ENGS}
        self.seen = {e: {} for e in ENGS}
        self.same_eng_sync = same_eng_sync
        self.lanes = {}
        self.lane_rr = {}
        for qe in ("sp", "act", "pool"):
            self.lanes[qe] = [["dma_%s_%d" % (qe, i), 0] for i in range(n_lanes)]
            self.lane_rr[qe] = 0
        self.semkeys = list(ENGS) + [l[0] for qe in self.lanes for l in self.lanes[qe]]

    def _deps(self, reads, writes):
        deps = set()
        for b in reads:
            if b.lw is not None:
                deps.add((b.lw[0], b.lw[1], True))
        for b in writes:
            if b.lw is not None:
                deps.add((b.lw[0], b.lw[1], False))
            for r in b.rd:
                deps.add((r[0], r[1], False))
        return deps

    def _emit_waits(self, eng, deps):
        mx = {}
        for k, v, raw in deps:
            if k == eng:
                if not (raw and self.same_eng_sync and eng in ("act", "dve", "pool")):
                    continue
            if mx.get(k, 0) < v:
                mx[k] = v
        for k, v in mx.items():
            if self.seen[eng].get(k, 0) >= v:
                continue
            self.seen[eng][k] = v
            self.q[eng].append(("wait", k, v))

    def _attr(self, ev, reads, writes):
        for b in reads:
            if len(b.rd) > 64:
                mx = {}
                for k, v in b.rd:
                    if mx.get(k, 0) < v:
                        mx[k] = v
                b.rd = list(mx.items())
            b.rd.append(ev)
        for b in writes:
            b.lw = ev
            b.rd = []

    def op(self, eng, fn, reads=(), writes=(), signal=True):
        self._emit_waits(eng, self._deps(reads, writes))
        ev = (eng, self.cnt[eng] + 1)
        self._attr(ev, reads, writes)
        if signal:
            self.cnt[eng] += 1
            self.q[eng].append(("op", fn, eng))
        else:
            self.q[eng].append(("op", fn, None))
        return ev

    def dma(self, qe, out, in_, reads=(), writes=()):
        lanes = self.lanes[qe]
        i = self.lane_rr[qe]
        self.lane_rr[qe] = (i + 1) % len(lanes)
        lane = lanes[i]
        deps = self._deps(reads, writes)
        if lane[1] > 0:
            deps.add((lane[0], 16 * lane[1], True))
        self._emit_waits(qe, deps)
        lane[1] += 1
        ev = (lane[0], 16 * lane[1])
        self._attr(ev, reads, writes)
        self.q[qe].append(("dma", (out, in_), lane[0]))
        return ev

    def _all_events(self):
        evs = [(e, self.cnt[e]) for e in ENGS if self.cnt[e] > 0]
        for qe in self.lanes:
            for key, c in self.lanes[qe]:
                if c > 0:
                    evs.append((key, 16 * c))
        return evs

    def _force(self, eng, evs):
        for k, v in evs:
            if k == eng or self.seen[eng].get(k, 0) >= v:
                continue
            self.seen[eng][k] = v
            self.q[eng].append(("wait", k, v))

    def barrier(self):
        evs = self._all_events()
        for e in ENGS:
            self._force(e, evs)

    def final_wait(self, eng="sp"):
        self._force(eng, self._all_events())

    def replay(self, stack):
        nc = self.nc
        used = set()
        for e in ENGS:
            for it in self.q[e]:
                if it[0] == "wait":
                    used.add(it[1])
                elif it[2] is not None:
                    used.add(it[2])
        sems = {k: stack.enter_context(nc.semaphore("s_" + k)) for k in self.semkeys if k in used}
        block = stack.enter_context(nc.Block())

        def run(eng_name):
            def body(e):
                for it in self.q[eng_name]:
                    if it[0] == "wait":
                        e.wait_ge(sems[it[1]], it[2])
                    elif it[0] == "op":
                        ins = it[1](e)
                        if it[2] is not None:
                            ins.then_inc(sems[it[2]], 1)
                    else:
                        out, in_ = it[1]
                        e.dma_start(out=out, in_=in_).then_inc(sems[it[2]], 16)
            return body

        if self.q["sp"]:
            block.sync(run("sp"))
        if self.q["pe"]:
            block.tensor(run("pe"))
        if self.q["act"]:
            block.scalar(run("act"))
        if self.q["dve"]:
            block.vector(run("dve"))
        if self.q["pool"]:
            block.gpsimd(run("pool"))


class Arena:
    def __init__(self, ap, words):
        self.ap = ap
        self.words = words
        self.top = 0

    def mark(self):
        return self.top

    def release(self, m):
        self.top = m

    def alloc(self, shape, dtype, parts=128):
        n = 1
        for s in shape:
            n *= s
        nbytes = n * (4 if dtype == F32 else 2)
        w = (nbytes + 3) // 4
        w = (w + 7) // 8 * 8
        assert self.top + w <= self.words, "SBUF arena overflow: need %d have %d" % (self.top + w, self.words)
        v = self.ap[:, self.top:self.top + (nbytes + 3) // 4]
        self.top += w
        if dtype != F32:
            v = v.bitcast(dtype)
        if len(shape) == 2:
            v = v.rearrange("p (a b) -> p a b", a=shape[0])
        elif len(shape) == 3:
            v = v.rearrange("p (a b c) -> p a b c", a=shape[0], b=shape[1])
        return v


def _col(v):
    v = np.asarray(v, np.float32)
    return np.ascontiguousarray(v.reshape(-1, 128).T)


def _wtile(w, ncols):
    K, N = w.shape
    assert K % 128 == 0 and N % ncols == 0
    t = w.reshape(K // 128, 128, N // ncols, ncols).transpose(2, 1, 0, 3)
    return np.ascontiguousarray(t, dtype=np.float32)


IN_GROUPS = [
    ("x0", 0, 512), ("x1", 512, 512), ("B", OFF_B, 256), ("dt", OFF_DT, 32),
    ("k0", OFF_K, 512), ("k1", OFF_K + 512, 512), ("v0", OFF_V, 512), ("v1", OFF_V + 512, 512),
    ("C", OFF_C, 256), ("z0", OFF_Z, 512), ("z1", OFF_Z + 512, 512), ("q0", OFF_Q, 512), ("q1", OFF_Q + 512, 512),
]
IN_GIDX = {g[0]: i for i, g in enumerate(IN_GROUPS)}

COLS = {}
_n = 0
for _name, _w in [("c", 8), ("cctx", 8), ("modb0", 48), ("modb1", 48), ("nmix0", 8), ("nmix1", 8), ("nmlp0", 8),
                  ("nmlp1", 8), ("fnw", 8), ("cw0", 12), ("cw1", 12), ("cw2", 12), ("cb", 12),
                  ("scw0", 8), ("scw1", 8), ("scw2", 8)]:
    COLS[_name] = (_n, _w)
    _n += _w
NCOL = _n
ROWS = {}
_n = 0
for _name, _w in [("ssdnw", 1024), ("dtb", 32), ("alog", 32), ("dskip", 16)]:
    ROWS[_name] = (_n, _w)
    _n += _w
NROW = _n
NCONST = 5 * 128


def prep_inputs(inp):
    f = lambda a: np.ascontiguousarray(np.asarray(a, np.float32))
    shared = {}
    shared["modw"] = np.stack([_wtile(f(inp["mod_w"][l]), 1024) for l in range(2)])
    in_w = f(inp["ssdna_in_w"][0])
    inw = np.zeros((len(IN_GROUPS), 128, 8, 512), np.float32)
    for gi, (nm, c0, nc_) in enumerate(IN_GROUPS):
        inw[gi, :, :, :nc_] = in_w[:, c0:c0 + nc_].reshape(8, 128, nc_).transpose(1, 0, 2)
    shared["inw"] = inw
    shared["outw0"] = _wtile(f(inp["ssdna_out_w"][0]), 256)
    shared["w1"] = np.stack([_wtile(f(inp["mlp_w1"][l]), 512) for l in range(2)])
    shared["w2"] = np.stack([_wtile(f(inp["mlp_w2"][l]), 128) for l in range(2)])
    shared["scin"] = _wtile(f(inp["sc_in_w"][0]), 512)
    shared["scout"] = _wtile(f(inp["sc_out_w"][0]), 512)
    rows = np.zeros((1, NROW), np.float32)
    rows[0, ROWS["ssdnw"][0]:ROWS["ssdnw"][0] + 1024] = f(inp["ssd_norm_w"][0])
    rows[0, ROWS["dtb"][0]:ROWS["dtb"][0] + 32] = f(inp["ssd_dt_bias"][0]).reshape(32)
    rows[0, ROWS["alog"][0]:ROWS["alog"][0] + 32] = f(inp["ssd_a_log"][0]).reshape(32)
    rows[0, ROWS["dskip"][0]:ROWS["dskip"][0] + 16] = f(inp["ssd_d"][0])
    shared["rows"] = rows
    k = np.arange(128)
    consts = np.concatenate([np.eye(128, dtype=np.float32),
                             (k[:, None] <= k[None, :]).astype(np.float32),
                             (k[:, None] >= k[None, :]).astype(np.float32),
                             (k[:, None] < k[None, :]).astype(np.float32),
                             (k[:, None] > k[None, :]).astype(np.float32)], axis=1)
    shared["consts"] = np.ascontiguousarray(consts)
    rpb = f(inp["na_rpb"][0])
    c_ = np.arange(64)
    ws = np.clip(c_ - 8, 0, 48)
    kc_ = np.arange(64)
    ok = (kc_[:, None] >= ws[None, :]) & (kc_[:, None] < ws[None, :] + 16)
    dc = np.clip(kc_[:, None] - c_[None, :], -15, 15) + 15
    tblk = rpb[:, :, dc]
    tblk = np.where(ok[None, None], tblk, np.float32(NEG)).astype(np.float32)
    shared["tblk"] = np.ascontiguousarray(tblk)
    cols_common = np.zeros((128, NCOL), np.float32)

    def put(name, arr):
        o, w = COLS[name]
        cols_common[:, o:o + w] = arr
    put("cctx", _col(inp["c_ctx"]))
    for l in range(2):
        put("modb%d" % l, _col(inp["mod_b"][l]))
        put("nmix%d" % l, _col(inp["norm_mix_w"][l]))
        put("nmlp%d" % l, _col(inp["norm_mlp_w"][l]))
    put("fnw", _col(inp["final_norm_w"]))
    cw = f(inp["ssdna_conv_w"][0])
    for t in range(3):
        put("cw%d" % t, _col(cw[t]))
        put("scw%d" % t, _col(f(inp["sc_conv_w"][0])[t]))
    put("cb", _col(inp["ssdna_conv_b"][0]))
    per_core = []
    x = np.asarray(inp["x"], np.float32)
    ctx = np.asarray(inp["ctx"], np.float32)
    for b in range(8):
        cols = cols_common.copy()
        o, w = COLS["c"]
        cols[:, o:o + w] = _col(inp["c"][b])
        xT = np.ascontiguousarray(x[b].reshape(SEQ, 8, 128).transpose(2, 1, 0))
        cT = np.ascontiguousarray(ctx[b].reshape(CTX, 8, 128).transpose(2, 1, 0))
        per_core.append({"xT": xT, "ctxT": cT, "cols": cols})
    return shared, per_core


class Prog:
    def __init__(self, stop_after=None, debug=()):
        self.stop_after = stop_after
        self.debug = debug
        nc = self.nc = bass.Bass("TRN2", target_bir_lowering=False)
        self.dr = {}
        ein = lambda n, s: nc.dram_tensor(n, list(s), F32, kind="ExternalInput").ap()
        self.xT = ein("xT", (128, 8, SEQ))
        self.ctxT = ein("ctxT", (128, 8, CTX))
        self.cols_d = ein("cols", (128, NCOL))
        self.rows_d = ein("rows", (1, NROW))
        self.consts_d = ein("consts", (128, NCONST))
        self.modw = ein("modw", (2, 6, 128, 8, 1024))
        self.inw = ein("inw", (len(IN_GROUPS), 128, 8, 512))
        self.outw0 = ein("outw0", (4, 128, 16, 256))
        self.w1 = ein("w1", (2, 8, 128, 8, 512))
        self.w2 = ein("w2", (2, 8, 128, 32, 128))
        self.scin = ein("scin", (6, 128, 8, 512))
        self.scout = ein("scout", (2, 128, 8, 512))
        self.tblk = ein("tblk", (16, 15, 64, 64))
        self.outT = nc.dram_tensor("outT", [128, 8, SEQ], F32, kind="ExternalOutput").ap()
        scr = lambda n, s, dt: nc.dram_tensor(n, list(s), dt, kind="Internal").ap()
        self.s_xtok = scr("s_xtok", (NTOK, 1024), BF16)
        self.s_BT = scr("s_BT", (256, NTOK), BF16)
        self.s_CT = scr("s_CT", (256, SEQ), BF16)
        self.s_Btok = scr("s_Btok", (NTOK, 256), BF16)
        self.s_z = scr("s_z", (SEQ, 1024), BF16)
        self.s_qT2 = scr("s_qT", (128, 8, SEQ), BF16)
        self.s_kT2 = scr("s_kT", (128, 8, NTOK), BF16)
        self.s_qT = self.s_qT2.rearrange("(par d) c t -> d c par t", par=2)
        self.s_kT = self.s_kT2.rearrange("(par d) c t -> d c par t", par=2)
        self.s_vtok = scr("s_vtok", (NTOK, 16, 65), BF16)
        self.s_ycatT = scr("s_ycatT", (128, 16, SEQ), BF16)
        self.wsb = {"in": scr("wsb_in", (len(IN_GROUPS), 128, 4096), BF16), "out0": scr("wsb_out0", (4, 128, 4096), BF16),
                    "w1": scr("wsb_w1", (2, 8, 128, 4096), BF16), "w2": scr("wsb_w2", (2, 8, 128, 4096), BF16),
                    "scin": scr("wsb_scin", (6, 128, 4096), BF16), "scout": scr("wsb_scout", (2, 128, 4096), BF16)}
        self.b_wsb = {}
        self.tail_casts = []
        self.dbg_out = {}

    def dbg_tensor(self, name, shape, dtype=F32):
        ap = self.nc.dram_tensor("dbg_" + name, list(shape), dtype, kind="ExternalOutput").ap()
        self.dbg_out[name] = ap
        return ap

    def mm(self, out, lhsT, rhs, start, stop, reads, writes, signal, skip=False):
        if skip:
            self.S.op("pe", lambda e, o=out, l=lhsT, r=rhs, s=start, t=stop: e.matmul(o, l, r, start=s, stop=t, skip_group_check=True),
                      reads=reads, writes=writes, signal=signal)
            return
        self.S.op("pe", lambda e, o=out, l=lhsT, r=rhs, s=start, t=stop: e.matmul(o, l, r, start=s, stop=t),
                  reads=reads, writes=writes, signal=signal)

    def tr(self, out, in_, ident, reads, writes, signal=True):
        self.S.op("pe", lambda e, o=out, i=in_, d=ident: e.transpose(o, i, d), reads=reads, writes=writes, signal=signal)

    def act(self, out, in_, func, reads, writes, bias=None, scale=None, accum_out=None, eng="act"):
        kw = {}
        if bias is not None:
            kw["bias"] = bias
        if scale is not None:
            kw["scale"] = scale
        if accum_out is not None:
            kw["accum_out"] = accum_out
        self.S.op(eng, lambda e, o=out, i=in_, f=func, kw=kw: e.activation(out=o, in_=i, func=f, **kw),
                  reads=reads, writes=writes)

    def tt(self, eng, out, in0, in1, op, reads, writes):
        self.S.op(eng, lambda e, o=out, a=in0, b=in1, p=op: e.tensor_tensor(out=o, in0=a, in1=b, op=p),
                  reads=reads, writes=writes)

    def ts(self, eng, out, in0, s1, op0, reads, writes, s2=None, op1=None):
        if op1 is None:
            self.S.op(eng, lambda e, o=out, a=in0, s=s1, p=op0: e.tensor_scalar(out=o, in0=a, scalar1=s, scalar2=None, op0=p),
                      reads=reads, writes=writes)
        else:
            self.S.op(eng, lambda e, o=out, a=in0, s=s1, t=s2, p=op0, q=op1:
                      e.tensor_scalar(out=o, in0=a, scalar1=s, scalar2=t, op0=p, op1=q), reads=reads, writes=writes)

    def stt(self, eng, out, in0, scalar, in1, op0, op1, reads, writes):
        self.S.op(eng, lambda e, o=out, a=in0, s=scalar, b=in1, p=op0, q=op1:
                  e.scalar_tensor_tensor(out=o, in0=a, scalar=s, in1=b, op0=p, op1=q), reads=reads, writes=writes)

    def cp(self, eng, out, in_, reads, writes):
        if eng == "act":
            self.S.op("act", lambda e, o=out, i=in_: e.activation(out=o, in_=i, func=AF.Copy), reads=reads, writes=writes)
        else:
            self.S.op(eng, lambda e, o=out, i=in_: e.tensor_copy(out=o, in_=i), reads=reads, writes=writes)

    def recip(self, out, in_, reads, writes):
        self.S.op("dve", lambda e, o=out, i=in_: e.reciprocal(out=o, in_=i), reads=reads, writes=writes)

    def memset(self, eng, ap, val, writes):
        self.S.op(eng, lambda e, a=ap, v=val: e.memset(a, v), reads=(), writes=writes)

    def dma(self, qe, out, in_, reads=(), writes=()):
        self.S.dma(qe, out, in_, reads=reads, writes=writes)

    def build(self):
        nc = self.nc
        with ExitStack() as st:
            words = 51 * 1024
            arena_t = st.enter_context(nc.sbuf_tensor("arena", [128, words], F32))
            self.A = Arena(arena_t, words)
            self.ps = [st.enter_context(nc.psum_tensor("ps%d" % i, [128, 512], F32)) for i in range(8)]
            self.psb = [Buf("ps%d" % i) for i in range(8)]
            self.S = Sched(nc)
            self.phase_setup()
            self.precast("in")
            done = self.run_phases()
            self.S.barrier()
            self.S.final_wait("sp")
            self.S.replay(st)
        return nc

    def run_phases(self):
        self.phase_mod_and_h()
        if self.stop_after == "h":
            return
        self.S.barrier()
        self.phase_inproj()
        if self.stop_after == "inproj":
            return
        self.S.barrier()
        if "skip_ssd" not in self.debug:
            self.phase_ssd()
        else:
            self.b_ycat = Buf("ycat")
        if self.stop_after == "ssd":
            return
        self.S.barrier()
        if "skip_na" not in self.debug:
            self.phase_na()
        if self.stop_after == "na":
            return
        self.S.barrier()
        self.phase_tail()

    def phase_setup(self):
        A = self.A
        self.cols = A.alloc([NCOL], F32)
        self.b_cols = Buf("cols")
        self.dma("sp", self.cols, self.cols_d, writes=[self.b_cols])
        self.consts = A.alloc([NCONST], F32)
        self.b_consts = Buf("consts")
        self.dma("sp", self.consts, self.consts_d, writes=[self.b_consts])
        self.ident_f = self.consts[:, 0:128]
        self.L_f = self.consts[:, 128:256]
        self.U_f = self.consts[:, 256:384]
        self.SL_f = self.consts[:, 384:512]
        self.SU_f = self.consts[:, 512:640]
        self.dt_raw = A.alloc([34, 32], F32)
        self.b_dtraw = Buf("dtraw")
        self.ident_b = A.alloc([128], BF16)
        self.ones_b = A.alloc([128], BF16)
        self.b_cb = Buf("constb")
        self.cp("dve", self.ident_b, self.ident_f, [self.b_consts], [self.b_cb])
        self.memset("dve", self.ones_b, 1.0, [self.b_cb])
        self.modv = [A.alloc([48], F32) for _ in range(2)]
        self.b_modv = [[Buf("modv%d_%d" % (l, v)) for v in range(6)] for l in range(2)]
        self.modc = A.alloc([16], F32)
        self.b_modc = Buf("modc")
        self.gcols = A.alloc([64], F32)
        self.b_g = {}

    def precast(self, which):
        def one(key, idx, src):
            b = Buf("wsb_%s_%s" % (key, idx))
            self.b_wsb[(key,) + idx] = b
            dst = self.wsb[key]
            for i in idx:
                dst = dst[i]
                src = src[i]
            if which == "in":
                self.dma("pool", dst, src.rearrange("p k n -> p (k n)"), writes=[b])
            else:
                self.tail_casts.append(lambda gate, dst=dst, src=src, b=b: self.dma(
                    "pool", dst, src.rearrange("p k n -> p (k n)"), reads=[gate], writes=[b]))
        if which == "in":
            for g in range(len(IN_GROUPS)):
                one("in", (g,), self.inw)
            return
        for g in range(4):
            one("out0", (g,), self.outw0)
        for l in range(2):
            if l == 1:
                for g in range(6):
                    one("scin", (g,), self.scin)
                for g in range(2):
                    one("scout", (g,), self.scout)
            for g in range(8):
                one("w1", (l, g), self.w1)
            for g in range(8):
                one("w2", (l, g), self.w2)

    def cast_some(self, n, gate):
        for _ in range(n):
            if self.tail_casts:
                self.tail_casts.pop(0)(gate)

    def wsrc(self, key, idx, kc):
        src = self.wsb[key]
        for i in idx:
            src = src[i]
        return src.rearrange("p (k n) -> p k n", k=kc), self.b_wsb[(key,) + idx]

    def colv(self, name, j=None):
        o, w = COLS[name]
        if j is None:
            return self.cols[:, o:o + w]
        return self.cols[:, o + j:o + j + 1]

    def phase_mod_and_h(self):
        A, S = self.A, self.S
        m0 = A.mark()
        self.mark_base = m0
        self.hT = A.alloc([8, SEQ], BF16)
        self.hcT = A.alloc([8, CTX], BF16)
        self.b_hT = [Buf("hT%d" % t) for t in range(8)]
        self.b_hcT = Buf("hcT")
        m1 = A.mark()
        s2 = A.alloc([8, 2], F32)
        b_s2 = Buf("s2")
        self.act(s2[:, :, 0], self.colv("c"), AF.Silu, [self.b_cols], [b_s2])
        self.act(s2[:, :, 1], self.colv("cctx"), AF.Silu, [self.b_cols], [b_s2])
        wbuf = [A.alloc([8, 1024], F32) for _ in range(2)]
        b_w = [Buf("modw%d" % i) for i in range(2)]
        modcnt = [0]

        def mod_group(l, v):
            n = modcnt[0]
            modcnt[0] += 1
            wb, bw = wbuf[n % 2], b_w[n % 2]
            self.dma("sp", wb, self.modw[l, v], writes=[bw])
            pst = self.ps[n % 2][:, 0:16].rearrange("p (j t) -> p j t", t=2)
            bps = self.psb[n % 2]
            for j in range(8):
                for kc in range(8):
                    self.mm(pst[:, j, :], wb[:, kc, j * 128:(j + 1) * 128], s2[:, kc, :], kc == 0, kc == 7,
                            [bw, b_s2], [bps], signal=(j == 7 and kc == 7))
            o, _ = COLS["modb%d" % l]
            self.tt("dve", self.modv[l][:, v * 8:(v + 1) * 8], pst[:, :, 0], self.cols[:, o + v * 8:o + v * 8 + 8],
                    ALU.add, [bps, self.b_cols], [self.b_modv[l][v]])
            if l == 0 and v < 2:
                self.tt("dve", self.modc[:, v * 8:(v + 1) * 8], pst[:, :, 1], self.cols[:, o + v * 8:o + v * 8 + 8],
                        ALU.add, [bps, self.b_cols], [self.b_modc])

        mod_group(0, 0)
        mod_group(0, 1)
        rest = [(0, v) for v in range(2, 6)] + [(1, v) for v in range(6)]
        def gain(slot, normname, scale_ap, rd):
            g = self.gcols[:, slot * 8:(slot + 1) * 8]
            b = Buf("g%d" % slot)
            self.stt("dve", g, scale_ap, 1.0, self.colv(normname), ALU.add, ALU.mult, rd + [self.b_cols], [b])
            return g, b
        self.g_a0, self.b_ga0 = gain(0, "nmix0", self.modv[0][:, 8:16], [self.b_modv[0][1]])
        self.g_c, self.b_gc = gain(4, "nmix0", self.modc[:, 8:16], [self.b_modc])
        xbuf = [A.alloc([8, 512], F32) for _ in range(2)]
        b_x = [Buf("xb%d" % i) for i in range(2)]
        wk = self.norm_alloc(512)
        wk["use_pool"] = False
        for t in range(9):
            xb, bx = xbuf[t % 2], b_x[t % 2]
            if t < 8:
                ntok = 512
                self.dma("sp", xb, self.xT[:, :, t * 512:(t + 1) * 512], writes=[bx])
                self.norm_tile(xb, bx, ntok, self.g_a0, self.b_ga0, self.modv[0][:, 0:8], self.b_modv[0][0],
                               self.hT[:, :, t * 512:(t + 1) * 512], self.b_hT[t], wk, t)
            else:
                ntok = CTX
                self.dma("sp", xb[:, :, 0:CTX], self.ctxT, writes=[bx])
                self.norm_tile(xb[:, :, 0:CTX], bx, ntok, self.g_c, self.b_gc, self.modc[:, 0:8], self.b_modc,
                               self.hcT, self.b_hcT, wk, t)
            for _ in range(2 if t == 0 else 1):
                if rest:
                    mod_group(*rest.pop(0))
        while rest:
            mod_group(*rest.pop(0))
        self.g_f0, self.b_gf0 = gain(1, "nmlp0", self.modv[0][:, 32:40], [self.b_modv[0][4]])
        self.g_a1, self.b_ga1 = gain(2, "nmix1", self.modv[1][:, 8:16], [self.b_modv[1][1]])
        self.g_f1, self.b_gf1 = gain(3, "nmlp1", self.modv[1][:, 32:40], [self.b_modv[1][4]])
        self.precast("tail")
        if "h" in self.debug:
            d = self.dbg_tensor("hT", (128, 8, SEQ), BF16)
            self.dma("sp", d, self.hT, reads=self.b_hT)
            d = self.dbg_tensor("hcT", (128, 8, CTX), BF16)
            self.dma("sp", d, self.hcT, reads=[self.b_hcT])
            d = self.dbg_tensor("modv0", (128, 48))
            self.dma("sp", d, self.modv[0], reads=self.b_modv[0])
            d = self.dbg_tensor("modv1", (128, 48))
            self.dma("sp", d, self.modv[1], reads=self.b_modv[1])
        self.mark_after_h = m1

    def norm_alloc(self, ntok):
        A = self.A
        wk = {"sq": [A.alloc([8, ntok], BF16) for _ in range(2)], "b_sq": [Buf("sq0"), Buf("sq1")],
              "rstd": [A.alloc([ntok], F32) for _ in range(2)], "b_rstd": [Buf("rs0"), Buf("rs1")],
              "tmp": [A.alloc([ntok], F32) for _ in range(3)], "b_tmp": [Buf("nt%d" % i) for i in range(3)],
              "n": 0}
        return wk

    def norm_accum(self, x, bx, m, wk, it, psi):
        i2 = it % 2
        sq = wk["sq"][i2]
        bsq = wk["b_sqc"][m]
        ps, bps = self.ps[psi + i2], self.psb[psi + i2]
        self.act(sq[:, m, :], x[:, m, :], AF.Square, [bx], [bsq])
        return lambda: self.mm(ps[:, :], self.ones_b, sq[:, m, :], m == 0, m == 7, [bsq, self.b_cb], [bps], signal=(m == 7))

    def norm_tile(self, x, bx, ntok, g, bg, shift, bshift, out, bout, wk, it, psi=2, final=False, accum_done=False):
        i2 = it % 2
        sq, bsq = wk["sq"][i2], wk["b_sq"][i2]
        rstd, brs = wk["rstd"][i2], wk["b_rstd"][i2]
        ps, bps = self.ps[psi + i2], self.psb[psi + i2]
        if not accum_done:
            self.act(sq[:, :, 0:ntok], x, AF.Square, [bx], [bsq])
            for kc in range(8):
                self.mm(ps[:, 0:ntok], self.ones_b, sq[:, kc, 0:ntok], kc == 0, kc == 7, [bsq, self.b_cb], [bps], signal=(kc == 7))
        self.ts("dve", rstd[:, 0:ntok], ps[:, 0:ntok], 1.0 / D, ALU.mult, [bps], [brs], s2=EPS, op1=ALU.add)
        self.act(rstd[:, 0:ntok], rstd[:, 0:ntok], AF.Sqrt, [brs], [brs])
        self.recip(rstd[:, 0:ntok], rstd[:, 0:ntok], [brs], [brs])
        for kc in range(8):
            k3 = wk["n"] % 3
            wk["n"] += 1
            tmp, btmp = wk["tmp"][k3], wk["b_tmp"][k3]
            bxk = bx[kc] if isinstance(bx, list) else bx
            self.tt("dve" if (kc % 2 == 0 or not wk.get("use_pool", True)) else "pool", tmp[:, 0:ntok], x[:, kc, :], rstd[:, 0:ntok],
                    ALU.mult, [bxk, brs], [btmp])
            boutk = bout[kc] if isinstance(bout, list) else bout
            if final:
                self.act(out[:, kc, :], tmp[:, 0:ntok], AF.Identity, [btmp, bg], [boutk], scale=g[:, kc:kc + 1])
            else:
                self.act(out[:, kc, :], tmp[:, 0:ntok], AF.Identity, [btmp, bshift, bg], [boutk], bias=shift[:, kc:kc + 1],
                         scale=g[:, kc:kc + 1])

    def phase_inproj(self):
        A, S = self.A, self.S
        A.release(self.mark_after_h)
        wts = [A.alloc([8, 512], BF16) for _ in range(3)]
        b_wts = [Buf("inw%d" % i) for i in range(3)]
        pre = [A.alloc([SEQ + 2 + CTX + 2], F32) for _ in range(2)]
        b_pre = [Buf("pre0"), Buf("pre1")]
        for i in range(2):
            for c in (0, SEQ + 1, SEQ + 2, SEQ + 2 + CTX + 1):
                self.memset("pool", pre[i][:, c:c + 1], 0.0, [b_pre[i]])
        cacc = A.alloc([NTOK], F32)
        b_cacc = Buf("cacc")
        cv = [A.alloc([NTOK], BF16) for _ in range(2)]
        b_cv = [Buf("cv0"), Buf("cv1")]
        xst = [A.alloc([34, 128], BF16) for _ in range(2)]
        b_xst = [Buf("xst0"), Buf("xst1")]
        NK = 6
        kst = [A.alloc([512], BF16) for _ in range(NK)]
        b_kst = [Buf("kst%d" % i) for i in range(NK)]
        vst = [A.alloc([16, 65], BF16) for _ in range(2)]
        b_vst = [Buf("vst0"), Buf("vst1")]
        for i in range(2):
            self.memset("pool", vst[i][:, :, 64:65], 1.0, [b_vst[i]])
        zst = [A.alloc([1024], BF16) for _ in range(2)]
        b_zst = [Buf("zst0"), Buf("zst1")]
        st = {"w": 0, "ps": 0, "pre": 0, "cv": 0, "xst": 0, "kst": 0, "vst": 0, "zst": 0, "tp": 0}
        psT = [self.ps[4][:, :].bitcast(BF16), self.ps[5][:, :].bitcast(BF16)]
        b_scr = {k: Buf(k) for k in ("xtok", "BT", "CT", "Btok", "z", "qT", "kT", "vtok")}
        self.b_scr = b_scr

        def load_w(gname):
            i = st["w"] % 3
            st["w"] += 1
            src, bsrc = self.wsrc("in", (IN_GIDX[gname],), 8)
            self.dma("sp", wts[i], src, reads=[bsrc], writes=[b_wts[i]])
            return wts[i], b_wts[i]

        def fm_tile(wt, bw, j, tt):
            pi = st["ps"] % 4
            st["ps"] += 1
            ps, bps = self.ps[pi], self.psb[pi]
            if tt < 8:
                ntok = 512
                rhs = lambda kc: self.hT[:, kc, tt * 512:(tt + 1) * 512]
                rb = self.b_hT[tt]
            else:
                ntok = CTX
                rhs = lambda kc: self.hcT[:, kc, :]
                rb = self.b_hcT
            for kc in range(8):
                self.mm(ps[:, 0:ntok], wt[:, kc, j * 128:(j + 1) * 128], rhs(kc), kc == 0, kc == 7, [bw, rb], [bps],
                        signal=(kc == 7))
            return ps[:, 0:ntok], bps

        def conv_chunk(wt, bw, j, ch, has_ctx):
            i = st["pre"] % 2
            st["pre"] += 1
            p, bp = pre[i], b_pre[i]
            for tt in range(9 if has_ctx else 8):
                ps, bps = fm_tile(wt, bw, j, tt)
                if tt < 8:
                    dst = p[:, 1 + tt * 512:1 + (tt + 1) * 512]
                else:
                    dst = p[:, SEQ + 3:SEQ + 3 + CTX]
                self.cp("act", dst, ps, [bps], [bp])
            ci = st["cv"] % 2
            st["cv"] += 1
            c, bc = cv[ci], b_cv[ci]
            w0, w1, w2, cb = (self.colv("cw0", ch), self.colv("cw1", ch), self.colv("cw2", ch), self.colv("cb", ch))
            segs = [(0, 0, SEQ)] + ([(SEQ + 2, SEQ, CTX)] if has_ctx else [])
            for (po, co, n) in segs:
                self.act(cacc[:, co:co + n], p[:, po:po + n], AF.Identity, [bp, self.b_cols], [b_cacc], bias=cb, scale=w0)
                self.stt("dve", cacc[:, co:co + n], p[:, po + 1:po + 1 + n], w1, cacc[:, co:co + n], ALU.mult, ALU.add,
                         [bp, b_cacc, self.b_cols], [b_cacc])
                self.stt("dve", cacc[:, co:co + n], p[:, po + 2:po + 2 + n], w2, cacc[:, co:co + n], ALU.mult, ALU.add,
                         [bp, b_cacc, self.b_cols], [b_cacc])
                self.act(c[:, co:co + n], cacc[:, co:co + n], AF.Silu, [b_cacc], [bc])
            self.cast_some(2, bc)
            return c, bc

        def to_tokmajor(c, bc, nblk, dst_ap, bdst):
            xi = st["xst"] % 2
            st["xst"] += 1
            xs_, bxs = xst[xi], b_xst[xi]
            blk = 0
            while blk < nblk:
                nb = min(8, nblk - blk)
                ti = st["tp"] % 2
                st["tp"] += 1
                pt, bpt = psT[ti], self.psb[4 + ti]
                for q in range(nb):
                    self.tr(pt[:, q * 128:(q + 1) * 128], c[:, (blk + q) * 128:(blk + q + 1) * 128], self.ident_b,
                            [bc, self.b_cb], [bpt], signal=(q == nb - 1))
                self.cp("dve", xs_[:, blk:blk + nb, :], pt[:, 0:nb * 128].rearrange("p (a b) -> p a b", a=nb), [bpt], [bxs])
                blk += nb
            for b0 in range(0, nblk, 9):
                b1 = min(nblk, b0 + 9)
                self.dma("sp", dst_ap[:, b0:b1, :], xs_[:, b0:b1, :], reads=[bxs], writes=[bdst])

        xtok_v = self.s_xtok.rearrange("(b p) c -> p b c", p=128)
        btok_v = self.s_Btok.rearrange("(b p) c -> p b c", p=128)
        deferred = []

        def defer(fn):
            if deferred:
                deferred.pop(0)()
            if fn is not None:
                deferred.append(fn)

        for gi, gname in enumerate(("x0", "x1")):
            wt, bw = load_w(gname)
            for j in range(4):
                ch = gi * 4 + j
                c, bc = conv_chunk(wt, bw, j, ch, True)
                defer(lambda c=c, bc=bc, ch=ch: to_tokmajor(c, bc, 34, xtok_v[:, :, ch * 128:(ch + 1) * 128], b_scr["xtok"]))
        wt, bw = load_w("B")
        for j in range(2):
            c, bc = conv_chunk(wt, bw, j, 8 + j, True)
            self.dma("sp", self.s_BT[j * 128:(j + 1) * 128, :], c, reads=[bc], writes=[b_scr["BT"]])
            defer(lambda c=c, bc=bc, j=j: to_tokmajor(c, bc, 34, btok_v[:, :, j * 128:(j + 1) * 128], b_scr["Btok"]))
        wt, bw = load_w("C")
        for j in range(2):
            c, bc = conv_chunk(wt, bw, j, 10 + j, False)
            self.dma("sp", self.s_CT[j * 128:(j + 1) * 128, :], c[:, 0:SEQ], reads=[bc], writes=[b_scr["CT"]])
            defer(None)
        wt, bw = load_w("dt")
        for blk in range(34):
            ps, bps = self.ps[6], self.psb[6]
            o = (blk % 8) * 32
            for kc in range(8):
                lhs = self.hT[:, kc, blk * 128:(blk + 1) * 128] if blk < 32 else self.hcT[:, kc, (blk - 32) * 128:(blk - 31) * 128]
                rb = self.b_hT[blk // 4] if blk < 32 else self.b_hcT
                self.mm(ps[:, o:o + 32], lhs, wt[:, kc, 0:32], kc == 0, kc == 7, [bw, rb], [bps], signal=(kc == 7))
            self.cp("dve", self.dt_raw[:, blk, :], ps[:, o:o + 32], [bps], [self.b_dtraw])
        for gname, dst, bd, has_ctx, scale in (("k0", self.s_kT2, b_scr["kT"], True, None), ("k1", self.s_kT2, b_scr["kT"], True, None),
                                               ("q0", self.s_qT2, b_scr["qT"], False, 0.125), ("q1", self.s_qT2, b_scr["qT"], False, 0.125)):
            wt, bw = load_w(gname)
            for j in range(4):
                hp = (int(gname[1]) * 4 + j) * 2
                for tt in range(9 if has_ctx else 8):
                    ps, bps = fm_tile(wt, bw, j, tt)
                    ntok = 512 if tt < 8 else CTX
                    ki = st["kst"] % NK
                    st["kst"] += 1
                    ks, bks = kst[ki], b_kst[ki]
                    if scale is None:
                        self.cp("act", ks[:, 0:ntok], ps, [bps], [bks])
                    else:
                        self.act(ks[:, 0:ntok], ps, AF.Copy, [bps], [bks], scale=scale)
                    t0 = tt * 512
                    self.dma("sp", dst[:, hp // 2, t0:t0 + ntok], ks[:, 0:ntok], reads=[bks], writes=[bd])
                self.cast_some(2, bks)
        for which in ("v", "z"):
            w0, bw0 = load_w(which + "0")
            w1, bw1 = load_w(which + "1")
            nblk = 34 if which == "v" else 32
            for blk in range(nblk):
                lhs = (lambda kc, blk=blk: self.hT[:, kc, blk * 128:(blk + 1) * 128]) if blk < 32 else \
                    (lambda kc, blk=blk: self.hcT[:, kc, (blk - 32) * 128:(blk - 31) * 128])
                rb = self.b_hT[blk // 4] if blk < 32 else self.b_hcT
                if which == "v":
                    si = st["vst"] % 2
                    st["vst"] += 1
                    sb, bsb = vst[si], b_vst[si]
                else:
                    si = st["zst"] % 2
                    st["zst"] += 1
                    sb, bsb = zst[si], b_zst[si]
                for half, (wt, bw) in enumerate(((w0, bw0), (w1, bw1))):
                    pi = st["ps"] % 4
                    st["ps"] += 1
                    ps, bps = self.ps[pi], self.psb[pi]
                    for kc in range(8):
                        self.mm(ps[:, :], lhs(kc), wt[:, kc, :], kc == 0, kc == 7, [bw, rb], [bps], signal=(kc == 7))
                    if which == "v":
                        self.cp("act" if half == 0 else "dve", sb[:, half * 8:(half + 1) * 8, 0:64],
                                ps[:, :].rearrange("p (a b) -> p a b", a=8), [bps], [bsb])
                    else:
                        self.act(sb[:, half * 512:(half + 1) * 512], ps[:, :], AF.Silu, [bps], [bsb])
                if which == "v":
                    self.dma("sp", self.s_vtok[blk * 128:(blk + 1) * 128, :, :], sb, reads=[bsb], writes=[b_scr["vtok"]])
                else:
                    self.dma("sp", self.s_z[blk * 128:(blk + 1) * 128, :], sb, reads=[bsb], writes=[b_scr["z"]])
        self.cast_some(1000, self.b_dtraw)
        if "inproj" in self.debug:
            for nm, ap in (("xtok", self.s_xtok), ("BT", self.s_BT), ("CT", self.s_CT), ("Btok", self.s_Btok), ("z", self.s_z),
                           ("qT", self.s_qT2), ("kT", self.s_kT2), ("vtok", self.s_vtok)):
                d = self.dbg_tensor(nm, ap.shape, BF16)
                self.dma("sp", d, ap, reads=[b_scr[nm]])
            d = self.dbg_tensor("dtraw", (128, 34, 32))
            self.dma("sp", d, self.dt_raw, reads=[self.b_dtraw])


    def phase_ssd(self):
        A, S = self.A, self.S
        A.release(self.mark_base)
        b_scr = self.b_scr
        self.b_ycat = Buf("ycat")
        rows = A.alloc([NROW], F32)
        b_rows = Buf("rows")
        self.dma("sp", rows, self.rows_d.partition_broadcast(128), writes=[b_rows])
        rv = lambda nm: rows[:, ROWS[nm][0]:ROWS[nm][0] + ROWS[nm][1]]
        normw_bc, dtb_bc, alog_bc, D_bc = rv("ssdnw"), rv("dtb"), rv("alog"), rv("dskip")
        dtv = A.alloc([34, 32], F32)
        av = A.alloc([34, 32], F32)
        aneg = A.alloc([32], F32)
        b_dt = Buf("dtv")
        self.tt("dve", dtv, self.dt_raw, dtb_bc[:, None, :].to_broadcast([128, 34, 32]), ALU.add, [self.b_dtraw, b_rows], [b_dt])
        self.act(dtv, dtv, AF.Exp, [b_dt], [b_dt])
        self.act(dtv, dtv, AF.Ln, [b_dt], [b_dt], bias=1.0, scale=1.0)
        self.act(aneg, alog_bc, AF.Exp, [b_rows], [b_dt])
        self.ts("dve", aneg, aneg, -1.0, ALU.mult, [b_dt], [b_dt])
        self.tt("dve", av, dtv, aneg[:, None, :].to_broadcast([128, 34, 32]), ALU.mult, [b_dt], [b_dt])
        negL = A.alloc([128], F32)
        negU = A.alloc([128], F32)
        ones_f = A.alloc([128], F32)
        b_c2 = Buf("c2")
        self.ts("dve", negL, self.L_f, -1.0, ALU.mult, [self.b_consts], [b_c2])
        self.ts("dve", negU, self.U_f, -1.0, ALU.mult, [self.b_consts], [b_c2])
        self.memset("dve", ones_f, 1.0, [b_c2])
        hstore = A.alloc([32, 1024], BF16)
        b_hst = [Buf("hst%d" % i) for i in range(32)]
        hf = [A.alloc([1024], F32) for _ in range(2)]
        b_hf = [Buf("hf0"), Buf("hf1")]
        hb = A.alloc([1024], BF16)
        b_hb = Buf("hb")
        for d in range(2):
            self.memset("pool", hf[d], 0.0, [b_hf[d]])
        NB = 2
        xt = [A.alloc([16, 64], BF16) for _ in range(NB)]
        Bt = [A.alloc([256], BF16) for _ in range(NB)]
        BTc = [A.alloc([2, 128], BF16) for _ in range(NB)]
        CTc = [A.alloc([2, 128], BF16) for _ in range(NB)]
        zc = [A.alloc([1024], BF16) for _ in range(NB)]
        b_ld = [{k: Buf("%s%d" % (k, i)) for k in ("xt", "Bt", "BTc", "CTc", "zc")} for i in range(NB)]
        ex = [A.alloc([48], F32) for _ in range(2)]
        ws = [A.alloc([16], F32) for _ in range(2)]
        b_ex = [Buf("ex0"), Buf("ex1")]
        xdt = [A.alloc([16, 64], BF16) for _ in range(2)]
        b_xdt = [Buf("xdt0"), Buf("xdt1")]
        xs = [A.alloc([16, 64], BF16) for _ in range(2)]
        b_xs = [Buf("xs0"), Buf("xs1")]
        xD = A.alloc([16, 64], BF16)
        b_xD = Buf("xD")
        G = [A.alloc([16, 128], F32) for _ in range(2)]
        b_G = [Buf("G0"), Buf("G1")]
        Ebc = [A.alloc([4, 128], F32) for _ in range(3)]
        b_Ebc = [Buf("Ebc%d" % i) for i in range(3)]
        E = [A.alloc([4, 128], F32) for _ in range(3)]
        b_E = [Buf("E%d" % i) for i in range(3)]
        MT = [A.alloc([4, 128], BF16) for _ in range(3)]
        b_MT = [Buf("MT%d" % i) for i in range(3)]
        Cp = [A.alloc([4, 128], BF16) for _ in range(3)]
        b_Cp = [Buf("Cp%d" % i) for i in range(3)]
        cbm = [A.alloc([2, 128], F32) for _ in range(2)]
        b_cbm = Buf("cbm")
        yg = A.alloc([1024], F32)
        b_yg = Buf("yg")
        ysq = A.alloc([1024], F32)
        ss = A.alloc([2], F32)
        b_ss = Buf("ss")
        yn = A.alloc([1024], BF16)
        b_yn = Buf("yn")
        ycst = [A.alloc([8, 512], BF16) for _ in range(2)]
        b_ycst = [Buf("ycst0"), Buf("ycst1")]
        stmp = A.alloc([1024], F32)
        b_stmp = Buf("stmp")
        cnt = {"ld": 0, "ex": 0, "eb": 0, "mt": 0, "seg": 0}
        tri = [self.L_f, self.U_f]
        stri = [self.SU_f, self.SL_f]
        negtri = [negL, negU]
        BTv = self.s_BT.rearrange("(g n) t -> n g t", g=2)
        CTv = self.s_CT.rearrange("(g n) t -> n g t", g=2)
        ps_small, b_small = self.ps[0], self.psb[0]
        ps_cb, b_cb_ps = self.ps[1], self.psb[1]
        ps_S, b_S = self.ps[6], self.psb[6]
        psT, b_psT = self.ps[7][:, :].bitcast(BF16), self.psb[7]

        def load(blk, full):
            i = cnt["ld"] % NB
            cnt["ld"] += 1
            L = b_ld[i]
            self.dma("sp", xt[i], self.s_xtok[blk * 128:(blk + 1) * 128, :].rearrange("p (h d) -> p h d", h=16),
                     reads=[b_scr["xtok"]], writes=[L["xt"]])
            self.dma("sp", Bt[i], self.s_Btok[blk * 128:(blk + 1) * 128, :], reads=[b_scr["Btok"]], writes=[L["Bt"]])
            if full:
                self.dma("sp", BTc[i], BTv[:, :, blk * 128:(blk + 1) * 128], reads=[b_scr["BT"]], writes=[L["BTc"]])
                self.dma("sp", CTc[i], CTv[:, :, blk * 128:(blk + 1) * 128], reads=[b_scr["CT"]], writes=[L["CTc"]])
                self.dma("sp", zc[i], self.s_z[blk * 128:(blk + 1) * 128, :], reads=[b_scr["z"]], writes=[L["zc"]])
            return i

        def small(blk, d):
            a = av[:, blk, d * 16:(d + 1) * 16]
            o = d * 64
            self.mm(ps_small[:, o:o + 16], tri[d], a, True, True, [self.b_consts, b_dt], [b_small], signal=False)
            self.mm(ps_small[:, o + 16:o + 32], stri[d], a, True, True, [self.b_consts, b_dt], [b_small], signal=False)
            self.mm(ps_small[:, o + 32:o + 48], ones_f, a, True, True, [b_c2, b_dt], [b_small], signal=True)
            self.act(ex[d], ps_small[:, o:o + 48], AF.Exp, [b_small], [b_ex[d]])
            self.tt("dve", ws[d], ex[d][:, 16:32], dtv[:, blk, d * 16:(d + 1) * 16], ALU.mult, [b_ex[d], b_dt], [b_ex[d]])

        def xs_prep(d, li):
            self.tt("pool", xs[d], xt[li], ws[d][:, :, None].to_broadcast([128, 16, 64]), ALU.mult,
                    [b_ld[li]["xt"], b_ex[d]], [b_xs[d]])

        def state_step(d, li, store_blk=None, prep=True):
            if prep:
                xs_prep(d, li)
            if store_blk is not None:
                self.cp("act", hstore[:, store_blk, :], hf[d], [b_hf[d]], [b_hst[store_blk]])
            for g in range(2):
                self.mm(ps_S[:, :], Bt[li][:, g * 128:(g + 1) * 128], xs[d][:, g * 8:(g + 1) * 8, :], True, True,
                        [b_ld[li]["Bt"], b_xs[d]], [b_S], signal=True)
                hv = hf[d][:, g * 512:(g + 1) * 512].rearrange("p (h q) -> p h q", h=8)
                tv = stmp[:, g * 512:(g + 1) * 512].rearrange("p (h q) -> p h q", h=8)
                self.tt("pool", tv, hv, ex[d][:, 32 + g * 8:32 + (g + 1) * 8][:, :, None].to_broadcast([128, 8, 64]), ALU.mult,
                        [b_hf[d], b_ex[d]], [b_stmp])
                self.tt("dve", hf[d][:, g * 512:(g + 1) * 512], stmp[:, g * 512:(g + 1) * 512], ps_S[:, :], ALU.add,
                        [b_stmp, b_S], [b_hf[d]])

        for blk in [33, 32] + list(range(31, -1, -1)):
            li = load(blk, False)
            small(blk, 1)
            state_step(1, li, store_blk=(blk if blk < 32 else None))
        for blk in (32, 33):
            li = load(blk, False)
            small(blk, 0)
            state_step(0, li)
        self.cp("act", hb, hf[0], [b_hf[0]], [b_hb])
        pending_fin = []
        for c in range(32):
            li = load(c, True)
            L = b_ld[li]
            for g in range(2):
                self.mm(ps_cb[:, g * 128:(g + 1) * 128], BTc[li][:, g, :], CTc[li][:, g, :], True, True, [L["BTc"], L["CTc"]],
                        [b_cb_ps], signal=(g == 1))
            pcb = ps_cb[:, 0:256].rearrange("p (g i) -> p g i", g=2)
            self.tt("dve", cbm[0], pcb, self.L_f[:, None, :].to_broadcast([128, 2, 128]), ALU.mult, [b_cb_ps, self.b_consts], [b_cbm])
            self.tt("dve", cbm[1], pcb, self.U_f[:, None, :].to_broadcast([128, 2, 128]), ALU.mult, [b_cb_ps, self.b_consts], [b_cbm])
            self.tt("pool", xD, xt[li], D_bc[:, :, None].to_broadcast([128, 16, 64]), ALU.mult, [L["xt"], b_rows], [b_xD])
            for g in range(2):
                self.mm(self.ps[4 + g][:, :], self.ident_b, xD[:, g * 8:(g + 1) * 8, :], True, False, [self.b_cb, b_xD],
                        [self.psb[4 + g]], signal=False)
            for d in range(2):
                small(c, d)
                if d == 0:
                    xs_prep(0, li)
                dtc = dtv[:, c, d * 16:(d + 1) * 16]
                a = av[:, c, d * 16:(d + 1) * 16]
                self.tt("pool", xdt[d], xt[li], dtc[:, :, None].to_broadcast([128, 16, 64]), ALU.mult, [L["xt"], b_dt], [b_xdt[d]])
                self.tt("pool" if "g_pool" in self.debug else "dve", G[d], tri[d][:, None, :].to_broadcast([128, 16, 128]),
                        a[:, :, None].to_broadcast([128, 16, 128]), ALU.mult, [self.b_consts, b_dt], [b_G[d]])

            def mm1(it):
                d, q = it // 4, it % 4
                k = it % 3
                pseg, b_seg = self.ps[2 + it % 2], self.psb[2 + it % 2]
                self.mm(pseg[:, :], ones_f, G[d][:, 4 * q:4 * q + 4, :], True, True, [b_c2, b_G[d]], [b_seg], signal=True)
                self.act(Ebc[k], pseg[:, :].rearrange("p (a b) -> p a b", a=4), AF.Exp, [b_seg], [b_Ebc[k]])

            def mm2(it):
                d, q = it // 4, it % 4
                k = it % 3
                g = q // 2
                a = av[:, c, d * 16:(d + 1) * 16]
                pseg, b_seg = self.ps[2 + it % 2], self.psb[2 + it % 2]
                self.mm(pseg[:, :].rearrange("p (a b) -> p a b", a=4), negtri[d],
                        a[:, 4 * q:4 * q + 4][:, :, None].to_broadcast([128, 4, 128]), False, True, [b_c2, b_dt, b_Ebc[k]], [b_seg],
                        signal=True, skip=True)
                self.act(E[k], pseg[:, :].rearrange("p (a b) -> p a b", a=4), AF.Exp, [b_seg], [b_E[k]])
                self.stt("dve", MT[k], E[k], 1.0, cbm[d][:, g, :][:, None, :].to_broadcast([128, 4, 128]), ALU.min, ALU.mult,
                         [b_E[k], b_cbm], [b_MT[k]])
                self.tt("pool", Cp[k], Ebc[k], CTc[li][:, g, :][:, None, :].to_broadcast([128, 4, 128]), ALU.mult,
                        [b_Ebc[k], L["CTc"]], [b_Cp[k]])

            def ymm(it):
                d, q = it // 4, it % 4
                k = it % 3
                g = q // 2
                for hh in range(4):
                    h = 4 * q + hh
                    hl = h % 8
                    yps = self.ps[4 + g][:, hl * 64:(hl + 1) * 64]
                    hsrc = hb[:, h * 64:(h + 1) * 64] if d == 0 else hstore[:, c, h * 64:(h + 1) * 64]
                    hbuf = b_hb if d == 0 else b_hst[c]
                    self.mm(yps, MT[k][:, hh, :], xdt[d][:, h, :], False, False, [b_MT[k], b_xdt[d]], [self.psb[4 + g]], signal=False)
                    self.mm(yps, Cp[k][:, hh, :], hsrc, False, (d == 1), [b_Cp[k], hbuf], [self.psb[4 + g]], signal=(hh == 3))

            if "ssd_seq" in self.debug:
                for it in range(8):
                    mm1(it)
                    mm2(it)
                    ymm(it)
            else:
                for s_ in range(10):
                    if s_ < 8:
                        mm1(s_)
                    if 1 <= s_ <= 8:
                        mm2(s_ - 1)
                    if s_ >= 2:
                        ymm(s_ - 2)
                    if s_ == 3 and pending_fin:
                        pending_fin.pop(0)()
            state_step(0, li, prep=False)
            self.cp("act", hb, hf[0], [b_hf[0]], [b_hb])
            for g in range(2):
                self.tt("dve", yg[:, g * 512:(g + 1) * 512], self.ps[4 + g][:, :], zc[li][:, g * 512:(g + 1) * 512], ALU.mult,
                        [self.psb[4 + g], L["zc"]], [b_yg])
            self.memset("dve", ss, 0.0, [b_ss])
            self.act(ysq, yg, AF.Square, [b_yg, b_ss], [b_ss], accum_out=ss[:, 0:1])
            self.ts("dve", ss[:, 1:2], ss[:, 0:1], 1.0 / 1024, ALU.mult, [b_ss], [b_ss], s2=EPS, op1=ALU.add)
            self.act(ss[:, 1:2], ss[:, 1:2], AF.Sqrt, [b_ss], [b_ss])
            self.recip(ss[:, 1:2], ss[:, 1:2], [b_ss], [b_ss])
            self.stt("dve", yn, yg, ss[:, 1:2], normw_bc, ALU.mult, ALU.mult, [b_yg, b_ss, b_rows], [b_yn])
            def fin(c=c):
                for j in range(8):
                    self.tr(psT[:, j * 128:(j + 1) * 128], yn[:, j * 128:(j + 1) * 128], self.ident_b, [b_yn, self.b_cb], [b_psT],
                            signal=(j == 7))
                yi = (c // 4) % 2
                self.cp("act", ycst[yi][:, :, (c % 4) * 128:(c % 4 + 1) * 128], psT[:, :].rearrange("p (a b) -> p a b", a=8),
                        [b_psT], [b_ycst[yi]])
                if c % 4 == 3:
                    t0 = (c // 4) * 512
                    self.dma("sp", self.s_ycatT[:, 0:8, t0:t0 + 512], ycst[yi], reads=[b_ycst[yi]], writes=[self.b_ycat])
            pending_fin.append(fin)
        while pending_fin:
            pending_fin.pop(0)()
        if "ssd" in self.debug:
            d_ = self.dbg_tensor("ycatT", (128, 16, SEQ), BF16)
            self.dma("sp", d_, self.s_ycatT, reads=[self.b_ycat])
            d_ = self.dbg_tensor("hstore", (128, 32, 1024), BF16)
            self.dma("sp", d_, hstore, reads=b_hst)
            d_ = self.dbg_tensor("dtv", (128, 34, 32))
            self.dma("sp", d_, dtv, reads=[b_dt])


    def phase_na(self):
        A, S = self.A, self.S
        A.release(self.mark_base)
        b_scr = self.b_scr
        Kc = A.alloc([16, CTX], BF16)
        Vc = A.alloc([2, 16, 65], BF16)
        b_ctx = Buf("nactx")
        self.dma("sp", Kc[0:64].rearrange("p (c par) t -> p c par t", par=2), self.s_kT[:, :, :, SEQ:NTOK], reads=[b_scr["kT"]], writes=[b_ctx])
        self.dma("sp", Vc, self.s_vtok[SEQ:NTOK].rearrange("(b p) h e -> p b h e", p=128), reads=[b_scr["vtok"]], writes=[b_ctx])
        bias = A.alloc([16, 5, 128], F32)
        b_bias = Buf("nabias")
        Kwin = [A.alloc([16, 576], BF16) for _ in range(2)]
        Qp = [A.alloc([16, 128], BF16) for _ in range(2)]
        Vwin = [A.alloc([5, 16, 65], BF16) for _ in range(2)]
        b_win = [{k: Buf(k + str(i)) for k in ("K", "Q", "V")} for i in range(2)]
        Sw = [A.alloc([5, 128], F32) for _ in range(2)]
        b_Sw = [Buf("Sw0"), Buf("Sw1")]
        for i in range(2):
            self.memset("pool", Sw[i][64:128, 4, :], NEG, [b_Sw[i]])
        PT = [A.alloc([7, 128], BF16) for _ in range(3)]
        b_PT = [Buf("PT%d" % i) for i in range(3)]
        Otok = [A.alloc([16, 64], BF16) for _ in range(2)]
        b_Otok = [Buf("Ot0"), Buf("Ot1")]
        rinv = A.alloc([16], F32)
        b_rinv = Buf("rinv")
        ycst = [A.alloc([8, 512], BF16) for _ in range(2)]
        b_ycst = [Buf("nyc0"), Buf("nyc1")]
        psT, b_psT = self.ps[7][:, :].bitcast(BF16), self.psb[7]
        variants = {0: ((0, 0), (0, -1)), 1: ((0, -2), (0, -3)), 30: ((1, -5), (1, -6)), 31: ((1, -7), (1, -8))}
        interior = ((0, -4), (1, -5))
        cur_var = [None]

        def build_bias(var):
            self.memset("pool", bias, NEG, [b_bias])
            for rq in range(2):
                lo, off = var[rq]
                for ck in range(5):
                    js = [jr for jr in (2 * ck, 2 * ck + 1) if lo <= jr < lo + 8 and jr <= 8]
                    if not js:
                        continue
                    p0 = (js[0] % 2) * 64
                    dr0 = js[0] + off + 7
                    src = self.tblk[:, dr0:dr0 + len(js), :, :].rearrange("h j k c -> (j k) h c")
                    self.dma("sp", bias[p0:p0 + 64 * len(js), :, ck, rq * 64:(rq + 1) * 64], src, writes=[b_bias])

        hgroups = [(4, 0, 7), (5, 7, 14), (6, 14, 16)]
        hg_of = {}
        for (ob_, h0_, h1_) in hgroups:
            for h_ in range(h0_, h1_):
                hg_of[h_] = (ob_, h0_, h1_)
        cnt = {"sw": 0, "pt": 0, "st": 0}
        state = {}

        def emit_loads(rp):
            r = 2 * rp
            kb = min(max(r - 4, 0), 55)
            w0 = kb * 64
            wi = rp % 2
            W = b_win[wi]
            self.dma("sp", Kwin[wi][0:64].rearrange("p (c par) t -> p c par t", par=2), self.s_kT[:, :, :, w0:w0 + 576],
                     reads=[b_scr["kT"]], writes=[W["K"]])
            self.dma("sp", Qp[wi][0:64].rearrange("p (c par) t -> p c par t", par=2), self.s_qT[:, :, :, r * 64:r * 64 + 128],
                     reads=[b_scr["qT"]], writes=[W["Q"]])
            self.dma("sp", Vwin[wi][:, 0:4], self.s_vtok[w0:w0 + 512].rearrange("(b p) h e -> p b h e", p=128),
                     reads=[b_scr["vtok"]], writes=[W["V"]])
            self.dma("sp", Vwin[wi][0:64, 4], self.s_vtok[w0 + 512:w0 + 576], reads=[b_scr["vtok"]], writes=[W["V"]])

        def emit_st(rp, h):
            var = variants.get(rp, interior)
            if var != cur_var[0]:
                build_bias(var)
                cur_var[0] = var
            wi = rp % 2
            W = b_win[wi]
            si = cnt["st"] % 2
            cnt["st"] += 1
            psA, bA = self.ps[2 * si], self.psb[2 * si]
            psB, bB = self.ps[2 * si + 1], self.psb[2 * si + 1]
            q = Qp[wi][0:64, h, :]
            for ck in range(4):
                self.mm(psA[:, ck * 128:(ck + 1) * 128], Kwin[wi][0:64, h, ck * 128:(ck + 1) * 128], q, True, True,
                        [W["K"], W["Q"]], [bA], signal=(ck == 3))
            self.mm(psB[0:64, 0:128], Kwin[wi][0:64, h, 512:576], q, True, True, [W["K"], W["Q"]], [bB], signal=False)
            self.mm(psB[:, 128:256], Kc[0:64, h, 0:128], q, True, True, [b_ctx, W["Q"]], [bB], signal=False)
            self.mm(psB[:, 256:384], Kc[0:64, h, 128:256], q, True, True, [b_ctx, W["Q"]], [bB], signal=True)
            wi2 = cnt["sw"] % 2
            cnt["sw"] += 1
            sw, bsw = Sw[wi2], b_Sw[wi2]
            self.tt("dve", sw[:, 0:4, :], psA[:, :].rearrange("p (a b) -> p a b", a=4), bias[:, h, 0:4, :], ALU.add,
                    [bA, b_bias], [bsw])
            self.tt("dve", sw[0:64, 4, :], psB[0:64, 0:128], bias[0:64, h, 4, :], ALU.add, [bB, b_bias], [bsw])
            pi = cnt["pt"] % 3
            cnt["pt"] += 1
            pt, bpt = PT[pi], b_PT[pi]
            self.act(pt[:, 0:5, :], sw, AF.Exp, [bsw], [bpt])
            self.act(pt[:, 5:7, :], psB[:, 128:384].rearrange("p (a b) -> p a b", a=2), AF.Exp, [bB], [bpt])
            state[(rp, h)] = (pt, bpt)

        def emit_pv(rp, h):
            wi = rp % 2
            W = b_win[wi]
            pt, bpt = state.pop((rp, h))
            obank, h0, h1 = hg_of[h]
            ops, bops = self.ps[obank], self.psb[obank]
            ot, bot = Otok[rp % 2], b_Otok[rp % 2]
            o_ap = ops[:, (h - h0) * 65:(h - h0 + 1) * 65]
            for ck in range(7):
                M = 64 if ck == 4 else 128
                v = Vwin[wi][0:M, ck, h, :] if ck < 5 else Vc[:, ck - 5, h, :]
                vb = W["V"] if ck < 5 else b_ctx
                self.mm(o_ap, pt[0:M, ck, :], v, ck == 0, ck == 6, [bpt, vb], [bops], signal=(ck == 6))
            if h == h1 - 1:
                nh = h1 - h0
                o3 = ops[:, 0:nh * 65].rearrange("p (h e) -> p h e", h=nh)
                self.recip(rinv[:, h0:h1], o3[:, :, 64], [bops], [b_rinv])
                self.tt("dve", ot[:, h0:h1, :], o3[:, :, 0:64], rinv[:, h0:h1][:, :, None].to_broadcast([128, nh, 64]), ALU.mult,
                        [bops, b_rinv], [bot])
            if h == 15:
                otf = ot.rearrange("p h d -> p (h d)")
                for j in range(8):
                    self.tr(psT[:, j * 128:(j + 1) * 128], otf[:, j * 128:(j + 1) * 128], self.ident_b, [bot, self.b_cb], [b_psT],
                            signal=(j == 7))
                yi = (rp // 4) % 2
                self.cp("act", ycst[yi][:, :, (rp % 4) * 128:(rp % 4 + 1) * 128], psT[:, :].rearrange("p (a b) -> p a b", a=8),
                        [b_psT], [b_ycst[yi]])
                if rp % 4 == 3:
                    t0 = (rp // 4) * 512
                    self.dma("sp", self.s_ycatT[:, 8:16, t0:t0 + 512], ycst[yi], reads=[b_ycst[yi]], writes=[self.b_ycat])

        seq = [(rp, h) for rp in range(32) for h in range(16)]
        emit_loads(0)
        emit_st(*seq[0])
        for i, (rp, h) in enumerate(seq):
            if h == 0 and rp + 1 < 32:
                emit_loads(rp + 1)
            if i + 1 < len(seq):
                emit_st(*seq[i + 1])
            emit_pv(rp, h)
        if "na" in self.debug:
            d_ = self.dbg_tensor("ycatT2", (128, 16, SEQ), BF16)
            self.dma("sp", d_, self.s_ycatT, reads=[self.b_ycat])


    def phase_tail(self):
        A, S = self.A, self.S
        A.release(self.mark_base)
        NW = 4
        wb = [A.alloc([4096], BF16) for _ in range(NW)]
        b_wb = [Buf("tw%d" % i) for i in range(NW)]
        R1 = A.alloc([32, 512], BF16)
        b_R1 = Buf("R1")
        hid = R1
        ycat = R1[:, 0:16, :]
        ob = R1.rearrange("p a b -> p (a b)")[:, 0:8192].bitcast(F32).rearrange("p (a b) -> p a b", a=8)
        h = A.alloc([8, 512], BF16)
        b_h = [Buf("h%d" % i) for i in range(8)]
        sets = []
        for i in range(2):
            sets.append({"x": A.alloc([8, 512], F32), "bx": Buf("sx%d" % i), "bxm": [Buf("sx%d_%d" % (i, m)) for m in range(8)], "u": A.alloc([8, 514], F32), "bu": Buf("su%d" % i),
                         "gb": A.alloc([8, 512], BF16), "bgb": Buf("sg%d" % i)})
        gct = [A.alloc([512], F32) for _ in range(2)]
        b_gct = [Buf("gct0"), Buf("gct1")]
        rl = [A.alloc([512], F32) for _ in range(2)]
        b_rl = [Buf("rl0"), Buf("rl1")]
        cva = [A.alloc([512], F32) for _ in range(2)]
        b_cva = [Buf("cva0"), Buf("cva1")]
        gg = A.alloc([8, 512], BF16)
        b_gg = Buf("gg")
        wk = {"sq": [A.alloc([8, 512], BF16)] * 2, "b_sq": [Buf("sq")] * 2,
              "rstd": [A.alloc([512], F32) for _ in range(2)], "b_rstd": [Buf("rs0"), Buf("rs1")],
              "tmp": [A.alloc([512], F32) for _ in range(3)], "b_tmp": [Buf("nt%d" % i) for i in range(3)], "n": 0,
              "b_sqc": [Buf("sqc%d" % m) for m in range(8)]}
        cnt = {"w": 0, "ps": 0, "gct": 0, "rl": 0, "cva": 0, "norm": 0}

        pref = {}

        def prefetch(key, idx, kc):
            pref[(key, idx)] = wload(key, idx, kc)

        def wload(key, idx, kc):
            if (key, idx) in pref:
                return pref.pop((key, idx))
            i = cnt["w"] % NW
            cnt["w"] += 1
            v = wb[i].rearrange("p (k n) -> p k n", k=kc)
            src, bsrc = self.wsrc(key, idx, kc)
            self.dma("sp", v, src, reads=[bsrc], writes=[b_wb[i]])
            return v, b_wb[i]

        def nextps():
            pi = cnt["ps"] % 4
            cnt["ps"] += 1
            return self.ps[pi], self.psb[pi]

        def resid_evac(ps, bps, xs, m, gate_col, bgate):
            bxm = xs["bxm"][m]
            flush()
            self.stt("dve", xs["x"][:, m, :], ps[:, :], gate_col, xs["x"][:, m, :], ALU.mult, ALU.add, [bps, bgate, bxm], [bxm])
            pend.append(self.norm_accum(xs["x"], bxm, m, wk, cnt["norm"], 4))

        pend = []

        def flush():
            while pend:
                pend.pop(0)()

        def norm(xs, g, bg, shift, bshift, out, bout, final=False):
            flush()
            self.norm_tile(xs["x"], xs["bxm"], 512, g, bg, shift, bshift, out, bout, wk, cnt["norm"], psi=4, final=final,
                           accum_done=True)
            cnt["norm"] += 1

        def mlp(l, xs):
            gate = self.modv[l][:, 40:48]
            bgate = self.b_modv[l][5]
            for grp in range(8):
                wt, bw = wload("w1", (l, grp), 8)
                if grp == 0:
                    pss = [nextps() for _ in range(4)]
                    for kc in range(8):
                        for jj in range(4):
                            self.mm(pss[jj][0][:, :], wt[:, kc, jj * 128:(jj + 1) * 128], h[:, kc, :], kc == 0, kc == 7, [bw, b_h[kc]],
                                    [pss[jj][1]], signal=(kc == 7))
                for jj in range(4):
                    mh = grp * 4 + jj
                    if grp == 0:
                        ps, bps = pss[jj]
                    else:
                        ps, bps = nextps()
                        for kc in range(8):
                            self.mm(ps[:, :], wt[:, kc, jj * 128:(jj + 1) * 128], h[:, kc, :], kc == 0, kc == 7, [bw, b_h[kc]], [bps], signal=(kc == 7))
                    ri = cnt["rl"] % 2
                    cnt["rl"] += 1
                    self.act(rl[ri], ps[:, :], AF.Relu, [bps], [b_rl[ri]])
                    self.tt("pool", hid[:, mh, :], rl[ri], rl[ri], ALU.mult, [b_rl[ri]], [b_R1])
            for m in range(8):
                wt, bw = wload("w2", (l, m), 32)
                ps, bps = nextps()
                for kc in range(32):
                    self.mm(ps[:, :], wt[:, kc, :], hid[:, kc, :], kc == 0, kc == 31, [bw, b_R1], [bps], signal=(kc == 31))
                resid_evac(ps, bps, xs, m, gate[:, m:m + 1], bgate)

        def conv_b(xs, m):
            ci = cnt["cva"] % 2
            cnt["cva"] += 1
            cv_, bcv = cva[ci], b_cva[ci]
            self.act(cv_, xs["u"][:, m, 0:512], AF.Identity, [xs["bu"], self.b_cols], [bcv], scale=self.colv("scw0", m))
            self.stt("dve", cv_, xs["u"][:, m, 1:513], self.colv("scw1", m), cv_, ALU.mult, ALU.add, [xs["bu"], self.b_cols, bcv], [bcv])
            self.stt("dve", cv_, xs["u"][:, m, 2:514], self.colv("scw2", m), cv_, ALU.mult, ALU.add, [xs["bu"], self.b_cols, bcv], [bcv])
            self.tt("pool", gg[:, m, :], cv_, xs["gb"][:, m, :], ALU.mult, [bcv, xs["bgb"]], [b_gg])

        def load_ycat(t):
            self.dma("sp", ycat, self.s_ycatT[:, :, t * 512:(t + 1) * 512], reads=[self.b_ycat], writes=[b_R1])

        def stage_a(t, xs, prev):
            t0 = t * 512
            if t <= 1:
                load_ycat(t)
            self.dma("sp", xs["x"], self.xT[:, :, t0:t0 + 512], writes=xs["bxm"])
            for grp in range(4):
                wt, bw = wload("out0", (grp,), 16)
                for jj in range(2):
                    m = grp * 2 + jj
                    ps, bps = nextps()
                    for kc in range(16):
                        self.mm(ps[:, :], wt[:, kc, jj * 128:(jj + 1) * 128], ycat[:, kc, :], kc == 0, kc == 15, [bw, b_R1], [bps],
                                signal=(kc == 15))
                    resid_evac(ps, bps, xs, m, self.modv[0][:, 16 + m:17 + m], self.b_modv[0][2])
            norm(xs, self.g_f0, self.b_gf0, self.modv[0][:, 24:32], self.b_modv[0][3], h, b_h)
            mlp(0, xs)
            norm(xs, self.g_a1, self.b_ga1, self.modv[1][:, 0:8], self.b_modv[1][0], h, b_h)
            for grp in range(2):
                wt, bw = wload("scin", (grp,), 8)
                for jj in range(4):
                    m = grp * 4 + jj
                    ps, bps = nextps()
                    for kc in range(8):
                        self.mm(ps[:, :], wt[:, kc, jj * 128:(jj + 1) * 128], h[:, kc, :], kc == 0, kc == 7, [bw, b_h[kc]], [bps], signal=(kc == 7))
                    self.cp("act", xs["gb"][:, m, :], ps[:, :], [bps], [xs["bgb"]])
            for half in range(2):
                wc, bwc = wload("scin", (2 + half,), 8)
                wv, bwv = wload("scin", (4 + half,), 8)
                for jj in range(4):
                    m = half * 4 + jj
                    ps, bps = nextps()
                    for kc in range(8):
                        self.mm(ps[:, :], wc[:, kc, jj * 128:(jj + 1) * 128], h[:, kc, :], kc == 0, kc == 7, [bwc, b_h[kc]], [bps], signal=(kc == 7))
                    gi = cnt["gct"] % 2
                    cnt["gct"] += 1
                    self.cp("act", gct[gi], ps[:, :], [bps], [b_gct[gi]])
                    ps2, bps2 = nextps()
                    for kc in range(8):
                        self.mm(ps2[:, :], wv[:, kc, jj * 128:(jj + 1) * 128], h[:, kc, :], kc == 0, kc == 7, [bwv, b_h[kc]], [bps2], signal=(kc == 7))
                    self.tt("dve", xs["u"][:, m, 1:513], ps2[:, :], gct[gi], ALU.mult, [bps2, b_gct[gi]], [xs["bu"]])
                    if t == 0:
                        self.memset("pool", xs["u"][:, m, 0:1], 0.0, [xs["bu"]])
                    else:
                        self.cp("pool", xs["u"][:, m, 0:1], prev["u"][:, m, 512:513], [prev["bu"]], [xs["bu"]])
                        self.cp("pool", prev["u"][:, m, 513:514], xs["u"][:, m, 1:2], [xs["bu"]], [prev["bu"]])
                        conv_b(prev, m)
                    if t == 7:
                        self.memset("pool", xs["u"][:, m, 513:514], 0.0, [xs["bu"]])

        def stage_b(t, xs, do_conv):
            t0 = t * 512
            if do_conv:
                for m in range(8):
                    conv_b(xs, m)
            for grp in range(2):
                wt, bw = wload("scout", (grp,), 8)
                for jj in range(4):
                    m = grp * 4 + jj
                    ps, bps = nextps()
                    for kc in range(8):
                        self.mm(ps[:, :], wt[:, kc, jj * 128:(jj + 1) * 128], gg[:, kc, :], kc == 0, kc == 7, [bw, b_gg], [bps], signal=(kc == 7))
                    resid_evac(ps, bps, xs, m, self.modv[1][:, 16 + m:17 + m], self.b_modv[1][2])
            norm(xs, self.g_f1, self.b_gf1, self.modv[1][:, 24:32], self.b_modv[1][3], h, b_h)
            mlp(1, xs)
            if t + 2 < 8:
                load_ycat(t + 2)
                for grp in range(3):
                    prefetch("out0", (grp,), 16)
            norm(xs, self.colv("fnw"), self.b_cols, None, None, xs["x"], xs["bxm"], final=True)
            self.dma("sp", self.outT[:, :, t0:t0 + 512], xs["x"], reads=xs["bxm"])

        for t in range(9):
            cur = sets[t % 2]
            prev = sets[(t - 1) % 2]
            if t < 8:
                stage_a(t, cur, prev)
            if t >= 1:
                stage_b(t - 1, prev, do_conv=(t == 8))


def build_program(stop_after=None, debug=()):
    p = Prog(stop_after, debug)
    p.build()
    return p


_CACHE = {}


def kernel(**inputs):
    shared, per_core = prep_inputs(inputs)
    p = build_program()
    in_maps = []
    for b in range(8):
        m = dict(shared)
        m.update(per_core[b])
        in_maps.append(m)
    res = run_bass_kernel_spmd(p.nc, in_maps, core_ids=list(range(8)))
    out = np.empty((8, SEQ, D), np.float32)
    for b in range(8):
        oT = np.asarray(res.results[b]["outT"])
        out[b] = oT.transpose(2, 1, 0).reshape(SEQ, D)
    return out
```

```python
from contextlib import ExitStack
import numpy as np
import concourse.bass as bass
import concourse.mybir as mybir
from concourse.bass_utils import run_bass_kernel_spmd

F32 = mybir.dt.float32
BF16 = mybir.dt.bfloat16
AF = mybir.ActivationFunctionType
ALU = mybir.AluOpType

ENGS = ("pe", "act", "dve", "pool", "sp")

D = 1024
SEQ = 4096
CTX = 256
NTOK = SEQ + CTX
OFF_X, OFF_B, OFF_DT, OFF_K, OFF_V, OFF_C, OFF_Z, OFF_Q = 0, 1024, 1280, 1312, 2336, 3360, 3616, 4640
NEG = -30000.0
EPS = 1e-6


class Buf:
    __slots__ = ("name", "lw", "rd")

    def __init__(self, name=""):
        self.name = name
        self.lw = None
        self.rd = []


class Sched:
    def __init__(self, nc, n_lanes=6, same_eng_sync=True):
        self.nc = nc
        self.q = {e: [] for e in ENGS}
        self.cnt = {e: 0 for e in ENGS}
        self.seen = {e: {} for e in ENGS}
        self.same_eng_sync = same_eng_sync
        self.lanes = {}
        self.lane_rr = {}
        for qe in ("sp", "act", "pool"):
            self.lanes[qe] = [["dma_%s_%d" % (qe, i), 0] for i in range(n_lanes)]
            self.lane_rr[qe] = 0
        self.semkeys = list(ENGS) + [l[0] for qe in self.lanes for l in self.lanes[qe]]

    def _deps(self, reads, writes):
        deps = set()
        for b in reads:
            if b.lw is not None:
                deps.add((b.lw[0], b.lw[1], True))
        for b in writes:
            if b.lw is not None:
                deps.add((b.lw[0], b.lw[1], False))
            for r in b.rd:
                deps.add((r[0], r[1], False))
        return deps

    def _emit_waits(self, eng, deps):
        mx = {}
        for k, v, raw in deps:
            if k == eng:
                if not (raw and self.same_eng_sync and eng in ("act", "dve", "pool")):
                    continue
            if mx.get(k, 0) < v:
                mx[k] = v
        for k, v in mx.items():
            if self.seen[eng].get(k, 0) >= v:
                continue
            self.seen[eng][k] = v
            self.q[eng].append(("wait", k, v))

    def _attr(self, ev, reads, writes):
        for b in reads:
            if len(b.rd) > 64:
                mx = {}
                for k, v in b.rd:
                    if mx.get(k, 0) < v:
                        mx[k] = v
                b.rd = list(mx.items())
            b.rd.append(ev)
        for b in writes:
            b.lw = ev
            b.rd = []

    def op(self, eng, fn, reads=(), writes=(), signal=True):
        self._emit_waits(eng, self._deps(reads, writes))
        ev = (eng, self.cnt[eng] + 1)
        self._attr(ev, reads, writes)
        if signal:
            self.cnt[eng] += 1
            self.q[eng].append(("op", fn, eng))
        else:
            self.q[eng].append(("op", fn, None))
        return ev

    def dma(self, qe, out, in_, reads=(), writes=()):
        lanes = self.lanes[qe]
        i = self.lane_rr[qe]
        self.lane_rr[qe] = (i + 1) % len(lanes)
        lane = lanes[i]
        deps = self._deps(reads, writes)
        if lane[1] > 0:
            deps.add((lane[0], 16 * lane[1], True))
        self._emit_waits(qe, deps)
        lane[1] += 1
        ev = (lane[0], 16 * lane[1])
        self._attr(ev, reads, writes)
        self.q[qe].append(("dma", (out, in_), lane[0]))
        return ev

    def _all_events(self):
        evs = [(e, self.cnt[e]) for e in ENGS if self.cnt[e] > 0]
        for qe in self.lanes:
            for key, c in self.lanes[qe]:
                if c > 0:
                    evs.append((key, 16 * c))
        return evs

    def _force(self, eng, evs):
        for k, v in evs:
            if k == eng or self.seen[eng].get(k, 0) >= v:
                continue
            self.seen[eng][k] = v
            self.q[eng].append(("wait", k, v))

    def barrier(self):
        evs = self._all_events()
        for e in ENGS:
            self._force(e, evs)

    def final_wait(self, eng="sp"):
        self._force(eng, self._all_events())

    def replay(self, stack):
        nc = self.nc
        used = set()
        for e in ENGS:
            for it in self.q[e]:
                if it[0] == "wait":
                    used.add(it[1])
                elif it[2] is not None:
                    used.add(it[2])
        sems = {k: stack.enter_context(nc.semaphore("s_" + k)) for k in self.semkeys if k in used}
        block = stack.enter_context(nc.Block())

        def run(eng_name):
            def body(e):
                for it in self.q[eng_name]:
                    if it[0] == "wait":
                        e.wait_ge(sems[it[1]], it[2])
                    elif it[0] == "op":
                        ins = it[1](e)
                        if it[2] is not None:
                            ins.then_inc(sems[it[2]], 1)
                    else:
                        out, in_ = it[1]
                        e.dma_start(out=out, in_=in_).then_inc(sems[it[2]], 16)
            return body

        if self.q["sp"]:
            block.sync(run("sp"))
        if self.q["pe"]:
            block.tensor(run("pe"))
        if self.q["act"]:
            block.scalar(run("act"))
        if self.q["dve"]:
            block.vector(run("dve"))
        if self.q["pool"]:
            block.gpsimd(run("pool"))


class Arena:
    def __init__(self, ap, words):
        self.ap = ap
        self.words = words
        self.top = 0

    def mark(self):
        return self.top

    def release(self, m):
        self.top = m

    def alloc(self, shape, dtype, parts=128):
        n = 1
        for s in shape:
            n *= s
        nbytes = n * (4 if dtype == F32 else 2)
        w = (nbytes + 3) // 4
        w = (w + 7) // 8 * 8
        assert self.top + w <= self.words, "SBUF arena overflow: need %d have %d" % (self.top + w, self.words)
        v = self.ap[:, self.top:self.top + (nbytes + 3) // 4]
        self.top += w
        if dtype != F32:
            v = v.bitcast(dtype)
        if len(shape) == 2:
            v = v.rearrange("p (a b) -> p a b", a=shape[0])
        elif len(shape) == 3:
            v = v.rearrange("p (a b c) -> p a b c", a=shape[0], b=shape[1])
        return v


def _col(v):
    v = np.asarray(v, np.float32)
    return np.ascontiguousarray(v.reshape(-1, 128).T)


def _wtile(w, ncols):
    K, N = w.shape
    assert K % 128 == 0 and N % ncols == 0
    t = w.reshape(K // 128, 128, N // ncols, ncols).transpose(2, 1, 0, 3)
    return np.ascontiguousarray(t, dtype=np.float32)


IN_GROUPS = [
    ("x0", 0, 512), ("x1", 512, 512), ("B", OFF_B, 256), ("dt", OFF_DT, 32),
    ("k0", OFF_K, 512), ("k1", OFF_K + 512, 512), ("v0", OFF_V, 512), ("v1", OFF_V + 512, 512),
    ("C", OFF_C, 256), ("z0", OFF_Z, 512), ("z1", OFF_Z + 512, 512), ("q0", OFF_Q, 512), ("q1", OFF_Q + 512, 512),
]
IN_GIDX = {g[0]: i for i, g in enumerate(IN_GROUPS)}

COLS = {}
_n = 0
for _name, _w in [("c", 8), ("cctx", 8), ("modb0", 48), ("modb1", 48), ("nmix0", 8), ("nmix1", 8), ("nmlp0", 8),
                  ("nmlp1", 8), ("fnw", 8), ("cw0", 12), ("cw1", 12), ("cw2", 12), ("cb", 12),
                  ("scw0", 8), ("scw1", 8), ("scw2", 8)]:
    COLS[_name] = (_n, _w)
    _n += _w
NCOL = _n
ROWS = {}
_n = 0
for _name, _w in [("ssdnw", 1024), ("dtb", 32), ("alog", 32), ("dskip", 16)]:
    ROWS[_name] = (_n, _w)
    _n += _w
NROW = _n
NCONST = 5 * 128


def prep_inputs(inp):
    f = lambda a: np.ascontiguousarray(np.asarray(a, np.float32))
    shared = {}
    shared["modw"] = np.stack([_wtile(f(inp["mod_w"][l]), 1024) for l in range(2)])
    in_w = f(inp["ssdna_in_w"][0])
    inw = np.zeros((len(IN_GROUPS), 128, 8, 512), np.float32)
    for gi, (nm, c0, nc_) in enumerate(IN_GROUPS):
        inw[gi, :, :, :nc_] = in_w[:, c0:c0 + nc_].reshape(8, 128, nc_).transpose(1, 0, 2)
    shared["inw"] = inw
    shared["outw0"] = _wtile(f(inp["ssdna_out_w"][0]), 256)
    shared["w1"] = np.stack([_wtile(f(inp["mlp_w1"][l]), 512) for l in range(2)])
    shared["w2"] = np.stack([_wtile(f(inp["mlp_w2"][l]), 128) for l in range(2)])
    shared["scin"] = _wtile(f(inp["sc_in_w"][0]), 512)
    shared["scout"] = _wtile(f(inp["sc_out_w"][0]), 512)
    rows = np.zeros((1, NROW), np.float32)
    rows[0, ROWS["ssdnw"][0]:ROWS["ssdnw"][0] + 1024] = f(inp["ssd_norm_w"][0])
    rows[0, ROWS["dtb"][0]:ROWS["dtb"][0] + 32] = f(inp["ssd_dt_bias"][0]).reshape(32)
    rows[0, ROWS["alog"][0]:ROWS["alog"][0] + 32] = f(inp["ssd_a_log"][0]).reshape(32)
    rows[0, ROWS["dskip"][0]:ROWS["dskip"][0] + 16] = f(inp["ssd_d"][0])
    shared["rows"] = rows
    k = np.arange(128)
    consts = np.concatenate([np.eye(128, dtype=np.float32),
                             (k[:, None] <= k[None, :]).astype(np.float32),
                             (k[:, None] >= k[None, :]).astype(np.float32),
                             (k[:, None] < k[None, :]).astype(np.float32),
                             (k[:, None] > k[None, :]).astype(np.float32)], axis=1)
    shared["consts"] = np.ascontiguousarray(consts)
    rpb = f(inp["na_rpb"][0])
    c_ = np.arange(64)
    ws = np.clip(c_ - 8, 0, 48)
    kc_ = np.arange(64)
    ok = (kc_[:, None] >= ws[None, :]) & (kc_[:, None] < ws[None, :] + 16)
    dc = np.clip(kc_[:, None] - c_[None, :], -15, 15) + 15
    tblk = rpb[:, :, dc]
    tblk = np.where(ok[None, None], tblk, np.float32(NEG)).astype(np.float32)
    shared["tblk"] = np.ascontiguousarray(tblk)
    cols_common = np.zeros((128, NCOL), np.float32)

    def put(name, arr):
        o, w = COLS[name]
        cols_common[:, o:o + w] = arr
    put("cctx", _col(inp["c_ctx"]))
    for l in range(2):
        put("modb%d" % l, _col(inp["mod_b"][l]))
        put("nmix%d" % l, _col(inp["norm_mix_w"][l]))
        put("nmlp%d" % l, _col(inp["norm_mlp_w"][l]))
    put("fnw", _col(inp["final_norm_w"]))
    cw = f(inp["ssdna_conv_w"][0])
    for t in range(3):
        put("cw%d" % t, _col(cw[t]))
        put("scw%d" % t, _col(f(inp["sc_conv_w"][0])[t]))
    put("cb", _col(inp["ssdna_conv_b"][0]))
    per_core = []
    x = np.asarray(inp["x"], np.float32)
    ctx = np.asarray(inp["ctx"], np.float32)
    for b in range(8):
        cols = cols_common.copy()
        o, w = COLS["c"]
        cols[:, o:o + w] = _col(inp["c"][b])
        xT = np.ascontiguousarray(x[b].reshape(SEQ, 8, 128).transpose(2, 1, 0))
        cT = np.ascontiguousarray(ctx[b].reshape(CTX, 8, 128).transpose(2, 1, 0))
        per_core.append({"xT": xT, "ctxT": cT, "cols": cols})
    return shared, per_core


class Prog:
    def __init__(self, stop_after=None, debug=()):
        self.stop_after = stop_after
        self.debug = debug
        nc = self.nc = bass.Bass("TRN2", target_bir_lowering=False)
        self.dr = {}
        ein = lambda n, s: nc.dram_tensor(n, list(s), F32, kind="ExternalInput").ap()
        self.xT = ein("xT", (128, 8, SEQ))
        self.ctxT = ein("ctxT", (128, 8, CTX))
        self.cols_d = ein("cols", (128, NCOL))
        self.rows_d = ein("rows", (1, NROW))
        self.consts_d = ein("consts", (128, NCONST))
        self.modw = ein("modw", (2, 6, 128, 8, 1024))
        self.inw = ein("inw", (len(IN_GROUPS), 128, 8, 512))
        self.outw0 = ein("outw0", (4, 128, 16, 256))
        self.w1 = ein("w1", (2, 8, 128, 8, 512))
        self.w2 = ein("w2", (2, 8, 128, 32, 128))
        self.scin = ein("scin", (6, 128, 8, 512))
        self.scout = ein("scout", (2, 128, 8, 512))
        self.tblk = ein("tblk", (16, 15, 64, 64))
        self.outT = nc.dram_tensor("outT", [128, 8, SEQ], F32, kind="ExternalOutput").ap()
        scr = lambda n, s, dt: nc.dram_tensor(n, list(s), dt, kind="Internal").ap()
        self.s_xtok = scr("s_xtok", (NTOK, 1024), BF16)
        self.s_BT = scr("s_BT", (256, NTOK), BF16)
        self.s_CT = scr("s_CT", (256, SEQ), BF16)
        self.s_Btok = scr("s_Btok", (NTOK, 256), BF16)
        self.s_z = scr("s_z", (SEQ, 1024), BF16)
        self.s_qT2 = scr("s_qT", (128, 8, SEQ), BF16)
        self.s_kT2 = scr("s_kT", (128, 8, NTOK), BF16)
        self.s_qT = self.s_qT2.rearrange("(par d) c t -> d c par t", par=2)
        self.s_kT = self.s_kT2.rearrange("(par d) c t -> d c par t", par=2)
        self.s_vtok = scr("s_vtok", (NTOK, 16, 65), BF16)
        self.s_ycatT = scr("s_ycatT", (128, 16, SEQ), BF16)
        self.wsb = {"in": scr("wsb_in", (len(IN_GROUPS), 128, 4096), BF16), "out0": scr("wsb_out0", (4, 128, 4096), BF16),
                    "w1": scr("wsb_w1", (2, 8, 128, 4096), BF16), "w2": scr("wsb_w2", (2, 8, 128, 4096), BF16),
                    "scin": scr("wsb_scin", (6, 128, 4096), BF16), "scout": scr("wsb_scout", (2, 128, 4096), BF16)}
        self.b_wsb = {}
        self.tail_casts = []
        self.dbg_out = {}

    def dbg_tensor(self, name, shape, dtype=F32):
        ap = self.nc.dram_tensor("dbg_" + name, list(shape), dtype, kind="ExternalOutput").ap()
        self.dbg_out[name] = ap
        return ap

    def mm(self, out, lhsT, rhs, start, stop, reads, writes, signal, skip=False):
        if skip:
            self.S.op("pe", lambda e, o=out, l=lhsT, r=rhs, s=start, t=stop: e.matmul(o, l, r, start=s, stop=t, skip_group_check=True),
                      reads=reads, writes=writes, signal=signal)
            return
        self.S.op("pe", lambda e, o=out, l=lhsT, r=rhs, s=start, t=stop: e.matmul(o, l, r, start=s, stop=t),
                  reads=reads, writes=writes, signal=signal)

    def tr(self, out, in_, ident, reads, writes, signal=True):
        self.S.op("pe", lambda e, o=out, i=in_, d=ident: e.transpose(o, i, d), reads=reads, writes=writes, signal=signal)

    def act(self, out, in_, func, reads, writes, bias=None, scale=None, accum_out=None, eng="act"):
        kw = {}
        if bias is not None:
            kw["bias"] = bias
        if scale is not None:
            kw["scale"] = scale
        if accum_out is not None:
            kw["accum_out"] = accum_out
        self.S.op(eng, lambda e, o=out, i=in_, f=func, kw=kw: e.activation(out=o, in_=i, func=f, **kw),
                  reads=reads, writes=writes)

    def tt(self, eng, out, in0, in1, op, reads, writes):
        self.S.op(eng, lambda e, o=out, a=in0, b=in1, p=op: e.tensor_tensor(out=o, in0=a, in1=b, op=p),
                  reads=reads, writes=writes)

    def ts(self, eng, out, in0, s1, op0, reads, writes, s2=None, op1=None):
        if op1 is None:
            self.S.op(eng, lambda e, o=out, a=in0, s=s1, p=op0: e.tensor_scalar(out=o, in0=a, scalar1=s, scalar2=None, op0=p),
                      reads=reads, writes=writes)
        else:
            self.S.op(eng, lambda e, o=out, a=in0, s=s1, t=s2, p=op0, q=op1:
                      e.tensor_scalar(out=o, in0=a, scalar1=s, scalar2=t, op0=p, op1=q), reads=reads, writes=writes)

    def stt(self, eng, out, in0, scalar, in1, op0, op1, reads, writes):
        self.S.op(eng, lambda e, o=out, a=in0, s=scalar, b=in1, p=op0, q=op1:
                  e.scalar_tensor_tensor(out=o, in0=a, scalar=s, in1=b, op0=p, op1=q), reads=reads, writes=writes)

    def cp(self, eng, out, in_, reads, writes):
        if eng == "act":
            self.S.op("act", lambda e, o=out, i=in_: e.activation(out=o, in_=i, func=AF.Copy), reads=reads, writes=writes)
        else:
            self.S.op(eng, lambda e, o=out, i=in_: e.tensor_copy(out=o, in_=i), reads=reads, writes=writes)

    def recip(self, out, in_, reads, writes):
        self.S.op("dve", lambda e, o=out, i=in_: e.reciprocal(out=o, in_=i), reads=reads, writes=writes)

    def memset(self, eng, ap, val, writes):
        self.S.op(eng, lambda e, a=ap, v=val: e.memset(a, v), reads=(), writes=writes)

    def dma(self, qe, out, in_, reads=(), writes=()):
        self.S.dma(qe, out, in_, reads=reads, writes=writes)

    def build(self):
        nc = self.nc
        with ExitStack() as st:
            words = 51 * 1024
            arena_t = st.enter_context(nc.sbuf_tensor("arena", [128, words], F32))
            self.A = Arena(arena_t, words)
            self.ps = [st.enter_context(nc.psum_tensor("ps%d" % i, [128, 512], F32)) for i in range(8)]
            self.psb = [Buf("ps%d" % i) for i in range(8)]
            self.S = Sched(nc)
            self.phase_setup()
            self.precast("in")
            done = self.run_phases()
            self.S.barrier()
            self.S.final_wait("sp")
            self.S.replay(st)
        return nc

    def run_phases(self):
        self.phase_mod_and_h()
        if self.stop_after == "h":
            return
        self.S.barrier()
        self.phase_inproj()
        if self.stop_after == "inproj":
            return
        self.S.barrier()
        if "skip_ssd" not in self.debug:
            self.phase_ssd()
        else:
            self.b_ycat = Buf("ycat")
        if self.stop_after == "ssd":
            return
        self.S.barrier()
        if "skip_na" not in self.debug:
            self.phase_na()
        if self.stop_after == "na":
            return
        self.S.barrier()
        self.phase_tail()

    def phase_setup(self):
        A = self.A
        self.cols = A.alloc([NCOL], F32)
        self.b_cols = Buf("cols")
        self.dma("sp", self.cols, self.cols_d, writes=[self.b_cols])
        self.consts = A.alloc([NCONST], F32)
        self.b_consts = Buf("consts")
        self.dma("sp", self.consts, self.consts_d, writes=[self.b_consts])
        self.ident_f = self.consts[:, 0:128]
        self.L_f = self.consts[:, 128:256]
        self.U_f = self.consts[:, 256:384]
        self.SL_f = self.consts[:, 384:512]
        self.SU_f = self.consts[:, 512:640]
        self.dt_raw = A.alloc([34, 32], F32)
        self.b_dtraw = Buf("dtraw")
        self.ident_b = A.alloc([128], BF16)
        self.ones_b = A.alloc([128], BF16)
        self.b_cb = Buf("constb")
        self.cp("dve", self.ident_b, self.ident_f, [self.b_consts], [self.b_cb])
        self.memset("dve", self.ones_b, 1.0, [self.b_cb])
        self.modv = [A.alloc([48], F32) for _ in range(2)]
        self.b_modv = [[Buf("modv%d_%d" % (l, v)) for v in range(6)] for l in range(2)]
        self.modc = A.alloc([16], F32)
        self.b_modc = Buf("modc")
        self.gcols = A.alloc([64], F32)
        self.b_g = {}

    def precast(self, which):
        def one(key, idx, src):
            b = Buf("wsb_%s_%s" % (key, idx))
            self.b_wsb[(key,) + idx] = b
            dst = self.wsb[key]
            for i in idx:
                dst = dst[i]
                src = src[i]
            if which == "in":
                self.dma("pool", dst, src.rearrange("p k n -> p (k n)"), writes=[b])
            else:
                self.tail_casts.append(lambda gate, dst=dst, src=src, b=b: self.dma(
                    "pool", dst, src.rearrange("p k n -> p (k n)"), reads=[gate], writes=[b]))
        if which == "in":
            for g in range(len(IN_GROUPS)):
                one("in", (g,), self.inw)
            return
        for g in range(4):
            one("out0", (g,), self.outw0)
        for l in range(2):
            if l == 1:
                for g in range(6):
                    one("scin", (g,), self.scin)
                for g in range(2):
                    one("scout", (g,), self.scout)
            for g in range(8):
                one("w1", (l, g), self.w1)
            for g in range(8):
                one("w2", (l, g), self.w2)

    def cast_some(self, n, gate):
        for _ in range(n):
            if self.tail_casts:
                self.tail_casts.pop(0)(gate)

    def wsrc(self, key, idx, kc):
        src = self.wsb[key]
        for i in idx:
            src = src[i]
        return src.rearrange("p (k n) -> p k n", k=kc), self.b_wsb[(key,) + idx]

    def colv(self, name, j=None):
        o, w = COLS[name]
        if j is None:
            return self.cols[:, o:o + w]
        return self.cols[:, o + j:o + j + 1]

    def phase_mod_and_h(self):
        A, S = self.A, self.S
        m0 = A.mark()
        self.mark_base = m0
        self.hT = A.alloc([8, SEQ], BF16)
        self.hcT = A.alloc([8, CTX], BF16)
        self.b_hT = [Buf("hT%d" % t) for t in range(8)]
        self.b_hcT = Buf("hcT")
        m1 = A.mark()
        s2 = A.alloc([8, 2], F32)
        b_s2 = Buf("s2")
        self.act(s2[:, :, 0], self.colv("c"), AF.Silu, [self.b_cols], [b_s2])
        self.act(s2[:, :, 1], self.colv("cctx"), AF.Silu, [self.b_cols], [b_s2])
        wbuf = [A.alloc([8, 1024], F32) for _ in range(2)]
        b_w = [Buf("modw%d" % i) for i in range(2)]
        modcnt = [0]
        rsb = [A.alloc([1024], F32)] * 2
        b_rsb = [Buf("rsb0")] * 2

        def mod_group(l, v):
            n = modcnt[0]
            modcnt[0] += 1
            wb, bw = wbuf[n % 2], b_w[n % 2]
            self.dma("sp", wb, self.modw[l, v], writes=[bw])
            pst = self.ps[n % 2][:, 0:16].rearrange("p (j t) -> p j t", t=2)
            bps = self.psb[n % 2]
            rs, brs_ = rsb[n % 2], b_rsb[n % 2]
            for nh in range(2):
                pr, bpr = self.ps[4 + nh], self.psb[4 + nh]
                for kc in range(8):
                    self.mm(pr[0:2, :], s2[:, kc, :], wb[:, kc, nh * 512:(nh + 1) * 512], kc == 0, kc == 7, [bw, b_s2], [bpr],
                            signal=(kc == 7))
                self.cp("act" if nh == 0 else "dve", rs[0:2, nh * 512:(nh + 1) * 512], pr[0:2, :], [bpr], [brs_])
            for j in range(8):
                self.mm(pst[:, j, :], rs[0:2, j * 128:(j + 1) * 128], self.ident_f[0:2, 0:2], True, True, [brs_, self.b_consts], [bps],
                        signal=(j == 7))
            o, _ = COLS["modb%d" % l]
            self.tt("dve", self.modv[l][:, v * 8:(v + 1) * 8], pst[:, :, 0], self.cols[:, o + v * 8:o + v * 8 + 8],
                    ALU.add, [bps, self.b_cols], [self.b_modv[l][v]])
            if l == 0 and v < 2:
                self.tt("dve", self.modc[:, v * 8:(v + 1) * 8], pst[:, :, 1], self.cols[:, o + v * 8:o + v * 8 + 8],
                        ALU.add, [bps, self.b_cols], [self.b_modc])

        mod_group(0, 0)
        mod_group(0, 1)
        rest = [(0, v) for v in range(2, 6)] + [(1, v) for v in range(6)]
        def gain(slot, normname, scale_ap, rd):
            g = self.gcols[:, slot * 8:(slot + 1) * 8]
            b = Buf("g%d" % slot)
            self.stt("dve", g, scale_ap, 1.0, self.colv(normname), ALU.add, ALU.mult, rd + [self.b_cols], [b])
            return g, b
        self.g_a0, self.b_ga0 = gain(0, "nmix0", self.modv[0][:, 8:16], [self.b_modv[0][1]])
        self.g_c, self.b_gc = gain(4, "nmix0", self.modc[:, 8:16], [self.b_modc])
        xbuf = [A.alloc([8, 512], F32) for _ in range(2)]
        b_x = [Buf("xb%d" % i) for i in range(2)]
        wk = self.norm_alloc(512)
        wk["use_pool"] = False
        for t in range(9):
            xb, bx = xbuf[t % 2], b_x[t % 2]
            if t < 8:
                ntok = 512
                self.dma("sp", xb, self.xT[:, :, t * 512:(t + 1) * 512], writes=[bx])
                self.norm_tile(xb, bx, ntok, self.g_a0, self.b_ga0, self.modv[0][:, 0:8], self.b_modv[0][0],
                               self.hT[:, :, t * 512:(t + 1) * 512], self.b_hT[t], wk, t)
            else:
                ntok = CTX
                self.dma("sp", xb[:, :, 0:CTX], self.ctxT, writes=[bx])
                self.norm_tile(xb[:, :, 0:CTX], bx, ntok, self.g_c, self.b_gc, self.modc[:, 0:8], self.b_modc,
                               self.hcT, self.b_hcT, wk, t)
            for _ in range(2 if t == 0 else 1):
                if rest:
                    mod_group(*rest.pop(0))
        while rest:
            mod_group(*rest.pop(0))
        self.g_f0, self.b_gf0 = gain(1, "nmlp0", self.modv[0][:, 32:40], [self.b_modv[0][4]])
        self.g_a1, self.b_ga1 = gain(2, "nmix1", self.modv[1][:, 8:16], [self.b_modv[1][1]])
        self.g_f1, self.b_gf1 = gain(3, "nmlp1", self.modv[1][:, 32:40], [self.b_modv[1][4]])
        self.precast("tail")
        if "h" in self.debug:
            d = self.dbg_tensor("hT", (128, 8, SEQ), BF16)
            self.dma("sp", d, self.hT, reads=self.b_hT)
            d = self.dbg_tensor("hcT", (128, 8, CTX), BF16)
            self.dma("sp", d, self.hcT, reads=[self.b_hcT])
            d = self.dbg_tensor("modv0", (128, 48))
            self.dma("sp", d, self.modv[0], reads=self.b_modv[0])
            d = self.dbg_tensor("modv1", (128, 48))
            self.dma("sp", d, self.modv[1], reads=self.b_modv[1])
        self.mark_after_h = m1

    def norm_alloc(self, ntok):
        A = self.A
        wk = {"sq": [A.alloc([8, ntok], BF16) for _ in range(2)], "b_sq": [Buf("sq0"), Buf("sq1")],
              "rstd": [A.alloc([ntok], F32) for _ in range(2)], "b_rstd": [Buf("rs0"), Buf("rs1")],
              "tmp": [A.alloc([ntok], F32) for _ in range(3)], "b_tmp": [Buf("nt%d" % i) for i in range(3)],
              "n": 0}
        return wk

    def norm_accum(self, x, bx, m, wk, it, psi):
        i2 = it % 2
        sq = wk["sq"][i2]
        bsq = wk["b_sqc"][m]
        ps, bps = self.ps[psi + i2], self.psb[psi + i2]
        self.act(sq[:, m, :], x[:, m, :], AF.Square, [bx], [bsq])
        return lambda: self.mm(ps[:, :], self.ones_b, sq[:, m, :], m == 0, m == 7, [bsq, self.b_cb], [bps], signal=(m == 7))

    def norm_tile(self, x, bx, ntok, g, bg, shift, bshift, out, bout, wk, it, psi=2, final=False, accum_done=False):
        i2 = it % 2
        sq, bsq = wk["sq"][i2], wk["b_sq"][i2]
        rstd, brs = wk["rstd"][i2], wk["b_rstd"][i2]
        ps, bps = self.ps[psi + i2], self.psb[psi + i2]
        if not accum_done:
            self.act(sq[:, :, 0:ntok], x, AF.Square, [bx], [bsq])
            for kc in range(8):
                self.mm(ps[:, 0:ntok], self.ones_b, sq[:, kc, 0:ntok], kc == 0, kc == 7, [bsq, self.b_cb], [bps], signal=(kc == 7))
        self.ts("dve", rstd[:, 0:ntok], ps[:, 0:ntok], 1.0 / D, ALU.mult, [bps], [brs], s2=EPS, op1=ALU.add)
        self.act(rstd[:, 0:ntok], rstd[:, 0:ntok], AF.Sqrt, [brs], [brs])
        self.recip(rstd[:, 0:ntok], rstd[:, 0:ntok], [brs], [brs])
        for kc in range(8):
            k3 = wk["n"] % 3
            wk["n"] += 1
            tmp, btmp = wk["tmp"][k3], wk["b_tmp"][k3]
            bxk = bx[kc] if isinstance(bx, list) else bx
            self.tt("dve" if (kc % 2 == 0 or not wk.get("use_pool", True)) else "pool", tmp[:, 0:ntok], x[:, kc, :], rstd[:, 0:ntok],
                    ALU.mult, [bxk, brs], [btmp])
            boutk = bout[kc] if isinstance(bout, list) else bout
            if final:
                self.act(out[:, kc, :], tmp[:, 0:ntok], AF.Identity, [btmp, bg], [boutk], scale=g[:, kc:kc + 1])
            else:
                self.act(out[:, kc, :], tmp[:, 0:ntok], AF.Identity, [btmp, bshift, bg], [boutk], bias=shift[:, kc:kc + 1],
                         scale=g[:, kc:kc + 1])

    def phase_inproj(self):
        A, S = self.A, self.S
        A.release(self.mark_after_h)
        wts = [A.alloc([8, 512], BF16) for _ in range(3)]
        b_wts = [Buf("inw%d" % i) for i in range(3)]
        pre = [A.alloc([SEQ + 2 + CTX + 2], F32) for _ in range(2)]
        b_pre = [Buf("pre0"), Buf("pre1")]
        for i in range(2):
            for c in (0, SEQ + 1, SEQ + 2, SEQ + 2 + CTX + 1):
                self.memset("pool", pre[i][:, c:c + 1], 0.0, [b_pre[i]])
        cacc = A.alloc([NTOK], F32)
        b_cacc = Buf("cacc")
        cv = [A.alloc([NTOK], BF16) for _ in range(2)]
        b_cv = [Buf("cv0"), Buf("cv1")]
        xst = [A.alloc([34, 128], BF16) for _ in range(2)]
        b_xst = [Buf("xst0"), Buf("xst1")]
        NK = 6
        kst = [A.alloc([512], BF16) for _ in range(NK)]
        b_kst = [Buf("kst%d" % i) for i in range(NK)]
        vst = [A.alloc([16, 65], BF16) for _ in range(2)]
        b_vst = [Buf("vst0"), Buf("vst1")]
        for i in range(2):
            self.memset("pool", vst[i][:, :, 64:65], 1.0, [b_vst[i]])
        zst = [A.alloc([1024], BF16) for _ in range(2)]
        b_zst = [Buf("zst0"), Buf("zst1")]
        st = {"w": 0, "ps": 0, "pre": 0, "cv": 0, "xst": 0, "kst": 0, "vst": 0, "zst": 0, "tp": 0}
        psT = [self.ps[4][:, :].bitcast(BF16), self.ps[5][:, :].bitcast(BF16)]
        b_scr = {k: Buf(k) for k in ("xtok", "BT", "CT", "Btok", "z", "qT", "kT", "vtok")}
        self.b_scr = b_scr

        def load_w(gname):
            i = st["w"] % 3
            st["w"] += 1
            src, bsrc = self.wsrc("in", (IN_GIDX[gname],), 8)
            self.dma("sp", wts[i], src, reads=[bsrc], writes=[b_wts[i]])
            return wts[i], b_wts[i]

        def fm_tile(wt, bw, j, tt):
            pi = st["ps"] % 4
            st["ps"] += 1
            ps, bps = self.ps[pi], self.psb[pi]
            if tt < 8:
                ntok = 512
                rhs = lambda kc: self.hT[:, kc, tt * 512:(tt + 1) * 512]
                rb = self.b_hT[tt]
            else:
                ntok = CTX
                rhs = lambda kc: self.hcT[:, kc, :]
                rb = self.b_hcT
            for kc in range(8):
                self.mm(ps[:, 0:ntok], wt[:, kc, j * 128:(j + 1) * 128], rhs(kc), kc == 0, kc == 7, [bw, rb], [bps],
                        signal=(kc == 7))
            return ps[:, 0:ntok], bps

        def conv_chunk(wt, bw, j, ch, has_ctx):
            i = st["pre"] % 2
            st["pre"] += 1
            p, bp = pre[i], b_pre[i]
            for tt in range(9 if has_ctx else 8):
                ps, bps = fm_tile(wt, bw, j, tt)
                if tt < 8:
                    dst = p[:, 1 + tt * 512:1 + (tt + 1) * 512]
                else:
                    dst = p[:, SEQ + 3:SEQ + 3 + CTX]
                self.cp("act", dst, ps, [bps], [bp])
            ci = st["cv"] % 2
            st["cv"] += 1
            c, bc = cv[ci], b_cv[ci]
            w0, w1, w2, cb = (self.colv("cw0", ch), self.colv("cw1", ch), self.colv("cw2", ch), self.colv("cb", ch))
            segs = [(0, 0, SEQ)] + ([(SEQ + 2, SEQ, CTX)] if has_ctx else [])
            for (po, co, n) in segs:
                self.act(cacc[:, co:co + n], p[:, po:po + n], AF.Identity, [bp, self.b_cols], [b_cacc], bias=cb, scale=w0)
                self.stt("dve", cacc[:, co:co + n], p[:, po + 1:po + 1 + n], w1, cacc[:, co:co + n], ALU.mult, ALU.add,
                         [bp, b_cacc, self.b_cols], [b_cacc])
                self.stt("dve", cacc[:, co:co + n], p[:, po + 2:po + 2 + n], w2, cacc[:, co:co + n], ALU.mult, ALU.add,
                         [bp, b_cacc, self.b_cols], [b_cacc])
                self.act(c[:, co:co + n], cacc[:, co:co + n], AF.Silu, [b_cacc], [bc])
            self.cast_some(2, bc)
            return c, bc

        def to_tokmajor(c, bc, nblk, dst_ap, bdst):
            xi = st["xst"] % 2
            st["xst"] += 1
            xs_, bxs = xst[xi], b_xst[xi]
            blk = 0
            while blk < nblk:
                nb = min(8, nblk - blk)
                ti = st["tp"] % 2
                st["tp"] += 1
                pt, bpt = psT[ti], self.psb[4 + ti]
                for q in range(nb):
                    self.tr(pt[:, q * 128:(q + 1) * 128], c[:, (blk + q) * 128:(blk + q + 1) * 128], self.ident_b,
                            [bc, self.b_cb], [bpt], signal=(q == nb - 1))
                self.cp("dve", xs_[:, blk:blk + nb, :], pt[:, 0:nb * 128].rearrange("p (a b) -> p a b", a=nb), [bpt], [bxs])
                blk += nb
            for b0 in range(0, nblk, 9):
                b1 = min(nblk, b0 + 9)
                self.dma("sp", dst_ap[:, b0:b1, :], xs_[:, b0:b1, :], reads=[bxs], writes=[bdst])

        xtok_v = self.s_xtok.rearrange("(b p) c -> p b c", p=128)
        btok_v = self.s_Btok.rearrange("(b p) c -> p b c", p=128)
        deferred = []

        def defer(fn):
            if deferred:
                deferred.pop(0)()
            if fn is not None:
                deferred.append(fn)

        for gi, gname in enumerate(("x0", "x1")):
            wt, bw = load_w(gname)
            for j in range(4):
                ch = gi * 4 + j
                c, bc = conv_chunk(wt, bw, j, ch, True)
                defer(lambda c=c, bc=bc, ch=ch: to_tokmajor(c, bc, 34, xtok_v[:, :, ch * 128:(ch + 1) * 128], b_scr["xtok"]))
        wt, bw = load_w("B")
        for j in range(2):
            c, bc = conv_chunk(wt, bw, j, 8 + j, True)
            self.dma("sp", self.s_BT[j * 128:(j + 1) * 128, :], c, reads=[bc], writes=[b_scr["BT"]])
            defer(lambda c=c, bc=bc, j=j: to_tokmajor(c, bc, 34, btok_v[:, :, j * 128:(j + 1) * 128], b_scr["Btok"]))
        wt, bw = load_w("C")
        for j in range(2):
            c, bc = conv_chunk(wt, bw, j, 10 + j, False)
            self.dma("sp", self.s_CT[j * 128:(j + 1) * 128, :], c[:, 0:SEQ], reads=[bc], writes=[b_scr["CT"]])
            defer(None)
        wt, bw = load_w("dt")
        for blk in range(34):
            ps, bps = self.ps[6], self.psb[6]
            o = (blk % 8) * 32
            for kc in range(8):
                lhs = self.hT[:, kc, blk * 128:(blk + 1) * 128] if blk < 32 else self.hcT[:, kc, (blk - 32) * 128:(blk - 31) * 128]
                rb = self.b_hT[blk // 4] if blk < 32 else self.b_hcT
                self.mm(ps[:, o:o + 32], lhs, wt[:, kc, 0:32], kc == 0, kc == 7, [bw, rb], [bps], signal=(kc == 7))
            self.cp("dve", self.dt_raw[:, blk, :], ps[:, o:o + 32], [bps], [self.b_dtraw])
        for gname, dst, bd, has_ctx, scale in (("k0", self.s_kT2, b_scr["kT"], True, None), ("k1", self.s_kT2, b_scr["kT"], True, None),
                                               ("q0", self.s_qT2, b_scr["qT"], False, 0.125), ("q1", self.s_qT2, b_scr["qT"], False, 0.125)):
            wt, bw = load_w(gname)
            for j in range(4):
                hp = (int(gname[1]) * 4 + j) * 2
                for tt in range(9 if has_ctx else 8):
                    ps, bps = fm_tile(wt, bw, j, tt)
                    ntok = 512 if tt < 8 else CTX
                    ki = st["kst"] % NK
                    st["kst"] += 1
                    ks, bks = kst[ki], b_kst[ki]
                    if scale is None:
                        self.cp("act", ks[:, 0:ntok], ps, [bps], [bks])
                    else:
                        self.act(ks[:, 0:ntok], ps, AF.Copy, [bps], [bks], scale=scale)
                    t0 = tt * 512
                    self.dma("sp", dst[:, hp // 2, t0:t0 + ntok], ks[:, 0:ntok], reads=[bks], writes=[bd])
                self.cast_some(2, bks)
        for which in ("v", "z"):
            w0, bw0 = load_w(which + "0")
            w1, bw1 = load_w(which + "1")
            nblk = 34 if which == "v" else 32
            for blk in range(nblk):
                lhs = (lambda kc, blk=blk: self.hT[:, kc, blk * 128:(blk + 1) * 128]) if blk < 32 else \
                    (lambda kc, blk=blk: self.hcT[:, kc, (blk - 32) * 128:(blk - 31) * 128])
                rb = self.b_hT[blk // 4] if blk < 32 else self.b_hcT
                if which == "v":
                    si = st["vst"] % 2
                    st["vst"] += 1
                    sb, bsb = vst[si], b_vst[si]
                else:
                    si = st["zst"] % 2
                    st["zst"] += 1
                    sb, bsb = zst[si], b_zst[si]
                for half, (wt, bw) in enumerate(((w0, bw0), (w1, bw1))):
                    pi = st["ps"] % 4
                    st["ps"] += 1
                    ps, bps = self.ps[pi], self.psb[pi]
                    for kc in range(8):
                        self.mm(ps[:, :], lhs(kc), wt[:, kc, :], kc == 0, kc == 7, [bw, rb], [bps], signal=(kc == 7))
                    if which == "v":
                        self.cp("act" if half == 0 else "dve", sb[:, half * 8:(half + 1) * 8, 0:64],
                                ps[:, :].rearrange("p (a b) -> p a b", a=8), [bps], [bsb])
                    else:
                        self.act(sb[:, half * 512:(half + 1) * 512], ps[:, :], AF.Silu, [bps], [bsb])
                if which == "v":
                    self.dma("sp", self.s_vtok[blk * 128:(blk + 1) * 128, :, :], sb, reads=[bsb], writes=[b_scr["vtok"]])
                else:
                    self.dma("sp", self.s_z[blk * 128:(blk + 1) * 128, :], sb, reads=[bsb], writes=[b_scr["z"]])
        self.cast_some(1000, self.b_dtraw)
        if "inproj" in self.debug:
            for nm, ap in (("xtok", self.s_xtok), ("BT", self.s_BT), ("CT", self.s_CT), ("Btok", self.s_Btok), ("z", self.s_z),
                           ("qT", self.s_qT2), ("kT", self.s_kT2), ("vtok", self.s_vtok)):
                d = self.dbg_tensor(nm, ap.shape, BF16)
                self.dma("sp", d, ap, reads=[b_scr[nm]])
            d = self.dbg_tensor("dtraw", (128, 34, 32))
            self.dma("sp", d, self.dt_raw, reads=[self.b_dtraw])


    def phase_ssd(self):
        A, S = self.A, self.S
        A.release(self.mark_base)
        b_scr = self.b_scr
        self.b_ycat = Buf("ycat")
        rows = A.alloc([NROW], F32)
        b_rows = Buf("rows")
        self.dma("sp", rows, self.rows_d.partition_broadcast(128), writes=[b_rows])
        rv = lambda nm: rows[:, ROWS[nm][0]:ROWS[nm][0] + ROWS[nm][1]]
        normw_bc, dtb_bc, alog_bc, D_bc = rv("ssdnw"), rv("dtb"), rv("alog"), rv("dskip")
        dtv = A.alloc([34, 32], F32)
        av = A.alloc([34, 32], F32)
        aneg = A.alloc([32], F32)
        b_dt = Buf("dtv")
        self.tt("dve", dtv, self.dt_raw, dtb_bc[:, None, :].to_broadcast([128, 34, 32]), ALU.add, [self.b_dtraw, b_rows], [b_dt])
        self.act(dtv, dtv, AF.Exp, [b_dt], [b_dt])
        self.act(dtv, dtv, AF.Ln, [b_dt], [b_dt], bias=1.0, scale=1.0)
        self.act(aneg, alog_bc, AF.Exp, [b_rows], [b_dt])
        self.ts("dve", aneg, aneg, -1.0, ALU.mult, [b_dt], [b_dt])
        self.tt("dve", av, dtv, aneg[:, None, :].to_broadcast([128, 34, 32]), ALU.mult, [b_dt], [b_dt])
        negL = A.alloc([128], F32)
        negU = A.alloc([128], F32)
        ones_f = A.alloc([128], F32)
        b_c2 = Buf("c2")
        self.ts("dve", negL, self.L_f, -1.0, ALU.mult, [self.b_consts], [b_c2])
        self.ts("dve", negU, self.U_f, -1.0, ALU.mult, [self.b_consts], [b_c2])
        self.memset("dve", ones_f, 1.0, [b_c2])
        hstore = A.alloc([32, 1024], BF16)
        b_hst = [Buf("hst%d" % i) for i in range(32)]
        hf = [A.alloc([1024], F32) for _ in range(2)]
        b_hf = [Buf("hf0"), Buf("hf1")]
        hb = A.alloc([1024], BF16)
        b_hb = Buf("hb")
        for d in range(2):
            self.memset("pool", hf[d], 0.0, [b_hf[d]])
        NB = 2
        xt = [A.alloc([16, 64], BF16) for _ in range(NB)]
        Bt = [A.alloc([256], BF16) for _ in range(NB)]
        BTc = [A.alloc([2, 128], BF16) for _ in range(NB)]
        CTc = [A.alloc([2, 128], BF16) for _ in range(NB)]
        zc = [A.alloc([1024], BF16) for _ in range(NB)]
        b_ld = [{k: Buf("%s%d" % (k, i)) for k in ("xt", "Bt", "BTc", "CTc", "zc")} for i in range(NB)]
        ex = [A.alloc([48], F32) for _ in range(2)]
        ws = [A.alloc([16], F32) for _ in range(2)]
        b_ex = [Buf("ex0"), Buf("ex1")]
        xdt = [A.alloc([16, 64], BF16) for _ in range(2)]
        b_xdt = [Buf("xdt0"), Buf("xdt1")]
        xs = [A.alloc([16, 64], BF16) for _ in range(2)]
        b_xs = [Buf("xs0"), Buf("xs1")]
        xD = A.alloc([16, 64], BF16)
        b_xD = Buf("xD")
        G = [A.alloc([16, 128], F32) for _ in range(2)]
        b_G = [Buf("G0"), Buf("G1")]
        Ebc = [A.alloc([4, 128], F32) for _ in range(3)]
        b_Ebc = [Buf("Ebc%d" % i) for i in range(3)]
        E = [A.alloc([4, 128], F32) for _ in range(3)]
        b_E = [Buf("E%d" % i) for i in range(3)]
        MT = [A.alloc([4, 128], BF16) for _ in range(3)]
        b_MT = [Buf("MT%d" % i) for i in range(3)]
        Cp = [A.alloc([4, 128], BF16) for _ in range(3)]
        b_Cp = [Buf("Cp%d" % i) for i in range(3)]
        cbm = [A.alloc([2, 128], F32) for _ in range(2)]
        b_cbm = Buf("cbm")
        yg = A.alloc([1024], F32)
        b_yg = Buf("yg")
        ysq = A.alloc([1024], F32)
        ss = A.alloc([2], F32)
        b_ss = Buf("ss")
        yn = A.alloc([1024], BF16)
        b_yn = Buf("yn")
        ycst = [A.alloc([8, 512], BF16) for _ in range(2)]
        b_ycst = [Buf("ycst0"), Buf("ycst1")]
        stmp = A.alloc([1024], F32)
        b_stmp = Buf("stmp")
        cnt = {"ld": 0, "ex": 0, "eb": 0, "mt": 0, "seg": 0}
        tri = [self.L_f, self.U_f]
        stri = [self.SU_f, self.SL_f]
        negtri = [negL, negU]
        BTv = self.s_BT.rearrange("(g n) t -> n g t", g=2)
        CTv = self.s_CT.rearrange("(g n) t -> n g t", g=2)
        ps_small, b_small = self.ps[0], self.psb[0]
        ps_cb, b_cb_ps = self.ps[1], self.psb[1]
        ps_S, b_S = self.ps[6], self.psb[6]
        psT, b_psT = self.ps[7][:, :].bitcast(BF16), self.psb[7]

        def load(blk, full):
            i = cnt["ld"] % NB
            cnt["ld"] += 1
            L = b_ld[i]
            self.dma("sp", xt[i], self.s_xtok[blk * 128:(blk + 1) * 128, :].rearrange("p (h d) -> p h d", h=16),
                     reads=[b_scr["xtok"]], writes=[L["xt"]])
            self.dma("sp", Bt[i], self.s_Btok[blk * 128:(blk + 1) * 128, :], reads=[b_scr["Btok"]], writes=[L["Bt"]])
            if full:
                self.dma("sp", BTc[i], BTv[:, :, blk * 128:(blk + 1) * 128], reads=[b_scr["BT"]], writes=[L["BTc"]])
                self.dma("sp", CTc[i], CTv[:, :, blk * 128:(blk + 1) * 128], reads=[b_scr["CT"]], writes=[L["CTc"]])
                self.dma("sp", zc[i], self.s_z[blk * 128:(blk + 1) * 128, :], reads=[b_scr["z"]], writes=[L["zc"]])
            return i

        def small(blk, d):
            a = av[:, blk, d * 16:(d + 1) * 16]
            o = d * 64
            self.mm(ps_small[:, o:o + 16], tri[d], a, True, True, [self.b_consts, b_dt], [b_small], signal=False)
            self.mm(ps_small[:, o + 16:o + 32], stri[d], a, True, True, [self.b_consts, b_dt], [b_small], signal=False)
            self.mm(ps_small[:, o + 32:o + 48], ones_f, a, True, True, [b_c2, b_dt], [b_small], signal=True)
            self.act(ex[d], ps_small[:, o:o + 48], AF.Exp, [b_small], [b_ex[d]])
            self.tt("dve", ws[d], ex[d][:, 16:32], dtv[:, blk, d * 16:(d + 1) * 16], ALU.mult, [b_ex[d], b_dt], [b_ex[d]])

        def xs_prep(d, li):
            self.tt("pool", xs[d], xt[li], ws[d][:, :, None].to_broadcast([128, 16, 64]), ALU.mult,
                    [b_ld[li]["xt"], b_ex[d]], [b_xs[d]])

        def state_step(d, li, store_blk=None, prep=True):
            if prep:
                xs_prep(d, li)
            if store_blk is not None:
                self.cp("act", hstore[:, store_blk, :], hf[d], [b_hf[d]], [b_hst[store_blk]])
            for g in range(2):
                self.mm(ps_S[:, :], Bt[li][:, g * 128:(g + 1) * 128], xs[d][:, g * 8:(g + 1) * 8, :], True, True,
                        [b_ld[li]["Bt"], b_xs[d]], [b_S], signal=True)
                hv = hf[d][:, g * 512:(g + 1) * 512].rearrange("p (h q) -> p h q", h=8)
                tv = stmp[:, g * 512:(g + 1) * 512].rearrange("p (h q) -> p h q", h=8)
                self.tt("pool", tv, hv, ex[d][:, 32 + g * 8:32 + (g + 1) * 8][:, :, None].to_broadcast([128, 8, 64]), ALU.mult,
                        [b_hf[d], b_ex[d]], [b_stmp])
                self.tt("dve", hf[d][:, g * 512:(g + 1) * 512], stmp[:, g * 512:(g + 1) * 512], ps_S[:, :], ALU.add,
                        [b_stmp, b_S], [b_hf[d]])

        for blk in [33, 32] + list(range(31, -1, -1)):
            li = load(blk, False)
            small(blk, 1)
            state_step(1, li, store_blk=(blk if blk < 32 else None))
        for blk in (32, 33):
            li = load(blk, False)
            small(blk, 0)
            state_step(0, li)
        self.cp("act", hb, hf[0], [b_hf[0]], [b_hb])
        pending_fin = []
        for c in range(32):
            li = load(c, True)
            L = b_ld[li]
            for g in range(2):
                self.mm(ps_cb[:, g * 128:(g + 1) * 128], BTc[li][:, g, :], CTc[li][:, g, :], True, True, [L["BTc"], L["CTc"]],
                        [b_cb_ps], signal=(g == 1))
            pcb = ps_cb[:, 0:256].rearrange("p (g i) -> p g i", g=2)
            self.tt("dve", cbm[0], pcb, self.L_f[:, None, :].to_broadcast([128, 2, 128]), ALU.mult, [b_cb_ps, self.b_consts], [b_cbm])
            self.tt("dve", cbm[1], pcb, self.U_f[:, None, :].to_broadcast([128, 2, 128]), ALU.mult, [b_cb_ps, self.b_consts], [b_cbm])
            self.tt("pool", xD, xt[li], D_bc[:, :, None].to_broadcast([128, 16, 64]), ALU.mult, [L["xt"], b_rows], [b_xD])
            for g in range(2):
                self.mm(self.ps[4 + g][:, :], self.ident_b, xD[:, g * 8:(g + 1) * 8, :], True, False, [self.b_cb, b_xD],
                        [self.psb[4 + g]], signal=False)
            for d in range(2):
                small(c, d)
                if d == 0:
                    xs_prep(0, li)
                dtc = dtv[:, c, d * 16:(d + 1) * 16]
                a = av[:, c, d * 16:(d + 1) * 16]
                self.tt("pool", xdt[d], xt[li], dtc[:, :, None].to_broadcast([128, 16, 64]), ALU.mult, [L["xt"], b_dt], [b_xdt[d]])
                self.tt("pool" if "g_pool" in self.debug else "dve", G[d], tri[d][:, None, :].to_broadcast([128, 16, 128]),
                        a[:, :, None].to_broadcast([128, 16, 128]), ALU.mult, [self.b_consts, b_dt], [b_G[d]])

            def mm1(it):
                d, q = it // 4, it % 4
                k = it % 3
                pseg, b_seg = self.ps[2 + it % 2], self.psb[2 + it % 2]
                self.mm(pseg[:, :], ones_f, G[d][:, 4 * q:4 * q + 4, :], True, True, [b_c2, b_G[d]], [b_seg], signal=True)
                self.act(Ebc[k], pseg[:, :].rearrange("p (a b) -> p a b", a=4), AF.Exp, [b_seg], [b_Ebc[k]])

            def mm2(it):
                d, q = it // 4, it % 4
                k = it % 3
                g = q // 2
                a = av[:, c, d * 16:(d + 1) * 16]
                pseg, b_seg = self.ps[2 + it % 2], self.psb[2 + it % 2]
                self.mm(pseg[:, :].rearrange("p (a b) -> p a b", a=4), negtri[d],
                        a[:, 4 * q:4 * q + 4][:, :, None].to_broadcast([128, 4, 128]), False, True, [b_c2, b_dt, b_Ebc[k]], [b_seg],
                        signal=True, skip=True)
                self.act(E[k], pseg[:, :].rearrange("p (a b) -> p a b", a=4), AF.Exp, [b_seg], [b_E[k]])
                self.stt("dve", MT[k], E[k], 1.0, cbm[d][:, g, :][:, None, :].to_broadcast([128, 4, 128]), ALU.min, ALU.mult,
                         [b_E[k], b_cbm], [b_MT[k]])
                self.tt("pool", Cp[k], Ebc[k], CTc[li][:, g, :][:, None, :].to_broadcast([128, 4, 128]), ALU.mult,
                        [b_Ebc[k], L["CTc"]], [b_Cp[k]])

            def ymm(it):
                d, q = it // 4, it % 4
                k = it % 3
                g = q // 2
                for hh in range(4):
                    h = 4 * q + hh
                    hl = h % 8
                    yps = self.ps[4 + g][:, hl * 64:(hl + 1) * 64]
                    hsrc = hb[:, h * 64:(h + 1) * 64] if d == 0 else hstore[:, c, h * 64:(h + 1) * 64]
                    hbuf = b_hb if d == 0 else b_hst[c]
                    self.mm(yps, MT[k][:, hh, :], xdt[d][:, h, :], False, False, [b_MT[k], b_xdt[d]], [self.psb[4 + g]], signal=False)
                    self.mm(yps, Cp[k][:, hh, :], hsrc, False, (d == 1), [b_Cp[k], hbuf], [self.psb[4 + g]], signal=(hh == 3))

            if "ssd_seq" in self.debug:
                for it in range(8):
                    mm1(it)
                    mm2(it)
                    ymm(it)
            else:
                for s_ in range(10):
                    if s_ < 8:
                        mm1(s_)
                    if 1 <= s_ <= 8:
                        mm2(s_ - 1)
                    if s_ >= 2:
                        ymm(s_ - 2)
                    if s_ == 3 and pending_fin:
                        pending_fin.pop(0)()
            state_step(0, li, prep=False)
            self.cp("act", hb, hf[0], [b_hf[0]], [b_hb])
            for g in range(2):
                self.tt("dve", yg[:, g * 512:(g + 1) * 512], self.ps[4 + g][:, :], zc[li][:, g * 512:(g + 1) * 512], ALU.mult,
                        [self.psb[4 + g], L["zc"]], [b_yg])
            self.memset("dve", ss, 0.0, [b_ss])
            self.act(ysq, yg, AF.Square, [b_yg, b_ss], [b_ss], accum_out=ss[:, 0:1])
            self.ts("dve", ss[:, 1:2], ss[:, 0:1], 1.0 / 1024, ALU.mult, [b_ss], [b_ss], s2=EPS, op1=ALU.add)
            self.act(ss[:, 1:2], ss[:, 1:2], AF.Sqrt, [b_ss], [b_ss])
            self.recip(ss[:, 1:2], ss[:, 1:2], [b_ss], [b_ss])
            self.stt("dve", yn, yg, ss[:, 1:2], normw_bc, ALU.mult, ALU.mult, [b_yg, b_ss, b_rows], [b_yn])
            def fin(c=c):
                for j in range(8):
                    self.tr(psT[:, j * 128:(j + 1) * 128], yn[:, j * 128:(j + 1) * 128], self.ident_b, [b_yn, self.b_cb], [b_psT],
                            signal=(j == 7))
                yi = (c // 4) % 2
                self.cp("act", ycst[yi][:, :, (c % 4) * 128:(c % 4 + 1) * 128], psT[:, :].rearrange("p (a b) -> p a b", a=8),
                        [b_psT], [b_ycst[yi]])
                if c % 4 == 3:
                    t0 = (c // 4) * 512
                    self.dma("sp", self.s_ycatT[:, 0:8, t0:t0 + 512], ycst[yi], reads=[b_ycst[yi]], writes=[self.b_ycat])
            pending_fin.append(fin)
        while pending_fin:
            pending_fin.pop(0)()
        if "ssd" in self.debug:
            d_ = self.dbg_tensor("ycatT", (128, 16, SEQ), BF16)
            self.dma("sp", d_, self.s_ycatT, reads=[self.b_ycat])
            d_ = self.dbg_tensor("hstore", (128, 32, 1024), BF16)
            self.dma("sp", d_, hstore, reads=b_hst)
            d_ = self.dbg_tensor("dtv", (128, 34, 32))
            self.dma("sp", d_, dtv, reads=[b_dt])


    def phase_na(self):
        A, S = self.A, self.S
        A.release(self.mark_base)
        b_scr = self.b_scr
        Kc = A.alloc([16, CTX], BF16)
        Vc = A.alloc([2, 16, 65], BF16)
        b_ctx = Buf("nactx")
        self.dma("sp", Kc[0:64].rearrange("p (c par) t -> p c par t", par=2), self.s_kT[:, :, :, SEQ:NTOK], reads=[b_scr["kT"]], writes=[b_ctx])
        self.dma("sp", Vc, self.s_vtok[SEQ:NTOK].rearrange("(b p) h e -> p b h e", p=128), reads=[b_scr["vtok"]], writes=[b_ctx])
        bias = A.alloc([16, 5, 128], F32)
        b_bias = Buf("nabias")
        Kwin = [A.alloc([16, 576], BF16) for _ in range(2)]
        Qp = [A.alloc([16, 128], BF16) for _ in range(2)]
        Vwin = [A.alloc([5, 16, 65], BF16) for _ in range(2)]
        b_win = [{k: Buf(k + str(i)) for k in ("K", "Q", "V")} for i in range(2)]
        Sw = [A.alloc([5, 128], F32) for _ in range(2)]
        b_Sw = [Buf("Sw0"), Buf("Sw1")]
        for i in range(2):
            self.memset("pool", Sw[i][64:128, 4, :], NEG, [b_Sw[i]])
        PT = [A.alloc([7, 128], BF16) for _ in range(3)]
        b_PT = [Buf("PT%d" % i) for i in range(3)]
        Otok = [A.alloc([16, 64], BF16) for _ in range(2)]
        b_Otok = [Buf("Ot0"), Buf("Ot1")]
        rinv = A.alloc([16], F32)
        b_rinv = Buf("rinv")
        ycst = [A.alloc([8, 512], BF16) for _ in range(2)]
        b_ycst = [Buf("nyc0"), Buf("nyc1")]
        psT, b_psT = self.ps[7][:, :].bitcast(BF16), self.psb[7]
        variants = {0: ((0, 0), (0, -1)), 1: ((0, -2), (0, -3)), 30: ((1, -5), (1, -6)), 31: ((1, -7), (1, -8))}
        interior = ((0, -4), (1, -5))
        cur_var = [None]

        def build_bias(var):
            self.memset("pool", bias, NEG, [b_bias])
            for rq in range(2):
                lo, off = var[rq]
                for ck in range(5):
                    js = [jr for jr in (2 * ck, 2 * ck + 1) if lo <= jr < lo + 8 and jr <= 8]
                    if not js:
                        continue
                    p0 = (js[0] % 2) * 64
                    dr0 = js[0] + off + 7
                    src = self.tblk[:, dr0:dr0 + len(js), :, :].rearrange("h j k c -> (j k) h c")
                    self.dma("sp", bias[p0:p0 + 64 * len(js), :, ck, rq * 64:(rq + 1) * 64], src, writes=[b_bias])

        hgroups = [(4, 0, 7), (5, 7, 14), (6, 14, 16)]
        hg_of = {}
        for (ob_, h0_, h1_) in hgroups:
            for h_ in range(h0_, h1_):
                hg_of[h_] = (ob_, h0_, h1_)
        cnt = {"sw": 0, "pt": 0, "st": 0}
        state = {}

        def emit_loads(rp):
            r = 2 * rp
            kb = min(max(r - 4, 0), 55)
            w0 = kb * 64
            wi = rp % 2
            W = b_win[wi]
            self.dma("sp", Kwin[wi][0:64].rearrange("p (c par) t -> p c par t", par=2), self.s_kT[:, :, :, w0:w0 + 576],
                     reads=[b_scr["kT"]], writes=[W["K"]])
            self.dma("sp", Qp[wi][0:64].rearrange("p (c par) t -> p c par t", par=2), self.s_qT[:, :, :, r * 64:r * 64 + 128],
                     reads=[b_scr["qT"]], writes=[W["Q"]])
            self.dma("sp", Vwin[wi][:, 0:4], self.s_vtok[w0:w0 + 512].rearrange("(b p) h e -> p b h e", p=128),
                     reads=[b_scr["vtok"]], writes=[W["V"]])
            self.dma("sp", Vwin[wi][0:64, 4], self.s_vtok[w0 + 512:w0 + 576], reads=[b_scr["vtok"]], writes=[W["V"]])

        def emit_st(rp, h):
            var = variants.get(rp, interior)
            if var != cur_var[0]:
                build_bias(var)
                cur_var[0] = var
            wi = rp % 2
            W = b_win[wi]
            si = cnt["st"] % 2
            cnt["st"] += 1
            psA, bA = self.ps[2 * si], self.psb[2 * si]
            psB, bB = self.ps[2 * si + 1], self.psb[2 * si + 1]
            q = Qp[wi][0:64, h, :]
            for ck in range(4):
                self.mm(psA[:, ck * 128:(ck + 1) * 128], Kwin[wi][0:64, h, ck * 128:(ck + 1) * 128], q, True, True,
                        [W["K"], W["Q"]], [bA], signal=(ck == 3))
            self.mm(psB[0:64, 0:128], Kwin[wi][0:64, h, 512:576], q, True, True, [W["K"], W["Q"]], [bB], signal=False)
            self.mm(psB[:, 128:256], Kc[0:64, h, 0:128], q, True, True, [b_ctx, W["Q"]], [bB], signal=False)
            self.mm(psB[:, 256:384], Kc[0:64, h, 128:256], q, True, True, [b_ctx, W["Q"]], [bB], signal=True)
            wi2 = cnt["sw"] % 2
            cnt["sw"] += 1
            sw, bsw = Sw[wi2], b_Sw[wi2]
            self.tt("dve", sw[:, 0:4, :], psA[:, :].rearrange("p (a b) -> p a b", a=4), bias[:, h, 0:4, :], ALU.add,
                    [bA, b_bias], [bsw])
            self.tt("dve", sw[0:64, 4, :], psB[0:64, 0:128], bias[0:64, h, 4, :], ALU.add, [bB, b_bias], [bsw])
            pi = cnt["pt"] % 3
            cnt["pt"] += 1
            pt, bpt = PT[pi], b_PT[pi]
            self.act(pt[:, 0:5, :], sw, AF.Exp, [bsw], [bpt])
            self.act(pt[:, 5:7, :], psB[:, 128:384].rearrange("p (a b) -> p a b", a=2), AF.Exp, [bB], [bpt])
            state[(rp, h)] = (pt, bpt)

        def emit_pv(rp, h):
            wi = rp % 2
            W = b_win[wi]
            pt, bpt = state.pop((rp, h))
            obank, h0, h1 = hg_of[h]
            ops, bops = self.ps[obank], self.psb[obank]
            ot, bot = Otok[rp % 2], b_Otok[rp % 2]
            o_ap = ops[:, (h - h0) * 65:(h - h0 + 1) * 65]
            for ck in range(7):
                M = 64 if ck == 4 else 128
                v = Vwin[wi][0:M, ck, h, :] if ck < 5 else Vc[:, ck - 5, h, :]
                vb = W["V"] if ck < 5 else b_ctx
                self.mm(o_ap, pt[0:M, ck, :], v, ck == 0, ck == 6, [bpt, vb], [bops], signal=(ck == 6))
            if h == h1 - 1:
                nh = h1 - h0
                o3 = ops[:, 0:nh * 65].rearrange("p (h e) -> p h e", h=nh)
                self.recip(rinv[:, h0:h1], o3[:, :, 64], [bops], [b_rinv])
                self.tt("dve", ot[:, h0:h1, :], o3[:, :, 0:64], rinv[:, h0:h1][:, :, None].to_broadcast([128, nh, 64]), ALU.mult,
                        [bops, b_rinv], [bot])
            if h == 15:
                otf = ot.rearrange("p h d -> p (h d)")
                for j in range(8):
                    self.tr(psT[:, j * 128:(j + 1) * 128], otf[:, j * 128:(j + 1) * 128], self.ident_b, [bot, self.b_cb], [b_psT],
                            signal=(j == 7))
                yi = (rp // 4) % 2
                self.cp("act", ycst[yi][:, :, (rp % 4) * 128:(rp % 4 + 1) * 128], psT[:, :].rearrange("p (a b) -> p a b", a=8),
                        [b_psT], [b_ycst[yi]])
                if rp % 4 == 3:
                    t0 = (rp // 4) * 512
                    self.dma("sp", self.s_ycatT[:, 8:16, t0:t0 + 512], ycst[yi], reads=[b_ycst[yi]], writes=[self.b_ycat])

        seq = [(rp, h) for rp in range(32) for h in range(16)]
        emit_loads(0)
        emit_st(*seq[0])
        for i, (rp, h) in enumerate(seq):
            if h == 0 and rp + 1 < 32:
                emit_loads(rp + 1)
            if i + 1 < len(seq):
                emit_st(*seq[i + 1])
            emit_pv(rp, h)
        if "na" in self.debug:
            d_ = self.dbg_tensor("ycatT2", (128, 16, SEQ), BF16)
            self.dma("sp", d_, self.s_ycatT, reads=[self.b_ycat])


    def phase_tail(self):
        A, S = self.A, self.S
        A.release(self.mark_base)
        NW = 4
        wb = [A.alloc([4096], BF16) for _ in range(NW)]
        b_wb = [Buf("tw%d" % i) for i in range(NW)]
        R1 = A.alloc([32, 512], BF16)
        b_R1 = Buf("R1")
        hid = R1
        ycat = R1[:, 0:16, :]
        ob = R1.rearrange("p a b -> p (a b)")[:, 0:8192].bitcast(F32).rearrange("p (a b) -> p a b", a=8)
        h = A.alloc([8, 512], BF16)
        b_h = [Buf("h%d" % i) for i in range(8)]
        sets = []
        for i in range(2):
            sets.append({"x": A.alloc([8, 512], F32), "bx": Buf("sx%d" % i), "bxm": [Buf("sx%d_%d" % (i, m)) for m in range(8)], "u": A.alloc([8, 514], F32), "bu": Buf("su%d" % i),
                         "gb": A.alloc([8, 512], BF16), "bgb": Buf("sg%d" % i)})
        gct = [A.alloc([512], F32) for _ in range(2)]
        b_gct = [Buf("gct0"), Buf("gct1")]
        rl = [A.alloc([512], F32) for _ in range(2)]
        b_rl = [Buf("rl0"), Buf("rl1")]
        cva = [A.alloc([512], F32) for _ in range(2)]
        b_cva = [Buf("cva0"), Buf("cva1")]
        gg = A.alloc([8, 512], BF16)
        b_gg = Buf("gg")
        wk = {"sq": [A.alloc([8, 512], BF16)] * 2, "b_sq": [Buf("sq")] * 2,
              "rstd": [A.alloc([512], F32) for _ in range(2)], "b_rstd": [Buf("rs0"), Buf("rs1")],
              "tmp": [A.alloc([512], F32) for _ in range(3)], "b_tmp": [Buf("nt%d" % i) for i in range(3)], "n": 0,
              "b_sqc": [Buf("sqc%d" % m) for m in range(8)]}
        cnt = {"w": 0, "ps": 0, "gct": 0, "rl": 0, "cva": 0, "norm": 0}

        pref = {}

        def prefetch(key, idx, kc):
            pref[(key, idx)] = wload(key, idx, kc)

        def wload(key, idx, kc):
            if (key, idx) in pref:
                return pref.pop((key, idx))
            i = cnt["w"] % NW
            cnt["w"] += 1
            v = wb[i].rearrange("p (k n) -> p k n", k=kc)
            src, bsrc = self.wsrc(key, idx, kc)
            self.dma("sp", v, src, reads=[bsrc], writes=[b_wb[i]])
            return v, b_wb[i]

        def nextps():
            pi = cnt["ps"] % 4
            cnt["ps"] += 1
            return self.ps[pi], self.psb[pi]

        def resid_evac(ps, bps, xs, m, gate_col, bgate):
            bxm = xs["bxm"][m]
            flush()
            self.stt("dve", xs["x"][:, m, :], ps[:, :], gate_col, xs["x"][:, m, :], ALU.mult, ALU.add, [bps, bgate, bxm], [bxm])
            pend.append(self.norm_accum(xs["x"], bxm, m, wk, cnt["norm"], 4))

        pend = []

        def flush():
            while pend:
                pend.pop(0)()

        def norm(xs, g, bg, shift, bshift, out, bout, final=False):
            flush()
            self.norm_tile(xs["x"], xs["bxm"], 512, g, bg, shift, bshift, out, bout, wk, cnt["norm"], psi=4, final=final,
                           accum_done=True)
            cnt["norm"] += 1

        def mlp(l, xs):
            gate = self.modv[l][:, 40:48]
            bgate = self.b_modv[l][5]
            for grp in range(8):
                wt, bw = wload("w1", (l, grp), 8)
                if grp == 0:
                    pss = [nextps() for _ in range(4)]
                    for kc in range(8):
                        for jj in range(4):
                            self.mm(pss[jj][0][:, :], wt[:, kc, jj * 128:(jj + 1) * 128], h[:, kc, :], kc == 0, kc == 7, [bw, b_h[kc]],
                                    [pss[jj][1]], signal=(kc == 7))
                for jj in range(4):
                    mh = grp * 4 + jj
                    if grp == 0:
                        ps, bps = pss[jj]
                    else:
                        ps, bps = nextps()
                        for kc in range(8):
                            self.mm(ps[:, :], wt[:, kc, jj * 128:(jj + 1) * 128], h[:, kc, :], kc == 0, kc == 7, [bw, b_h[kc]], [bps], signal=(kc == 7))
                    ri = cnt["rl"] % 2
                    cnt["rl"] += 1
                    self.act(rl[ri], ps[:, :], AF.Relu, [bps], [b_rl[ri]])
                    self.tt("pool", hid[:, mh, :], rl[ri], rl[ri], ALU.mult, [b_rl[ri]], [b_R1])
            for m in range(8):
                wt, bw = wload("w2", (l, m), 32)
                ps, bps = nextps()
                for kc in range(32):
                    self.mm(ps[:, :], wt[:, kc, :], hid[:, kc, :], kc == 0, kc == 31, [bw, b_R1], [bps], signal=(kc == 31))
                resid_evac(ps, bps, xs, m, gate[:, m:m + 1], bgate)

        def conv_b(xs, m):
            ci = cnt["cva"] % 2
            cnt["cva"] += 1
            cv_, bcv = cva[ci], b_cva[ci]
            self.act(cv_, xs["u"][:, m, 0:512], AF.Identity, [xs["bu"], self.b_cols], [bcv], scale=self.colv("scw0", m))
            self.stt("dve", cv_, xs["u"][:, m, 1:513], self.colv("scw1", m), cv_, ALU.mult, ALU.add, [xs["bu"], self.b_cols, bcv], [bcv])
            self.stt("dve", cv_, xs["u"][:, m, 2:514], self.colv("scw2", m), cv_, ALU.mult, ALU.add, [xs["bu"], self.b_cols, bcv], [bcv])
            self.tt("pool", gg[:, m, :], cv_, xs["gb"][:, m, :], ALU.mult, [bcv, xs["bgb"]], [b_gg])

        def load_ycat(t):
            self.dma("sp", ycat, self.s_ycatT[:, :, t * 512:(t + 1) * 512], reads=[self.b_ycat], writes=[b_R1])

        def stage_a(t, xs, prev):
            t0 = t * 512
            if t <= 1:
                load_ycat(t)
            self.dma("sp", xs["x"], self.xT[:, :, t0:t0 + 512], writes=xs["bxm"])
            for grp in range(4):
                wt, bw = wload("out0", (grp,), 16)
                for jj in range(2):
                    m = grp * 2 + jj
                    ps, bps = nextps()
                    for kc in range(16):
                        self.mm(ps[:, :], wt[:, kc, jj * 128:(jj + 1) * 128], ycat[:, kc, :], kc == 0, kc == 15, [bw, b_R1], [bps],
                                signal=(kc == 15))
                    resid_evac(ps, bps, xs, m, self.modv[0][:, 16 + m:17 + m], self.b_modv[0][2])
            norm(xs, self.g_f0, self.b_gf0, self.modv[0][:, 24:32], self.b_modv[0][3], h, b_h)
            mlp(0, xs)
            norm(xs, self.g_a1, self.b_ga1, self.modv[1][:, 0:8], self.b_modv[1][0], h, b_h)
            for grp in range(2):
                wt, bw = wload("scin", (grp,), 8)
                for jj in range(4):
                    m = grp * 4 + jj
                    ps, bps = nextps()
                    for kc in range(8):
                        self.mm(ps[:, :], wt[:, kc, jj * 128:(jj + 1) * 128], h[:, kc, :], kc == 0, kc == 7, [bw, b_h[kc]], [bps], signal=(kc == 7))
                    self.cp("act", xs["gb"][:, m, :], ps[:, :], [bps], [xs["bgb"]])
            for half in range(2):
                wc, bwc = wload("scin", (2 + half,), 8)
                wv, bwv = wload("scin", (4 + half,), 8)
                for jj in range(4):
                    m = half * 4 + jj
                    ps, bps = nextps()
                    for kc in range(8):
                        self.mm(ps[:, :], wc[:, kc, jj * 128:(jj + 1) * 128], h[:, kc, :], kc == 0, kc == 7, [bwc, b_h[kc]], [bps], signal=(kc == 7))
                    gi = cnt["gct"] % 2
                    cnt["gct"] += 1
                    self.cp("act", gct[gi], ps[:, :], [bps], [b_gct[gi]])
                    ps2, bps2 = nextps()
                    for kc in range(8):
                        self.mm(ps2[:, :], wv[:, kc, jj * 128:(jj + 1) * 128], h[:, kc, :], kc == 0, kc == 7, [bwv, b_h[kc]], [bps2], signal=(kc == 7))
                    self.tt("dve", xs["u"][:, m, 1:513], ps2[:, :], gct[gi], ALU.mult, [bps2, b_gct[gi]], [xs["bu"]])
                    if t == 0:
                        self.memset("pool", xs["u"][:, m, 0:1], 0.0, [xs["bu"]])
                    else:
                        self.cp("pool", xs["u"][:, m, 0:1], prev["u"][:, m, 512:513], [prev["bu"]], [xs["bu"]])
                        self.cp("pool", prev["u"][:, m, 513:514], xs["u"][:, m, 1:2], [xs["bu"]], [prev["bu"]])
                        conv_b(prev, m)
                    if t == 7:
                        self.memset("pool", xs["u"][:, m, 513:514], 0.0, [xs["bu"]])

        def stage_b(t, xs, do_conv):
            t0 = t * 512
            if do_conv:
                for m in range(8):
                    conv_b(xs, m)
            for grp in range(2):
                wt, bw = wload("scout", (grp,), 8)
                for jj in range(4):
                    m = grp * 4 + jj
                    ps, bps = nextps()
                    for kc in range(8):
                        self.mm(ps[:, :], wt[:, kc, jj * 128:(jj + 1) * 128], gg[:, kc, :], kc == 0, kc == 7, [bw, b_gg], [bps], signal=(kc == 7))
                    resid_evac(ps, bps, xs, m, self.modv[1][:, 16 + m:17 + m], self.b_modv[1][2])
            norm(xs, self.g_f1, self.b_gf1, self.modv[1][:, 24:32], self.b_modv[1][3], h, b_h)
            mlp(1, xs)
            if t + 2 < 8:
                load_ycat(t + 2)
                for grp in range(3):
                    prefetch("out0", (grp,), 16)
            norm(xs, self.colv("fnw"), self.b_cols, None, None, xs["x"], xs["bxm"], final=True)
            self.dma("sp", self.outT[:, :, t0:t0 + 512], xs["x"], reads=xs["bxm"])

        for t in range(9):
            cur = sets[t % 2]
            prev = sets[(t - 1) % 2]
            if t < 8:
                stage_a(t, cur, prev)
            if t >= 1:
                stage_b(t - 1, prev, do_conv=(t == 8))


def build_program(stop_after=None, debug=()):
    p = Prog(stop_after, debug)
    p.build()
    return p


_CACHE = {}


def kernel(**inputs):
    shared, per_core = prep_inputs(inputs)
    p = build_program()
    in_maps = []
    for b in range(8):
        m = dict(shared)
        m.update(per_core[b])
        in_maps.append(m)
    res = run_bass_kernel_spmd(p.nc, in_maps, core_ids=list(range(8)))
    out = np.empty((8, SEQ, D), np.float32)
    for b in range(8):
        oT = np.asarray(res.results[b]["outT"])
        out[b] = oT.transpose(2, 1, 0).reshape(SEQ, D)
    return out
```

```python
from contextlib import ExitStack
import numpy as np
import concourse.bass as bass
import concourse.mybir as mybir
from concourse.bass_utils import run_bass_kernel_spmd

F32 = mybir.dt.float32
BF16 = mybir.dt.bfloat16
AF = mybir.ActivationFunctionType
ALU = mybir.AluOpType

ENGS = ("pe", "act", "dve", "pool", "sp")

D = 1024
SEQ = 4096
CTX = 256
NTOK = SEQ + CTX
OFF_X, OFF_B, OFF_DT, OFF_K, OFF_V, OFF_C, OFF_Z, OFF_Q = 0, 1024, 1280, 1312, 2336, 3360, 3616, 4640
NEG = -30000.0
EPS = 1e-6


class Buf:
    __slots__ = ("name", "lw", "rd")

    def __init__(self, name=""):
        self.name = name
        self.lw = None
        self.rd = []


class Sched:
    def __init__(self, nc, n_lanes=6, same_eng_sync=True):
        self.nc = nc
        self.q = {e: [] for e in ENGS}
        self.cnt = {e: 0 for e in ENGS}
        self.seen = {e: {} for e in ENGS}
        self.same_eng_sync = same_eng_sync
        self.lanes = {}
        self.lane_rr = {}
        for qe in ("sp", "act", "pool"):
            self.lanes[qe] = [["dma_%s_%d" % (qe, i), 0] for i in range(n_lanes)]
            self.lane_rr[qe] = 0
        self.semkeys = list(ENGS) + [l[0] for qe in self.lanes for l in self.lanes[qe]]

    def _deps(self, reads, writes):
        deps = set()
        for b in reads:
            if b.lw is not None:
                deps.add((b.lw[0], b.lw[1], True))
        for b in writes:
            if b.lw is not None:
                deps.add((b.lw[0], b.lw[1], False))
            for r in b.rd:
                deps.add((r[0], r[1], False))
        return deps

    def _emit_waits(self, eng, deps):
        mx = {}
        for k, v, raw in deps:
            if k == eng:
                if not (raw and self.same_eng_sync and eng in ("act", "dve", "pool")):
                    continue
            if mx.get(k, 0) < v:
                mx[k] = v
        for k, v in mx.items():
            if self.seen[eng].get(k, 0) >= v:
                continue
            self.seen[eng][k] = v
            self.q[eng].append(("wait", k, v))

    def _attr(self, ev, reads, writes):
        for b in reads:
            if len(b.rd) > 64:
                mx = {}
                for k, v in b.rd:
                    if mx.get(k, 0) < v:
                        mx[k] = v
                b.rd = list(mx.items())
            b.rd.append(ev)
        for b in writes:
            b.lw = ev
            b.rd = []

    def op(self, eng, fn, reads=(), writes=(), signal=True):
        self._emit_waits(eng, self._deps(reads, writes))
        ev = (eng, self.cnt[eng] + 1)
        self._attr(ev, reads, writes)
        if signal:
            self.cnt[eng] += 1
            self.q[eng].append(("op", fn, eng))
        else:
            self.q[eng].append(("op", fn, None))
        return ev

    def dma(self, qe, out, in_, reads=(), writes=()):
        lanes = self.lanes[qe]
        i = self.lane_rr[qe]
        self.lane_rr[qe] = (i + 1) % len(lanes)
        lane = lanes[i]
        deps = self._deps(reads, writes)
        if lane[1] > 0:
            deps.add((lane[0], 16 * lane[1], True))
        self._emit_waits(qe, deps)
        lane[1] += 1
        ev = (lane[0], 16 * lane[1])
        self._attr(ev, reads, writes)
        self.q[qe].append(("dma", (out, in_), lane[0]))
        return ev

    def _all_events(self):
        evs = [(e, self.cnt[e]) for e in ENGS if self.cnt[e] > 0]
        for qe in self.lanes:
            for key, c in self.lanes[qe]:
                if c > 0:
                    evs.append((key, 16 * c))
        return evs

    def _force(self, eng, evs):
        for k, v in evs:
            if k == eng or self.seen[eng].get(k, 0) >= v:
                continue
            self.seen[eng][k] = v
            self.q[eng].append(("wait", k, v))

    def barrier(self):
        evs = self._all_events()
        for e in ENGS:
            self._force(e, evs)

    def final_wait(self, eng="sp"):
        self._force(eng, self._all_events())

    def replay(self, stack):
        nc = self.nc
        used = set()
        for e in ENGS:
            for it in self.q[e]:
                if it[0] == "wait":
                    used.add(it[1])
                elif it[2] is not None:
                    used.add(it[2])
        sems = {k: stack.enter_context(nc.semaphore("s_" + k)) for k in self.semkeys if k in used}
        block = stack.enter_context(nc.Block())

        def run(eng_name):
            def body(e):
                for it in self.q[eng_name]:
                    if it[0] == "wait":
                        e.wait_ge(sems[it[1]], it[2])
                    elif it[0] == "op":
                        ins = it[1](e)
                        if it[2] is not None:
                            ins.then_inc(sems[it[2]], 1)
                    else:
                        out, in_ = it[1]
                        e.dma_start(out=out, in_=in_).then_inc(sems[it[2]], 16)
            return body

        if self.q["sp"]:
            block.sync(run("sp"))
        if self.q["pe"]:
            block.tensor(run("pe"))
        if self.q["act"]:
            block.scalar(run("act"))
        if self.q["dve"]:
            block.vector(run("dve"))
        if self.q["pool"]:
            block.gpsimd(run("pool"))


class Arena:
    def __init__(self, ap, words):
        self.ap = ap
        self.words = words
        self.top = 0

    def mark(self):
        return self.top

    def release(self, m):
        self.top = m

    def alloc(self, shape, dtype, parts=128):
        n = 1
        for s in shape:
            n *= s
        nbytes = n * (4 if dtype == F32 else 2)
        w = (nbytes + 3) // 4
        w = (w + 7) // 8 * 8
        assert self.top + w <= self.words, "SBUF arena overflow: need %d have %d" % (self.top + w, self.words)
        v = self.ap[:, self.top:self.top + (nbytes + 3) // 4]
        self.top += w
        if dtype != F32:
            v = v.bitcast(dtype)
        if len(shape) == 2:
            v = v.rearrange("p (a b) -> p a b", a=shape[0])
        elif len(shape) == 3:
            v = v.rearrange("p (a b c) -> p a b c", a=shape[0], b=shape[1])
        return v


def _col(v):
    v = np.asarray(v, np.float32)
    return np.ascontiguousarray(v.reshape(-1, 128).T)


def _wtile(w, ncols):
    K, N = w.shape
    assert K % 128 == 0 and N % ncols == 0
    t = w.reshape(K // 128, 128, N // ncols, ncols).transpose(2, 1, 0, 3)
    return np.ascontiguousarray(t, dtype=np.float32)


IN_GROUPS = [
    ("x0", 0, 512), ("x1", 512, 512), ("B", OFF_B, 256), ("dt", OFF_DT, 32),
    ("k0", OFF_K, 512), ("k1", OFF_K + 512, 512), ("v0", OFF_V, 512), ("v1", OFF_V + 512, 512),
    ("C", OFF_C, 256), ("z0", OFF_Z, 512), ("z1", OFF_Z + 512, 512), ("q0", OFF_Q, 512), ("q1", OFF_Q + 512, 512),
]
IN_GIDX = {g[0]: i for i, g in enumerate(IN_GROUPS)}

COLS = {}
_n = 0
for _name, _w in [("c", 8), ("cctx", 8), ("modb0", 48), ("modb1", 48), ("nmix0", 8), ("nmix1", 8), ("nmlp0", 8),
                  ("nmlp1", 8), ("fnw", 8), ("cw0", 12), ("cw1", 12), ("cw2", 12), ("cb", 12),
                  ("scw0", 8), ("scw1", 8), ("scw2", 8)]:
    COLS[_name] = (_n, _w)
    _n += _w
NCOL = _n
ROWS = {}
_n = 0
for _name, _w in [("ssdnw", 1024), ("dtb", 32), ("alog", 32), ("dskip", 16)]:
    ROWS[_name] = (_n, _w)
    _n += _w
NROW = _n
NCONST = 5 * 128


def prep_inputs(inp):
    f = lambda a: np.ascontiguousarray(np.asarray(a, np.float32))
    shared = {}
    shared["modw"] = np.stack([_wtile(f(inp["mod_w"][l]), 1024) for l in range(2)])
    in_w = f(inp["ssdna_in_w"][0])
    inw = np.zeros((len(IN_GROUPS), 128, 8, 512), np.float32)
    for gi, (nm, c0, nc_) in enumerate(IN_GROUPS):
        inw[gi, :, :, :nc_] = in_w[:, c0:c0 + nc_].reshape(8, 128, nc_).transpose(1, 0, 2)
    shared["inw"] = inw
    shared["outw0"] = _wtile(f(inp["ssdna_out_w"][0]), 256)
    shared["w1"] = np.stack([_wtile(f(inp["mlp_w1"][l]), 512) for l in range(2)])
    shared["w2"] = np.stack([_wtile(f(inp["mlp_w2"][l]), 128) for l in range(2)])
    shared["scin"] = _wtile(f(inp["sc_in_w"][0]), 512)
    shared["scout"] = _wtile(f(inp["sc_out_w"][0]), 512)
    rows = np.zeros((1, NROW), np.float32)
    rows[0, ROWS["ssdnw"][0]:ROWS["ssdnw"][0] + 1024] = f(inp["ssd_norm_w"][0])
    rows[0, ROWS["dtb"][0]:ROWS["dtb"][0] + 32] = f(inp["ssd_dt_bias"][0]).reshape(32)
    rows[0, ROWS["alog"][0]:ROWS["alog"][0] + 32] = f(inp["ssd_a_log"][0]).reshape(32)
    rows[0, ROWS["dskip"][0]:ROWS["dskip"][0] + 16] = f(inp["ssd_d"][0])
    shared["rows"] = rows
    k = np.arange(128)
    consts = np.concatenate([np.eye(128, dtype=np.float32),
                             (k[:, None] <= k[None, :]).astype(np.float32),
                             (k[:, None] >= k[None, :]).astype(np.float32),
                             (k[:, None] < k[None, :]).astype(np.float32),
                             (k[:, None] > k[None, :]).astype(np.float32)], axis=1)
    shared["consts"] = np.ascontiguousarray(consts)
    rpb = f(inp["na_rpb"][0])
    c_ = np.arange(64)
    ws = np.clip(c_ - 8, 0, 48)
    kc_ = np.arange(64)
    ok = (kc_[:, None] >= ws[None, :]) & (kc_[:, None] < ws[None, :] + 16)
    dc = np.clip(kc_[:, None] - c_[None, :], -15, 15) + 15
    tblk = rpb[:, :, dc]
    tblk = np.where(ok[None, None], tblk, np.float32(NEG)).astype(np.float32)
    shared["tblk"] = np.ascontiguousarray(tblk)
    cols_common = np.zeros((128, NCOL), np.float32)

    def put(name, arr):
        o, w = COLS[name]
        cols_common[:, o:o + w] = arr
    put("cctx", _col(inp["c_ctx"]))
    for l in range(2):
        put("modb%d" % l, _col(inp["mod_b"][l]))
        put("nmix%d" % l, _col(inp["norm_mix_w"][l]))
        put("nmlp%d" % l, _col(inp["norm_mlp_w"][l]))
    put("fnw", _col(inp["final_norm_w"]))
    cw = f(inp["ssdna_conv_w"][0])
    for t in range(3):
        put("cw%d" % t, _col(cw[t]))
        put("scw%d" % t, _col(f(inp["sc_conv_w"][0])[t]))
    put("cb", _col(inp["ssdna_conv_b"][0]))
    per_core = []
    x = np.asarray(inp["x"], np.float32)
    ctx = np.asarray(inp["ctx"], np.float32)
    for b in range(8):
        cols = cols_common.copy()
        o, w = COLS["c"]
        cols[:, o:o + w] = _col(inp["c"][b])
        xT = np.ascontiguousarray(x[b].reshape(SEQ, 8, 128).transpose(2, 1, 0))
        cT = np.ascontiguousarray(ctx[b].reshape(CTX, 8, 128).transpose(2, 1, 0))
        per_core.append({"xT": xT, "ctxT": cT, "cols": cols})
    return shared, per_core


class Prog:
    def __init__(self, stop_after=None, debug=()):
        self.stop_after = stop_after
        self.debug = debug
        nc = self.nc = bass.Bass("TRN2", target_bir_lowering=False)
        self.dr = {}
        ein = lambda n, s: nc.dram_tensor(n, list(s), F32, kind="ExternalInput").ap()
        self.xT = ein("xT", (128, 8, SEQ))
        self.ctxT = ein("ctxT", (128, 8, CTX))
        self.cols_d = ein("cols", (128, NCOL))
        self.rows_d = ein("rows", (1, NROW))
        self.consts_d = ein("consts", (128, NCONST))
        self.modw = ein("modw", (2, 6, 128, 8, 1024))
        self.inw = ein("inw", (len(IN_GROUPS), 128, 8, 512))
        self.outw0 = ein("outw0", (4, 128, 16, 256))
        self.w1 = ein("w1", (2, 8, 128, 8, 512))
        self.w2 = ein("w2", (2, 8, 128, 32, 128))
        self.scin = ein("scin", (6, 128, 8, 512))
        self.scout = ein("scout", (2, 128, 8, 512))
        self.tblk = ein("tblk", (16, 15, 64, 64))
        self.outT = nc.dram_tensor("outT", [128, 8, SEQ], F32, kind="ExternalOutput").ap()
        scr = lambda n, s, dt: nc.dram_tensor(n, list(s), dt, kind="Internal").ap()
        self.s_xtok = scr("s_xtok", (NTOK, 1024), BF16)
        self.s_BT = scr("s_BT", (256, NTOK), BF16)
        self.s_CT = scr("s_CT", (256, SEQ), BF16)
        self.s_Btok = scr("s_Btok", (NTOK, 256), BF16)
        self.s_z = scr("s_z", (SEQ, 1024), BF16)
        self.s_qT2 = scr("s_qT", (128, 8, SEQ), BF16)
        self.s_kT2 = scr("s_kT", (128, 8, NTOK), BF16)
        self.s_qT = self.s_qT2.rearrange("(par d) c t -> d c par t", par=2)
        self.s_kT = self.s_kT2.rearrange("(par d) c t -> d c par t", par=2)
        self.s_vtok = scr("s_vtok", (NTOK, 16, 65), BF16)
        self.s_ycatT = scr("s_ycatT", (128, 16, SEQ), BF16)
        self.wsb = {"in": scr("wsb_in", (len(IN_GROUPS), 128, 4096), BF16), "out0": scr("wsb_out0", (4, 128, 4096), BF16),
                    "w1": scr("wsb_w1", (2, 8, 128, 4096), BF16), "w2": scr("wsb_w2", (2, 8, 128, 4096), BF16),
                    "scin": scr("wsb_scin", (6, 128, 4096), BF16), "scout": scr("wsb_scout", (2, 128, 4096), BF16)}
        self.b_wsb = {}
        self.tail_casts = []
        self.dbg_out = {}

    def dbg_tensor(self, name, shape, dtype=F32):
        ap = self.nc.dram_tensor("dbg_" + name, list(shape), dtype, kind="ExternalOutput").ap()
        self.dbg_out[name] = ap
        return ap

    def mm(self, out, lhsT, rhs, start, stop, reads, writes, signal, skip=False):
        if skip:
            self.S.op("pe", lambda e, o=out, l=lhsT, r=rhs, s=start, t=stop: e.matmul(o, l, r, start=s, stop=t, skip_group_check=True),
                      reads=reads, writes=writes, signal=signal)
            return
        self.S.op("pe", lambda e, o=out, l=lhsT, r=rhs, s=start, t=stop: e.matmul(o, l, r, start=s, stop=t),
                  reads=reads, writes=writes, signal=signal)

    def tr(self, out, in_, ident, reads, writes, signal=True):
        self.S.op("pe", lambda e, o=out, i=in_, d=ident: e.transpose(o, i, d), reads=reads, writes=writes, signal=signal)

    def act(self, out, in_, func, reads, writes, bias=None, scale=None, accum_out=None, eng="act"):
        kw = {}
        if bias is not None:
            kw["bias"] = bias
        if scale is not None:
            kw["scale"] = scale
        if accum_out is not None:
            kw["accum_out"] = accum_out
        self.S.op(eng, lambda e, o=out, i=in_, f=func, kw=kw: e.activation(out=o, in_=i, func=f, **kw),
                  reads=reads, writes=writes)

    def tt(self, eng, out, in0, in1, op, reads, writes):
        self.S.op(eng, lambda e, o=out, a=in0, b=in1, p=op: e.tensor_tensor(out=o, in0=a, in1=b, op=p),
                  reads=reads, writes=writes)

    def ts(self, eng, out, in0, s1, op0, reads, writes, s2=None, op1=None):
        if op1 is None:
            self.S.op(eng, lambda e, o=out, a=in0, s=s1, p=op0: e.tensor_scalar(out=o, in0=a, scalar1=s, scalar2=None, op0=p),
                      reads=reads, writes=writes)
        else:
            self.S.op(eng, lambda e, o=out, a=in0, s=s1, t=s2, p=op0, q=op1:
                      e.tensor_scalar(out=o, in0=a, scalar1=s, scalar2=t, op0=p, op1=q), reads=reads, writes=writes)

    def stt(self, eng, out, in0, scalar, in1, op0, op1, reads, writes):
        self.S.op(eng, lambda e, o=out, a=in0, s=scalar, b=in1, p=op0, q=op1:
                  e.scalar_tensor_tensor(out=o, in0=a, scalar=s, in1=b, op0=p, op1=q), reads=reads, writes=writes)

    def cp(self, eng, out, in_, reads, writes):
        if eng == "act":
            self.S.op("act", lambda e, o=out, i=in_: e.activation(out=o, in_=i, func=AF.Copy), reads=reads, writes=writes)
        else:
            self.S.op(eng, lambda e, o=out, i=in_: e.tensor_copy(out=o, in_=i), reads=reads, writes=writes)

    def recip(self, out, in_, reads, writes):
        self.S.op("dve", lambda e, o=out, i=in_: e.reciprocal(out=o, in_=i), reads=reads, writes=writes)

    def memset(self, eng, ap, val, writes):
        self.S.op(eng, lambda e, a=ap, v=val: e.memset(a, v), reads=(), writes=writes)

    def dma(self, qe, out, in_, reads=(), writes=()):
        self.S.dma(qe, out, in_, reads=reads, writes=writes)

    def build(self):
        nc = self.nc
        with ExitStack() as st:
            words = 51 * 1024
            arena_t = st.enter_context(nc.sbuf_tensor("arena", [128, words], F32))
            self.A = Arena(arena_t, words)
            self.ps = [st.enter_context(nc.psum_tensor("ps%d" % i, [128, 512], F32)) for i in range(8)]
            self.psb = [Buf("ps%d" % i) for i in range(8)]
            self.S = Sched(nc)
            self.phase_setup()
            self.precast("in")
            done = self.run_phases()
            self.S.barrier()
            self.S.final_wait("sp")
            self.S.replay(st)
        return nc

    def run_phases(self):
        self.phase_mod_and_h()
        if self.stop_after == "h":
            return
        self.S.barrier()
        self.phase_inproj()
        if self.stop_after == "inproj":
            return
        self.S.barrier()
        if "skip_ssd" not in self.debug:
            self.phase_ssd()
        else:
            self.b_ycat = Buf("ycat")
        if self.stop_after == "ssd":
            return
        self.S.barrier()
        if "skip_na" not in self.debug:
            self.phase_na()
        if self.stop_after == "na":
            return
        self.S.barrier()
        self.phase_tail()

    def phase_setup(self):
        A = self.A
        self.cols = A.alloc([NCOL], F32)
        self.b_cols = Buf("cols")
        self.dma("sp", self.cols, self.cols_d, writes=[self.b_cols])
        self.consts = A.alloc([NCONST], F32)
        self.b_consts = Buf("consts")
        self.dma("sp", self.consts, self.consts_d, writes=[self.b_consts])
        self.ident_f = self.consts[:, 0:128]
        self.L_f = self.consts[:, 128:256]
        self.U_f = self.consts[:, 256:384]
        self.SL_f = self.consts[:, 384:512]
        self.SU_f = self.consts[:, 512:640]
        self.dt_raw = A.alloc([34, 32], F32)
        self.b_dtraw = Buf("dtraw")
        self.ident_b = A.alloc([128], BF16)
        self.ones_b = A.alloc([128], BF16)
        self.b_cb = Buf("constb")
        self.cp("dve", self.ident_b, self.ident_f, [self.b_consts], [self.b_cb])
        self.memset("dve", self.ones_b, 1.0, [self.b_cb])
        self.modv = [A.alloc([48], F32) for _ in range(2)]
        self.b_modv = [[Buf("modv%d_%d" % (l, v)) for v in range(6)] for l in range(2)]
        self.modc = A.alloc([16], F32)
        self.b_modc = Buf("modc")
        self.gcols = A.alloc([64], F32)
        self.b_g = {}

    def precast(self, which):
        def one(key, idx, src):
            b = Buf("wsb_%s_%s" % (key, idx))
            self.b_wsb[(key,) + idx] = b
            dst = self.wsb[key]
            for i in idx:
                dst = dst[i]
                src = src[i]
            if which == "in":
                self.dma("pool", dst, src.rearrange("p k n -> p (k n)"), writes=[b])
            else:
                self.tail_casts.append(lambda gate, dst=dst, src=src, b=b: self.dma(
                    "pool", dst, src.rearrange("p k n -> p (k n)"), reads=[gate], writes=[b]))
        if which == "in":
            for g in range(len(IN_GROUPS)):
                one("in", (g,), self.inw)
            return
        for g in range(4):
            one("out0", (g,), self.outw0)
        for l in range(2):
            if l == 1:
                for g in range(6):
                    one("scin", (g,), self.scin)
                for g in range(2):
                    one("scout", (g,), self.scout)
            for g in range(8):
                one("w1", (l, g), self.w1)
            for g in range(8):
                one("w2", (l, g), self.w2)

    def cast_some(self, n, gate):
        for _ in range(n):
            if self.tail_casts:
                self.tail_casts.pop(0)(gate)

    def wsrc(self, key, idx, kc):
        src = self.wsb[key]
        for i in idx:
            src = src[i]
        return src.rearrange("p (k n) -> p k n", k=kc), self.b_wsb[(key,) + idx]

    def colv(self, name, j=None):
        o, w = COLS[name]
        if j is None:
            return self.cols[:, o:o + w]
        return self.cols[:, o + j:o + j + 1]

    def phase_mod_and_h(self):
        A, S = self.A, self.S
        m0 = A.mark()
        self.mark_base = m0
        self.hT = A.alloc([8, SEQ], BF16)
        self.hcT = A.alloc([8, CTX], BF16)
        self.b_hT = [Buf("hT%d" % t) for t in range(8)]
        self.b_hcT = Buf("hcT")
        m1 = A.mark()
        s2 = A.alloc([8, 2], F32)
        b_s2 = Buf("s2")
        self.act(s2[:, :, 0], self.colv("c"), AF.Silu, [self.b_cols], [b_s2])
        self.act(s2[:, :, 1], self.colv("cctx"), AF.Silu, [self.b_cols], [b_s2])
        wbuf = [A.alloc([8, 1024], F32) for _ in range(2)]
        b_w = [Buf("modw%d" % i) for i in range(2)]
        modcnt = [0]
        rsb = [A.alloc([1024], F32)] * 2
        b_rsb = [Buf("rsb0")] * 2

        def mod_group(l, v):
            n = modcnt[0]
            modcnt[0] += 1
            wb, bw = wbuf[n % 2], b_w[n % 2]
            self.dma("sp", wb, self.modw[l, v], writes=[bw])
            pst = self.ps[n % 2][:, 0:16].rearrange("p (j t) -> p j t", t=2)
            bps = self.psb[n % 2]
            rs, brs_ = rsb[n % 2], b_rsb[n % 2]
            for nh in range(2):
                pr, bpr = self.ps[4 + nh], self.psb[4 + nh]
                for kc in range(8):
                    self.mm(pr[0:2, :], s2[:, kc, :], wb[:, kc, nh * 512:(nh + 1) * 512], kc == 0, kc == 7, [bw, b_s2], [bpr],
                            signal=(kc == 7))
                self.cp("act" if nh == 0 else "dve", rs[0:2, nh * 512:(nh + 1) * 512], pr[0:2, :], [bpr], [brs_])
            for j in range(8):
                self.mm(pst[:, j, :], rs[0:2, j * 128:(j + 1) * 128], self.ident_f[0:2, 0:2], True, True, [brs_, self.b_consts], [bps],
                        signal=(j == 7))
            o, _ = COLS["modb%d" % l]
            self.tt("dve", self.modv[l][:, v * 8:(v + 1) * 8], pst[:, :, 0], self.cols[:, o + v * 8:o + v * 8 + 8],
                    ALU.add, [bps, self.b_cols], [self.b_modv[l][v]])
            if l == 0 and v < 2:
                self.tt("dve", self.modc[:, v * 8:(v + 1) * 8], pst[:, :, 1], self.cols[:, o + v * 8:o + v * 8 + 8],
                        ALU.add, [bps, self.b_cols], [self.b_modc])

        mod_group(0, 0)
        mod_group(0, 1)
        rest = [(0, v) for v in range(2, 6)] + [(1, v) for v in range(6)]
        def gain(slot, normname, scale_ap, rd):
            g = self.gcols[:, slot * 8:(slot + 1) * 8]
            b = Buf("g%d" % slot)
            self.stt("dve", g, scale_ap, 1.0, self.colv(normname), ALU.add, ALU.mult, rd + [self.b_cols], [b])
            return g, b
        self.g_a0, self.b_ga0 = gain(0, "nmix0", self.modv[0][:, 8:16], [self.b_modv[0][1]])
        self.g_c, self.b_gc = gain(4, "nmix0", self.modc[:, 8:16], [self.b_modc])
        xbuf = [A.alloc([8, 512], F32) for _ in range(2)]
        b_x = [Buf("xb%d" % i) for i in range(2)]
        wk = self.norm_alloc(512)
        wk["use_pool"] = False
        for t in range(9):
            xb, bx = xbuf[t % 2], b_x[t % 2]
            if t < 8:
                ntok = 512
                self.dma("sp", xb, self.xT[:, :, t * 512:(t + 1) * 512], writes=[bx])
                self.norm_tile(xb, bx, ntok, self.g_a0, self.b_ga0, self.modv[0][:, 0:8], self.b_modv[0][0],
                               self.hT[:, :, t * 512:(t + 1) * 512], self.b_hT[t], wk, t)
            else:
                ntok = CTX
                self.dma("sp", xb[:, :, 0:CTX], self.ctxT, writes=[bx])
                self.norm_tile(xb[:, :, 0:CTX], bx, ntok, self.g_c, self.b_gc, self.modc[:, 0:8], self.b_modc,
                               self.hcT, self.b_hcT, wk, t)
            for _ in range(2 if t == 0 else 1):
                if rest:
                    mod_group(*rest.pop(0))
        while rest:
            mod_group(*rest.pop(0))
        self.g_f0, self.b_gf0 = gain(1, "nmlp0", self.modv[0][:, 32:40], [self.b_modv[0][4]])
        self.g_a1, self.b_ga1 = gain(2, "nmix1", self.modv[1][:, 8:16], [self.b_modv[1][1]])
        self.g_f1, self.b_gf1 = gain(3, "nmlp1", self.modv[1][:, 32:40], [self.b_modv[1][4]])
        self.precast("tail")
        if "h" in self.debug:
            d = self.dbg_tensor("hT", (128, 8, SEQ), BF16)
            self.dma("sp", d, self.hT, reads=self.b_hT)
            d = self.dbg_tensor("hcT", (128, 8, CTX), BF16)
            self.dma("sp", d, self.hcT, reads=[self.b_hcT])
            d = self.dbg_tensor("modv0", (128, 48))
            self.dma("sp", d, self.modv[0], reads=self.b_modv[0])
            d = self.dbg_tensor("modv1", (128, 48))
            self.dma("sp", d, self.modv[1], reads=self.b_modv[1])
        self.mark_after_h = m1

    def norm_alloc(self, ntok):
        A = self.A
        wk = {"sq": [A.alloc([8, ntok], BF16) for _ in range(2)], "b_sq": [Buf("sq0"), Buf("sq1")],
              "rstd": [A.alloc([ntok], F32) for _ in range(2)], "b_rstd": [Buf("rs0"), Buf("rs1")],
              "tmp": [A.alloc([ntok], F32) for _ in range(3)], "b_tmp": [Buf("nt%d" % i) for i in range(3)],
              "n": 0}
        return wk

    def norm_accum(self, x, bx, m, wk, it, psi):
        i2 = it % 2
        sq = wk["sq"][i2]
        bsq = wk["b_sqc"][m]
        ps, bps = self.ps[psi + i2], self.psb[psi + i2]
        self.act(sq[:, m, :], x[:, m, :], AF.Square, [bx], [bsq])
        return lambda: self.mm(ps[:, :], self.ones_b, sq[:, m, :], m == 0, m == 7, [bsq, self.b_cb], [bps], signal=(m == 7))

    def norm_tile(self, x, bx, ntok, g, bg, shift, bshift, out, bout, wk, it, psi=2, final=False, accum_done=False):
        i2 = it % 2
        sq, bsq = wk["sq"][i2], wk["b_sq"][i2]
        rstd, brs = wk["rstd"][i2], wk["b_rstd"][i2]
        ps, bps = self.ps[psi + i2], self.psb[psi + i2]
        if not accum_done:
            self.act(sq[:, :, 0:ntok], x, AF.Square, [bx], [bsq])
            for kc in range(8):
                self.mm(ps[:, 0:ntok], self.ones_b, sq[:, kc, 0:ntok], kc == 0, kc == 7, [bsq, self.b_cb], [bps], signal=(kc == 7))
        self.ts("dve", rstd[:, 0:ntok], ps[:, 0:ntok], 1.0 / D, ALU.mult, [bps], [brs], s2=EPS, op1=ALU.add)
        self.act(rstd[:, 0:ntok], rstd[:, 0:ntok], AF.Sqrt, [brs], [brs])
        self.recip(rstd[:, 0:ntok], rstd[:, 0:ntok], [brs], [brs])
        for kc in range(8):
            k3 = wk["n"] % 3
            wk["n"] += 1
            tmp, btmp = wk["tmp"][k3], wk["b_tmp"][k3]
            bxk = bx[kc] if isinstance(bx, list) else bx
            self.tt("dve" if (kc % 2 == 0 or not wk.get("use_pool", True)) else "pool", tmp[:, 0:ntok], x[:, kc, :], rstd[:, 0:ntok],
                    ALU.mult, [bxk, brs], [btmp])
            boutk = bout[kc] if isinstance(bout, list) else bout
            if final:
                self.act(out[:, kc, :], tmp[:, 0:ntok], AF.Identity, [btmp, bg], [boutk], scale=g[:, kc:kc + 1])
            else:
                self.act(out[:, kc, :], tmp[:, 0:ntok], AF.Identity, [btmp, bshift, bg], [boutk], bias=shift[:, kc:kc + 1],
                         scale=g[:, kc:kc + 1])

    def phase_inproj(self):
        A, S = self.A, self.S
        A.release(self.mark_after_h)
        wts = [A.alloc([8, 512], BF16) for _ in range(3)]
        b_wts = [Buf("inw%d" % i) for i in range(3)]
        pre = [A.alloc([SEQ + 2 + CTX + 2], F32) for _ in range(2)]
        b_pre = [Buf("pre0"), Buf("pre1")]
        for i in range(2):
            for c in (0, SEQ + 1, SEQ + 2, SEQ + 2 + CTX + 1):
                self.memset("pool", pre[i][:, c:c + 1], 0.0, [b_pre[i]])
        cacc = A.alloc([NTOK], F32)
        b_cacc = Buf("cacc")
        cv = [A.alloc([NTOK], BF16) for _ in range(2)]
        b_cv = [Buf("cv0"), Buf("cv1")]
        xst = [A.alloc([34, 128], BF16) for _ in range(2)]
        b_xst = [Buf("xst0"), Buf("xst1")]
        NK = 6
        kst = [A.alloc([512], BF16) for _ in range(NK)]
        b_kst = [Buf("kst%d" % i) for i in range(NK)]
        vst = [A.alloc([16, 65], BF16) for _ in range(2)]
        b_vst = [Buf("vst0"), Buf("vst1")]
        for i in range(2):
            self.memset("pool", vst[i][:, :, 64:65], 1.0, [b_vst[i]])
        zst = [A.alloc([1024], BF16) for _ in range(2)]
        b_zst = [Buf("zst0"), Buf("zst1")]
        st = {"w": 0, "ps": 0, "pre": 0, "cv": 0, "xst": 0, "kst": 0, "vst": 0, "zst": 0, "tp": 0}
        psT = [self.ps[4][:, :].bitcast(BF16), self.ps[5][:, :].bitcast(BF16)]
        b_scr = {k: Buf(k) for k in ("xtok", "BT", "CT", "Btok", "z", "qT", "kT", "vtok")}
        self.b_scr = b_scr

        def load_w(gname):
            i = st["w"] % 3
            st["w"] += 1
            src, bsrc = self.wsrc("in", (IN_GIDX[gname],), 8)
            self.dma("sp", wts[i], src, reads=[bsrc], writes=[b_wts[i]])
            return wts[i], b_wts[i]

        def fm_tile(wt, bw, j, tt):
            pi = st["ps"] % 4
            st["ps"] += 1
            ps, bps = self.ps[pi], self.psb[pi]
            if tt < 8:
                ntok = 512
                rhs = lambda kc: self.hT[:, kc, tt * 512:(tt + 1) * 512]
                rb = self.b_hT[tt]
            else:
                ntok = CTX
                rhs = lambda kc: self.hcT[:, kc, :]
                rb = self.b_hcT
            for kc in range(8):
                self.mm(ps[:, 0:ntok], wt[:, kc, j * 128:(j + 1) * 128], rhs(kc), kc == 0, kc == 7, [bw, rb], [bps],
                        signal=(kc == 7))
            return ps[:, 0:ntok], bps

        def conv_mm(wt, bw, j, i, tiles):
            p, bp = pre[i], b_pre[i]
            for tt in tiles:
                ps, bps = fm_tile(wt, bw, j, tt)
                if tt < 8:
                    dst = p[:, 1 + tt * 512:1 + (tt + 1) * 512]
                else:
                    dst = p[:, SEQ + 3:SEQ + 3 + CTX]
                self.cp("act", dst, ps, [bps], [bp])

        def conv_a(i, ch, has_ctx):
            p, bp = pre[i], b_pre[i]
            w0, w1, w2, cb = (self.colv("cw0", ch), self.colv("cw1", ch), self.colv("cw2", ch), self.colv("cb", ch))
            segs = [(0, 0, SEQ)] + ([(SEQ + 2, SEQ, CTX)] if has_ctx else [])
            for (po, co, n) in segs:
                self.act(cacc[:, co:co + n], p[:, po:po + n], AF.Identity, [bp, self.b_cols], [b_cacc], bias=cb, scale=w0)
                self.stt("dve", cacc[:, co:co + n], p[:, po + 1:po + 1 + n], w1, cacc[:, co:co + n], ALU.mult, ALU.add,
                         [bp, b_cacc, self.b_cols], [b_cacc])
                self.stt("dve", cacc[:, co:co + n], p[:, po + 2:po + 2 + n], w2, cacc[:, co:co + n], ALU.mult, ALU.add,
                         [bp, b_cacc, self.b_cols], [b_cacc])

        def conv_s(i, has_ctx):
            c, bc = cv[i], b_cv[i]
            n = NTOK if has_ctx else SEQ
            self.act(c[:, 0:n], cacc[:, 0:n], AF.Silu, [b_cacc], [bc])
            self.cast_some(2, bc)
            return c, bc

        def to_tokmajor(c, bc, nblk, dst_ap, bdst):
            xi = st["xst"] % 2
            st["xst"] += 1
            xs_, bxs = xst[xi], b_xst[xi]
            blk = 0
            while blk < nblk:
                nb = min(8, nblk - blk)
                ti = st["tp"] % 2
                st["tp"] += 1
                pt, bpt = psT[ti], self.psb[4 + ti]
                for q in range(nb):
                    self.tr(pt[:, q * 128:(q + 1) * 128], c[:, (blk + q) * 128:(blk + q + 1) * 128], self.ident_b,
                            [bc, self.b_cb], [bpt], signal=(q == nb - 1))
                self.cp("dve", xs_[:, blk:blk + nb, :], pt[:, 0:nb * 128].rearrange("p (a b) -> p a b", a=nb), [bpt], [bxs])
                blk += nb
            for b0 in range(0, nblk, 9):
                b1 = min(nblk, b0 + 9)
                self.dma("sp", dst_ap[:, b0:b1, :], xs_[:, b0:b1, :], reads=[bxs], writes=[bdst])

        xtok_v = self.s_xtok.rearrange("(b p) c -> p b c", p=128)
        btok_v = self.s_Btok.rearrange("(b p) c -> p b c", p=128)
        deferred = []

        def defer(fn):
            if deferred:
                deferred.pop(0)()
            if fn is not None:
                deferred.append(fn)

        def post(ch, c, bc):
            if ch < 8:
                defer(lambda: to_tokmajor(c, bc, 34, xtok_v[:, :, ch * 128:(ch + 1) * 128], b_scr["xtok"]))
            elif ch < 10:
                j = ch - 8
                self.dma("sp", self.s_BT[j * 128:(j + 1) * 128, :], c, reads=[bc], writes=[b_scr["BT"]])
                defer(lambda: to_tokmajor(c, bc, 34, btok_v[:, :, j * 128:(j + 1) * 128], b_scr["Btok"]))
            else:
                j = ch - 10
                self.dma("sp", self.s_CT[j * 128:(j + 1) * 128, :], c[:, 0:SEQ], reads=[bc], writes=[b_scr["CT"]])
                defer(None)

        chunks = [("x0", j, j, True) for j in range(4)] + [("x1", j, 4 + j, True) for j in range(4)] + \
                 [("B", j, 8 + j, True) for j in range(2)] + [("C", j, 10 + j, False) for j in range(2)]
        wcur = {}
        prev_c = None
        for n_, (gname, j, ch, has_ctx) in enumerate(chunks):
            if gname not in wcur:
                wcur.clear()
                wcur[gname] = load_w(gname)
            wt, bw = wcur[gname]
            i = n_ % 2
            tiles = list(range(9 if has_ctx else 8))
            if prev_c is None:
                conv_mm(wt, bw, j, i, tiles)
            else:
                pi_, pch, pctx = prev_c
                conv_mm(wt, bw, j, i, tiles[:5])
                conv_a(pi_, pch, pctx)
                conv_mm(wt, bw, j, i, tiles[5:])
                c, bc = conv_s(pi_, pctx)
                post(pch, c, bc)
            prev_c = (i, ch, has_ctx)
        pi_, pch, pctx = prev_c
        conv_a(pi_, pch, pctx)
        c, bc = conv_s(pi_, pctx)
        post(pch, c, bc)
        defer(None)
        wt, bw = load_w("dt")
        for blk in range(34):
            ps, bps = self.ps[6], self.psb[6]
            o = (blk % 8) * 32
            for kc in range(8):
                lhs = self.hT[:, kc, blk * 128:(blk + 1) * 128] if blk < 32 else self.hcT[:, kc, (blk - 32) * 128:(blk - 31) * 128]
                rb = self.b_hT[blk // 4] if blk < 32 else self.b_hcT
                self.mm(ps[:, o:o + 32], lhs, wt[:, kc, 0:32], kc == 0, kc == 7, [bw, rb], [bps], signal=(kc == 7))
            self.cp("dve", self.dt_raw[:, blk, :], ps[:, o:o + 32], [bps], [self.b_dtraw])
        for gname, dst, bd, has_ctx, scale in (("k0", self.s_kT2, b_scr["kT"], True, None), ("k1", self.s_kT2, b_scr["kT"], True, None),
                                               ("q0", self.s_qT2, b_scr["qT"], False, 0.125), ("q1", self.s_qT2, b_scr["qT"], False, 0.125)):
            wt, bw = load_w(gname)
            for j in range(4):
                hp = (int(gname[1]) * 4 + j) * 2
                for tt in range(9 if has_ctx else 8):
                    ps, bps = fm_tile(wt, bw, j, tt)
                    ntok = 512 if tt < 8 else CTX
                    ki = st["kst"] % NK
                    st["kst"] += 1
                    ks, bks = kst[ki], b_kst[ki]
                    if scale is None:
                        self.cp("act", ks[:, 0:ntok], ps, [bps], [bks])
                    else:
                        self.act(ks[:, 0:ntok], ps, AF.Copy, [bps], [bks], scale=scale)
                    t0 = tt * 512
                    self.dma("sp", dst[:, hp // 2, t0:t0 + ntok], ks[:, 0:ntok], reads=[bks], writes=[bd])
                self.cast_some(2, bks)
        for which in ("v", "z"):
            w0, bw0 = load_w(which + "0")
            w1, bw1 = load_w(which + "1")
            nblk = 34 if which == "v" else 32
            for blk in range(nblk):
                lhs = (lambda kc, blk=blk: self.hT[:, kc, blk * 128:(blk + 1) * 128]) if blk < 32 else \
                    (lambda kc, blk=blk: self.hcT[:, kc, (blk - 32) * 128:(blk - 31) * 128])
                rb = self.b_hT[blk // 4] if blk < 32 else self.b_hcT
                if which == "v":
                    si = st["vst"] % 2
                    st["vst"] += 1
                    sb, bsb = vst[si], b_vst[si]
                else:
                    si = st["zst"] % 2
                    st["zst"] += 1
                    sb, bsb = zst[si], b_zst[si]
                for half, (wt, bw) in enumerate(((w0, bw0), (w1, bw1))):
                    pi = st["ps"] % 4
                    st["ps"] += 1
                    ps, bps = self.ps[pi], self.psb[pi]
                    for kc in range(8):
                        self.mm(ps[:, :], lhs(kc), wt[:, kc, :], kc == 0, kc == 7, [bw, rb], [bps], signal=(kc == 7))
                    if which == "v":
                        self.cp("act" if half == 0 else "dve", sb[:, half * 8:(half + 1) * 8, 0:64],
                                ps[:, :].rearrange("p (a b) -> p a b", a=8), [bps], [bsb])
                    else:
                        self.act(sb[:, half * 512:(half + 1) * 512], ps[:, :], AF.Silu, [bps], [bsb])
                if which == "v":
                    self.dma("sp", self.s_vtok[blk * 128:(blk + 1) * 128, :, :], sb, reads=[bsb], writes=[b_scr["vtok"]])
                else:
                    self.dma("sp", self.s_z[blk * 128:(blk + 1) * 128, :], sb, reads=[bsb], writes=[b_scr["z"]])
        self.cast_some(1000, self.b_dtraw)
        if "inproj" in self.debug:
            for nm, ap in (("xtok", self.s_xtok), ("BT", self.s_BT), ("CT", self.s_CT), ("Btok", self.s_Btok), ("z", self.s_z),
                           ("qT", self.s_qT2), ("kT", self.s_kT2), ("vtok", self.s_vtok)):
                d = self.dbg_tensor(nm, ap.shape, BF16)
                self.dma("sp", d, ap, reads=[b_scr[nm]])
            d = self.dbg_tensor("dtraw", (128, 34, 32))
            self.dma("sp", d, self.dt_raw, reads=[self.b_dtraw])


    def phase_ssd(self):
        A, S = self.A, self.S
        A.release(self.mark_base)
        b_scr = self.b_scr
        self.b_ycat = Buf("ycat")
        rows = A.alloc([NROW], F32)
        b_rows = Buf("rows")
        self.dma("sp", rows, self.rows_d.partition_broadcast(128), writes=[b_rows])
        rv = lambda nm: rows[:, ROWS[nm][0]:ROWS[nm][0] + ROWS[nm][1]]
        normw_bc, dtb_bc, alog_bc, D_bc = rv("ssdnw"), rv("dtb"), rv("alog"), rv("dskip")
        dtv = A.alloc([34, 32], F32)
        av = A.alloc([34, 32], F32)
        aneg = A.alloc([32], F32)
        b_dt = Buf("dtv")
        self.tt("dve", dtv, self.dt_raw, dtb_bc[:, None, :].to_broadcast([128, 34, 32]), ALU.add, [self.b_dtraw, b_rows], [b_dt])
        self.act(dtv, dtv, AF.Exp, [b_dt], [b_dt])
        self.act(dtv, dtv, AF.Ln, [b_dt], [b_dt], bias=1.0, scale=1.0)
        self.act(aneg, alog_bc, AF.Exp, [b_rows], [b_dt])
        self.ts("dve", aneg, aneg, -1.0, ALU.mult, [b_dt], [b_dt])
        self.tt("dve", av, dtv, aneg[:, None, :].to_broadcast([128, 34, 32]), ALU.mult, [b_dt], [b_dt])
        negL = A.alloc([128], F32)
        negU = A.alloc([128], F32)
        ones_f = A.alloc([128], F32)
        b_c2 = Buf("c2")
        self.ts("dve", negL, self.L_f, -1.0, ALU.mult, [self.b_consts], [b_c2])
        self.ts("dve", negU, self.U_f, -1.0, ALU.mult, [self.b_consts], [b_c2])
        self.memset("dve", ones_f, 1.0, [b_c2])
        hstore = A.alloc([32, 1024], BF16)
        b_hst = [Buf("hst%d" % i) for i in range(32)]
        hf = [A.alloc([1024], F32) for _ in range(2)]
        b_hf = [Buf("hf0"), Buf("hf1")]
        hb = A.alloc([1024], BF16)
        b_hb = Buf("hb")
        for d in range(2):
            self.memset("pool", hf[d], 0.0, [b_hf[d]])
        NB = 2
        xt = [A.alloc([16, 64], BF16) for _ in range(NB)]
        Bt = [A.alloc([256], BF16) for _ in range(NB)]
        BTc = [A.alloc([2, 128], BF16) for _ in range(NB)]
        CTc = [A.alloc([2, 128], BF16) for _ in range(NB)]
        zc = [A.alloc([1024], BF16) for _ in range(NB)]
        b_ld = [{k: Buf("%s%d" % (k, i)) for k in ("xt", "Bt", "BTc", "CTc", "zc")} for i in range(NB)]
        ex = [A.alloc([48], F32) for _ in range(2)]
        ws = [A.alloc([16], F32) for _ in range(2)]
        b_ex = [Buf("ex0"), Buf("ex1")]
        xdt = [A.alloc([16, 64], BF16) for _ in range(2)]
        b_xdt = [Buf("xdt0"), Buf("xdt1")]
        xs = [A.alloc([16, 64], BF16) for _ in range(2)]
        b_xs = [Buf("xs0"), Buf("xs1")]
        xD = A.alloc([16, 64], BF16)
        b_xD = Buf("xD")
        G = [A.alloc([16, 128], F32) for _ in range(2)]
        b_G = [Buf("G0"), Buf("G1")]
        Ebc = [A.alloc([4, 128], F32) for _ in range(3)]
        b_Ebc = [Buf("Ebc%d" % i) for i in range(3)]
        E = [A.alloc([4, 128], F32) for _ in range(3)]
        b_E = [Buf("E%d" % i) for i in range(3)]
        MT = [A.alloc([4, 128], BF16) for _ in range(3)]
        b_MT = [Buf("MT%d" % i) for i in range(3)]
        Cp = [A.alloc([4, 128], BF16) for _ in range(3)]
        b_Cp = [Buf("Cp%d" % i) for i in range(3)]
        cbm = [A.alloc([2, 128], F32) for _ in range(2)]
        b_cbm = Buf("cbm")
        yg = A.alloc([1024], F32)
        b_yg = Buf("yg")
        ysq = A.alloc([1024], F32)
        ss = A.alloc([2], F32)
        b_ss = Buf("ss")
        yn = A.alloc([1024], BF16)
        b_yn = Buf("yn")
        ycst = [A.alloc([8, 512], BF16) for _ in range(2)]
        b_ycst = [Buf("ycst0"), Buf("ycst1")]
        stmp = A.alloc([1024], F32)
        b_stmp = Buf("stmp")
        cnt = {"ld": 0, "ex": 0, "eb": 0, "mt": 0, "seg": 0}
        tri = [self.L_f, self.U_f]
        stri = [self.SU_f, self.SL_f]
        negtri = [negL, negU]
        BTv = self.s_BT.rearrange("(g n) t -> n g t", g=2)
        CTv = self.s_CT.rearrange("(g n) t -> n g t", g=2)
        ps_small, b_small = self.ps[0], self.psb[0]
        ps_cb, b_cb_ps = self.ps[1], self.psb[1]
        ps_S, b_S = self.ps[6], self.psb[6]
        psT, b_psT = self.ps[7][:, :].bitcast(BF16), self.psb[7]

        def load(blk, full):
            i = cnt["ld"] % NB
            cnt["ld"] += 1
            L = b_ld[i]
            self.dma("sp", xt[i], self.s_xtok[blk * 128:(blk + 1) * 128, :].rearrange("p (h d) -> p h d", h=16),
                     reads=[b_scr["xtok"]], writes=[L["xt"]])
            self.dma("sp", Bt[i], self.s_Btok[blk * 128:(blk + 1) * 128, :], reads=[b_scr["Btok"]], writes=[L["Bt"]])
            if full:
                self.dma("sp", BTc[i], BTv[:, :, blk * 128:(blk + 1) * 128], reads=[b_scr["BT"]], writes=[L["BTc"]])
                self.dma("sp", CTc[i], CTv[:, :, blk * 128:(blk + 1) * 128], reads=[b_scr["CT"]], writes=[L["CTc"]])
                self.dma("sp", zc[i], self.s_z[blk * 128:(blk + 1) * 128, :], reads=[b_scr["z"]], writes=[L["zc"]])
            return i

        def small(blk, d):
            a = av[:, blk, d * 16:(d + 1) * 16]
            o = d * 64
            self.mm(ps_small[:, o:o + 16], tri[d], a, True, True, [self.b_consts, b_dt], [b_small], signal=False)
            self.mm(ps_small[:, o + 16:o + 32], stri[d], a, True, True, [self.b_consts, b_dt], [b_small], signal=False)
            self.mm(ps_small[:, o + 32:o + 48], ones_f, a, True, True, [b_c2, b_dt], [b_small], signal=True)
            self.act(ex[d], ps_small[:, o:o + 48], AF.Exp, [b_small], [b_ex[d]])
            self.tt("dve", ws[d], ex[d][:, 16:32], dtv[:, blk, d * 16:(d + 1) * 16], ALU.mult, [b_ex[d], b_dt], [b_ex[d]])

        def xs_prep(d, li):
            self.tt("pool", xs[d], xt[li], ws[d][:, :, None].to_broadcast([128, 16, 64]), ALU.mult,
                    [b_ld[li]["xt"], b_ex[d]], [b_xs[d]])

        def state_step(d, li, store_blk=None, prep=True):
            if prep:
                xs_prep(d, li)
            if store_blk is not None:
                self.cp("act", hstore[:, store_blk, :], hf[d], [b_hf[d]], [b_hst[store_blk]])
            for g in range(2):
                self.mm(ps_S[:, :], Bt[li][:, g * 128:(g + 1) * 128], xs[d][:, g * 8:(g + 1) * 8, :], True, True,
                        [b_ld[li]["Bt"], b_xs[d]], [b_S], signal=True)
                hv = hf[d][:, g * 512:(g + 1) * 512].rearrange("p (h q) -> p h q", h=8)
                tv = stmp[:, g * 512:(g + 1) * 512].rearrange("p (h q) -> p h q", h=8)
                self.tt("pool", tv, hv, ex[d][:, 32 + g * 8:32 + (g + 1) * 8][:, :, None].to_broadcast([128, 8, 64]), ALU.mult,
                        [b_hf[d], b_ex[d]], [b_stmp])
                self.tt("dve", hf[d][:, g * 512:(g + 1) * 512], stmp[:, g * 512:(g + 1) * 512], ps_S[:, :], ALU.add,
                        [b_stmp, b_S], [b_hf[d]])

        for blk in [33, 32] + list(range(31, -1, -1)):
            li = load(blk, False)
            small(blk, 1)
            state_step(1, li, store_blk=(blk if blk < 32 else None))
        for blk in (32, 33):
            li = load(blk, False)
            small(blk, 0)
            state_step(0, li)
        self.cp("act", hb, hf[0], [b_hf[0]], [b_hb])
        pending_fin = []
        for c in range(32):
            li = load(c, True)
            L = b_ld[li]
            for g in range(2):
                self.mm(ps_cb[:, g * 128:(g + 1) * 128], BTc[li][:, g, :], CTc[li][:, g, :], True, True, [L["BTc"], L["CTc"]],
                        [b_cb_ps], signal=(g == 1))
            pcb = ps_cb[:, 0:256].rearrange("p (g i) -> p g i", g=2)
            self.tt("dve", cbm[0], pcb, self.L_f[:, None, :].to_broadcast([128, 2, 128]), ALU.mult, [b_cb_ps, self.b_consts], [b_cbm])
            self.tt("dve", cbm[1], pcb, self.U_f[:, None, :].to_broadcast([128, 2, 128]), ALU.mult, [b_cb_ps, self.b_consts], [b_cbm])
            self.tt("pool", xD, xt[li], D_bc[:, :, None].to_broadcast([128, 16, 64]), ALU.mult, [L["xt"], b_rows], [b_xD])
            for g in range(2):
                self.mm(self.ps[4 + g][:, :], self.ident_b, xD[:, g * 8:(g + 1) * 8, :], True, False, [self.b_cb, b_xD],
                        [self.psb[4 + g]], signal=False)
            for d in range(2):
                small(c, d)
                if d == 0:
                    xs_prep(0, li)
                dtc = dtv[:, c, d * 16:(d + 1) * 16]
                a = av[:, c, d * 16:(d + 1) * 16]
                self.tt("pool", xdt[d], xt[li], dtc[:, :, None].to_broadcast([128, 16, 64]), ALU.mult, [L["xt"], b_dt], [b_xdt[d]])
                self.tt("pool" if "g_pool" in self.debug else "dve", G[d], tri[d][:, None, :].to_broadcast([128, 16, 128]),
                        a[:, :, None].to_broadcast([128, 16, 128]), ALU.mult, [self.b_consts, b_dt], [b_G[d]])

            def mm1(it):
                d, q = it // 4, it % 4
                k = it % 3
                pseg, b_seg = self.ps[2 + it % 2], self.psb[2 + it % 2]
                self.mm(pseg[:, :], ones_f, G[d][:, 4 * q:4 * q + 4, :], True, True, [b_c2, b_G[d]], [b_seg], signal=True)
                self.act(Ebc[k], pseg[:, :].rearrange("p (a b) -> p a b", a=4), AF.Exp, [b_seg], [b_Ebc[k]])

            def mm2(it):
                d, q = it // 4, it % 4
                k = it % 3
                g = q // 2
                a = av[:, c, d * 16:(d + 1) * 16]
                pseg, b_seg = self.ps[2 + it % 2], self.psb[2 + it % 2]
                self.mm(pseg[:, :].rearrange("p (a b) -> p a b", a=4), negtri[d],
                        a[:, 4 * q:4 * q + 4][:, :, None].to_broadcast([128, 4, 128]), False, True, [b_c2, b_dt, b_Ebc[k]], [b_seg],
                        signal=True, skip=True)
                self.act(E[k], pseg[:, :].rearrange("p (a b) -> p a b", a=4), AF.Exp, [b_seg], [b_E[k]])
                self.stt("dve", MT[k], E[k], 1.0, cbm[d][:, g, :][:, None, :].to_broadcast([128, 4, 128]), ALU.min, ALU.mult,
                         [b_E[k], b_cbm], [b_MT[k]])
                self.tt("pool", Cp[k], Ebc[k], CTc[li][:, g, :][:, None, :].to_broadcast([128, 4, 128]), ALU.mult,
                        [b_Ebc[k], L["CTc"]], [b_Cp[k]])

            def ymm(it):
                d, q = it // 4, it % 4
                k = it % 3
                g = q // 2
                for hh in range(4):
                    h = 4 * q + hh
                    hl = h % 8
                    yps = self.ps[4 + g][:, hl * 64:(hl + 1) * 64]
                    hsrc = hb[:, h * 64:(h + 1) * 64] if d == 0 else hstore[:, c, h * 64:(h + 1) * 64]
                    hbuf = b_hb if d == 0 else b_hst[c]
                    self.mm(yps, MT[k][:, hh, :], xdt[d][:, h, :], False, False, [b_MT[k], b_xdt[d]], [self.psb[4 + g]], signal=False)
                    self.mm(yps, Cp[k][:, hh, :], hsrc, False, (d == 1), [b_Cp[k], hbuf], [self.psb[4 + g]], signal=(hh == 3))

            if "ssd_seq" in self.debug:
                for it in range(8):
                    mm1(it)
                    mm2(it)
                    ymm(it)
            else:
                for s_ in range(10):
                    if s_ < 8:
                        mm1(s_)
                    if 1 <= s_ <= 8:
                        mm2(s_ - 1)
                    if s_ >= 2:
                        ymm(s_ - 2)
                    if s_ == 3 and pending_fin:
                        pending_fin.pop(0)()
            state_step(0, li, prep=False)
            self.cp("act", hb, hf[0], [b_hf[0]], [b_hb])
            for g in range(2):
                self.tt("dve", yg[:, g * 512:(g + 1) * 512], self.ps[4 + g][:, :], zc[li][:, g * 512:(g + 1) * 512], ALU.mult,
                        [self.psb[4 + g], L["zc"]], [b_yg])
            self.memset("dve", ss, 0.0, [b_ss])
            self.act(ysq, yg, AF.Square, [b_yg, b_ss], [b_ss], accum_out=ss[:, 0:1])
            self.ts("dve", ss[:, 1:2], ss[:, 0:1], 1.0 / 1024, ALU.mult, [b_ss], [b_ss], s2=EPS, op1=ALU.add)
            self.act(ss[:, 1:2], ss[:, 1:2], AF.Sqrt, [b_ss], [b_ss])
            self.recip(ss[:, 1:2], ss[:, 1:2], [b_ss], [b_ss])
            self.stt("dve", yn, yg, ss[:, 1:2], normw_bc, ALU.mult, ALU.mult, [b_yg, b_ss, b_rows], [b_yn])
            def fin(c=c):
                for j in range(8):
                    self.tr(psT[:, j * 128:(j + 1) * 128], yn[:, j * 128:(j + 1) * 128], self.ident_b, [b_yn, self.b_cb], [b_psT],
                            signal=(j == 7))
                yi = (c // 4) % 2
                self.cp("act", ycst[yi][:, :, (c % 4) * 128:(c % 4 + 1) * 128], psT[:, :].rearrange("p (a b) -> p a b", a=8),
                        [b_psT], [b_ycst[yi]])
                if c % 4 == 3:
                    t0 = (c // 4) * 512
                    self.dma("sp", self.s_ycatT[:, 0:8, t0:t0 + 512], ycst[yi], reads=[b_ycst[yi]], writes=[self.b_ycat])
            pending_fin.append(fin)
        while pending_fin:
            pending_fin.pop(0)()
        if "ssd" in self.debug:
            d_ = self.dbg_tensor("ycatT", (128, 16, SEQ), BF16)
            self.dma("sp", d_, self.s_ycatT, reads=[self.b_ycat])
            d_ = self.dbg_tensor("hstore", (128, 32, 1024), BF16)
            self.dma("sp", d_, hstore, reads=b_hst)
            d_ = self.dbg_tensor("dtv", (128, 34, 32))
            self.dma("sp", d_, dtv, reads=[b_dt])


    def phase_na(self):
        A, S = self.A, self.S
        A.release(self.mark_base)
        b_scr = self.b_scr
        Kc = A.alloc([16, CTX], BF16)
        Vc = A.alloc([2, 16, 65], BF16)
        b_ctx = Buf("nactx")
        self.dma("sp", Kc[0:64].rearrange("p (c par) t -> p c par t", par=2), self.s_kT[:, :, :, SEQ:NTOK], reads=[b_scr["kT"]], writes=[b_ctx])
        self.dma("sp", Vc, self.s_vtok[SEQ:NTOK].rearrange("(b p) h e -> p b h e", p=128), reads=[b_scr["vtok"]], writes=[b_ctx])
        bias = A.alloc([16, 5, 128], F32)
        b_bias = Buf("nabias")
        Kwin = [A.alloc([16, 576], BF16) for _ in range(2)]
        Qp = [A.alloc([16, 128], BF16) for _ in range(2)]
        Vwin = [A.alloc([5, 16, 65], BF16) for _ in range(2)]
        b_win = [{k: Buf(k + str(i)) for k in ("K", "Q", "V")} for i in range(2)]
        Sw = [A.alloc([5, 128], F32) for _ in range(2)]
        b_Sw = [Buf("Sw0"), Buf("Sw1")]
        for i in range(2):
            self.memset("pool", Sw[i][64:128, 4, :], NEG, [b_Sw[i]])
        PT = [A.alloc([7, 128], BF16) for _ in range(3)]
        b_PT = [Buf("PT%d" % i) for i in range(3)]
        Otok = [A.alloc([16, 64], BF16) for _ in range(2)]
        b_Otok = [Buf("Ot0"), Buf("Ot1")]
        rinv = A.alloc([16], F32)
        b_rinv = Buf("rinv")
        ycst = [A.alloc([8, 512], BF16) for _ in range(2)]
        b_ycst = [Buf("nyc0"), Buf("nyc1")]
        psT, b_psT = self.ps[7][:, :].bitcast(BF16), self.psb[7]
        variants = {0: ((0, 0), (0, -1)), 1: ((0, -2), (0, -3)), 30: ((1, -5), (1, -6)), 31: ((1, -7), (1, -8))}
        interior = ((0, -4), (1, -5))
        cur_var = [None]

        def build_bias(var):
            self.memset("pool", bias, NEG, [b_bias])
            for rq in range(2):
                lo, off = var[rq]
                for ck in range(5):
                    js = [jr for jr in (2 * ck, 2 * ck + 1) if lo <= jr < lo + 8 and jr <= 8]
                    if not js:
                        continue
                    p0 = (js[0] % 2) * 64
                    dr0 = js[0] + off + 7
                    src = self.tblk[:, dr0:dr0 + len(js), :, :].rearrange("h j k c -> (j k) h c")
                    self.dma("sp", bias[p0:p0 + 64 * len(js), :, ck, rq * 64:(rq + 1) * 64], src, writes=[b_bias])

        hgroups = [(4, 0, 7), (5, 7, 14), (6, 14, 16)]
        hg_of = {}
        for (ob_, h0_, h1_) in hgroups:
            for h_ in range(h0_, h1_):
                hg_of[h_] = (ob_, h0_, h1_)
        cnt = {"sw": 0, "pt": 0, "st": 0}
        state = {}

        def emit_loads(rp):
            r = 2 * rp
            kb = min(max(r - 4, 0), 55)
            w0 = kb * 64
            wi = rp % 2
            W = b_win[wi]
            self.dma("sp", Kwin[wi][0:64].rearrange("p (c par) t -> p c par t", par=2), self.s_kT[:, :, :, w0:w0 + 576],
                     reads=[b_scr["kT"]], writes=[W["K"]])
            self.dma("sp", Qp[wi][0:64].rearrange("p (c par) t -> p c par t", par=2), self.s_qT[:, :, :, r * 64:r * 64 + 128],
                     reads=[b_scr["qT"]], writes=[W["Q"]])
            self.dma("sp", Vwin[wi][:, 0:4], self.s_vtok[w0:w0 + 512].rearrange("(b p) h e -> p b h e", p=128),
                     reads=[b_scr["vtok"]], writes=[W["V"]])
            self.dma("sp", Vwin[wi][0:64, 4], self.s_vtok[w0 + 512:w0 + 576], reads=[b_scr["vtok"]], writes=[W["V"]])

        def emit_st(rp, h):
            var = variants.get(rp, interior)
            if var != cur_var[0]:
                build_bias(var)
                cur_var[0] = var
            wi = rp % 2
            W = b_win[wi]
            si = cnt["st"] % 2
            cnt["st"] += 1
            psA, bA = self.ps[2 * si], self.psb[2 * si]
            psB, bB = self.ps[2 * si + 1], self.psb[2 * si + 1]
            q = Qp[wi][0:64, h, :]
            for ck in range(4):
                self.mm(psA[:, ck * 128:(ck + 1) * 128], Kwin[wi][0:64, h, ck * 128:(ck + 1) * 128], q, True, True,
                        [W["K"], W["Q"]], [bA], signal=(ck == 3))
            self.mm(psB[0:64, 0:128], Kwin[wi][0:64, h, 512:576], q, True, True, [W["K"], W["Q"]], [bB], signal=False)
            self.mm(psB[:, 128:256], Kc[0:64, h, 0:128], q, True, True, [b_ctx, W["Q"]], [bB], signal=False)
            self.mm(psB[:, 256:384], Kc[0:64, h, 128:256], q, True, True, [b_ctx, W["Q"]], [bB], signal=True)
            wi2 = cnt["sw"] % 2
            cnt["sw"] += 1
            sw, bsw = Sw[wi2], b_Sw[wi2]
            self.tt("dve", sw[:, 0:4, :], psA[:, :].rearrange("p (a b) -> p a b", a=4), bias[:, h, 0:4, :], ALU.add,
                    [bA, b_bias], [bsw])
            self.tt("dve", sw[0:64, 4, :], psB[0:64, 0:128], bias[0:64, h, 4, :], ALU.add, [bB, b_bias], [bsw])
            pi = cnt["pt"] % 3
            cnt["pt"] += 1
            pt, bpt = PT[pi], b_PT[pi]
            self.act(pt[:, 0:5, :], sw, AF.Exp, [bsw], [bpt])
            self.act(pt[:, 5:7, :], psB[:, 128:384].rearrange("p (a b) -> p a b", a=2), AF.Exp, [bB], [bpt])
            state[(rp, h)] = (pt, bpt)

        def emit_pv(rp, h):
            wi = rp % 2
            W = b_win[wi]
            pt, bpt = state.pop((rp, h))
            obank, h0, h1 = hg_of[h]
            ops, bops = self.ps[obank], self.psb[obank]
            ot, bot = Otok[rp % 2], b_Otok[rp % 2]
            o_ap = ops[:, (h - h0) * 65:(h - h0 + 1) * 65]
            for ck in range(7):
                M = 64 if ck == 4 else 128
                v = Vwin[wi][0:M, ck, h, :] if ck < 5 else Vc[:, ck - 5, h, :]
                vb = W["V"] if ck < 5 else b_ctx
                self.mm(o_ap, pt[0:M, ck, :], v, ck == 0, ck == 6, [bpt, vb], [bops], signal=(ck == 6))
            if h == h1 - 1:
                nh = h1 - h0
                o3 = ops[:, 0:nh * 65].rearrange("p (h e) -> p h e", h=nh)
                self.recip(rinv[:, h0:h1], o3[:, :, 64], [bops], [b_rinv])
                self.tt("dve", ot[:, h0:h1, :], o3[:, :, 0:64], rinv[:, h0:h1][:, :, None].to_broadcast([128, nh, 64]), ALU.mult,
                        [bops, b_rinv], [bot])
            if h == 15:
                otf = ot.rearrange("p h d -> p (h d)")
                for j in range(8):
                    self.tr(psT[:, j * 128:(j + 1) * 128], otf[:, j * 128:(j + 1) * 128], self.ident_b, [bot, self.b_cb], [b_psT],
                            signal=(j == 7))
                yi = (rp // 4) % 2
                self.cp("act", ycst[yi][:, :, (rp % 4) * 128:(rp % 4 + 1) * 128], psT[:, :].rearrange("p (a b) -> p a b", a=8),
                        [b_psT], [b_ycst[yi]])
                if rp % 4 == 3:
                    t0 = (rp // 4) * 512
                    self.dma("sp", self.s_ycatT[:, 8:16, t0:t0 + 512], ycst[yi], reads=[b_ycst[yi]], writes=[self.b_ycat])

        seq = [(rp, h) for rp in range(32) for h in range(16)]
        emit_loads(0)
        emit_st(*seq[0])
        for i, (rp, h) in enumerate(seq):
            if h == 0 and rp + 1 < 32:
                emit_loads(rp + 1)
            if i + 1 < len(seq):
                emit_st(*seq[i + 1])
            emit_pv(rp, h)
        if "na" in self.debug:
            d_ = self.dbg_tensor("ycatT2", (128, 16, SEQ), BF16)
            self.dma("sp", d_, self.s_ycatT, reads=[self.b_ycat])


    def phase_tail(self):
        A, S = self.A, self.S
        A.release(self.mark_base)
        NW = 4
        wb = [A.alloc([4096], BF16) for _ in range(NW)]
        b_wb = [Buf("tw%d" % i) for i in range(NW)]
        R1 = A.alloc([32, 512], BF16)
        b_R1 = Buf("R1")
        hid = R1
        ycat = R1[:, 0:16, :]
        ob = R1.rearrange("p a b -> p (a b)")[:, 0:8192].bitcast(F32).rearrange("p (a b) -> p a b", a=8)
        h = A.alloc([8, 512], BF16)
        b_h = [Buf("h%d" % i) for i in range(8)]
        sets = []
        for i in range(2):
            sets.append({"x": A.alloc([8, 512], F32), "bx": Buf("sx%d" % i), "bxm": [Buf("sx%d_%d" % (i, m)) for m in range(8)], "u": A.alloc([8, 514], F32), "bu": Buf("su%d" % i),
                         "gb": A.alloc([8, 512], BF16), "bgb": Buf("sg%d" % i)})
        gct = [A.alloc([512], F32) for _ in range(2)]
        b_gct = [Buf("gct0"), Buf("gct1")]
        rl = [A.alloc([512], F32) for _ in range(2)]
        b_rl = [Buf("rl0"), Buf("rl1")]
        cva = [A.alloc([512], F32) for _ in range(2)]
        b_cva = [Buf("cva0"), Buf("cva1")]
        gg = A.alloc([8, 512], BF16)
        b_gg = Buf("gg")
        wk = {"sq": [A.alloc([8, 512], BF16)] * 2, "b_sq": [Buf("sq")] * 2,
              "rstd": [A.alloc([512], F32) for _ in range(2)], "b_rstd": [Buf("rs0"), Buf("rs1")],
              "tmp": [A.alloc([512], F32) for _ in range(3)], "b_tmp": [Buf("nt%d" % i) for i in range(3)], "n": 0,
              "b_sqc": [Buf("sqc%d" % m) for m in range(8)]}
        cnt = {"w": 0, "ps": 0, "gct": 0, "rl": 0, "cva": 0, "norm": 0}

        pref = {}

        def prefetch(key, idx, kc):
            pref[(key, idx)] = wload(key, idx, kc)

        def wload(key, idx, kc):
            if (key, idx) in pref:
                return pref.pop((key, idx))
            i = cnt["w"] % NW
            cnt["w"] += 1
            v = wb[i].rearrange("p (k n) -> p k n", k=kc)
            src, bsrc = self.wsrc(key, idx, kc)
            self.dma("sp", v, src, reads=[bsrc], writes=[b_wb[i]])
            return v, b_wb[i]

        def nextps():
            pi = cnt["ps"] % 4
            cnt["ps"] += 1
            return self.ps[pi], self.psb[pi]

        def resid_evac(ps, bps, xs, m, gate_col, bgate):
            bxm = xs["bxm"][m]
            flush()
            self.stt("dve", xs["x"][:, m, :], ps[:, :], gate_col, xs["x"][:, m, :], ALU.mult, ALU.add, [bps, bgate, bxm], [bxm])
            pend.append(self.norm_accum(xs["x"], bxm, m, wk, cnt["norm"], 4))

        pend = []

        def flush():
            while pend:
                pend.pop(0)()

        def norm(xs, g, bg, shift, bshift, out, bout, final=False):
            flush()
            self.norm_tile(xs["x"], xs["bxm"], 512, g, bg, shift, bshift, out, bout, wk, cnt["norm"], psi=4, final=final,
                           accum_done=True)
            cnt["norm"] += 1

        def mlp(l, xs):
            gate = self.modv[l][:, 40:48]
            bgate = self.b_modv[l][5]
            for grp in range(8):
                wt, bw = wload("w1", (l, grp), 8)
                if grp == 0:
                    pss = [nextps() for _ in range(4)]
                    for kc in range(8):
                        for jj in range(4):
                            self.mm(pss[jj][0][:, :], wt[:, kc, jj * 128:(jj + 1) * 128], h[:, kc, :], kc == 0, kc == 7, [bw, b_h[kc]],
                                    [pss[jj][1]], signal=(kc == 7))
                for jj in range(4):
                    mh = grp * 4 + jj
                    if grp == 0:
                        ps, bps = pss[jj]
                    else:
                        ps, bps = nextps()
                        for kc in range(8):
                            self.mm(ps[:, :], wt[:, kc, jj * 128:(jj + 1) * 128], h[:, kc, :], kc == 0, kc == 7, [bw, b_h[kc]], [bps], signal=(kc == 7))
                    ri = cnt["rl"] % 2
                    cnt["rl"] += 1
                    self.act(rl[ri], ps[:, :], AF.Relu, [bps], [b_rl[ri]])
                    self.tt("pool", hid[:, mh, :], rl[ri], rl[ri], ALU.mult, [b_rl[ri]], [b_R1])
            for m in range(8):
                wt, bw = wload("w2", (l, m), 32)
                ps, bps = nextps()
                for kc in range(32):
                    self.mm(ps[:, :], wt[:, kc, :], hid[:, kc, :], kc == 0, kc == 31, [bw, b_R1], [bps], signal=(kc == 31))
                resid_evac(ps, bps, xs, m, gate[:, m:m + 1], bgate)

        def conv_b(xs, m):
            ci = cnt["cva"] % 2
            cnt["cva"] += 1
            cv_, bcv = cva[ci], b_cva[ci]
            self.act(cv_, xs["u"][:, m, 0:512], AF.Identity, [xs["bu"], self.b_cols], [bcv], scale=self.colv("scw0", m))
            self.stt("dve", cv_, xs["u"][:, m, 1:513], self.colv("scw1", m), cv_, ALU.mult, ALU.add, [xs["bu"], self.b_cols, bcv], [bcv])
            self.stt("dve", cv_, xs["u"][:, m, 2:514], self.colv("scw2", m), cv_, ALU.mult, ALU.add, [xs["bu"], self.b_cols, bcv], [bcv])
            self.tt("pool", gg[:, m, :], cv_, xs["gb"][:, m, :], ALU.mult, [bcv, xs["bgb"]], [b_gg])

        def load_ycat(t):
            self.dma("sp", ycat, self.s_ycatT[:, :, t * 512:(t + 1) * 512], reads=[self.b_ycat], writes=[b_R1])

        def stage_a(t, xs, prev):
            t0 = t * 512
            if t <= 1:
                load_ycat(t)
            self.dma("sp", xs["x"], self.xT[:, :, t0:t0 + 512], writes=xs["bxm"])
            for grp in range(4):
                wt, bw = wload("out0", (grp,), 16)
                for jj in range(2):
                    m = grp * 2 + jj
                    ps, bps = nextps()
                    for kc in range(16):
                        self.mm(ps[:, :], wt[:, kc, jj * 128:(jj + 1) * 128], ycat[:, kc, :], kc == 0, kc == 15, [bw, b_R1], [bps],
                                signal=(kc == 15))
                    resid_evac(ps, bps, xs, m, self.modv[0][:, 16 + m:17 + m], self.b_modv[0][2])
            norm(xs, self.g_f0, self.b_gf0, self.modv[0][:, 24:32], self.b_modv[0][3], h, b_h)
            mlp(0, xs)
            norm(xs, self.g_a1, self.b_ga1, self.modv[1][:, 0:8], self.b_modv[1][0], h, b_h)
            for grp in range(2):
                wt, bw = wload("scin", (grp,), 8)
                for jj in range(4):
                    m = grp * 4 + jj
                    ps, bps = nextps()
                    for kc in range(8):
                        self.mm(ps[:, :], wt[:, kc, jj * 128:(jj + 1) * 128], h[:, kc, :], kc == 0, kc == 7, [bw, b_h[kc]], [bps], signal=(kc == 7))
                    self.cp("act", xs["gb"][:, m, :], ps[:, :], [bps], [xs["bgb"]])
            for half in range(2):
                wc, bwc = wload("scin", (2 + half,), 8)
                wv, bwv = wload("scin", (4 + half,), 8)
                for jj in range(4):
                    m = half * 4 + jj
                    ps, bps = nextps()
                    for kc in range(8):
                        self.mm(ps[:, :], wc[:, kc, jj * 128:(jj + 1) * 128], h[:, kc, :], kc == 0, kc == 7, [bwc, b_h[kc]], [bps], signal=(kc == 7))
                    gi = cnt["gct"] % 2
                    cnt["gct"] += 1
                    self.cp("act", gct[gi], ps[:, :], [bps], [b_gct[gi]])
                    ps2, bps2 = nextps()
                    for kc in range(8):
                        self.mm(ps2[:, :], wv[:, kc, jj * 128:(jj + 1) * 128], h[:, kc, :], kc == 0, kc == 7, [bwv, b_h[kc]], [bps2], signal=(kc == 7))
                    self.tt("dve", xs["u"][:, m, 1:513], ps2[:, :], gct[gi], ALU.mult, [bps2, b_gct[gi]], [xs["bu"]])
                    if t == 0:
                        self.memset("pool", xs["u"][:, m, 0:1], 0.0, [xs["bu"]])
                    else:
                        self.cp("pool", xs["u"][:, m, 0:1], prev["u"][:, m, 512:513], [prev["bu"]], [xs["bu"]])
                        self.cp("pool", prev["u"][:, m, 513:514], xs["u"][:, m, 1:2], [xs["bu"]], [prev["bu"]])
                        conv_b(prev, m)
                    if t == 7:
                        self.memset("pool", xs["u"][:, m, 513:514], 0.0, [xs["bu"]])

        def stage_b(t, xs, do_conv):
            t0 = t * 512
            if do_conv:
                for m in range(8):
                    conv_b(xs, m)
            for grp in range(2):
                wt, bw = wload("scout", (grp,), 8)
                for jj in range(4):
                    m = grp * 4 + jj
                    ps, bps = nextps()
                    for kc in range(8):
                        self.mm(ps[:, :], wt[:, kc, jj * 128:(jj + 1) * 128], gg[:, kc, :], kc == 0, kc == 7, [bw, b_gg], [bps], signal=(kc == 7))
                    resid_evac(ps, bps, xs, m, self.modv[1][:, 16 + m:17 + m], self.b_modv[1][2])
            norm(xs, self.g_f1, self.b_gf1, self.modv[1][:, 24:32], self.b_modv[1][3], h, b_h)
            mlp(1, xs)
            if t + 2 < 8:
                load_ycat(t + 2)
                for grp in range(3):
                    prefetch("out0", (grp,), 16)
            norm(xs, self.colv("fnw"), self.b_cols, None, None, xs["x"], xs["bxm"], final=True)
            self.dma("sp", self.outT[:, :, t0:t0 + 512], xs["x"], reads=xs["bxm"])

        for t in range(9):
            cur = sets[t % 2]
            prev = sets[(t - 1) % 2]
            if t < 8:
                stage_a(t, cur, prev)
            if t >= 1:
                stage_b(t - 1, prev, do_conv=(t == 8))


def build_program(stop_after=None, debug=()):
    p = Prog(stop_after, debug)
    p.build()
    return p


_CACHE = {}


def kernel(**inputs):
    shared, per_core = prep_inputs(inputs)
    p = build_program()
    in_maps = []
    for b in range(8):
        m = dict(shared)
        m.update(per_core[b])
        in_maps.append(m)
    res = run_bass_kernel_spmd(p.nc, in_maps, core_ids=list(range(8)))
    out = np.empty((8, SEQ, D), np.float32)
    for b in range(8):
        oT = np.asarray(res.results[b]["outT"])
        out[b] = oT.transpose(2, 1, 0).reshape(SEQ, D)
    return out
```
